# Optimizing a Trainium2 kernel written in Bass

```python
import math
import jax, jax.numpy as jnp
from jax import lax
import numpy as np

D_MODEL = 1024
BATCH = 8
SEQ = 2048
DEPTH = 2

EPS = 1e-6
MEM_LEN = 256
N_BRANCH = 4

A_WIDTH = 256
A_GROUPS = 4
A_GROUP_DIM = A_WIDTH // A_GROUPS
A_CHUNK = 128

B_HEADS = 8
B_NOPE = 64
B_ROPE = 32
B_QK_DIM = B_NOPE + B_ROPE
B_VDIM = 64
B_Q_RANK = 768
B_KV_RANK = 256
B_WIDTH = B_HEADS * B_VDIM
ATTN_BLOCK = 128
ROPE_THETA = 10000.0

C_WIDTH = 256
C_GROUP = 16
C_GROUPS = C_WIDTH // C_GROUP
C_STATE = 64

M_HEADS = 4
M_HEAD_DIM = 64
M_WIDTH = M_HEADS * M_HEAD_DIM

IN_SIZES = (A_WIDTH, A_WIDTH, A_WIDTH,
            B_Q_RANK, B_KV_RANK, B_ROPE, B_WIDTH,
            C_WIDTH, C_WIDTH,
            M_WIDTH, M_WIDTH,
            N_BRANCH * D_MODEL)
IN_WIDTH = sum(IN_SIZES)

kernel_name = "hybrid_gated_gmlp_mla_s5_memxattn"


def rms_norm(x, g):
    xf = x.astype(jnp.float32)
    y = xf * lax.rsqrt(jnp.mean(xf * xf, axis=-1, keepdims=True) + EPS)
    return (y * g.astype(jnp.float32)).astype(x.dtype)


def apply_rope(x, positions):
    half = x.shape[-1] // 2
    inv_freq = ROPE_THETA ** (-jnp.arange(half, dtype=jnp.float32) / half)
    ang = positions.astype(jnp.float32)[..., None] * inv_freq
    cos = jnp.cos(ang)[:, :, None, :]
    sin = jnp.sin(ang)[:, :, None, :]
    xf = x.astype(jnp.float32)
    x1, x2 = xf[..., :half], xf[..., half:]
    out = jnp.concatenate([x1 * cos - x2 * sin, x2 * cos + x1 * sin], axis=-1)
    return out.astype(x.dtype)


def chunked_spatial_gating(u, v, g_v, w_s, b_s):
    bsz, seq, _ = v.shape
    n_chunks = seq // A_CHUNK
    v = rms_norm(v, g_v)
    vc = v.reshape(bsz, n_chunks, A_CHUNK, A_GROUPS, A_GROUP_DIM)
    causal = jnp.tril(jnp.ones((A_CHUNK, A_CHUNK), dtype=bool))
    w = jnp.where(causal, w_s, jnp.zeros((), w_s.dtype))
    s = jnp.einsum('gts,bnsgc->bntgc', w, vc) + b_s.T[None, None, :, :, None]
    return u * s.reshape(bsz, seq, A_WIDTH)


def causal_block_attention(q, k, v):
    bsz, seq, n_heads, dqk = q.shape
    dv = v.shape[-1]
    n_blocks = seq // ATTN_BLOCK
    scale = dqk ** -0.5
    qb = q.reshape(bsz, n_blocks, ATTN_BLOCK, n_heads, dqk).transpose(1, 0, 3, 2, 4)
    key_pos = jnp.arange(seq)

    def one_block(args):
        q_blk, blk = args
        s = jnp.einsum('bhtd,bshd->bhts', q_blk, k).astype(jnp.float32) * scale
        query_pos = blk * ATTN_BLOCK + jnp.arange(ATTN_BLOCK)
        s = jnp.where(key_pos[None, :] <= query_pos[:, None], s, -jnp.inf)
        p = jax.nn.softmax(s, axis=-1).astype(v.dtype)
        return jnp.einsum('bhts,bshd->bthd', p, v)

    out = lax.map(one_block, (qb, jnp.arange(n_blocks)))
    return out.transpose(1, 0, 2, 3, 4).reshape(bsz, seq, n_heads * dv)


def latent_attention(c_q, c_kv, k_rope, positions, q_norm_g, kv_norm_g, w_uq, w_ukv,
                     qk_g_q, qk_g_k):
    bsz, seq, _ = c_q.shape
    q = (rms_norm(c_q, q_norm_g) @ w_uq).reshape(bsz, seq, B_HEADS, B_QK_DIM)
    kv = (rms_norm(c_kv, kv_norm_g) @ w_ukv).reshape(bsz, seq, B_HEADS, B_NOPE + B_VDIM)
    k_nope, v = kv[..., :B_NOPE], kv[..., B_NOPE:]
    k_pe = jnp.broadcast_to(k_rope[:, :, None, :], (bsz, seq, B_HEADS, B_ROPE))
    k = jnp.concatenate([k_nope, k_pe], axis=-1)
    q = rms_norm(q, qk_g_q)
    k = rms_norm(k, qk_g_k)
    q = jnp.concatenate([q[..., :B_NOPE], apply_rope(q[..., B_NOPE:], positions)], axis=-1)
    k = jnp.concatenate([k[..., :B_NOPE], apply_rope(k[..., B_NOPE:], positions)], axis=-1)
    return causal_block_attention(q, k, v)


def s5_layer(u, a_re, a_im, log_dt, b_re, b_im, c_re, c_im, d_skip, w_glu, b_glu):
    bsz, seq, _ = u.shape
    f32 = jnp.float32
    uf = u.astype(f32).reshape(bsz, seq, C_GROUPS, C_GROUP)
    lam = lax.complex(a_re.astype(f32), a_im.astype(f32))
    dt = jnp.exp(log_dt.astype(f32))[:, None]
    a_bar = jnp.exp(lam * dt)
    b_mat = lax.complex(b_re.astype(f32), b_im.astype(f32))
    b_bar = ((a_bar - 1.0) / lam)[..., None] * b_mat
    bu = jnp.einsum('gpc,bsgc->bsgp', b_bar, uf.astype(jnp.complex64))
    a_elems = jnp.broadcast_to(a_bar, bu.shape)

    def combine(left, right):
        a_l, b_l = left
        a_r, b_r = right
        return a_r * a_l, a_r * b_l + b_r

    _, states = lax.associative_scan(combine, (a_elems, bu), axis=1)
    c_mat = lax.complex(c_re.astype(f32), c_im.astype(f32))
    y = jnp.einsum('gcp,bsgp->bsgc', c_mat, states).real
    y = y + d_skip.astype(f32).reshape(C_GROUPS, C_GROUP) * uf
    y = jax.nn.gelu(y.reshape(bsz, seq, C_WIDTH))
    y = y * jax.nn.sigmoid(y @ w_glu.astype(f32) + b_glu.astype(f32))
    return y.astype(u.dtype)


def memory_attention(q, mem_h, w_kv, qk_g_q, qk_g_k):
    bsz, seq, _ = q.shape
    kv = mem_h @ w_kv
    k = kv[..., :M_WIDTH].reshape(bsz, -1, M_HEADS, M_HEAD_DIM)
    v = kv[..., M_WIDTH:].reshape(bsz, -1, M_HEADS, M_HEAD_DIM)
    q = rms_norm(q.reshape(bsz, seq, M_HEADS, M_HEAD_DIM), qk_g_q)
    k = rms_norm(k, qk_g_k)
    s = jnp.einsum('bshd,bmhd->bhsm', q, k).astype(jnp.float32) * (M_HEAD_DIM ** -0.5)
    p = jax.nn.softmax(s, axis=-1).astype(v.dtype)
    return jnp.einsum('bhsm,bmhd->bshd', p, v).reshape(bsz, seq, M_WIDTH)


def hybrid_layer(x, mem, positions, norm_g, w_in, b_merge, a_norm_g, a_w_s, a_b_s,
                 b_q_norm_g, b_kv_norm_g, b_w_uq, b_w_ukv, b_qk_g_q, b_qk_g_k,
                 c_a_re, c_a_im, c_log_dt, c_b_re, c_b_im, c_c_re, c_c_im, c_d,
                 c_w_glu, c_b_glu, m_norm_g, m_w_kv, m_qk_g_q, m_qk_g_k,
                 w_br_a, w_br_b, w_br_c, w_br_m, w_out):
    bsz, seq, _ = x.shape
    h = rms_norm(x, norm_g)
    z = h @ w_in
    splits = [int(s) for s in np.cumsum(IN_SIZES)[:-1]]
    (a_u, a_v, a_gate, b_cq, b_ckv, b_krope, b_gate, c_in, c_gate,
     m_q, m_gate, merge_logits) = jnp.split(z, splits, axis=-1)

    y_a = chunked_spatial_gating(jax.nn.gelu(a_u), jax.nn.gelu(a_v), a_norm_g, a_w_s, a_b_s)
    y_a = y_a * jax.nn.silu(a_gate)
    y_b = latent_attention(b_cq, b_ckv, b_krope, positions, b_q_norm_g, b_kv_norm_g,
                           b_w_uq, b_w_ukv, b_qk_g_q, b_qk_g_k) * jax.nn.silu(b_gate)
    y_c = s5_layer(c_in, c_a_re, c_a_im, c_log_dt, c_b_re, c_b_im, c_c_re, c_c_im,
                   c_d, c_w_glu, c_b_glu) * jax.nn.silu(c_gate)
    y_m = memory_attention(m_q, rms_norm(mem, m_norm_g), m_w_kv, m_qk_g_q, m_qk_g_k)
    y_m = y_m * jax.nn.silu(m_gate)

    gates = jax.nn.sigmoid(merge_logits + b_merge).reshape(bsz, seq, N_BRANCH, D_MODEL)
    merged = (gates[:, :, 0] * (y_a @ w_br_a) + gates[:, :, 1] * (y_b @ w_br_b)
              + gates[:, :, 2] * (y_c @ w_br_c) + gates[:, :, 3] * (y_m @ w_br_m))
    return x + merged @ w_out


def setup_inputs(seed: int = 0) -> dict:
    key = jax.random.key(seed)
    ks = iter(jax.random.split(key, 40))
    L, D = DEPTH, D_MODEL

    def nrm(shape, scale):
        return jax.random.normal(next(ks), shape, jnp.float32) * scale

    def gain(shape):
        return 1.0 + nrm(shape, 0.02)

    x = nrm((BATCH, SEQ, D), 1.0)
    mem = nrm((BATCH, MEM_LEN, D), 1.0)
    offsets = jax.random.randint(next(ks), (BATCH, 1), 0, 4096, dtype=jnp.int32)
    positions = jnp.arange(SEQ, dtype=jnp.int32)[None, :] + offsets

    state_idx = jnp.arange(C_STATE, dtype=jnp.float32)
    return {
        "x": x,
        "mem": mem,
        "positions": positions,
        "norm_g": gain((L, D)),
        "w_in": nrm((L, D, IN_WIDTH), D ** -0.5),
        "b_merge": nrm((L, N_BRANCH * D), 0.02),
        "a_norm_g": gain((L, A_WIDTH)),
        "a_w_s": nrm((L, A_GROUPS, A_CHUNK, A_CHUNK), A_CHUNK ** -0.5),
        "a_b_s": 1.0 + nrm((L, A_GROUPS, A_CHUNK), 0.1),
        "b_q_norm_g": gain((L, B_Q_RANK)),
        "b_kv_norm_g": gain((L, B_KV_RANK)),
        "b_w_uq": nrm((L, B_Q_RANK, B_HEADS * B_QK_DIM), B_Q_RANK ** -0.5),
        "b_w_ukv": nrm((L, B_KV_RANK, B_HEADS * (B_NOPE + B_VDIM)), B_KV_RANK ** -0.5),
        "b_qk_g_q": gain((L, B_QK_DIM)),
        "b_qk_g_k": gain((L, B_QK_DIM)),
        "c_a_re": -0.5 + nrm((L, C_GROUPS, C_STATE), 0.01),
        "c_a_im": jnp.broadcast_to(math.pi * state_idx, (L, C_GROUPS, C_STATE)),
        "c_log_dt": jax.random.uniform(next(ks), (L, C_GROUPS), jnp.float32,
                                       math.log(1e-3), math.log(1e-1)),
        "c_b_re": nrm((L, C_GROUPS, C_STATE, C_GROUP), (2.0 * C_GROUP) ** -0.5),
        "c_b_im": nrm((L, C_GROUPS, C_STATE, C_GROUP), (2.0 * C_GROUP) ** -0.5),
        "c_c_re": nrm((L, C_GROUPS, C_GROUP, C_STATE), (2.0 * C_STATE) ** -0.5),
        "c_c_im": nrm((L, C_GROUPS, C_GROUP, C_STATE), (2.0 * C_STATE) ** -0.5),
        "c_d": nrm((L, C_WIDTH), 1.0),
        "c_w_glu": nrm((L, C_WIDTH, C_WIDTH), C_WIDTH ** -0.5),
        "c_b_glu": nrm((L, C_WIDTH), 0.02),
        "m_norm_g": gain((L, D)),
        "m_w_kv": nrm((L, D, 2 * M_WIDTH), D ** -0.5),
        "m_qk_g_q": gain((L, M_HEAD_DIM)),
        "m_qk_g_k": gain((L, M_HEAD_DIM)),
        "w_br_a": nrm((L, A_WIDTH, D), A_WIDTH ** -0.5),
        "w_br_b": nrm((L, B_WIDTH, D), B_WIDTH ** -0.5),
        "w_br_c": nrm((L, C_WIDTH, D), C_WIDTH ** -0.5),
        "w_br_m": nrm((L, M_WIDTH, D), M_WIDTH ** -0.5),
        "w_out": nrm((L, D, D), D ** -0.5),
    }


def reference(x, mem, positions, norm_g, w_in, b_merge, a_norm_g, a_w_s, a_b_s,
              b_q_norm_g, b_kv_norm_g, b_w_uq, b_w_ukv, b_qk_g_q, b_qk_g_k,
              c_a_re, c_a_im, c_log_dt, c_b_re, c_b_im, c_c_re, c_c_im, c_d,
              c_w_glu, c_b_glu, m_norm_g, m_w_kv, m_qk_g_q, m_qk_g_k,
              w_br_a, w_br_b, w_br_c, w_br_m, w_out):
    for l in range(DEPTH):
        x = hybrid_layer(x, mem, positions, norm_g[l], w_in[l], b_merge[l],
                         a_norm_g[l], a_w_s[l], a_b_s[l],
                         b_q_norm_g[l], b_kv_norm_g[l], b_w_uq[l], b_w_ukv[l],
                         b_qk_g_q[l], b_qk_g_k[l],
                         c_a_re[l], c_a_im[l], c_log_dt[l], c_b_re[l], c_b_im[l],
                         c_c_re[l], c_c_im[l], c_d[l], c_w_glu[l], c_b_glu[l],
                         m_norm_g[l], m_w_kv[l], m_qk_g_q[l], m_qk_g_k[l],
                         w_br_a[l], w_br_b[l], w_br_c[l], w_br_m[l], w_out[l])
    return x
```

```python
import math
import contextlib
import numpy as np
import concourse.bass as bass
import concourse.mybir as mybir
from concourse.bass_utils import run_bass_kernel_spmd

F32 = mybir.dt.float32
BF16 = mybir.dt.bfloat16
I32 = mybir.dt.int32
AF = mybir.ActivationFunctionType
ALU = mybir.AluOpType

D = 1024
T = 2048
L = 2
NB = 4
EPS = 1e-6
IN_W = 7456
OFF = dict(a_u=0, a_v=256, a_g=512, cq=768, ckv=1536, kr=1792, bg=1824,
           cin=2336, cg=2592, mq=2848, mg=3104, mrg=3360)
PERM = list(range(16, 32)) + list(range(0, 16))
WIN_C0 = [0, 256, 512, 768, 1024, 1280, 1536, 1824, 2080, 2336, 2592, 2848, 3104] + \
         [3360 + br * 1024 + j2 * 256 for br in range(4) for j2 in range(4)]
WIN_G = {c: i for i, c in enumerate(WIN_C0)}
NWG = len(WIN_C0)
TWO_PI = 2.0 * math.pi

C_NG, C_MNG, C_QNG, C_KVNG, C_BM = 0, 8, 16, 22, 24
C_GQ, C_GQP, C_GK, C_GKP, C_MGQ, C_MGK = 56, 57, 58, 59, 60, 61
C_CD, C_BGLU, C_SRE, C_SIM, C_SDT, C_ABS, C_ROW = 62, 64, 66, 74, 82, 90, 346
NCOL = 346 + 640
R_ANG, R_SRE, R_SIM, R_SDT = 0, 256, 1280, 2304
NROW = 3328
K_TRI, K_IOT, K_IOS, K_INVF, K_SGN, K_GM = 0, 128, 256, 257, 258, 259
NCONST = 267


class Tok:
    __slots__ = ("w", "rs", "excl")

    def __init__(self):
        self.w = None
        self.rs = {}
        self.excl = False


class Op:
    __slots__ = ("eng", "fn", "deps", "is_dma", "sig", "users")

    def __init__(self, eng, fn, is_dma):
        self.eng = eng
        self.fn = fn
        self.is_dma = is_dma
        self.deps = []
        self.sig = None
        self.users = 0


ENGS = ("pe", "act", "dve", "pool", "sp")
SYNC_ALL = True
WQ = ("sp", "act")
DMAQ = ("sp", "act", "pool")


class Prog:
    def __init__(self, nc, n_dma_sems=8):
        self.nc = nc
        self.ops = {e: [] for e in ENGS}
        self.all = []
        self.n_dma_sems = n_dma_sems
        self.toks = {}
        self.dmas_since_bar = []

    def tok(self, *key):
        t = self.toks.get(key)
        if t is None:
            t = self.toks[key] = Tok()
        return t

    def _add(self, eng, fn, reads, writes, is_dma, extra_deps=()):
        op = Op(eng, fn, is_dma)
        deps = list(extra_deps)
        raw = set()
        for t in reads:
            if t.w is not None:
                deps.append(t.w)
                raw.add(id(t.w))
            if t.excl:
                deps.extend(o for o in t.rs.values() if o.eng != eng)
        for t in writes:
            if t.w is not None:
                deps.append(t.w)
            deps.extend(t.rs.values())
        rkey = ("dma", id(op)) if is_dma else eng
        for t in reads:
            t.rs[rkey] = op
        for t in writes:
            t.w = op
            t.rs = {}
        seen = set()
        for d in deps:
            if d is op or id(d) in seen:
                continue
            seen.add(id(d))
            if (not d.is_dma) and (not is_dma) and d.eng == eng:
                if eng == "pe" or (id(d) not in raw and not SYNC_ALL):
                    continue
            if (not d.is_dma) and is_dma and d.eng == eng:
                pass
            op.deps.append(d)
            d.users += 1
        self.ops[eng].append(op)
        self.all.append(op)
        if is_dma:
            self.dmas_since_bar.append(op)
        return op

    def op(self, eng, fn, reads=(), writes=()):
        return self._add(eng, fn, reads, writes, False)

    def dma(self, eng, fn, reads=(), writes=()):
        assert eng in DMAQ
        return self._add(eng, fn, reads, writes, True)

    def barrier(self):
        dm = self.dmas_since_bar
        self.dmas_since_bar = []
        bt = [self.tok("__bar", e) for e in ENGS]
        for i, e in enumerate(ENGS):
            self._add(e, lambda eng: eng.drain(), [], [bt[i]], False,
                      extra_deps=dm if e == "sp" else ())
        for e in ENGS:
            self._add(e, lambda eng: eng.nop(), bt, [], False)

    def emit(self, final_wait_ops=()):
        nc = self.nc
        with contextlib.ExitStack() as es:
            esem = {e: es.enter_context(nc.semaphore("s_" + e)) for e in ENGS}
            dsem = {e: [es.enter_context(nc.semaphore("d_%s%d" % (e, i)))
                        for i in range(self.n_dma_sems)] for e in DMAQ}
            ecount = {e: 0 for e in ENGS}
            dcount = {e: 0 for e in DMAQ}
            duse = {e: [0] * self.n_dma_sems for e in DMAQ}
            fw = set(id(o) for o in final_wait_ops)
            for op in self.all:
                if op.is_dma:
                    j = dcount[op.eng]
                    dcount[op.eng] += 1
                    s = j % self.n_dma_sems
                    duse[op.eng][s] += 1
                    op.sig = (dsem[op.eng][s], 16 * duse[op.eng][s], ("d", op.eng, s))
                elif op.users > 0 or id(op) in fw:
                    ecount[op.eng] += 1
                    op.sig = (esem[op.eng], ecount[op.eng], ("e", op.eng))
            self.stats = dict(ecount=ecount, dcount=dcount,
                              nops={e: len(self.ops[e]) for e in ENGS})
            block = es.enter_context(nc.Block())

            def run_engine(e, engine):
                waited = {}

                def wait(sem, val, key):
                    if waited.get(key, 0) >= val:
                        return
                    waited[key] = val
                    engine.wait_ge(sem, val)

                for op in self.ops[e]:
                    for d in op.deps:
                        wait(*d.sig)
                    if op.is_dma:
                        sem, val, key = op.sig
                        if val > 16:
                            wait(sem, val - 16, key)
                    ins = op.fn(engine)
                    if op.sig is not None:
                        ins.then_inc(op.sig[0], 16 if op.is_dma else 1)
                if e == "sp":
                    for op in final_wait_ops:
                        wait(*op.sig)

            block.tensor(lambda eng: run_engine("pe", eng))
            block.scalar(lambda eng: run_engine("act", eng))
            block.vector(lambda eng: run_engine("dve", eng))
            block.gpsimd(lambda eng: run_engine("pool", eng))
            block.sync(lambda eng: run_engine("sp", eng))


class Arena:
    def __init__(self, ap2d, nfloats):
        self.a = ap2d
        self.n = nfloats
        self.off = 0
        self.peak = 0

    def alloc(self, free_shape, dt):
        free_shape = list(free_shape)
        isz = 2 if dt == BF16 else 4
        n = int(np.prod(free_shape))
        n32 = (n * isz + 3) // 4
        assert self.off + n32 <= self.n, ("arena overflow", self.off, n32, self.n)
        v = self.a[:, self.off:self.off + n32]
        self.off += n32
        self.peak = max(self.peak, self.off)
        if dt == BF16:
            v = v.bitcast(BF16)[:, 0:n]
        elif dt == I32:
            v = v.bitcast(I32)
        if len(free_shape) == 2:
            v = v.rearrange("p (a b) -> p a b", a=free_shape[0])
        elif len(free_shape) == 3:
            v = v.rearrange("p (a b c) -> p a b c", a=free_shape[0], b=free_shape[1])
        elif len(free_shape) == 4:
            v = v.rearrange("p (a b c d) -> p a b c d", a=free_shape[0], b=free_shape[1],
                            c=free_shape[2])
        return v

    def mark(self):
        return self.off

    def reset(self, m):
        self.off = m


class Builder:
    def __init__(self, nlayers=L, stop=None, dumps=()):
        self.nlayers = nlayers
        self.stop = stop
        self.dumps = dict()
        self.want = set(dumps)
        self.nc = nc = bass.Bass("TRN2", target_bir_lowering=False)
        self.P = Prog(nc)
        self.dr = {}
        self.finals = []
        self.dump_specs = []

    def din(self, name, shape, dt=F32):
        self.dr[name] = self.nc.dram_tensor(name, list(shape), dt, kind="ExternalInput").ap()
        return self.dr[name]

    def T(self, *k):
        return self.P.tok(*k)

    def mm(self, out, lhsT, rhs, start, stop, reads, writes):
        return self.P.op("pe", lambda e: e.matmul(out, lhsT=lhsT, rhs=rhs, start=start, stop=stop),
                         reads=reads, writes=writes)

    def mmg(self, out, pairs, reads, wtok):
        n = len(pairs)
        for i, (a, b) in enumerate(pairs):
            self.mm(out, a, b, i == 0, i == n - 1, reads, [wtok])

    def act(self, out, in_, func, reads, writes, bias=0.0, scale=1.0, eng="act", accum_out=None):
        if accum_out is None:
            return self.P.op("act", lambda e: e.activation(out=out, in_=in_, func=func, bias=bias, scale=scale),
                             reads=reads, writes=writes)
        return self.P.op("act", lambda e: e.activation(out=out, in_=in_, func=func, bias=bias, scale=scale,
                                                       accum_out=accum_out),
                         reads=reads, writes=writes)

    def tt(self, eng, out, in0, in1, op, reads, writes):
        return self.P.op(eng, lambda e: e.tensor_tensor(out=out, in0=in0, in1=in1, op=op),
                         reads=reads, writes=writes)

    def ts(self, eng, out, in0, s1, op0, reads, writes, s2=None, op1=None):
        if op1 is None:
            return self.P.op(eng, lambda e: e.tensor_scalar(out=out, in0=in0, scalar1=s1, scalar2=None, op0=op0),
                             reads=reads, writes=writes)
        return self.P.op(eng, lambda e: e.tensor_scalar(out=out, in0=in0, scalar1=s1, scalar2=s2, op0=op0, op1=op1),
                         reads=reads, writes=writes)

    def stt(self, eng, out, in0, scalar, in1, op0, op1, reads, writes):
        return self.P.op(eng, lambda e: e.scalar_tensor_tensor(out=out, in0=in0, scalar=scalar, in1=in1,
                                                                op0=op0, op1=op1),
                         reads=reads, writes=writes)

    def cp(self, eng, out, in_, reads, writes):
        if eng == "act":
            return self.P.op("act", lambda e: e.copy(out=out, in_=in_), reads=reads, writes=writes)
        return self.P.op(eng, lambda e: e.tensor_copy(out=out, in_=in_), reads=reads, writes=writes)

    def memset(self, eng, ap, val, writes):
        return self.P.op(eng, lambda e: e.memset(ap, val), reads=(), writes=writes)

    def dma(self, out, in_, reads, writes, q="sp"):
        def ndesc(ap):
            dims = [(int(st), int(n)) for st, n in ap.ap]
            total = 1
            for st, n in dims:
                total *= n
            run = 1
            for st, n in reversed(dims[1:]):
                if st == run:
                    run *= n
                else:
                    break
            return total // run
        self.desc_count = getattr(self, "desc_count", {})
        self.desc_count[q] = self.desc_count.get(q, 0) + max(ndesc(out), ndesc(in_))
        return self.P.dma(q, lambda e: e.dma_start(out=out, in_=in_), reads=reads, writes=writes)

    def rstd(self, out, in_, n, reads, writes):
        self.act(out, in_, AF.Ln, reads, writes, bias=EPS, scale=1.0 / n)
        self.act(out, out, AF.Exp, writes, writes, scale=-0.5)

    def wload(self, dst, src, dtok, cast_eng="pool", scale=None):
        fs = list(dst.shape[1:])
        n = int(np.prod(fs))
        assert n <= self.stage_n, (n, self.stage_n)
        slot = self.wslot
        self.wslot = (slot + 1) % len(self.stage)
        st = self.stage[slot][:, 0:n]
        if len(fs) == 2:
            st = st.rearrange("p (a b) -> p a b", a=fs[0])
        elif len(fs) == 3:
            st = st.rearrange("p (a b c) -> p a b c", a=fs[0], b=fs[1])
        stok = self.T("stage", slot)
        self.wq_i = getattr(self, "wq_i", 0) + 1
        self.dma(st, src, [], [stok], q=WQ[self.wq_i % len(WQ)])
        if scale is None:
            self.cp(cast_eng, dst, st, [stok], [dtok])
        else:
            self.ts(cast_eng, dst, st, scale, ALU.mult, [stok], [dtok])

    def dump(self, name, ap, tok):
        if name not in self.want:
            return
        self.P.barrier()
        shp = list(ap.shape)
        d = self.nc.dram_tensor("dbg_" + name, shp, ap.dtype, kind="ExternalOutput").ap()
        op = self.dma(d, ap, [tok], [])
        self.finals.append(op)
        self.dump_specs.append(name)

    def build(self):
        nc = self.nc
        P = self.P
        din = self.din
        din("xT", [NB, 128, 8, 512])
        din("memT", [1, 128, 8, 256])
        din("pos", [1, T], I32)
        din("consts", [128, NCONST])
        din("colpack", [L, 128, NCOL])
        din("rowpack", [L, 128, NROW])
        din("w_in", [L, NWG, 128, 8, 256])
        din("w_krm", [L, 128, 8, 96])
        din("w_krp", [L, 128, 8, 96])
        din("wq_m", [L, 8, 128, 6, 96])
        din("wq_p", [L, 8, 128, 6, 96])
        din("w_ukv", [L, 8, 128, 2, 128])
        din("a_w_sT", [L, 128, 4, 128])
        din("c_reT", [L, 2, 128, 4, 128])
        din("c_imT", [L, 2, 128, 4, 128])
        din("c_w_glu", [L, 128, 2, 256])
        din("m_w_kv", [L, 2, 128, 8, 256])
        din("w_br", [L, 4, 128, 10, 256])
        din("w_out", [L, 4, 128, 8, 256])
        self.outT = nc.dram_tensor("outT", [NB, 128, 8, 512], F32, kind="ExternalOutput").ap()
        self.x1T = nc.dram_tensor("x1T", [NB, 128, 8, 512], F32, kind="Internal").ap()

        with contextlib.ExitStack() as es:
            NA = 52000
            arena_t = es.enter_context(nc.sbuf_tensor("arena", [128, NA], F32))
            self.A = A = Arena(arena_t[:, :], NA)
            self.ps = [es.enter_context(nc.psum_tensor("ps%d" % i, [128, 512], F32)) for i in range(8)]
            self.pst = [self.T("ps", i) for i in range(8)]
            for t in self.pst:
                t.excl = True

            self.cst = A.alloc([NCONST], F32)
            self.eps_col = A.alloc([1], F32)
            self.tri = A.alloc([128], BF16)
            self.ntri = A.alloc([128], BF16)
            self.ones = A.alloc([128], BF16)
            self.bd64 = A.alloc([128], BF16)
            self.ones_f = A.alloc([128], F32)
            self.cosT = A.alloc([T], F32)
            self.sinT = A.alloc([T], F32)
            self.colp = A.alloc([NCOL], F32)
            self.hT = A.alloc([8, T], BF16)
            self.yall = A.alloc([10, T], BF16)
            self.kmem = A.alloc([2, 256], BF16)
            self.vmem = A.alloc([2, 256], BF16)
            self.stage_n = 2048
            self.stage = [A.alloc([self.stage_n], F32) for _ in range(2)]
            self.wslot = 0
            self.base_mark = A.mark()

            self.setup_consts()
            if self.stop != "consts":
                for l in range(self.nlayers):
                    if self.run_layer(l):
                        break
            P.barrier()
            P.emit(final_wait_ops=self.finals)
        return nc

    def setup_consts(self):
        A, T_ = self.A, self.T
        tc = T_("cst")
        self.dma(self.cst, self.dr["consts"][:, :], [], [tc])
        self.memset("dve", self.eps_col, EPS, [tc])
        self.cp("dve", self.tri, self.cst[:, K_TRI:K_TRI + 128], [tc], [tc])
        self.ts("dve", self.ntri, self.cst[:, K_TRI:K_TRI + 128], -1.0, ALU.mult, [tc], [tc])
        self.memset("dve", self.ones, 1.0, [tc])
        self.memset("dve", self.ones_f, 1.0, [tc])
        self.memset("dve", self.bd64, 0.0, [tc])
        self.memset("dve", self.bd64[0:64, 0:64], 1.0, [tc])
        self.memset("dve", self.bd64[64:128, 64:128], 1.0, [tc])
        if getattr(self, "skip", None):
            self.memset("pool", self.yall, 0.25, [T_("yall_init")])
        m = A.mark()
        posi = A.alloc([T], I32)
        ang = A.alloc([T], F32)
        kf = A.alloc([T], F32)
        ki = A.alloc([T], I32)
        tr = T_("rope")
        self.dma(posi[0:96, :], self.dr["pos"][0:1, :].partition_broadcast(96), [], [tr])
        R = slice(64, 96)
        self.cp("dve", ang[R, :], posi[R, :], [tr], [tr])
        self.ts("dve", ang[R, :], ang[R, :], self.cst[R, K_INVF:K_INVF + 1], ALU.mult, [tr, tc], [tr])

        def sin_of(dst, shift, post_scale_col):
            self.ts("dve", ki[R, :], ang[R, :], shift, ALU.add, [tr], [tr], s2=1.0 / TWO_PI, op1=ALU.mult)
            self.cp("dve", kf[R, :], ki[R, :], [tr], [tr])
            self.stt("dve", kf[R, :], kf[R, :], -TWO_PI, ang[R, :], ALU.mult, ALU.add, [tr], [tr])
            self.ts("dve", kf[R, :], kf[R, :], shift, ALU.add, [tr], [tr], s2=math.pi, op1=ALU.min)
            self.ts("dve", kf[R, :], kf[R, :], -math.pi, ALU.max, [tr], [tr])
            self.act(dst[R, :], kf[R, :], AF.Sin, [tr], [tr])
            if post_scale_col is not None:
                self.ts("dve", dst[R, :], dst[R, :], post_scale_col, ALU.mult, [tr, tc], [tr])

        sin_of(self.sinT, 0.0, self.cst[R, K_SGN:K_SGN + 1])
        sin_of(self.cosT, math.pi / 2, None)
        self.dump("cosT", self.cosT[R, :], tr)
        self.dump("sinT", self.sinT[R, :], tr)
        self.P.barrier()
        A.reset(m)

    def run_layer(self, l):
        P, A, T_ = self.P, self.A, self.T
        src = self.dr["xT"] if l == 0 else self.x1T
        dst = self.x1T if l == 0 else self.outT
        if self.nlayers == 1:
            dst = self.outT
        tcp = T_("colp")
        self.dma(self.colp, self.dr["colpack"][l, :, :], [], [tcp])
        self.tcp = tcp
        stages = [("N", self.stage_norm), ("MP", self.stage_memprep), ("A", self.stage_a),
                  ("C", self.stage_c), ("M", self.stage_m), ("B", self.stage_b),
                  ("P2", self.stage_p2)]
        for name, fn in stages:
            if name in getattr(self, "skip", ()):
                continue
            m = A.mark()
            if name == "N":
                fn(l, src)
            elif name == "P2":
                fn(l, src, dst)
            else:
                fn(l)
            P.barrier()
            A.reset(m)
            if self.stop == (name, l):
                return True
        return False

    def norm_fm(self, srcT, n_tok, gcol, dst, dst_tok_fn, tag):
        A, T_ = self.A, self.T
        W = min(512, n_tok)
        nblk = n_tok // W
        xb = [A.alloc([8, W], F32) for _ in range(2)]
        sq = A.alloc([8, W], BF16)
        rs = A.alloc([W], F32)
        for b in range(nblk):
            x = xb[b % 2]
            tx = T_(tag + "x", b % 2)
            ts_ = T_(tag + "sq")
            trs = T_(tag + "rs")
            for hh in range(2):
                self.dma(x[:, hh * 4:(hh + 1) * 4, :], srcT[b, :, hh * 4:(hh + 1) * 4, :], [], [tx])
            pb = 0
            for kt in range(8):
                self.act(sq[:, kt, :], x[:, kt, :], AF.Square, [tx], [ts_])
            self.mmg(self.ps[pb][:, 0:W], [(self.ones[:, :], sq[:, kt, :]) for kt in range(8)],
                     [ts_], self.pst[pb])
            self.rstd(rs[:, :], self.ps[pb][:, 0:W], float(D), [self.pst[pb]], [trs])
            for kt in range(8):
                self.stt("dve", dst[:, kt, b * W:(b + 1) * W], x[:, kt, :], gcol[:, kt:kt + 1], rs[:, :],
                         ALU.mult, ALU.mult, [tx, trs, self.tcp], [dst_tok_fn(b)])

    def stage_norm(self, l, src):
        self.norm_fm(src, T, self.colp[:, C_NG:C_NG + 8], self.hT, lambda b: self.T("hT", b), "n")
        for b in range(NB):
            self.dump("hT%d_%d" % (l, b), self.hT[:, :, b * 512:(b + 1) * 512], self.T("hT", b))

    def load_win(self, l, c0, ncols, wt, wtok):
        assert ncols == 256
        self.wload(wt[:, :, 0:ncols], self.dr["w_in"][l, WIN_G[c0]], wtok)

    def zproj_blk(self, wt, ncols, blk, pb, wtok, m_off=0):
        self.mmg(self.ps[pb][m_off:m_off + ncols, :],
                 [(wt[:, kt, 0:ncols], self.hT[:, kt, blk * 512:(blk + 1) * 512]) for kt in range(8)],
                 [wtok, self.T("hT", blk)], self.pst[pb])

    def stage_memprep(self, l):
        A, T_ = self.A, self.T
        hm = A.alloc([8, 256], BF16)
        thm = T_("hm")
        self.norm_fm(self.dr["memT"], 256, self.colp[:, C_MNG:C_MNG + 8], hm, lambda b: thm, "m")
        wkv = A.alloc([8, 512], BF16)
        twk = T_("wkv")
        for half in range(2):
            self.wload(wkv[:, :, half * 256:(half + 1) * 256],
                       self.dr["m_w_kv"][l, half],
                       twk)
        sq = A.alloc([256], BF16)
        rs = A.alloc([256], F32)
        tsq, trs, tkm, tvm = T_("mp_sq"), T_("mp_rs"), T_("kmem"), T_("vmem")
        for ct in range(2):
            self.mmg(self.ps[0][:, 0:256], [(wkv[:, kt, ct * 128:(ct + 1) * 128], hm[:, kt, :]) for kt in range(8)],
                     [twk, thm], self.pst[0])
            self.act(sq[:, :], self.ps[0][:, 0:256], AF.Square, [self.pst[0]], [tsq])
            self.mmg(self.ps[1][:, 0:256], [(self.bd64[:, :], sq[:, :])], [tsq, T_("cst")], self.pst[1])
            self.rstd(rs[:, :], self.ps[1][:, 0:256], 64.0, [self.pst[1]], [trs])
            self.stt("dve", self.kmem[:, ct, :], self.ps[0][:, 0:256], self.colp[:, C_MGK:C_MGK + 1], rs[:, :],
                     ALU.mult, ALU.mult, [self.pst[0], trs, self.tcp], [tkm])
        for mt in range(2):
            self.mmg(self.ps[2][:, 0:256], [(hm[:, kt, mt * 128:(mt + 1) * 128], wkv[:, kt, 256:512]) for kt in range(8)],
                     [twk, thm], self.pst[2])
            self.cp("dve", self.vmem[:, mt, :], self.ps[2][:, 0:256], [self.pst[2]], [tvm])
        self.dump("kmem%d" % l, self.kmem, tkm)
        self.dump("vmem%d" % l, self.vmem, tvm)

    def stage_a(self, l):
        A, T_ = self.A, self.T
        ya = self.yall
        wt = [A.alloc([8, 256], BF16) for _ in range(2)]
        gnb = A.alloc([256], F32)
        wsT = A.alloc([4, 128], BF16)
        tmp = A.alloc([512], BF16)
        tgn, tws, ttmp = T_("a_gn"), T_("a_ws"), T_("a_tmp")
        self.dma(gnb, self.dr["rowpack"][l, :, R_ANG:R_ANG + 256], [], [tgn])
        slot = self.wslot
        self.wslot = (slot + 1) % 2
        st = self.stage[slot][:, 0:512].rearrange("p (g t) -> p g t", g=4)
        stok = T_("stage", slot)
        self.dma(st, self.dr["a_w_sT"][l], [], [stok])
        for g in range(4):
            self.tt("dve", wsT[:, g, :], st[:, g, :], self.cst[:, K_TRI:K_TRI + 128], ALU.mult, [stok, T_("cst")], [tws])
        tw0, tw1 = T_("a_w", 0), T_("a_w", 1)
        self.load_win(l, OFF["a_u"], 256, wt[0], tw0)
        self.load_win(l, OFF["a_g"], 256, wt[1], tw1)
        for ct in range(2):
            for blk in range(NB):
                pb = (ct * NB + blk) % 2
                self.mmg(self.ps[pb][:, :],
                         [(wt[0][:, kt, ct * 128:(ct + 1) * 128], self.hT[:, kt, blk * 512:(blk + 1) * 512])
                          for kt in range(8)], [tw0, T_("hT", blk)], self.pst[pb])
                self.act(ya[:, ct, blk * 512:(blk + 1) * 512], self.ps[pb][:, :], AF.Gelu_apprx_tanh,
                         [self.pst[pb]], [T_("ya", ct, blk)])
        for ct in range(2):
            for blk in range(NB):
                pb = 2 + (ct * NB + blk) % 2
                self.mmg(self.ps[pb][:, :],
                         [(wt[1][:, kt, ct * 128:(ct + 1) * 128], self.hT[:, kt, blk * 512:(blk + 1) * 512])
                          for kt in range(8)], [tw1, T_("hT", blk)], self.pst[pb])
                self.act(tmp[:, :], self.ps[pb][:, :], AF.Silu, [self.pst[pb]], [ttmp])
                sl = ya[:, ct, blk * 512:(blk + 1) * 512]
                self.tt("pool", sl, sl, tmp[:, :], ALU.mult, [ttmp], [T_("ya", ct, blk)])
        twv = T_("a_w", 0)
        self.load_win(l, OFF["a_v"], 256, wt[0], twv)
        gv = A.alloc([256], F32)
        ssq = A.alloc([1], F32)
        vn = [A.alloc([256], BF16) for _ in range(2)]
        sb = A.alloc([2, 128], F32)
        junk = A.alloc([256], BF16)
        tgv, tss, tsb = T_("a_gv"), T_("a_ss"), T_("a_sb")
        absT = self.colp[:, C_ABS:C_ABS + 256].rearrange("p (c t) -> p c t", c=2)
        for tt_ in range(16):
            blk = tt_ // 4
            pb = 4 + tt_ % 2
            self.mmg(self.ps[pb][:, 0:256],
                     [(self.hT[:, kt, tt_ * 128:(tt_ + 1) * 128], wt[0][:, kt, :]) for kt in range(8)],
                     [twv, T_("hT", blk)], self.pst[pb])
            self.act(gv[:, :], self.ps[pb][:, 0:256], AF.Gelu_apprx_tanh, [self.pst[pb]], [tgv])
            self.act(junk[:, :], gv[:, :], AF.Square, [tgv], [tss], accum_out=ssq[:, :])
            self.rstd(ssq[:, :], ssq[:, :], 256.0, [tss], [tss])
            v = vn[tt_ % 2]
            tv = T_("a_vn", tt_ % 2)
            self.stt("dve", v[:, :], gv[:, :], ssq[:, 0:1], gnb[:, :], ALU.mult, ALU.mult, [tgv, tss, tgn], [tv])
            pq = 6 + tt_ % 2
            for g in range(4):
                ct, r0 = g // 2, (g % 2) * 64
                self.mm(self.ps[pq][r0:r0 + 64, ct * 128:(ct + 1) * 128], v[:, g * 64:(g + 1) * 64], wsT[:, g, :],
                        True, True, [tv, tws], [self.pst[pq]])
            psv = self.ps[pq][:, 0:256].rearrange("p (c t) -> p c t", c=2)
            self.tt("dve", sb[:, :, :], psv, absT, ALU.add, [self.pst[pq], self.tcp], [tsb])
            sl = ya[:, 0:2, tt_ * 128:(tt_ + 1) * 128]
            self.tt("pool", sl, sl, sb[:, :, :], ALU.mult, [tsb], [T_("ya", 0, blk), T_("ya", 1, blk)])
        self.dump("ya%d" % l, ya[:, 0:2, :], T_("ya", 0, 0))
        self.dump("a_vn%d" % l, vn[1], T_("a_vn", 1))
        self.dump("a_sb%d" % l, sb, tsb)
        self.dump("a_gv%d" % l, gv, tgv)
        self.dump("a_ssq%d" % l, ssq, tss)

    def sin_of(self, dst, ang, shift, ki, kf, tok, extra=()):
        rd = [tok] + list(extra)
        self.ts("dve", ki, ang, shift, ALU.add, rd, [tok], s2=1.0 / TWO_PI, op1=ALU.mult)
        self.cp("dve", kf, ki, [tok], [tok])
        self.stt("dve", kf, kf, -TWO_PI, ang, ALU.mult, ALU.add, [tok], [tok])
        self.ts("dve", kf, kf, shift, ALU.add, [tok], [tok], s2=math.pi, op1=ALU.min)
        self.ts("dve", kf, kf, -math.pi, ALU.max, [tok], [tok])
        self.act(dst, kf, AF.Sin, [tok], [tok])

    def stage_m(self, l):
        A, T_ = self.A, self.T
        ya = self.yall
        wq = A.alloc([8, 256], BF16)
        wg = A.alloc([8, 256], BF16)
        twq, twg = T_("m_wq"), T_("m_wg")
        self.load_win(l, OFF["mq"], 256, wq, twq)
        self.load_win(l, OFF["mg"], 256, wg, twg)
        for ct in range(2):
            for blk in range(NB):
                pb = (ct * NB + blk) % 2
                self.mmg(self.ps[pb][:, :],
                         [(wg[:, kt, ct * 128:(ct + 1) * 128], self.hT[:, kt, blk * 512:(blk + 1) * 512])
                          for kt in range(8)], [twg], self.pst[pb])
                self.act(ya[:, 8 + ct, blk * 512:(blk + 1) * 512], self.ps[pb][:, :], AF.Silu,
                         [self.pst[pb]], [T_("ym", ct, blk)])
        sq = A.alloc([512], BF16)
        rs = A.alloc([512], F32)
        qn = [A.alloc([2, 512], BF16) for _ in range(2)]
        pT = [A.alloc([512], BF16) for _ in range(4)]
        rc = A.alloc([512], F32)
        ot = A.alloc([512], BF16)
        tsq, trs, trc, tot = T_("m_sq"), T_("m_rs"), T_("m_rc"), T_("m_ot")
        scale = 64.0 ** -0.5
        for blk in range(NB):
            q = qn[blk % 2]
            tq = T_("m_qn", blk % 2)
            for ct in range(2):
                self.mmg(self.ps[2][:, :],
                         [(wq[:, kt, ct * 128:(ct + 1) * 128], self.hT[:, kt, blk * 512:(blk + 1) * 512])
                          for kt in range(8)], [twq], self.pst[2])
                self.act(sq[:, :], self.ps[2][:, :], AF.Square, [self.pst[2]], [tsq])
                self.mmg(self.ps[3][:, :], [(self.bd64[:, :], sq[:, :])], [tsq], self.pst[3])
                self.rstd(rs[:, :], self.ps[3][:, :], 64.0, [self.pst[3]], [trs])
                self.stt("dve", q[:, ct, :], self.ps[2][:, :], self.colp[:, C_MGQ:C_MGQ + 1], rs[:, :],
                         ALU.mult, ALU.mult, [self.pst[2], trs], [tq])
            for h in range(4):
                ct, r0 = h // 2, (h % 2) * 64
                R = slice(r0, r0 + 64)
                for mt in range(2):
                    pb = 4 + mt
                    p = pT[(h % 2) * 2 + mt]
                    tp = T_("m_pT", (h % 2) * 2 + mt)
                    self.mm(self.ps[pb][:, :], self.kmem[R, ct, mt * 128:(mt + 1) * 128], q[R, ct, :],
                            True, True, [tq], [self.pst[pb]])
                    self.act(p[:, :], self.ps[pb][:, :], AF.Exp, [self.pst[pb]], [tp], scale=scale)
                tps = [T_("m_pT", (h % 2) * 2 + mt) for mt in range(2)]
                ps_o, ps_d = self.ps[6], self.ps[7]
                self.mmg(ps_o[R, :], [(self.vmem[:, mt, h * 64:(h + 1) * 64], pT[(h % 2) * 2 + mt][:, :])
                                      for mt in range(2)], tps, self.pst[6])
                self.mmg(ps_d[R, :], [(self.ones[:, 0:64], pT[(h % 2) * 2 + mt][:, :]) for mt in range(2)],
                         tps, self.pst[7])
                self.P.op("dve", lambda e, o=rc[R, :], i=ps_d[R, :]: e.reciprocal(out=o, in_=i),
                          reads=[self.pst[7]], writes=[trc])
                self.tt("dve", ot[R, :], ps_o[R, :], rc[R, :], ALU.mult, [self.pst[6], trc], [tot])
                sl = ya[R, 8 + ct, blk * 512:(blk + 1) * 512]
                self.tt("pool", sl, sl, ot[R, :], ALU.mult, [tot], [T_("ym", ct, blk)])
        self.dump("ym%d" % l, ya[:, 8:10, :], T_("ym", 0, 0))

    def stage_c(self, l):
        A, T_ = self.A, self.T
        ya = self.yall
        uT = A.alloc([2, T], BF16)
        ygT = A.alloc([2, T], BF16)
        TAc = A.alloc([1024], F32)
        TAs = A.alloc([1024], F32)
        Dr = A.alloc([8, 128], F32)
        Di = A.alloc([8, 128], F32)
        a128r = A.alloc([8], F32)
        a128i = A.alloc([8], F32)
        Bm = [A.alloc([2, 8, 64], BF16) for _ in range(2)]
        CreT = A.alloc([8, 128], BF16)
        CreTn = A.alloc([8, 128], BF16)
        CimTn = A.alloc([8, 128], BF16)
        wglu = A.alloc([2, 256], BF16)
        m2 = A.mark()
        wu = A.alloc([8, 256], BF16)
        wg = A.alloc([8, 256], BF16)
        twu, twg = T_("c_wu"), T_("c_wg")
        self.load_win(l, OFF["cin"], 256, wu, twu)
        self.load_win(l, OFF["cg"], 256, wg, twg)
        for ct in range(2):
            for blk in range(NB):
                pb = (ct * NB + blk) % 2
                self.mmg(self.ps[pb][:, :],
                         [(wu[:, kt, ct * 128:(ct + 1) * 128], self.hT[:, kt, blk * 512:(blk + 1) * 512])
                          for kt in range(8)], [twu], self.pst[pb])
                self.cp("act", uT[:, ct, blk * 512:(blk + 1) * 512], self.ps[pb][:, :], [self.pst[pb]],
                        [T_("c_uT", blk)])
        for ct in range(2):
            for blk in range(NB):
                pb = 2 + (ct * NB + blk) % 2
                self.mmg(self.ps[pb][:, :],
                         [(wg[:, kt, ct * 128:(ct + 1) * 128], self.hT[:, kt, blk * 512:(blk + 1) * 512])
                          for kt in range(8)], [twg], self.pst[pb])
                self.act(ya[:, 6 + ct, blk * 512:(blk + 1) * 512], self.ps[pb][:, :], AF.Silu,
                         [self.pst[pb]], [T_("yc", ct, blk)])
        tt_ = T_("c_tab")
        rowp = A.alloc([3, 1024], F32)
        self.dma(rowp, self.dr["rowpack"][l, :, R_SRE:R_SRE + 3072].rearrange("p (a b) -> p a b", a=3), [], [tt_])
        s1 = A.alloc([1024], F32)
        s2 = A.alloc([1024], F32)
        si = A.alloc([1024], I32)
        negs = A.alloc([1], F32)
        tcst = T_("cst")
        self.ts("dve", negs, self.cst[:, K_IOS:K_IOS + 1], -1.0, ALU.mult, [tcst], [tt_])
        self.act(rowp[:, 2, :], rowp[:, 2, :], AF.Exp, [tt_], [tt_])
        self.tt("dve", rowp[:, 0, :], rowp[:, 0, :], rowp[:, 2, :], ALU.mult, [tt_], [tt_])
        self.tt("dve", rowp[:, 1, :], rowp[:, 1, :], rowp[:, 2, :], ALU.mult, [tt_], [tt_])
        self.act(s1, rowp[:, 0, :], AF.Exp, [tt_], [tt_], scale=negs[:, 0:1])
        self.ts("dve", s2, rowp[:, 1, :], self.cst[:, K_IOS:K_IOS + 1], ALU.mult, [tt_, tcst], [tt_])
        self.sin_of(TAs, s2, 0.0, si, rowp[:, 2, :], tt_)
        self.sin_of(TAc, s2, math.pi / 2, si, rowp[:, 2, :], tt_)
        self.tt("dve", TAs, TAs, s1, ALU.mult, [tt_], [tt_])
        self.tt("dve", TAc, TAc, s1, ALU.mult, [tt_], [tt_])
        s1v = s1.rearrange("p (j t) -> p j t", j=8)
        s2v = s2.rearrange("p (j t) -> p j t", j=8)
        siv = si.rearrange("p (j t) -> p j t", j=8)
        kfv = rowp[:, 2, :].rearrange("p (j t) -> p j t", j=8)
        dtj = A.alloc([8], F32)
        thrj = A.alloc([8], F32)
        thij = A.alloc([8], F32)
        e128 = A.alloc([8], F32)
        p128 = A.alloc([8], F32)
        k128 = A.alloc([8], F32)
        i128 = A.alloc([8], I32)
        tcp = self.tcp
        self.act(dtj, self.colp[:, C_SDT:C_SDT + 8], AF.Exp, [tcp, tt_], [tt_])
        self.tt("dve", thrj, self.colp[:, C_SRE:C_SRE + 8], dtj, ALU.mult, [tcp, tt_], [tt_])
        self.tt("dve", thij, self.colp[:, C_SIM:C_SIM + 8], dtj, ALU.mult, [tcp, tt_], [tt_])
        iot = self.cst[:, K_IOT:K_IOT + 128]
        for j in range(8):
            self.act(s1v[:, j, :], iot, AF.Exp, [tt_, tcst], [tt_], scale=thrj[:, j:j + 1])
            self.ts("dve", s2v[:, j, :], iot, thij[:, j:j + 1], ALU.mult, [tt_, tcst], [tt_])
        self.sin_of(Di.rearrange("p j t -> p (j t)"), s2, 0.0, si, rowp[:, 2, :], tt_)
        self.sin_of(Dr.rearrange("p j t -> p (j t)"), s2, math.pi / 2, si, rowp[:, 2, :], tt_)
        self.tt("dve", Di, Di, s1v, ALU.mult, [tt_], [tt_])
        self.tt("dve", Dr, Dr, s1v, ALU.mult, [tt_], [tt_])
        self.act(e128, thrj, AF.Exp, [tt_], [tt_], scale=128.0)
        self.ts("dve", p128, thij, 128.0, ALU.mult, [tt_], [tt_])
        self.sin_of(a128i, p128, 0.0, i128, k128, tt_)
        self.sin_of(a128r, p128, math.pi / 2, i128, k128, tt_)
        self.tt("dve", a128i, a128i, e128, ALU.mult, [tt_], [tt_])
        self.tt("dve", a128r, a128r, e128, ALU.mult, [tt_], [tt_])
        w = [A.alloc([64], F32) for _ in range(8)]
        wi = A.alloc([64], I32)
        gmask = self.cst[:, K_GM:K_GM + 8]
        for ct in range(2):
            base = C_ROW + ct * 320
            are = self.colp[:, base:base + 64]
            aim = self.colp[:, base + 64:base + 128]
            ldt = self.colp[:, base + 128:base + 192]
            bre = self.colp[:, base + 192:base + 256]
            bim = self.colp[:, base + 256:base + 320]
            dt_, thr, thi, ea, abr, abi, t0, t1 = w
            rd = [tt_, tcp]
            self.act(dt_, ldt, AF.Exp, rd, [tt_])
            self.tt("dve", thr, are, dt_, ALU.mult, rd, [tt_])
            self.tt("dve", thi, aim, dt_, ALU.mult, rd, [tt_])
            self.act(ea, thr, AF.Exp, [tt_], [tt_])
            self.sin_of(abi, thi, 0.0, wi, t0, tt_)
            self.sin_of(abr, thi, math.pi / 2, wi, t0, tt_)
            self.tt("dve", abi, abi, ea, ALU.mult, [tt_], [tt_])
            self.tt("dve", abr, abr, ea, ALU.mult, [tt_], [tt_])
            self.ts("dve", abr, abr, -1.0, ALU.add, [tt_], [tt_])
            self.tt("dve", dt_, are, are, ALU.mult, rd, [tt_])
            self.tt("dve", t0, aim, aim, ALU.mult, rd, [tt_])
            self.tt("dve", dt_, dt_, t0, ALU.add, [tt_], [tt_])
            self.P.op("dve", lambda e, o=dt_, i=dt_: e.reciprocal(out=o, in_=i), reads=[tt_], writes=[tt_])
            self.tt("dve", t0, abr, are, ALU.mult, rd, [tt_])
            self.tt("dve", t1, abi, aim, ALU.mult, rd, [tt_])
            self.tt("dve", t0, t0, t1, ALU.add, [tt_], [tt_])
            self.tt("dve", thr, t0, dt_, ALU.mult, [tt_], [tt_])
            self.tt("dve", t0, abi, are, ALU.mult, rd, [tt_])
            self.tt("dve", t1, abr, aim, ALU.mult, rd, [tt_])
            self.tt("dve", t0, t0, t1, ALU.subtract, [tt_], [tt_])
            self.tt("dve", thi, t0, dt_, ALU.mult, [tt_], [tt_])
            self.tt("dve", t0, thr, bre, ALU.mult, rd, [tt_])
            self.tt("dve", t1, thi, bim, ALU.mult, rd, [tt_])
            self.tt("dve", ea, t0, t1, ALU.subtract, [tt_], [tt_])
            self.tt("dve", t0, thr, bim, ALU.mult, rd, [tt_])
            self.tt("dve", t1, thi, bre, ALU.mult, rd, [tt_])
            self.tt("dve", abi, t0, t1, ALU.add, [tt_], [tt_])
            for ri, src_ in enumerate((ea, abi)):
                self.tt("dve", Bm[ct][:, ri, :, :], src_.unsqueeze(1).to_broadcast([128, 8, 64]),
                        gmask.unsqueeze(2).to_broadcast([128, 8, 64]), ALU.mult, [tt_, tcst], [tt_])
        tct = T_("c_ct")
        for j0 in (0, 4):
            srcr = self.dr["c_reT"][l, j0 // 4]
            srci = self.dr["c_imT"][l, j0 // 4]
            self.wload(CreT[:, j0:j0 + 4, :], srcr, tct)
            self.wload(CreTn[:, j0:j0 + 4, :], srcr, tct, scale=-1.0)
            self.wload(CimTn[:, j0:j0 + 4, :], srci, tct, scale=-1.0)
        self.wload(wglu, self.dr["c_w_glu"][l], tct)
        self.dump("c_TAc%d" % l, TAc, tt_)
        self.dump("c_TAs%d" % l, TAs, tt_)
        self.dump("c_Dr%d" % l, Dr, tt_)
        self.dump("c_Di%d" % l, Di, tt_)
        self.dump("c_Bm%d" % l, Bm[0], tt_)
        self.dump("c_a128r%d" % l, a128r, tt_)
        self.P.barrier()
        A.reset(m2)
        aS = A.alloc([16], F32)
        self.memset("dve", aS, 0.0, [T_("c_aS", 0), T_("c_aS", 1)])
        X = [A.alloc([4, 512], BF16) for _ in range(2)]
        Pf = A.alloc([8, 128], F32)
        Y = A.alloc([4, 4, 128], BF16)
        p127 = A.alloc([8], F32)
        c1 = A.alloc([8], F32)
        c2 = A.alloc([8], F32)
        ys = A.alloc([128], F32)
        tPf, tY, tys, tch = T_("c_Pf"), T_("c_Y"), T_("c_ys"), T_("c_ch")
        for n in range(16):
            blk = n // 4
            cs = slice(n * 128, (n + 1) * 128)
            for ct in range(2):
                hi = (n * 2 + ct) % 2
                pre_, pim_ = self.ps[2 * hi], self.ps[2 * hi + 1]
                tpre, tpim = self.pst[2 * hi], self.pst[2 * hi + 1]
                Bre = Bm[ct][:, 0, :, :].rearrange("p g q -> p (g q)")
                Bim = Bm[ct][:, 1, :, :].rearrange("p g q -> p (g q)")
                self.mm(pre_[:, :], uT[:, ct, cs], Bre, True, True, [T_("c_uT", blk), tt_], [tpre])
                self.mm(pim_[:, :], uT[:, ct, cs], Bim, True, True, [T_("c_uT", blk), tt_], [tpim])
                x = X[hi]
                tx = T_("c_X", hi)
                tc_ = TAc[:, ct * 512:(ct + 1) * 512]
                ts_ = TAs[:, ct * 512:(ct + 1) * 512]
                self.tt("dve", x[:, 0, :], pre_[:, :], tc_, ALU.mult, [tpre, tt_], [tx])
                self.tt("dve", x[:, 1, :], pim_[:, :], ts_, ALU.mult, [tpim, tt_], [tx])
                self.tt("dve", x[:, 2, :], pim_[:, :], tc_, ALU.mult, [tpim, tt_], [tx])
                self.tt("dve", x[:, 3, :], pre_[:, :], ts_, ALU.mult, [tpre, tt_], [tx])
                Pre, Pim = self.ps[4], self.ps[5]
                for jl in range(4):
                    js = slice(jl * 128, (jl + 1) * 128)
                    self.mmg(Pre[:, js], [(x[:, 0, js], self.tri[:, :]), (x[:, 1, js], self.tri[:, :])],
                             [tx], self.pst[4])
                    self.mmg(Pim[:, js], [(x[:, 2, js], self.tri[:, :]), (x[:, 3, js], self.ntri[:, :])],
                             [tx], self.pst[5])
                ta = T_("c_aS", ct)
                jr = slice(4 * ct, 4 * ct + 4)
                ji = slice(8 + 4 * ct, 8 + 4 * ct + 4)
                self.tt("dve", Pf[:, 0:4, :], Pre[:, :].rearrange("p (j t) -> p j t", j=4),
                        aS[:, jr].unsqueeze(2).to_broadcast([128, 4, 128]), ALU.add, [self.pst[4], ta], [tPf])
                self.tt("dve", Pf[:, 4:8, :], Pim[:, :].rearrange("p (j t) -> p j t", j=4),
                        aS[:, ji].unsqueeze(2).to_broadcast([128, 4, 128]), ALU.add, [self.pst[5], ta], [tPf])
                self.cp("dve", p127, Pf[:, :, 127], [tPf], [tch])
                self.tt("dve", c1[:, 0:4], a128r[:, jr], p127[:, 0:4], ALU.mult, [tch, tt_], [tch])
                self.tt("dve", c1[:, 4:8], a128i[:, jr], p127[:, 4:8], ALU.mult, [tch, tt_], [tch])
                self.tt("dve", c2[:, 0:4], a128r[:, jr], p127[:, 4:8], ALU.mult, [tch, tt_], [tch])
                self.tt("dve", c2[:, 4:8], a128i[:, jr], p127[:, 0:4], ALU.mult, [tch, tt_], [tch])
                self.tt("dve", aS[:, jr], c1[:, 0:4], c1[:, 4:8], ALU.subtract, [tch], [ta])
                self.tt("dve", aS[:, ji], c2[:, 0:4], c2[:, 4:8], ALU.add, [tch], [ta])
                self.tt("pool", Y[:, 0, :, :], Pf[:, 0:4, :], Dr[:, jr, :], ALU.mult, [tPf, tt_], [tY])
                self.tt("pool", Y[:, 1, :, :], Pf[:, 4:8, :], Di[:, jr, :], ALU.mult, [tPf, tt_], [tY])
                self.tt("pool", Y[:, 2, :, :], Pf[:, 4:8, :], Dr[:, jr, :], ALU.mult, [tPf, tt_], [tY])
                self.tt("pool", Y[:, 3, :, :], Pf[:, 0:4, :], Di[:, jr, :], ALU.mult, [tPf, tt_], [tY])
                py = self.ps[6]
                pairs = []
                for jl in range(4):
                    j = 4 * ct + jl
                    pairs += [(CreT[:, j, :], Y[:, 0, jl, :]), (CreTn[:, j, :], Y[:, 1, jl, :]),
                              (CimTn[:, j, :], Y[:, 2, jl, :]), (CimTn[:, j, :], Y[:, 3, jl, :])]
                self.mmg(py[:, 0:128], pairs, [tY, tct], self.pst[6])
                self.stt("dve", ys, uT[:, ct, cs], self.colp[:, C_CD + ct:C_CD + ct + 1], py[:, 0:128],
                         ALU.mult, ALU.add, [self.pst[6], T_("c_uT", blk), tcp], [tys])
                self.act(ygT[:, ct, cs], ys, AF.Gelu_apprx_tanh, [tys], [T_("c_yg", blk)])
        self.dump("c_yg%d" % l, ygT, T_("c_yg", 0))
        sg = A.alloc([512], BF16)
        tsg = T_("c_sg")
        for blk in range(NB):
            bs = slice(blk * 512, (blk + 1) * 512)
            for cc in range(2):
                pb = 7
                self.mmg(self.ps[pb][:, :], [(wglu[:, ct, cc * 128:(cc + 1) * 128], ygT[:, ct, bs]) for ct in range(2)],
                         [tct, T_("c_yg", blk)], self.pst[pb])
                self.act(sg, self.ps[pb][:, :], AF.Sigmoid, [self.pst[pb], tcp], [tsg],
                         bias=self.colp[:, C_BGLU + cc:C_BGLU + cc + 1])
                sl = ya[:, 6 + cc, bs]
                self.tt("pool", sg, sg, ygT[:, cc, bs], ALU.mult, [tsg, T_("c_yg", blk)], [tsg])
                self.tt("pool", sl, sl, sg, ALU.mult, [tsg], [T_("yc", cc, blk)])
        self.dump("yc%d" % l, ya[:, 6:8, :], T_("yc", 0, 0))

    def stage_b(self, l):
        A, T_, P = self.A, self.T, self.P
        ya = self.yall
        tcp = self.tcp
        tcst = T_("cst")
        cqn = A.alloc([6, T], BF16)
        ckvn = A.alloc([2, T], BF16)
        krr = A.alloc([T], F32)
        sqkr = A.alloc([T], BF16)
        m2 = A.mark()
        wt = A.alloc([8, 512], BF16)
        wq_in = A.alloc([8, 768], BF16)
        wkv_in = A.alloc([8, 256], BF16)
        wkr = [A.alloc([8, 96], BF16) for _ in range(2)]
        sq = A.alloc([512], BF16)
        rs = A.alloc([512], F32)
        t1 = A.alloc([512], F32)
        t2 = A.alloc([512], F32)
        tw = T_("b_w")
        for i in range(2):
            self.load_win(l, OFF["bg"] + i * 256, 256, wt[:, :, i * 256:(i + 1) * 256], tw)
        for i in range(3):
            self.load_win(l, OFF["cq"] + i * 256, 256, wq_in[:, :, i * 256:(i + 1) * 256], tw)
        self.load_win(l, OFF["ckv"], 256, wkv_in, tw)
        self.wload(wkr[0], self.dr["w_krm"][l], tw)
        self.wload(wkr[1], self.dr["w_krp"][l], tw)
        for ct in range(4):
            for blk in range(NB):
                pb = (ct * NB + blk) % 2
                self.zproj_blk(wt[:, :, ct * 128:(ct + 1) * 128], 128, blk, pb, tw)
                self.act(ya[:, 2 + ct, blk * 512:(blk + 1) * 512], self.ps[pb][:, :], AF.Silu,
                         [self.pst[pb]], [T_("yb", ct, blk)])
        tsq, trs = T_("b_sq"), T_("b_rs")
        R = slice(64, 96)
        for blk in range(NB):
            bs = slice(blk * 512, (blk + 1) * 512)
            for i in range(6):
                self.zproj_blk(wq_in[:, :, i * 128:(i + 1) * 128], 128, blk, i, tw)
                self.act(sq, self.ps[i][:, :], AF.Square, [self.pst[i]], [tsq])
                self.mm(self.ps[6][:, :], self.ones[:, :], sq, i == 0, i == 5, [tsq], [self.pst[6]])
            self.rstd(rs, self.ps[6][:, :], 768.0, [self.pst[6]], [trs])
            for i in range(6):
                self.stt("dve", cqn[:, i, bs], self.ps[i][:, :], self.colp[:, C_QNG + i:C_QNG + i + 1], rs,
                         ALU.mult, ALU.mult, [self.pst[i], trs, tcp], [T_("b_cqn", blk)])
            for i in range(2):
                self.zproj_blk(wkv_in[:, :, i * 128:(i + 1) * 128], 128, blk, i, tw)
                self.act(sq, self.ps[i][:, :], AF.Square, [self.pst[i]], [tsq])
                self.mm(self.ps[7][:, :], self.ones[:, :], sq, i == 0, i == 1, [tsq], [self.pst[7]])
            self.rstd(rs, self.ps[7][:, :], 256.0, [self.pst[7]], [trs])
            for i in range(2):
                self.stt("dve", ckvn[:, i, bs], self.ps[i][:, :], self.colp[:, C_KVNG + i:C_KVNG + i + 1], rs,
                         ALU.mult, ALU.mult, [self.pst[i], trs, tcp], [T_("b_ckvn", blk)])
            for i in range(2):
                self.zproj_blk(wkr[i], 96, blk, 2 + i, tw)
            self.act(sqkr[R, bs], self.ps[2][R, :], AF.Square, [self.pst[2]], [T_("b_kr", blk)])
            tt1 = T_("b_t1")
            self.stt("dve", t1[R, :], self.ps[2][R, :], self.colp[R, C_GK:C_GK + 1], self.cosT[R, bs],
                     ALU.mult, ALU.mult, [self.pst[2], tcp], [tt1])
            self.stt("dve", t2[R, :], self.ps[3][R, :], self.colp[R, C_GKP:C_GKP + 1], self.sinT[R, bs],
                     ALU.mult, ALU.mult, [self.pst[3], tcp], [tt1])
            self.tt("pool", krr[R, bs], t1[R, :], t2[R, :], ALU.add, [tt1], [T_("b_kr", blk)])
        self.dump("b_cqn%d" % l, cqn, T_("b_cqn", 0))
        self.dump("b_ckvn%d" % l, ckvn, T_("b_ckvn", 0))
        self.dump("b_krr%d" % l, krr[R, :], T_("b_kr", 0))
        P.barrier()
        A.reset(m2)
        wqm = [A.alloc([6, 96], BF16) for _ in range(2)]
        wqp = [A.alloc([6, 96], BF16) for _ in range(2)]
        wkv = [A.alloc([2, 128], BF16) for _ in range(2)]
        wk = [w_[:, :, 0:64] for w_ in wkv]
        wv = [w_[:, :, 64:128] for w_ in wkv]
        qn = [A.alloc([T], BF16) for _ in range(2)]
        kn = [A.alloc([T], BF16) for _ in range(2)]
        vh = [A.alloc([16, 64], BF16) for _ in range(2)]
        sq = [A.alloc([512], BF16) for _ in range(2)]
        rs = [A.alloc([512], F32) for _ in range(2)]
        t1 = A.alloc([512], F32)
        t2 = A.alloc([512], F32)
        pT = [A.alloc([512], BF16) for _ in range(4)]
        rc = A.alloc([512], F32)
        ot = A.alloc([512], BF16)
        scale = 96.0 ** -0.5
        for h in range(8):
            s = h % 2
            twh = T_("b_wh", s)
            self.wload(wqm[s], self.dr["wq_m"][l, h], twh)
            self.wload(wqp[s], self.dr["wq_p"][l, h], twh)
            self.wload(wkv[s], self.dr["w_ukv"][l, h], twh)
            tq, tk, tv = T_("b_qn", s), T_("b_kn", s), T_("b_vh", s)
            for half in range(2):
                for t8 in range(8):
                    tt_ = half * 8 + t8
                    self.mmg(self.ps[3][:, t8 * 64:(t8 + 1) * 64],
                             [(ckvn[:, kt, tt_ * 128:(tt_ + 1) * 128], wv[s][:, kt, :]) for kt in range(2)],
                             [twh], self.pst[3])
                self.cp("dve", vh[s][:, half * 8:(half + 1) * 8, :],
                        self.ps[3][:, :].rearrange("p (a b) -> p a b", a=8), [self.pst[3]], [tv])
            for blk in range(NB):
                bs = slice(blk * 512, (blk + 1) * 512)
                Q = slice(0, 96)
                N_ = slice(0, 64)
                self.mmg(self.ps[0][Q, :], [(wqm[s][:, kt, :], cqn[:, kt, bs]) for kt in range(6)], [twh], self.pst[0])
                self.mmg(self.ps[1][Q, :], [(wqp[s][:, kt, :], cqn[:, kt, bs]) for kt in range(6)], [twh], self.pst[1])
                tsq0, trs0 = T_("b_sq", 0), T_("b_rs", 0)
                self.act(sq[0][Q, :], self.ps[0][Q, :], AF.Square, [self.pst[0]], [tsq0])
                self.mmg(self.ps[2][Q, :], [(self.ones[Q, 0:96], sq[0][Q, :])], [tsq0], self.pst[2])
                self.rstd(rs[0][Q, :], self.ps[2][Q, :], 96.0, [self.pst[2]], [trs0])
                self.stt("dve", qn[s][N_, bs], self.ps[0][N_, :], self.colp[N_, C_GQ:C_GQ + 1], rs[0][N_, :],
                         ALU.mult, ALU.mult, [self.pst[0], trs0, tcp], [tq])
                tt1 = T_("b_t1")
                self.stt("dve", t1[R, :], self.ps[0][R, :], self.colp[R, C_GQ:C_GQ + 1], self.cosT[R, bs],
                         ALU.mult, ALU.mult, [self.pst[0], tcp], [tt1])
                self.stt("dve", t2[R, :], self.ps[1][R, :], self.colp[R, C_GQP:C_GQP + 1], self.sinT[R, bs],
                         ALU.mult, ALU.mult, [self.pst[1], tcp], [tt1])
                self.tt("pool", t1[R, :], t1[R, :], t2[R, :], ALU.add, [tt1], [tt1])
                self.tt("pool", qn[s][R, bs], t1[R, :], rs[0][R, :], ALU.mult, [tt1, trs0], [tq])
                self.mmg(self.ps[3][N_, :], [(wk[s][:, kt, :], ckvn[:, kt, bs]) for kt in range(2)], [twh], self.pst[3])
                tsq1, trs1 = T_("b_sq", 1), T_("b_rs", 1)
                self.act(sq[1][N_, :], self.ps[3][N_, :], AF.Square, [self.pst[3]], [tsq1])
                self.cp("pool", sq[1][R, :], sqkr[R, bs], [], [tsq1])
                self.mmg(self.ps[2][Q, :], [(self.ones[Q, 0:96], sq[1][Q, :])], [tsq1], self.pst[2])
                self.rstd(rs[1][Q, :], self.ps[2][Q, :], 96.0, [self.pst[2]], [trs1])
                self.stt("dve", kn[s][N_, bs], self.ps[3][N_, :], self.colp[N_, C_GK:C_GK + 1], rs[1][N_, :],
                         ALU.mult, ALU.mult, [self.pst[3], trs1, tcp], [tk])
                self.tt("pool", kn[s][R, bs], krr[R, bs], rs[1][R, :], ALU.mult, [trs1], [tk])
            if h == 0:
                self.dump("b_qn%d" % l, qn[0][0:96, :], tq)
                self.dump("b_kn%d" % l, kn[0][0:96, :], tk)
                self.dump("b_vh%d" % l, vh[0], tv)
            ct, r0 = h // 2, (h % 2) * 64
            RR = slice(r0, r0 + 64)
            for b in range(NB):
                nj = 4 * b + 4
                for j in range(nj):
                    jj = j - 4 * b
                    c0 = 128 * jj if jj > 0 else 0
                    pb = 4 + j % 2
                    p = pT[j % 4]
                    tp = T_("b_pT", j % 4)
                    self.mm(self.ps[pb][:, c0:512], kn[s][0:96, j * 128:(j + 1) * 128],
                            qn[s][0:96, b * 512 + c0:(b + 1) * 512], True, True, [tq, tk], [self.pst[pb]])
                    self.act(p[:, c0:512], self.ps[pb][:, c0:512], AF.Exp, [self.pst[pb]], [tp], scale=scale)
                    if jj >= 0:
                        self.tt("pool", p[:, c0:c0 + 128], p[:, c0:c0 + 128], self.tri[:, :], ALU.mult,
                                [tp, tcst], [tp])
                    self.mm(self.ps[6][RR, c0:512], vh[s][:, j, :], p[:, c0:512], j == 0, j == nj - 1,
                            [tv, tp], [self.pst[6]])
                    self.mm(self.ps[7][RR, c0:512], self.ones[:, 0:64], p[:, c0:512], j == 0, j == nj - 1,
                            [tp], [self.pst[7]])
                trc, tot = T_("b_rc"), T_("b_ot")
                self.P.op("dve", lambda e, o=rc[RR, :], i=self.ps[7][RR, :]: e.reciprocal(out=o, in_=i),
                          reads=[self.pst[7]], writes=[trc])
                self.tt("dve", ot[RR, :], self.ps[6][RR, :], rc[RR, :], ALU.mult, [self.pst[6], trc], [tot])
                sl = ya[RR, 2 + ct, b * 512:(b + 1) * 512]
                self.tt("pool", sl, sl, ot[RR, :], ALU.mult, [tot], [T_("yb", ct, b)])
        self.dump("yb%d" % l, ya[:, 2:6, :], T_("yb", 0, 0))

    def stage_p2(self, l, src, dst):
        A, T_, P = self.A, self.T, self.P
        ya = self.yall
        merged = A.alloc([8, T], BF16)
        m2 = A.mark()
        wl = A.alloc([8, 4, 256], BF16)
        wb = A.alloc([10, 256], BF16)
        g = [A.alloc([512], F32) for _ in range(2)]
        macc = A.alloc([512], F32)
        t2 = [A.alloc([512], F32) for _ in range(2)]
        brk = {0: [0, 1], 1: [2, 3, 4, 5], 2: [6, 7], 3: [8, 9]}
        brw = ["w_br_a", "w_br_b", "w_br_c", "w_br_m"]
        tcp = self.tcp
        twl, twb = T_("p_wl"), T_("p_wb")
        for j2 in range(4):
            for br in range(4):
                c0 = OFF["mrg"] + br * 1024 + j2 * 256
                self.wload(wl[:, :, br, :], self.dr["w_in"][l, WIN_G[c0]], twl)
            self.wload(wb[:, 0:6, :], self.dr["w_br"][l, j2, :, 0:6, :], twb)
            self.wload(wb[:, 6:10, :], self.dr["w_br"][l, j2, :, 6:10, :], twb)
            for jj in range(2):
                j = 2 * j2 + jj
                js = slice(jj * 128, (jj + 1) * 128)
                for blk in range(NB):
                    bs = slice(blk * 512, (blk + 1) * 512)
                    for br in range(4):
                        pl, pp = self.ps[2 * (br % 2)], self.ps[2 * (br % 2) + 1]
                        tpl, tpp = self.pst[2 * (br % 2)], self.pst[2 * (br % 2) + 1]
                        self.mmg(pl[:, :], [(wl[:, kt, br, js], self.hT[:, kt, bs]) for kt in range(8)], [twl], tpl)
                        self.mmg(pp[:, :], [(wb[:, kt, js], ya[:, kt, bs]) for kt in brk[br]], [twb], tpp)
                        gg, tg = g[br % 2], T_("p_g", br % 2)
                        self.act(gg, pl[:, :], AF.Sigmoid, [tpl, tcp], [tg],
                                 bias=self.colp[:, C_BM + br * 8 + j:C_BM + br * 8 + j + 1])
                        tm = T_("p_m")
                        if br == 0:
                            self.tt("dve", macc, pp[:, :], gg, ALU.mult, [tpp, tg], [tm])
                        else:
                            tt2 = T_("p_t2", br % 2)
                            self.tt("dve", t2[br % 2], pp[:, :], gg, ALU.mult, [tpp, tg], [tt2])
                            if br < 3:
                                self.tt("dve", macc, macc, t2[br % 2], ALU.add, [tt2, tm], [tm])
                            else:
                                self.tt("dve", merged[:, j, bs], macc, t2[br % 2], ALU.add, [tt2, tm],
                                        [T_("p_mg", blk)])
        self.dump("merged%d" % l, merged, T_("p_mg", 0))
        P.barrier()
        A.reset(m2)
        wo = A.alloc([8, D], BF16)
        two = T_("p_wo")
        for i in range(4):
            self.wload(wo[:, :, i * 256:(i + 1) * 256],
                       self.dr["w_out"][l, i], two)
        xt = A.alloc([8, 512], F32)
        xo = A.alloc([8, 512], F32)
        txt, txo = T_("p_xt"), T_("p_xo")
        for blk in range(NB):
            bs = slice(blk * 512, (blk + 1) * 512)
            for hh in range(2):
                self.dma(xt[:, hh * 4:(hh + 1) * 4, :], src[blk, :, hh * 4:(hh + 1) * 4, :], [], [txt])
            for d2 in range(8):
                pb = 4 + d2 % 2
                self.mmg(self.ps[pb][:, :], [(wo[:, kt, d2 * 128:(d2 + 1) * 128], merged[:, kt, bs]) for kt in range(8)],
                         [two, T_("p_mg", blk)], self.pst[pb])
                self.tt("dve", xo[:, d2, :], self.ps[pb][:, :], xt[:, d2, :], ALU.add, [self.pst[pb], txt], [txo])
            for hh in range(2):
                op = self.dma(dst[blk, :, hh * 4:(hh + 1) * 4, :], xo[:, hh * 4:(hh + 1) * 4, :], [txo], [], q="act")
                if dst is self.outT:
                    self.finals.append(op)


def make_consts():
    c = np.zeros((128, NCONST), np.float32)
    s = np.arange(128)
    c[:, K_TRI:K_TRI + 128] = (s[:, None] <= s[None, :]).astype(np.float32)
    c[:, K_IOT:K_IOT + 128] = s[None, :].astype(np.float32)
    c[:, K_IOS] = s.astype(np.float32)
    half = 16
    inv = (10000.0 ** (-np.arange(half, dtype=np.float32) / half)).astype(np.float32)
    for r in range(64, 96):
        c[r, K_INVF] = inv[(r - 64) % 16]
        c[r, K_SGN] = -1.0 if (r - 64) < 16 else 1.0
    for r in range(128):
        c[r, K_GM + r // 16] = 1.0
    return c


def host_prep(inp):
    f = lambda k: np.asarray(inp[k], dtype=np.float32)
    perm = np.array(PERM)
    sh = {}

    def rows_t(w):
        r, c = w.shape
        return np.ascontiguousarray(w.reshape(r // 128, 128, c).transpose(1, 0, 2))

    w_in = f("w_in")
    sh["w_in"] = np.stack([np.stack([rows_t(w_in[l][:, c0:c0 + 256]) for c0 in WIN_C0]) for l in range(L)])
    krm = np.zeros((L, D, 96), np.float32)
    krp = np.zeros((L, D, 96), np.float32)
    krm[:, :, 64:96] = w_in[:, :, OFF["kr"]:OFF["kr"] + 32]
    krp[:, :, 64:96] = w_in[:, :, OFF["kr"] + perm]
    sh["w_krm"] = np.stack([rows_t(krm[l]) for l in range(L)])
    sh["w_krp"] = np.stack([rows_t(krp[l]) for l in range(L)])
    wuq = f("b_w_uq").reshape(L, 768, 8, 96)
    wqp = np.zeros((L, 768, 8, 96), np.float32)
    wqp[:, :, :, 64:96] = wuq[:, :, :, 64 + perm]
    sh["wq_m"] = np.stack([np.stack([rows_t(wuq[l][:, h, :]) for h in range(8)]) for l in range(L)])
    sh["wq_p"] = np.stack([np.stack([rows_t(wqp[l][:, h, :]) for h in range(8)]) for l in range(L)])
    ukv = f("b_w_ukv")
    sh["w_ukv"] = np.stack([np.stack([rows_t(ukv[l][:, h * 128:(h + 1) * 128]) for h in range(8)]) for l in range(L)])
    sh["a_w_sT"] = np.ascontiguousarray(f("a_w_s").transpose(0, 3, 1, 2))
    c_re, c_im = f("c_c_re"), f("c_c_im")
    cre = np.zeros((L, 8, 128, 128), np.float32)
    cim = np.zeros((L, 8, 128, 128), np.float32)
    for j in range(8):
        for gl in range(2):
            g = 2 * j + gl
            col0 = 16 * (g % 8)
            cre[:, j, gl * 64:(gl + 1) * 64, col0:col0 + 16] = c_re[:, g].transpose(0, 2, 1)
            cim[:, j, gl * 64:(gl + 1) * 64, col0:col0 + 16] = c_im[:, g].transpose(0, 2, 1)
    sh["c_reT"] = np.ascontiguousarray(cre.reshape(L, 2, 4, 128, 128).transpose(0, 1, 3, 2, 4))
    sh["c_imT"] = np.ascontiguousarray(cim.reshape(L, 2, 4, 128, 128).transpose(0, 1, 3, 2, 4))
    sh["c_w_glu"] = np.stack([rows_t(f("c_w_glu")[l]) for l in range(L)])
    mkv = f("m_w_kv")
    sh["m_w_kv"] = np.stack([np.stack([rows_t(mkv[l][:, i * 256:(i + 1) * 256]) for i in range(2)]) for l in range(L)])
    wbr = np.concatenate([f("w_br_a"), f("w_br_b"), f("w_br_c"), f("w_br_m")], axis=1)
    sh["w_br"] = np.stack([np.stack([rows_t(wbr[l][:, i * 256:(i + 1) * 256]) for i in range(4)]) for l in range(L)])
    wo = f("w_out")
    sh["w_out"] = np.stack([np.stack([rows_t(wo[l][:, i * 256:(i + 1) * 256]) for i in range(4)]) for l in range(L)])
    cp = np.zeros((L, 128, NCOL), np.float32)
    cp[:, :, C_NG:C_NG + 8] = f("norm_g").reshape(L, 8, 128).transpose(0, 2, 1)
    cp[:, :, C_MNG:C_MNG + 8] = f("m_norm_g").reshape(L, 8, 128).transpose(0, 2, 1)
    cp[:, :, C_QNG:C_QNG + 6] = f("b_q_norm_g").reshape(L, 6, 128).transpose(0, 2, 1)
    cp[:, :, C_KVNG:C_KVNG + 2] = f("b_kv_norm_g").reshape(L, 2, 128).transpose(0, 2, 1)
    cp[:, :, C_BM:C_BM + 32] = f("b_merge").reshape(L, 32, 128).transpose(0, 2, 1)
    gq, gk = f("b_qk_g_q"), f("b_qk_g_k")
    cp[:, 0:96, C_GQ] = gq
    cp[:, 64:96, C_GQP] = gq[:, 64 + perm]
    cp[:, 0:96, C_GK] = gk
    cp[:, 64:96, C_GKP] = gk[:, 64 + perm]
    cp[:, :, C_MGQ] = np.tile(f("m_qk_g_q"), (1, 2))
    cp[:, :, C_MGK] = np.tile(f("m_qk_g_k"), (1, 2))
    cp[:, :, C_CD:C_CD + 2] = f("c_d").reshape(L, 2, 128).transpose(0, 2, 1)
    cp[:, :, C_BGLU:C_BGLU + 2] = f("c_b_glu").reshape(L, 2, 128).transpose(0, 2, 1)
    a_re, a_im, ldt = f("c_a_re"), f("c_a_im"), f("c_log_dt")
    cp[:, :, C_SRE:C_SRE + 8] = a_re.reshape(L, 8, 128).transpose(0, 2, 1)
    cp[:, :, C_SIM:C_SIM + 8] = a_im.reshape(L, 8, 128).transpose(0, 2, 1)
    ldt_rep = np.repeat(ldt[:, :, None], 64, axis=2)
    cp[:, :, C_SDT:C_SDT + 8] = ldt_rep.reshape(L, 8, 128).transpose(0, 2, 1)
    abs_ = f("a_b_s")
    for ct in range(2):
        for gl in range(2):
            cp[:, gl * 64:(gl + 1) * 64, C_ABS + ct * 128:C_ABS + (ct + 1) * 128] = abs_[:, 2 * ct + gl][:, None, :]
    b_re, b_im = f("c_b_re"), f("c_b_im")
    for ct in range(2):
        base = C_ROW + ct * 320
        for g8 in range(8):
            g = 8 * ct + g8
            rows = slice(16 * g8, 16 * g8 + 16)
            cp[:, rows, base + 0:base + 64] = a_re[:, g][:, None, :]
            cp[:, rows, base + 64:base + 128] = a_im[:, g][:, None, :]
            cp[:, rows, base + 128:base + 192] = ldt[:, g][:, None, None]
            cp[:, rows, base + 192:base + 256] = b_re[:, g].transpose(0, 2, 1)
            cp[:, rows, base + 256:base + 320] = b_im[:, g].transpose(0, 2, 1)
    sh["colpack"] = cp
    rp = np.zeros((L, 128, NROW), np.float32)
    rp[:, :, R_ANG:R_ANG + 256] = f("a_norm_g")[:, None, :]
    rp[:, :, R_SRE:R_SRE + 1024] = a_re.reshape(L, 1, 1024)
    rp[:, :, R_SIM:R_SIM + 1024] = a_im.reshape(L, 1, 1024)
    rp[:, :, R_SDT:R_SDT + 1024] = ldt_rep.reshape(L, 1, 1024)
    sh["rowpack"] = rp
    sh["consts"] = make_consts()
    x = f("x")
    mem = f("mem")
    pos = np.asarray(inp["positions"]).astype(np.int32)
    per_core = []
    for b in range(8):
        d = dict(sh)
        d["xT"] = tile_x(x[b])
        d["memT"] = np.ascontiguousarray(mem[b].T.reshape(8, 128, 1, 256).transpose(2, 1, 0, 3))
        d["pos"] = np.ascontiguousarray(pos[b][None, :])
        per_core.append(d)
    return per_core


def tile_x(xb):
    return np.ascontiguousarray(xb.T.reshape(8, 128, NB, 512).transpose(2, 1, 0, 3))


def untile_x(t):
    return np.ascontiguousarray(t.transpose(2, 1, 0, 3).reshape(D, T).T)


_CACHE = {}
LAYER_KEYS = ("colpack", "rowpack", "w_in", "w_krm", "w_krp", "wq_m", "wq_p", "w_ukv", "a_w_sT", "c_reT", "c_imT",
              "c_w_glu", "m_w_kv", "w_br", "w_out")
FUSED = True


def kernel(**inputs):
    in_maps = host_prep(inputs)
    if FUSED:
        if "nc" not in _CACHE:
            _CACHE["nc"] = Builder(nlayers=L).build()
        res = run_bass_kernel_spmd(_CACHE["nc"], in_maps, core_ids=list(range(8)))
        return np.stack([untile_x(r["outT"]) for r in res.results], axis=0).astype(np.float32)
    if "nc1" not in _CACHE:
        _CACHE["nc1"] = Builder(nlayers=1).build()
    nc = _CACHE["nc1"]
    xs = [m["xT"] for m in in_maps]
    for l in range(L):
        maps = []
        for c in range(8):
            d = dict(in_maps[c])
            for k in LAYER_KEYS:
                a = in_maps[c][k]
                d[k] = np.ascontiguousarray(np.concatenate([a[l:l + 1], a[l:l + 1]], axis=0))
            d["xT"] = xs[c]
            maps.append(d)
        res = run_bass_kernel_spmd(nc, maps, core_ids=list(range(8)))
        xs = [np.ascontiguousarray(r["outT"]) for r in res.results]
    return np.stack([untile_x(t) for t in xs], axis=0).astype(np.float32)
```

```python
import math
import contextlib
import numpy as np
import concourse.bass as bass
import concourse.mybir as mybir
from concourse.bass_utils import run_bass_kernel_spmd

F32 = mybir.dt.float32
BF16 = mybir.dt.bfloat16
I32 = mybir.dt.int32
AF = mybir.ActivationFunctionType
ALU = mybir.AluOpType

D = 1024
T = 2048
L = 2
NB = 4
EPS = 1e-6
IN_W = 7456
OFF = dict(a_u=0, a_v=256, a_g=512, cq=768, ckv=1536, kr=1792, bg=1824,
           cin=2336, cg=2592, mq=2848, mg=3104, mrg=3360)
PERM = list(range(16, 32)) + list(range(0, 16))
WIN_C0 = [0, 256, 512, 768, 1024, 1280, 1536, 1824, 2080, 2336, 2592, 2848, 3104] + \
         [3360 + br * 1024 + j2 * 256 for br in range(4) for j2 in range(4)]
WIN_G = {c: i for i, c in enumerate(WIN_C0)}
NWG = len(WIN_C0)
TWO_PI = 2.0 * math.pi

C_NG, C_MNG, C_QNG, C_KVNG, C_BM = 0, 8, 16, 22, 24
C_GQ, C_GQP, C_GK, C_GKP, C_MGQ, C_MGK = 56, 57, 58, 59, 60, 61
C_CD, C_BGLU, C_SRE, C_SIM, C_SDT, C_ABS, C_ROW = 62, 64, 66, 74, 82, 90, 346
NCOL = 346 + 640
R_ANG, R_SRE, R_SIM, R_SDT = 0, 256, 1280, 2304
NROW = 3328
K_TRI, K_IOT, K_IOS, K_INVF, K_SGN, K_GM = 0, 128, 256, 257, 258, 259
NCONST = 267


class Tok:
    __slots__ = ("w", "rs", "excl")

    def __init__(self):
        self.w = None
        self.rs = {}
        self.excl = False


class Op:
    __slots__ = ("eng", "fn", "deps", "is_dma", "sig", "users")

    def __init__(self, eng, fn, is_dma):
        self.eng = eng
        self.fn = fn
        self.is_dma = is_dma
        self.deps = []
        self.sig = None
        self.users = 0


ENGS = ("pe", "act", "dve", "pool", "sp")
SYNC_ALL = True
WQ = ("sp",)
DMAQ = ("sp", "act", "pool")


class Prog:
    def __init__(self, nc, n_dma_sems=8):
        self.nc = nc
        self.ops = {e: [] for e in ENGS}
        self.all = []
        self.n_dma_sems = n_dma_sems
        self.toks = {}
        self.dmas_since_bar = []

    def tok(self, *key):
        t = self.toks.get(key)
        if t is None:
            t = self.toks[key] = Tok()
        return t

    def _add(self, eng, fn, reads, writes, is_dma, extra_deps=()):
        op = Op(eng, fn, is_dma)
        deps = list(extra_deps)
        raw = set()
        for t in reads:
            if t.w is not None:
                deps.append(t.w)
                raw.add(id(t.w))
            if t.excl:
                deps.extend(o for o in t.rs.values() if o.eng != eng)
        for t in writes:
            if t.w is not None:
                deps.append(t.w)
            deps.extend(t.rs.values())
        rkey = ("dma", id(op)) if is_dma else eng
        for t in reads:
            t.rs[rkey] = op
        for t in writes:
            t.w = op
            t.rs = {}
        seen = set()
        for d in deps:
            if d is op or id(d) in seen:
                continue
            seen.add(id(d))
            if (not d.is_dma) and (not is_dma) and d.eng == eng:
                if eng == "pe" or (id(d) not in raw and not SYNC_ALL):
                    continue
            if (not d.is_dma) and is_dma and d.eng == eng:
                pass
            op.deps.append(d)
            d.users += 1
        self.ops[eng].append(op)
        self.all.append(op)
        if is_dma:
            self.dmas_since_bar.append(op)
        return op

    def op(self, eng, fn, reads=(), writes=()):
        return self._add(eng, fn, reads, writes, False)

    def dma(self, eng, fn, reads=(), writes=()):
        assert eng in DMAQ
        return self._add(eng, fn, reads, writes, True)

    def barrier(self):
        dm = self.dmas_since_bar
        self.dmas_since_bar = []
        bt = [self.tok("__bar", e) for e in ENGS]
        for i, e in enumerate(ENGS):
            self._add(e, lambda eng: eng.drain(), [], [bt[i]], False,
                      extra_deps=dm if e == "sp" else ())
        for e in ENGS:
            self._add(e, lambda eng: eng.nop(), bt, [], False)

    def emit(self, final_wait_ops=()):
        nc = self.nc
        with contextlib.ExitStack() as es:
            esem = {e: es.enter_context(nc.semaphore("s_" + e)) for e in ENGS}
            dsem = {e: [es.enter_context(nc.semaphore("d_%s%d" % (e, i)))
                        for i in range(self.n_dma_sems)] for e in DMAQ}
            ecount = {e: 0 for e in ENGS}
            dcount = {e: 0 for e in DMAQ}
            duse = {e: [0] * self.n_dma_sems for e in DMAQ}
            fw = set(id(o) for o in final_wait_ops)
            for op in self.all:
                if op.is_dma:
                    j = dcount[op.eng]
                    dcount[op.eng] += 1
                    s = j % self.n_dma_sems
                    duse[op.eng][s] += 1
                    op.sig = (dsem[op.eng][s], 16 * duse[op.eng][s], ("d", op.eng, s))
                elif op.users > 0 or id(op) in fw:
                    ecount[op.eng] += 1
                    op.sig = (esem[op.eng], ecount[op.eng], ("e", op.eng))
            self.stats = dict(ecount=ecount, dcount=dcount,
                              nops={e: len(self.ops[e]) for e in ENGS})
            block = es.enter_context(nc.Block())

            def run_engine(e, engine):
                waited = {}

                def wait(sem, val, key):
                    if waited.get(key, 0) >= val:
                        return
                    waited[key] = val
                    engine.wait_ge(sem, val)

                for op in self.ops[e]:
                    for d in op.deps:
                        wait(*d.sig)
                    if op.is_dma:
                        sem, val, key = op.sig
                        if val > 16:
                            wait(sem, val - 16, key)
                    ins = op.fn(engine)
                    if op.sig is not None:
                        ins.then_inc(op.sig[0], 16 if op.is_dma else 1)
                if e == "sp":
                    for op in final_wait_ops:
                        wait(*op.sig)

            block.tensor(lambda eng: run_engine("pe", eng))
            block.scalar(lambda eng: run_engine("act", eng))
            block.vector(lambda eng: run_engine("dve", eng))
            block.gpsimd(lambda eng: run_engine("pool", eng))
            block.sync(lambda eng: run_engine("sp", eng))


class Arena:
    def __init__(self, ap2d, nfloats):
        self.a = ap2d
        self.n = nfloats
        self.off = 0
        self.peak = 0

    def alloc(self, free_shape, dt):
        free_shape = list(free_shape)
        isz = 2 if dt == BF16 else 4
        n = int(np.prod(free_shape))
        n32 = (n * isz + 3) // 4
        assert self.off + n32 <= self.n, ("arena overflow", self.off, n32, self.n)
        v = self.a[:, self.off:self.off + n32]
        self.off += n32
        self.peak = max(self.peak, self.off)
        if dt == BF16:
            v = v.bitcast(BF16)[:, 0:n]
        elif dt == I32:
            v = v.bitcast(I32)
        if len(free_shape) == 2:
            v = v.rearrange("p (a b) -> p a b", a=free_shape[0])
        elif len(free_shape) == 3:
            v = v.rearrange("p (a b c) -> p a b c", a=free_shape[0], b=free_shape[1])
        elif len(free_shape) == 4:
            v = v.rearrange("p (a b c d) -> p a b c d", a=free_shape[0], b=free_shape[1],
                            c=free_shape[2])
        return v

    def mark(self):
        return self.off

    def reset(self, m):
        self.off = m


class Builder:
    def __init__(self, nlayers=L, stop=None, dumps=()):
        self.nlayers = nlayers
        self.stop = stop
        self.dumps = dict()
        self.want = set(dumps)
        self.nc = nc = bass.Bass("TRN2", target_bir_lowering=False)
        self.P = Prog(nc)
        self.dr = {}
        self.finals = []
        self.dump_specs = []

    def din(self, name, shape, dt=F32):
        self.dr[name] = self.nc.dram_tensor(name, list(shape), dt, kind="ExternalInput").ap()
        return self.dr[name]

    def T(self, *k):
        return self.P.tok(*k)

    def mm(self, out, lhsT, rhs, start, stop, reads, writes):
        return self.P.op("pe", lambda e: e.matmul(out, lhsT=lhsT, rhs=rhs, start=start, stop=stop),
                         reads=reads, writes=writes)

    def mmg(self, out, pairs, reads, wtok):
        n = len(pairs)
        for i, (a, b) in enumerate(pairs):
            self.mm(out, a, b, i == 0, i == n - 1, reads, [wtok])

    def act(self, out, in_, func, reads, writes, bias=0.0, scale=1.0, eng="act", accum_out=None):
        if accum_out is None:
            return self.P.op("act", lambda e: e.activation(out=out, in_=in_, func=func, bias=bias, scale=scale),
                             reads=reads, writes=writes)
        return self.P.op("act", lambda e: e.activation(out=out, in_=in_, func=func, bias=bias, scale=scale,
                                                       accum_out=accum_out),
                         reads=reads, writes=writes)

    def tt(self, eng, out, in0, in1, op, reads, writes):
        return self.P.op(eng, lambda e: e.tensor_tensor(out=out, in0=in0, in1=in1, op=op),
                         reads=reads, writes=writes)

    def ts(self, eng, out, in0, s1, op0, reads, writes, s2=None, op1=None):
        if op1 is None:
            return self.P.op(eng, lambda e: e.tensor_scalar(out=out, in0=in0, scalar1=s1, scalar2=None, op0=op0),
                             reads=reads, writes=writes)
        return self.P.op(eng, lambda e: e.tensor_scalar(out=out, in0=in0, scalar1=s1, scalar2=s2, op0=op0, op1=op1),
                         reads=reads, writes=writes)

    def stt(self, eng, out, in0, scalar, in1, op0, op1, reads, writes):
        return self.P.op(eng, lambda e: e.scalar_tensor_tensor(out=out, in0=in0, scalar=scalar, in1=in1,
                                                                op0=op0, op1=op1),
                         reads=reads, writes=writes)

    def cp(self, eng, out, in_, reads, writes):
        if eng == "act":
            return self.P.op("act", lambda e: e.copy(out=out, in_=in_), reads=reads, writes=writes)
        return self.P.op(eng, lambda e: e.tensor_copy(out=out, in_=in_), reads=reads, writes=writes)

    def memset(self, eng, ap, val, writes):
        return self.P.op(eng, lambda e: e.memset(ap, val), reads=(), writes=writes)

    def dma(self, out, in_, reads, writes, q="sp"):
        def ndesc(ap):
            dims = [(int(st), int(n)) for st, n in ap.ap]
            total = 1
            for st, n in dims:
                total *= n
            run = 1
            for st, n in reversed(dims[1:]):
                if st == run:
                    run *= n
                else:
                    break
            return total // run
        self.desc_count = getattr(self, "desc_count", {})
        self.desc_count[q] = self.desc_count.get(q, 0) + max(ndesc(out), ndesc(in_))
        return self.P.dma(q, lambda e: e.dma_start(out=out, in_=in_), reads=reads, writes=writes)

    def rstd(self, out, in_, n, reads, writes):
        self.act(out, in_, AF.Ln, reads, writes, bias=EPS, scale=1.0 / n)
        self.act(out, out, AF.Exp, writes, writes, scale=-0.5)

    def wload(self, dst, src, dtok, cast_eng="pool", scale=None):
        fs = list(dst.shape[1:])
        n = int(np.prod(fs))
        assert n <= self.stage_n, (n, self.stage_n)
        slot = self.wslot
        self.wslot = (slot + 1) % len(self.stage)
        st = self.stage[slot][:, 0:n]
        if len(fs) == 2:
            st = st.rearrange("p (a b) -> p a b", a=fs[0])
        elif len(fs) == 3:
            st = st.rearrange("p (a b c) -> p a b c", a=fs[0], b=fs[1])
        stok = self.T("stage", slot)
        self.wq_i = getattr(self, "wq_i", 0) + 1
        self.dma(st, src, [], [stok], q=WQ[self.wq_i % len(WQ)])
        if scale is None:
            self.cp(cast_eng, dst, st, [stok], [dtok])
        else:
            self.ts(cast_eng, dst, st, scale, ALU.mult, [stok], [dtok])

    def dump(self, name, ap, tok):
        if name not in self.want:
            return
        self.P.barrier()
        shp = list(ap.shape)
        d = self.nc.dram_tensor("dbg_" + name, shp, ap.dtype, kind="ExternalOutput").ap()
        op = self.dma(d, ap, [tok], [])
        self.finals.append(op)
        self.dump_specs.append(name)

    def build(self):
        nc = self.nc
        P = self.P
        din = self.din
        din("xT", [NB, 128, 8, 512])
        din("memT", [1, 128, 8, 256])
        din("pos", [1, T], I32)
        din("consts", [128, NCONST])
        din("colpack", [L, 128, NCOL])
        din("rowpack", [L, 128, NROW])
        din("w_in", [L, NWG, 128, 8, 256])
        din("w_krm", [L, 128, 8, 96])
        din("w_krp", [L, 128, 8, 96])
        din("wq_m", [L, 8, 128, 6, 96])
        din("wq_p", [L, 8, 128, 6, 96])
        din("w_ukv", [L, 8, 128, 2, 128])
        din("a_w_sT", [L, 128, 4, 128])
        din("c_reT", [L, 2, 128, 4, 128])
        din("c_imT", [L, 2, 128, 4, 128])
        din("c_w_glu", [L, 128, 2, 256])
        din("m_w_kv", [L, 2, 128, 8, 256])
        din("w_br", [L, 4, 128, 10, 256])
        din("w_out", [L, 4, 128, 8, 256])
        self.outT = nc.dram_tensor("outT", [NB, 128, 8, 512], F32, kind="ExternalOutput").ap()
        self.x1T = nc.dram_tensor("x1T", [NB, 128, 8, 512], F32, kind="Internal").ap()

        with contextlib.ExitStack() as es:
            NA = 52000
            arena_t = es.enter_context(nc.sbuf_tensor("arena", [128, NA], F32))
            self.A = A = Arena(arena_t[:, :], NA)
            self.ps = [es.enter_context(nc.psum_tensor("ps%d" % i, [128, 512], F32)) for i in range(8)]
            self.pst = [self.T("ps", i) for i in range(8)]
            for t in self.pst:
                t.excl = True

            self.cst = A.alloc([NCONST], F32)
            self.eps_col = A.alloc([1], F32)
            self.tri = A.alloc([128], BF16)
            self.ntri = A.alloc([128], BF16)
            self.ones = A.alloc([128], BF16)
            self.bd64 = A.alloc([128], BF16)
            self.ones_f = A.alloc([128], F32)
            self.cosT = A.alloc([T], F32)
            self.sinT = A.alloc([T], F32)
            self.colp = A.alloc([NCOL], F32)
            self.hT = A.alloc([8, T], BF16)
            self.yall = A.alloc([10, T], BF16)
            self.kmem = A.alloc([2, 256], BF16)
            self.vmem = A.alloc([2, 256], BF16)
            self.stage_n = 2048
            self.stage = [A.alloc([self.stage_n], F32) for _ in range(2)]
            self.wslot = 0
            self.base_mark = A.mark()

            self.setup_consts()
            if self.stop != "consts":
                for l in range(self.nlayers):
                    if self.run_layer(l):
                        break
            P.barrier()
            P.emit(final_wait_ops=self.finals)
        return nc

    def setup_consts(self):
        A, T_ = self.A, self.T
        tc = T_("cst")
        self.dma(self.cst, self.dr["consts"][:, :], [], [tc])
        self.memset("dve", self.eps_col, EPS, [tc])
        self.cp("dve", self.tri, self.cst[:, K_TRI:K_TRI + 128], [tc], [tc])
        self.ts("dve", self.ntri, self.cst[:, K_TRI:K_TRI + 128], -1.0, ALU.mult, [tc], [tc])
        self.memset("dve", self.ones, 1.0, [tc])
        self.memset("dve", self.ones_f, 1.0, [tc])
        self.memset("dve", self.bd64, 0.0, [tc])
        self.memset("dve", self.bd64[0:64, 0:64], 1.0, [tc])
        self.memset("dve", self.bd64[64:128, 64:128], 1.0, [tc])
        if getattr(self, "skip", None):
            self.memset("pool", self.yall, 0.25, [T_("yall_init")])
        m = A.mark()
        posi = A.alloc([T], I32)
        ang = A.alloc([T], F32)
        kf = A.alloc([T], F32)
        ki = A.alloc([T], I32)
        tr = T_("rope")
        self.dma(posi[0:96, :], self.dr["pos"][0:1, :].partition_broadcast(96), [], [tr])
        R = slice(64, 96)
        self.cp("dve", ang[R, :], posi[R, :], [tr], [tr])
        self.ts("dve", ang[R, :], ang[R, :], self.cst[R, K_INVF:K_INVF + 1], ALU.mult, [tr, tc], [tr])

        def sin_of(dst, shift, post_scale_col):
            self.ts("dve", ki[R, :], ang[R, :], shift, ALU.add, [tr], [tr], s2=1.0 / TWO_PI, op1=ALU.mult)
            self.cp("dve", kf[R, :], ki[R, :], [tr], [tr])
            self.stt("dve", kf[R, :], kf[R, :], -TWO_PI, ang[R, :], ALU.mult, ALU.add, [tr], [tr])
            self.ts("dve", kf[R, :], kf[R, :], shift, ALU.add, [tr], [tr], s2=math.pi, op1=ALU.min)
            self.ts("dve", kf[R, :], kf[R, :], -math.pi, ALU.max, [tr], [tr])
            self.act(dst[R, :], kf[R, :], AF.Sin, [tr], [tr])
            if post_scale_col is not None:
                self.ts("dve", dst[R, :], dst[R, :], post_scale_col, ALU.mult, [tr, tc], [tr])

        sin_of(self.sinT, 0.0, self.cst[R, K_SGN:K_SGN + 1])
        sin_of(self.cosT, math.pi / 2, None)
        self.dump("cosT", self.cosT[R, :], tr)
        self.dump("sinT", self.sinT[R, :], tr)
        self.P.barrier()
        A.reset(m)

    def run_layer(self, l):
        P, A, T_ = self.P, self.A, self.T
        src = self.dr["xT"] if l == 0 else self.x1T
        dst = self.x1T if l == 0 else self.outT
        if self.nlayers == 1:
            dst = self.outT
        tcp = T_("colp")
        self.dma(self.colp, self.dr["colpack"][l, :, :], [], [tcp])
        self.tcp = tcp
        stages = [("N", self.stage_norm), ("MP", self.stage_memprep), ("A", self.stage_a),
                  ("C", self.stage_c), ("M", self.stage_m), ("B", self.stage_b),
                  ("P2", self.stage_p2)]
        for name, fn in stages:
            if name in getattr(self, "skip", ()):
                continue
            m = A.mark()
            if name == "N":
                fn(l, src)
            elif name == "P2":
                fn(l, src, dst)
            else:
                fn(l)
            P.barrier()
            A.reset(m)
            if self.stop == (name, l):
                return True
        return False

    def norm_fm(self, srcT, n_tok, gcol, dst, dst_tok_fn, tag):
        A, T_ = self.A, self.T
        W = min(512, n_tok)
        nblk = n_tok // W
        xb = [A.alloc([8, W], F32) for _ in range(2)]
        sq = A.alloc([8, W], BF16)
        rs = A.alloc([W], F32)
        for b in range(nblk):
            x = xb[b % 2]
            tx = T_(tag + "x", b % 2)
            ts_ = T_(tag + "sq")
            trs = T_(tag + "rs")
            for hh in range(2):
                self.dma(x[:, hh * 4:(hh + 1) * 4, :], srcT[b, :, hh * 4:(hh + 1) * 4, :], [], [tx])
            pb = 0
            for kt in range(8):
                self.act(sq[:, kt, :], x[:, kt, :], AF.Square, [tx], [ts_])
            self.mmg(self.ps[pb][:, 0:W], [(self.ones[:, :], sq[:, kt, :]) for kt in range(8)],
                     [ts_], self.pst[pb])
            self.rstd(rs[:, :], self.ps[pb][:, 0:W], float(D), [self.pst[pb]], [trs])
            for kt in range(8):
                self.stt("dve", dst[:, kt, b * W:(b + 1) * W], x[:, kt, :], gcol[:, kt:kt + 1], rs[:, :],
                         ALU.mult, ALU.mult, [tx, trs, self.tcp], [dst_tok_fn(b)])

    def stage_norm(self, l, src):
        self.norm_fm(src, T, self.colp[:, C_NG:C_NG + 8], self.hT, lambda b: self.T("hT", b), "n")
        for b in range(NB):
            self.dump("hT%d_%d" % (l, b), self.hT[:, :, b * 512:(b + 1) * 512], self.T("hT", b))

    def load_win(self, l, c0, ncols, wt, wtok):
        assert ncols == 256
        self.wload(wt[:, :, 0:ncols], self.dr["w_in"][l, WIN_G[c0]], wtok)

    def zproj_blk(self, wt, ncols, blk, pb, wtok, m_off=0):
        self.mmg(self.ps[pb][m_off:m_off + ncols, :],
                 [(wt[:, kt, 0:ncols], self.hT[:, kt, blk * 512:(blk + 1) * 512]) for kt in range(8)],
                 [wtok, self.T("hT", blk)], self.pst[pb])

    def stage_memprep(self, l):
        A, T_ = self.A, self.T
        hm = A.alloc([8, 256], BF16)
        thm = T_("hm")
        self.norm_fm(self.dr["memT"], 256, self.colp[:, C_MNG:C_MNG + 8], hm, lambda b: thm, "m")
        wkv = A.alloc([8, 512], BF16)
        twk = T_("wkv")
        for half in range(2):
            self.wload(wkv[:, :, half * 256:(half + 1) * 256],
                       self.dr["m_w_kv"][l, half],
                       twk)
        sq = A.alloc([256], BF16)
        rs = A.alloc([256], F32)
        tsq, trs, tkm, tvm = T_("mp_sq"), T_("mp_rs"), T_("kmem"), T_("vmem")
        for ct in range(2):
            self.mmg(self.ps[0][:, 0:256], [(wkv[:, kt, ct * 128:(ct + 1) * 128], hm[:, kt, :]) for kt in range(8)],
                     [twk, thm], self.pst[0])
            self.act(sq[:, :], self.ps[0][:, 0:256], AF.Square, [self.pst[0]], [tsq])
            self.mmg(self.ps[1][:, 0:256], [(self.bd64[:, :], sq[:, :])], [tsq, T_("cst")], self.pst[1])
            self.rstd(rs[:, :], self.ps[1][:, 0:256], 64.0, [self.pst[1]], [trs])
            self.stt("dve", self.kmem[:, ct, :], self.ps[0][:, 0:256], self.colp[:, C_MGK:C_MGK + 1], rs[:, :],
                     ALU.mult, ALU.mult, [self.pst[0], trs, self.tcp], [tkm])
        for mt in range(2):
            self.mmg(self.ps[2][:, 0:256], [(hm[:, kt, mt * 128:(mt + 1) * 128], wkv[:, kt, 256:512]) for kt in range(8)],
                     [twk, thm], self.pst[2])
            self.cp("dve", self.vmem[:, mt, :], self.ps[2][:, 0:256], [self.pst[2]], [tvm])
        self.dump("kmem%d" % l, self.kmem, tkm)
        self.dump("vmem%d" % l, self.vmem, tvm)

    def stage_a(self, l):
        A, T_ = self.A, self.T
        ya = self.yall
        wt = [A.alloc([8, 256], BF16) for _ in range(2)]
        gnb = A.alloc([256], F32)
        wsT = A.alloc([4, 128], BF16)
        tmp = A.alloc([512], BF16)
        tgn, tws, ttmp = T_("a_gn"), T_("a_ws"), T_("a_tmp")
        self.dma(gnb, self.dr["rowpack"][l, :, R_ANG:R_ANG + 256], [], [tgn])
        slot = self.wslot
        self.wslot = (slot + 1) % 2
        st = self.stage[slot][:, 0:512].rearrange("p (g t) -> p g t", g=4)
        stok = T_("stage", slot)
        self.dma(st, self.dr["a_w_sT"][l], [], [stok])
        for g in range(4):
            self.tt("dve", wsT[:, g, :], st[:, g, :], self.cst[:, K_TRI:K_TRI + 128], ALU.mult, [stok, T_("cst")], [tws])
        tw0, tw1 = T_("a_w", 0), T_("a_w", 1)
        self.load_win(l, OFF["a_u"], 256, wt[0], tw0)
        self.load_win(l, OFF["a_g"], 256, wt[1], tw1)
        for ct in range(2):
            for blk in range(NB):
                pb = (ct * NB + blk) % 2
                self.mmg(self.ps[pb][:, :],
                         [(wt[0][:, kt, ct * 128:(ct + 1) * 128], self.hT[:, kt, blk * 512:(blk + 1) * 512])
                          for kt in range(8)], [tw0, T_("hT", blk)], self.pst[pb])
                self.act(ya[:, ct, blk * 512:(blk + 1) * 512], self.ps[pb][:, :], AF.Gelu_apprx_tanh,
                         [self.pst[pb]], [T_("ya", ct, blk)])
        for ct in range(2):
            for blk in range(NB):
                pb = 2 + (ct * NB + blk) % 2
                self.mmg(self.ps[pb][:, :],
                         [(wt[1][:, kt, ct * 128:(ct + 1) * 128], self.hT[:, kt, blk * 512:(blk + 1) * 512])
                          for kt in range(8)], [tw1, T_("hT", blk)], self.pst[pb])
                self.act(tmp[:, :], self.ps[pb][:, :], AF.Silu, [self.pst[pb]], [ttmp])
                sl = ya[:, ct, blk * 512:(blk + 1) * 512]
                self.tt("pool", sl, sl, tmp[:, :], ALU.mult, [ttmp], [T_("ya", ct, blk)])
        twv = T_("a_w", 0)
        self.load_win(l, OFF["a_v"], 256, wt[0], twv)
        gv = A.alloc([256], F32)
        ssq = A.alloc([1], F32)
        vn = [A.alloc([256], BF16) for _ in range(2)]
        sb = A.alloc([2, 128], F32)
        junk = A.alloc([256], BF16)
        tgv, tss, tsb = T_("a_gv"), T_("a_ss"), T_("a_sb")
        absT = self.colp[:, C_ABS:C_ABS + 256].rearrange("p (c t) -> p c t", c=2)
        for tt_ in range(16):
            blk = tt_ // 4
            pb = 4 + tt_ % 2
            self.mmg(self.ps[pb][:, 0:256],
                     [(self.hT[:, kt, tt_ * 128:(tt_ + 1) * 128], wt[0][:, kt, :]) for kt in range(8)],
                     [twv, T_("hT", blk)], self.pst[pb])
            self.act(gv[:, :], self.ps[pb][:, 0:256], AF.Gelu_apprx_tanh, [self.pst[pb]], [tgv])
            self.act(junk[:, :], gv[:, :], AF.Square, [tgv], [tss], accum_out=ssq[:, :])
            self.rstd(ssq[:, :], ssq[:, :], 256.0, [tss], [tss])
            v = vn[tt_ % 2]
            tv = T_("a_vn", tt_ % 2)
            self.stt("dve", v[:, :], gv[:, :], ssq[:, 0:1], gnb[:, :], ALU.mult, ALU.mult, [tgv, tss, tgn], [tv])
            pq = 6 + tt_ % 2
            for g in range(4):
                ct, r0 = g // 2, (g % 2) * 64
                self.mm(self.ps[pq][r0:r0 + 64, ct * 128:(ct + 1) * 128], v[:, g * 64:(g + 1) * 64], wsT[:, g, :],
                        True, True, [tv, tws], [self.pst[pq]])
            psv = self.ps[pq][:, 0:256].rearrange("p (c t) -> p c t", c=2)
            self.tt("dve", sb[:, :, :], psv, absT, ALU.add, [self.pst[pq], self.tcp], [tsb])
            sl = ya[:, 0:2, tt_ * 128:(tt_ + 1) * 128]
            self.tt("pool", sl, sl, sb[:, :, :], ALU.mult, [tsb], [T_("ya", 0, blk), T_("ya", 1, blk)])
        self.dump("ya%d" % l, ya[:, 0:2, :], T_("ya", 0, 0))
        self.dump("a_vn%d" % l, vn[1], T_("a_vn", 1))
        self.dump("a_sb%d" % l, sb, tsb)
        self.dump("a_gv%d" % l, gv, tgv)
        self.dump("a_ssq%d" % l, ssq, tss)

    def sin_of(self, dst, ang, shift, ki, kf, tok, extra=()):
        rd = [tok] + list(extra)
        self.ts("dve", ki, ang, shift, ALU.add, rd, [tok], s2=1.0 / TWO_PI, op1=ALU.mult)
        self.cp("dve", kf, ki, [tok], [tok])
        self.stt("dve", kf, kf, -TWO_PI, ang, ALU.mult, ALU.add, [tok], [tok])
        self.ts("dve", kf, kf, shift, ALU.add, [tok], [tok], s2=math.pi, op1=ALU.min)
        self.ts("dve", kf, kf, -math.pi, ALU.max, [tok], [tok])
        self.act(dst, kf, AF.Sin, [tok], [tok])

    def stage_m(self, l):
        A, T_ = self.A, self.T
        ya = self.yall
        wq = A.alloc([8, 256], BF16)
        wg = A.alloc([8, 256], BF16)
        twq, twg = T_("m_wq"), T_("m_wg")
        self.load_win(l, OFF["mq"], 256, wq, twq)
        self.load_win(l, OFF["mg"], 256, wg, twg)
        for ct in range(2):
            for blk in range(NB):
                pb = (ct * NB + blk) % 2
                self.mmg(self.ps[pb][:, :],
                         [(wg[:, kt, ct * 128:(ct + 1) * 128], self.hT[:, kt, blk * 512:(blk + 1) * 512])
                          for kt in range(8)], [twg], self.pst[pb])
                self.act(ya[:, 8 + ct, blk * 512:(blk + 1) * 512], self.ps[pb][:, :], AF.Silu,
                         [self.pst[pb]], [T_("ym", ct, blk)])
        sq = A.alloc([512], BF16)
        rs = A.alloc([512], F32)
        qn = [A.alloc([2, 512], BF16) for _ in range(2)]
        pT = [A.alloc([512], BF16) for _ in range(4)]
        rc = A.alloc([512], F32)
        ot = A.alloc([512], BF16)
        tsq, trs, trc, tot = T_("m_sq"), T_("m_rs"), T_("m_rc"), T_("m_ot")
        scale = 64.0 ** -0.5
        for blk in range(NB):
            q = qn[blk % 2]
            tq = T_("m_qn", blk % 2)
            for ct in range(2):
                self.mmg(self.ps[2][:, :],
                         [(wq[:, kt, ct * 128:(ct + 1) * 128], self.hT[:, kt, blk * 512:(blk + 1) * 512])
                          for kt in range(8)], [twq], self.pst[2])
                self.act(sq[:, :], self.ps[2][:, :], AF.Square, [self.pst[2]], [tsq])
                self.mmg(self.ps[3][:, :], [(self.bd64[:, :], sq[:, :])], [tsq], self.pst[3])
                self.rstd(rs[:, :], self.ps[3][:, :], 64.0, [self.pst[3]], [trs])
                self.stt("dve", q[:, ct, :], self.ps[2][:, :], self.colp[:, C_MGQ:C_MGQ + 1], rs[:, :],
                         ALU.mult, ALU.mult, [self.pst[2], trs], [tq])
            for h in range(4):
                ct, r0 = h // 2, (h % 2) * 64
                R = slice(r0, r0 + 64)
                for mt in range(2):
                    pb = 4 + mt
                    p = pT[(h % 2) * 2 + mt]
                    tp = T_("m_pT", (h % 2) * 2 + mt)
                    self.mm(self.ps[pb][:, :], self.kmem[R, ct, mt * 128:(mt + 1) * 128], q[R, ct, :],
                            True, True, [tq], [self.pst[pb]])
                    self.act(p[:, :], self.ps[pb][:, :], AF.Exp, [self.pst[pb]], [tp], scale=scale)
                tps = [T_("m_pT", (h % 2) * 2 + mt) for mt in range(2)]
                ps_o, ps_d = self.ps[6], self.ps[7]
                self.mmg(ps_o[R, :], [(self.vmem[:, mt, h * 64:(h + 1) * 64], pT[(h % 2) * 2 + mt][:, :])
                                      for mt in range(2)], tps, self.pst[6])
                self.mmg(ps_d[R, :], [(self.ones[:, 0:64], pT[(h % 2) * 2 + mt][:, :]) for mt in range(2)],
                         tps, self.pst[7])
                self.P.op("dve", lambda e, o=rc[R, :], i=ps_d[R, :]: e.reciprocal(out=o, in_=i),
                          reads=[self.pst[7]], writes=[trc])
                self.tt("dve", ot[R, :], ps_o[R, :], rc[R, :], ALU.mult, [self.pst[6], trc], [tot])
                sl = ya[R, 8 + ct, blk * 512:(blk + 1) * 512]
                self.tt("pool", sl, sl, ot[R, :], ALU.mult, [tot], [T_("ym", ct, blk)])
        self.dump("ym%d" % l, ya[:, 8:10, :], T_("ym", 0, 0))

    def stage_c(self, l):
        A, T_ = self.A, self.T
        ya = self.yall
        uT = A.alloc([2, T], BF16)
        ygT = A.alloc([2, T], BF16)
        TAc = A.alloc([1024], F32)
        TAs = A.alloc([1024], F32)
        Dr = A.alloc([8, 128], F32)
        Di = A.alloc([8, 128], F32)
        a128r = A.alloc([8], F32)
        a128i = A.alloc([8], F32)
        Bm = [A.alloc([2, 8, 64], BF16) for _ in range(2)]
        CreT = A.alloc([8, 128], BF16)
        CreTn = A.alloc([8, 128], BF16)
        CimTn = A.alloc([8, 128], BF16)
        wglu = A.alloc([2, 256], BF16)
        m2 = A.mark()
        wu = A.alloc([8, 256], BF16)
        wg = A.alloc([8, 256], BF16)
        twu, twg = T_("c_wu"), T_("c_wg")
        self.load_win(l, OFF["cin"], 256, wu, twu)
        self.load_win(l, OFF["cg"], 256, wg, twg)
        for ct in range(2):
            for blk in range(NB):
                pb = (ct * NB + blk) % 2
                self.mmg(self.ps[pb][:, :],
                         [(wu[:, kt, ct * 128:(ct + 1) * 128], self.hT[:, kt, blk * 512:(blk + 1) * 512])
                          for kt in range(8)], [twu], self.pst[pb])
                self.cp("act", uT[:, ct, blk * 512:(blk + 1) * 512], self.ps[pb][:, :], [self.pst[pb]],
                        [T_("c_uT", blk)])
        for ct in range(2):
            for blk in range(NB):
                pb = 2 + (ct * NB + blk) % 2
                self.mmg(self.ps[pb][:, :],
                         [(wg[:, kt, ct * 128:(ct + 1) * 128], self.hT[:, kt, blk * 512:(blk + 1) * 512])
                          for kt in range(8)], [twg], self.pst[pb])
                self.act(ya[:, 6 + ct, blk * 512:(blk + 1) * 512], self.ps[pb][:, :], AF.Silu,
                         [self.pst[pb]], [T_("yc", ct, blk)])
        tt_ = T_("c_tab")
        rowp = A.alloc([3, 1024], F32)
        self.dma(rowp, self.dr["rowpack"][l, :, R_SRE:R_SRE + 3072].rearrange("p (a b) -> p a b", a=3), [], [tt_])
        s1 = A.alloc([1024], F32)
        s2 = A.alloc([1024], F32)
        si = A.alloc([1024], I32)
        negs = A.alloc([1], F32)
        tcst = T_("cst")
        self.ts("dve", negs, self.cst[:, K_IOS:K_IOS + 1], -1.0, ALU.mult, [tcst], [tt_])
        self.act(rowp[:, 2, :], rowp[:, 2, :], AF.Exp, [tt_], [tt_])
        self.tt("dve", rowp[:, 0, :], rowp[:, 0, :], rowp[:, 2, :], ALU.mult, [tt_], [tt_])
        self.tt("dve", rowp[:, 1, :], rowp[:, 1, :], rowp[:, 2, :], ALU.mult, [tt_], [tt_])
        self.act(s1, rowp[:, 0, :], AF.Exp, [tt_], [tt_], scale=negs[:, 0:1])
        self.ts("dve", s2, rowp[:, 1, :], self.cst[:, K_IOS:K_IOS + 1], ALU.mult, [tt_, tcst], [tt_])
        self.sin_of(TAs, s2, 0.0, si, rowp[:, 2, :], tt_)
        self.sin_of(TAc, s2, math.pi / 2, si, rowp[:, 2, :], tt_)
        self.tt("dve", TAs, TAs, s1, ALU.mult, [tt_], [tt_])
        self.tt("dve", TAc, TAc, s1, ALU.mult, [tt_], [tt_])
        s1v = s1.rearrange("p (j t) -> p j t", j=8)
        s2v = s2.rearrange("p (j t) -> p j t", j=8)
        siv = si.rearrange("p (j t) -> p j t", j=8)
        kfv = rowp[:, 2, :].rearrange("p (j t) -> p j t", j=8)
        dtj = A.alloc([8], F32)
        thrj = A.alloc([8], F32)
        thij = A.alloc([8], F32)
        e128 = A.alloc([8], F32)
        p128 = A.alloc([8], F32)
        k128 = A.alloc([8], F32)
        i128 = A.alloc([8], I32)
        tcp = self.tcp
        self.act(dtj, self.colp[:, C_SDT:C_SDT + 8], AF.Exp, [tcp, tt_], [tt_])
        self.tt("dve", thrj, self.colp[:, C_SRE:C_SRE + 8], dtj, ALU.mult, [tcp, tt_], [tt_])
        self.tt("dve", thij, self.colp[:, C_SIM:C_SIM + 8], dtj, ALU.mult, [tcp, tt_], [tt_])
        iot = self.cst[:, K_IOT:K_IOT + 128]
        for j in range(8):
            self.act(s1v[:, j, :], iot, AF.Exp, [tt_, tcst], [tt_], scale=thrj[:, j:j + 1])
            self.ts("dve", s2v[:, j, :], iot, thij[:, j:j + 1], ALU.mult, [tt_, tcst], [tt_])
        self.sin_of(Di.rearrange("p j t -> p (j t)"), s2, 0.0, si, rowp[:, 2, :], tt_)
        self.sin_of(Dr.rearrange("p j t -> p (j t)"), s2, math.pi / 2, si, rowp[:, 2, :], tt_)
        self.tt("dve", Di, Di, s1v, ALU.mult, [tt_], [tt_])
        self.tt("dve", Dr, Dr, s1v, ALU.mult, [tt_], [tt_])
        self.act(e128, thrj, AF.Exp, [tt_], [tt_], scale=128.0)
        self.ts("dve", p128, thij, 128.0, ALU.mult, [tt_], [tt_])
        self.sin_of(a128i, p128, 0.0, i128, k128, tt_)
        self.sin_of(a128r, p128, math.pi / 2, i128, k128, tt_)
        self.tt("dve", a128i, a128i, e128, ALU.mult, [tt_], [tt_])
        self.tt("dve", a128r, a128r, e128, ALU.mult, [tt_], [tt_])
        w = [A.alloc([64], F32) for _ in range(8)]
        wi = A.alloc([64], I32)
        gmask = self.cst[:, K_GM:K_GM + 8]
        for ct in range(2):
            base = C_ROW + ct * 320
            are = self.colp[:, base:base + 64]
            aim = self.colp[:, base + 64:base + 128]
            ldt = self.colp[:, base + 128:base + 192]
            bre = self.colp[:, base + 192:base + 256]
            bim = self.colp[:, base + 256:base + 320]
            dt_, thr, thi, ea, abr, abi, t0, t1 = w
            rd = [tt_, tcp]
            self.act(dt_, ldt, AF.Exp, rd, [tt_])
            self.tt("dve", thr, are, dt_, ALU.mult, rd, [tt_])
            self.tt("dve", thi, aim, dt_, ALU.mult, rd, [tt_])
            self.act(ea, thr, AF.Exp, [tt_], [tt_])
            self.sin_of(abi, thi, 0.0, wi, t0, tt_)
            self.sin_of(abr, thi, math.pi / 2, wi, t0, tt_)
            self.tt("dve", abi, abi, ea, ALU.mult, [tt_], [tt_])
            self.tt("dve", abr, abr, ea, ALU.mult, [tt_], [tt_])
            self.ts("dve", abr, abr, -1.0, ALU.add, [tt_], [tt_])
            self.tt("dve", dt_, are, are, ALU.mult, rd, [tt_])
            self.tt("dve", t0, aim, aim, ALU.mult, rd, [tt_])
            self.tt("dve", dt_, dt_, t0, ALU.add, [tt_], [tt_])
            self.P.op("dve", lambda e, o=dt_, i=dt_: e.reciprocal(out=o, in_=i), reads=[tt_], writes=[tt_])
            self.tt("dve", t0, abr, are, ALU.mult, rd, [tt_])
            self.tt("dve", t1, abi, aim, ALU.mult, rd, [tt_])
            self.tt("dve", t0, t0, t1, ALU.add, [tt_], [tt_])
            self.tt("dve", thr, t0, dt_, ALU.mult, [tt_], [tt_])
            self.tt("dve", t0, abi, are, ALU.mult, rd, [tt_])
            self.tt("dve", t1, abr, aim, ALU.mult, rd, [tt_])
            self.tt("dve", t0, t0, t1, ALU.subtract, [tt_], [tt_])
            self.tt("dve", thi, t0, dt_, ALU.mult, [tt_], [tt_])
            self.tt("dve", t0, thr, bre, ALU.mult, rd, [tt_])
            self.tt("dve", t1, thi, bim, ALU.mult, rd, [tt_])
            self.tt("dve", ea, t0, t1, ALU.subtract, [tt_], [tt_])
            self.tt("dve", t0, thr, bim, ALU.mult, rd, [tt_])
            self.tt("dve", t1, thi, bre, ALU.mult, rd, [tt_])
            self.tt("dve", abi, t0, t1, ALU.add, [tt_], [tt_])
            for ri, src_ in enumerate((ea, abi)):
                self.tt("dve", Bm[ct][:, ri, :, :], src_.unsqueeze(1).to_broadcast([128, 8, 64]),
                        gmask.unsqueeze(2).to_broadcast([128, 8, 64]), ALU.mult, [tt_, tcst], [tt_])
        tct = T_("c_ct")
        for j0 in (0, 4):
            srcr = self.dr["c_reT"][l, j0 // 4]
            srci = self.dr["c_imT"][l, j0 // 4]
            self.wload(CreT[:, j0:j0 + 4, :], srcr, tct)
            self.wload(CreTn[:, j0:j0 + 4, :], srcr, tct, scale=-1.0)
            self.wload(CimTn[:, j0:j0 + 4, :], srci, tct, scale=-1.0)
        self.wload(wglu, self.dr["c_w_glu"][l], tct)
        self.dump("c_TAc%d" % l, TAc, tt_)
        self.dump("c_TAs%d" % l, TAs, tt_)
        self.dump("c_Dr%d" % l, Dr, tt_)
        self.dump("c_Di%d" % l, Di, tt_)
        self.dump("c_Bm%d" % l, Bm[0], tt_)
        self.dump("c_a128r%d" % l, a128r, tt_)
        self.P.barrier()
        A.reset(m2)
        aS = A.alloc([16], F32)
        self.memset("dve", aS, 0.0, [T_("c_aS", 0), T_("c_aS", 1)])
        X = [A.alloc([4, 512], BF16) for _ in range(2)]
        Pf = A.alloc([8, 128], F32)
        Y = A.alloc([4, 4, 128], BF16)
        p127 = A.alloc([8], F32)
        c1 = A.alloc([8], F32)
        c2 = A.alloc([8], F32)
        ys = A.alloc([128], F32)
        tPf, tY, tys, tch = T_("c_Pf"), T_("c_Y"), T_("c_ys"), T_("c_ch")
        for n in range(16):
            blk = n // 4
            cs = slice(n * 128, (n + 1) * 128)
            for ct in range(2):
                hi = (n * 2 + ct) % 2
                pre_, pim_ = self.ps[2 * hi], self.ps[2 * hi + 1]
                tpre, tpim = self.pst[2 * hi], self.pst[2 * hi + 1]
                Bre = Bm[ct][:, 0, :, :].rearrange("p g q -> p (g q)")
                Bim = Bm[ct][:, 1, :, :].rearrange("p g q -> p (g q)")
                self.mm(pre_[:, :], uT[:, ct, cs], Bre, True, True, [T_("c_uT", blk), tt_], [tpre])
                self.mm(pim_[:, :], uT[:, ct, cs], Bim, True, True, [T_("c_uT", blk), tt_], [tpim])
                x = X[hi]
                tx = T_("c_X", hi)
                tc_ = TAc[:, ct * 512:(ct + 1) * 512]
                ts_ = TAs[:, ct * 512:(ct + 1) * 512]
                self.tt("dve", x[:, 0, :], pre_[:, :], tc_, ALU.mult, [tpre, tt_], [tx])
                self.tt("dve", x[:, 1, :], pim_[:, :], ts_, ALU.mult, [tpim, tt_], [tx])
                self.tt("dve", x[:, 2, :], pim_[:, :], tc_, ALU.mult, [tpim, tt_], [tx])
                self.tt("dve", x[:, 3, :], pre_[:, :], ts_, ALU.mult, [tpre, tt_], [tx])
                Pre, Pim = self.ps[4], self.ps[5]
                for jl in range(4):
                    js = slice(jl * 128, (jl + 1) * 128)
                    self.mmg(Pre[:, js], [(x[:, 0, js], self.tri[:, :]), (x[:, 1, js], self.tri[:, :])],
                             [tx], self.pst[4])
                    self.mmg(Pim[:, js], [(x[:, 2, js], self.tri[:, :]), (x[:, 3, js], self.ntri[:, :])],
                             [tx], self.pst[5])
                ta = T_("c_aS", ct)
                jr = slice(4 * ct, 4 * ct + 4)
                ji = slice(8 + 4 * ct, 8 + 4 * ct + 4)
                self.tt("dve", Pf[:, 0:4, :], Pre[:, :].rearrange("p (j t) -> p j t", j=4),
                        aS[:, jr].unsqueeze(2).to_broadcast([128, 4, 128]), ALU.add, [self.pst[4], ta], [tPf])
                self.tt("dve", Pf[:, 4:8, :], Pim[:, :].rearrange("p (j t) -> p j t", j=4),
                        aS[:, ji].unsqueeze(2).to_broadcast([128, 4, 128]), ALU.add, [self.pst[5], ta], [tPf])
                self.cp("dve", p127, Pf[:, :, 127], [tPf], [tch])
                self.tt("dve", c1[:, 0:4], a128r[:, jr], p127[:, 0:4], ALU.mult, [tch, tt_], [tch])
                self.tt("dve", c1[:, 4:8], a128i[:, jr], p127[:, 4:8], ALU.mult, [tch, tt_], [tch])
                self.tt("dve", c2[:, 0:4], a128r[:, jr], p127[:, 4:8], ALU.mult, [tch, tt_], [tch])
                self.tt("dve", c2[:, 4:8], a128i[:, jr], p127[:, 0:4], ALU.mult, [tch, tt_], [tch])
                self.tt("dve", aS[:, jr], c1[:, 0:4], c1[:, 4:8], ALU.subtract, [tch], [ta])
                self.tt("dve", aS[:, ji], c2[:, 0:4], c2[:, 4:8], ALU.add, [tch], [ta])
                self.tt("pool", Y[:, 0, :, :], Pf[:, 0:4, :], Dr[:, jr, :], ALU.mult, [tPf, tt_], [tY])
                self.tt("pool", Y[:, 1, :, :], Pf[:, 4:8, :], Di[:, jr, :], ALU.mult, [tPf, tt_], [tY])
                self.tt("pool", Y[:, 2, :, :], Pf[:, 4:8, :], Dr[:, jr, :], ALU.mult, [tPf, tt_], [tY])
                self.tt("pool", Y[:, 3, :, :], Pf[:, 0:4, :], Di[:, jr, :], ALU.mult, [tPf, tt_], [tY])
                py = self.ps[6]
                pairs = []
                for jl in range(4):
                    j = 4 * ct + jl
                    pairs += [(CreT[:, j, :], Y[:, 0, jl, :]), (CreTn[:, j, :], Y[:, 1, jl, :]),
                              (CimTn[:, j, :], Y[:, 2, jl, :]), (CimTn[:, j, :], Y[:, 3, jl, :])]
                self.mmg(py[:, 0:128], pairs, [tY, tct], self.pst[6])
                self.stt("dve", ys, uT[:, ct, cs], self.colp[:, C_CD + ct:C_CD + ct + 1], py[:, 0:128],
                         ALU.mult, ALU.add, [self.pst[6], T_("c_uT", blk), tcp], [tys])
                self.act(ygT[:, ct, cs], ys, AF.Gelu_apprx_tanh, [tys], [T_("c_yg", blk)])
        self.dump("c_yg%d" % l, ygT, T_("c_yg", 0))
        sg = A.alloc([512], BF16)
        tsg = T_("c_sg")
        for blk in range(NB):
            bs = slice(blk * 512, (blk + 1) * 512)
            for cc in range(2):
                pb = 7
                self.mmg(self.ps[pb][:, :], [(wglu[:, ct, cc * 128:(cc + 1) * 128], ygT[:, ct, bs]) for ct in range(2)],
                         [tct, T_("c_yg", blk)], self.pst[pb])
                self.act(sg, self.ps[pb][:, :], AF.Sigmoid, [self.pst[pb], tcp], [tsg],
                         bias=self.colp[:, C_BGLU + cc:C_BGLU + cc + 1])
                sl = ya[:, 6 + cc, bs]
                self.tt("pool", sg, sg, ygT[:, cc, bs], ALU.mult, [tsg, T_("c_yg", blk)], [tsg])
                self.tt("pool", sl, sl, sg, ALU.mult, [tsg], [T_("yc", cc, blk)])
        self.dump("yc%d" % l, ya[:, 6:8, :], T_("yc", 0, 0))

    def stage_b(self, l):
        A, T_, P = self.A, self.T, self.P
        ya = self.yall
        tcp = self.tcp
        tcst = T_("cst")
        cqn = A.alloc([6, T], BF16)
        ckvn = A.alloc([2, T], BF16)
        krr = A.alloc([T], F32)
        sqkr = A.alloc([T], BF16)
        m2 = A.mark()
        wt = A.alloc([8, 512], BF16)
        wq_in = A.alloc([8, 768], BF16)
        wkv_in = A.alloc([8, 256], BF16)
        wkr = [A.alloc([8, 96], BF16) for _ in range(2)]
        sq = A.alloc([512], BF16)
        rs = A.alloc([512], F32)
        t1 = A.alloc([512], F32)
        t2 = A.alloc([512], F32)
        tw = T_("b_w")
        for i in range(2):
            self.load_win(l, OFF["bg"] + i * 256, 256, wt[:, :, i * 256:(i + 1) * 256], tw)
        for i in range(3):
            self.load_win(l, OFF["cq"] + i * 256, 256, wq_in[:, :, i * 256:(i + 1) * 256], tw)
        self.load_win(l, OFF["ckv"], 256, wkv_in, tw)
        self.wload(wkr[0], self.dr["w_krm"][l], tw)
        self.wload(wkr[1], self.dr["w_krp"][l], tw)
        for ct in range(4):
            for blk in range(NB):
                pb = (ct * NB + blk) % 2
                self.zproj_blk(wt[:, :, ct * 128:(ct + 1) * 128], 128, blk, pb, tw)
                self.act(ya[:, 2 + ct, blk * 512:(blk + 1) * 512], self.ps[pb][:, :], AF.Silu,
                         [self.pst[pb]], [T_("yb", ct, blk)])
        tsq, trs = T_("b_sq"), T_("b_rs")
        R = slice(64, 96)
        for blk in range(NB):
            bs = slice(blk * 512, (blk + 1) * 512)
            for i in range(6):
                self.zproj_blk(wq_in[:, :, i * 128:(i + 1) * 128], 128, blk, i, tw)
                self.act(sq, self.ps[i][:, :], AF.Square, [self.pst[i]], [tsq])
                self.mm(self.ps[6][:, :], self.ones[:, :], sq, i == 0, i == 5, [tsq], [self.pst[6]])
            self.rstd(rs, self.ps[6][:, :], 768.0, [self.pst[6]], [trs])
            for i in range(6):
                self.stt("dve", cqn[:, i, bs], self.ps[i][:, :], self.colp[:, C_QNG + i:C_QNG + i + 1], rs,
                         ALU.mult, ALU.mult, [self.pst[i], trs, tcp], [T_("b_cqn", blk)])
            for i in range(2):
                self.zproj_blk(wkv_in[:, :, i * 128:(i + 1) * 128], 128, blk, i, tw)
                self.act(sq, self.ps[i][:, :], AF.Square, [self.pst[i]], [tsq])
                self.mm(self.ps[7][:, :], self.ones[:, :], sq, i == 0, i == 1, [tsq], [self.pst[7]])
            self.rstd(rs, self.ps[7][:, :], 256.0, [self.pst[7]], [trs])
            for i in range(2):
                self.stt("dve", ckvn[:, i, bs], self.ps[i][:, :], self.colp[:, C_KVNG + i:C_KVNG + i + 1], rs,
                         ALU.mult, ALU.mult, [self.pst[i], trs, tcp], [T_("b_ckvn", blk)])
            for i in range(2):
                self.zproj_blk(wkr[i], 96, blk, 2 + i, tw)
            self.act(sqkr[R, bs], self.ps[2][R, :], AF.Square, [self.pst[2]], [T_("b_kr", blk)])
            tt1 = T_("b_t1")
            self.stt("dve", t1[R, :], self.ps[2][R, :], self.colp[R, C_GK:C_GK + 1], self.cosT[R, bs],
                     ALU.mult, ALU.mult, [self.pst[2], tcp], [tt1])
            self.stt("dve", t2[R, :], self.ps[3][R, :], self.colp[R, C_GKP:C_GKP + 1], self.sinT[R, bs],
                     ALU.mult, ALU.mult, [self.pst[3], tcp], [tt1])
            self.tt("pool", krr[R, bs], t1[R, :], t2[R, :], ALU.add, [tt1], [T_("b_kr", blk)])
        self.dump("b_cqn%d" % l, cqn, T_("b_cqn", 0))
        self.dump("b_ckvn%d" % l, ckvn, T_("b_ckvn", 0))
        self.dump("b_krr%d" % l, krr[R, :], T_("b_kr", 0))
        P.barrier()
        A.reset(m2)
        wqm = [A.alloc([6, 96], BF16) for _ in range(2)]
        wqp = [A.alloc([6, 96], BF16) for _ in range(2)]
        wkv = [A.alloc([2, 128], BF16) for _ in range(2)]
        wk = [w_[:, :, 0:64] for w_ in wkv]
        wv = [w_[:, :, 64:128] for w_ in wkv]
        qn = [A.alloc([T], BF16) for _ in range(2)]
        kn = [A.alloc([T], BF16) for _ in range(2)]
        vh = [A.alloc([16, 64], BF16) for _ in range(2)]
        sq = [A.alloc([512], BF16) for _ in range(2)]
        rs = [A.alloc([512], F32) for _ in range(2)]
        t1 = A.alloc([512], F32)
        t2 = A.alloc([512], F32)
        pT = [A.alloc([512], BF16) for _ in range(4)]
        rc = A.alloc([512], F32)
        ot = A.alloc([512], BF16)
        scale = 96.0 ** -0.5
        def b_load(h_):
            s_ = h_ % 2
            twh_ = T_("b_wh", s_)
            self.wload(wqm[s_], self.dr["wq_m"][l, h_], twh_)
            self.wload(wqp[s_], self.dr["wq_p"][l, h_], twh_)
            self.wload(wkv[s_], self.dr["w_ukv"][l, h_], twh_)

        b_load(0)
        for h in range(8):
            s = h % 2
            twh = T_("b_wh", s)
            tq, tk, tv = T_("b_qn", s), T_("b_kn", s), T_("b_vh", s)
            for half in range(2):
                for t8 in range(8):
                    tt_ = half * 8 + t8
                    self.mmg(self.ps[3][:, t8 * 64:(t8 + 1) * 64],
                             [(ckvn[:, kt, tt_ * 128:(tt_ + 1) * 128], wv[s][:, kt, :]) for kt in range(2)],
                             [twh], self.pst[3])
                self.cp("dve", vh[s][:, half * 8:(half + 1) * 8, :],
                        self.ps[3][:, :].rearrange("p (a b) -> p a b", a=8), [self.pst[3]], [tv])
            for blk in range(NB):
                bs = slice(blk * 512, (blk + 1) * 512)
                Q = slice(0, 96)
                N_ = slice(0, 64)
                self.mmg(self.ps[0][Q, :], [(wqm[s][:, kt, :], cqn[:, kt, bs]) for kt in range(6)], [twh], self.pst[0])
                self.mmg(self.ps[1][Q, :], [(wqp[s][:, kt, :], cqn[:, kt, bs]) for kt in range(6)], [twh], self.pst[1])
                tsq0, trs0 = T_("b_sq", 0), T_("b_rs", 0)
                self.act(sq[0][Q, :], self.ps[0][Q, :], AF.Square, [self.pst[0]], [tsq0])
                self.mmg(self.ps[2][Q, :], [(self.ones[Q, 0:96], sq[0][Q, :])], [tsq0], self.pst[2])
                self.rstd(rs[0][Q, :], self.ps[2][Q, :], 96.0, [self.pst[2]], [trs0])
                self.stt("dve", qn[s][N_, bs], self.ps[0][N_, :], self.colp[N_, C_GQ:C_GQ + 1], rs[0][N_, :],
                         ALU.mult, ALU.mult, [self.pst[0], trs0, tcp], [tq])
                tt1 = T_("b_t1")
                self.stt("dve", t1[R, :], self.ps[0][R, :], self.colp[R, C_GQ:C_GQ + 1], self.cosT[R, bs],
                         ALU.mult, ALU.mult, [self.pst[0], tcp], [tt1])
                self.stt("dve", t2[R, :], self.ps[1][R, :], self.colp[R, C_GQP:C_GQP + 1], self.sinT[R, bs],
                         ALU.mult, ALU.mult, [self.pst[1], tcp], [tt1])
                self.tt("pool", t1[R, :], t1[R, :], t2[R, :], ALU.add, [tt1], [tt1])
                self.tt("pool", qn[s][R, bs], t1[R, :], rs[0][R, :], ALU.mult, [tt1, trs0], [tq])
                self.mmg(self.ps[3][N_, :], [(wk[s][:, kt, :], ckvn[:, kt, bs]) for kt in range(2)], [twh], self.pst[3])
                tsq1, trs1 = T_("b_sq", 1), T_("b_rs", 1)
                self.act(sq[1][N_, :], self.ps[3][N_, :], AF.Square, [self.pst[3]], [tsq1])
                self.cp("pool", sq[1][R, :], sqkr[R, bs], [], [tsq1])
                self.mmg(self.ps[2][Q, :], [(self.ones[Q, 0:96], sq[1][Q, :])], [tsq1], self.pst[2])
                self.rstd(rs[1][Q, :], self.ps[2][Q, :], 96.0, [self.pst[2]], [trs1])
                self.stt("dve", kn[s][N_, bs], self.ps[3][N_, :], self.colp[N_, C_GK:C_GK + 1], rs[1][N_, :],
                         ALU.mult, ALU.mult, [self.pst[3], trs1, tcp], [tk])
                self.tt("pool", kn[s][R, bs], krr[R, bs], rs[1][R, :], ALU.mult, [trs1], [tk])
            if h == 0:
                self.dump("b_qn%d" % l, qn[0][0:96, :], tq)
                self.dump("b_kn%d" % l, kn[0][0:96, :], tk)
                self.dump("b_vh%d" % l, vh[0], tv)
            if h + 1 < 8:
                b_load(h + 1)
            ct, r0 = h // 2, (h % 2) * 64
            RR = slice(r0, r0 + 64)
            for b in range(NB):
                nj = 4 * b + 4
                for j in range(nj):
                    jj = j - 4 * b
                    c0 = 128 * jj if jj > 0 else 0
                    pb = 4 + j % 2
                    p = pT[j % 4]
                    tp = T_("b_pT", j % 4)
                    self.mm(self.ps[pb][:, c0:512], kn[s][0:96, j * 128:(j + 1) * 128],
                            qn[s][0:96, b * 512 + c0:(b + 1) * 512], True, True, [tq, tk], [self.pst[pb]])
                    self.act(p[:, c0:512], self.ps[pb][:, c0:512], AF.Exp, [self.pst[pb]], [tp], scale=scale)
                    if jj >= 0:
                        self.tt("pool", p[:, c0:c0 + 128], p[:, c0:c0 + 128], self.tri[:, :], ALU.mult,
                                [tp, tcst], [tp])
                    self.mm(self.ps[6][RR, c0:512], vh[s][:, j, :], p[:, c0:512], j == 0, j == nj - 1,
                            [tv, tp], [self.pst[6]])
                    self.mm(self.ps[7][RR, c0:512], self.ones[:, 0:64], p[:, c0:512], j == 0, j == nj - 1,
                            [tp], [self.pst[7]])
                trc, tot = T_("b_rc"), T_("b_ot")
                self.P.op("dve", lambda e, o=rc[RR, :], i=self.ps[7][RR, :]: e.reciprocal(out=o, in_=i),
                          reads=[self.pst[7]], writes=[trc])
                self.tt("dve", ot[RR, :], self.ps[6][RR, :], rc[RR, :], ALU.mult, [self.pst[6], trc], [tot])
                sl = ya[RR, 2 + ct, b * 512:(b + 1) * 512]
                self.tt("pool", sl, sl, ot[RR, :], ALU.mult, [tot], [T_("yb", ct, b)])
        self.dump("yb%d" % l, ya[:, 2:6, :], T_("yb", 0, 0))

    def stage_p2(self, l, src, dst):
        A, T_, P = self.A, self.T, self.P
        ya = self.yall
        merged = A.alloc([8, T], BF16)
        m2 = A.mark()
        wls = [A.alloc([8, 4, 256], BF16) for _ in range(2)]
        wbs = [A.alloc([10, 256], BF16) for _ in range(2)]
        g = [A.alloc([512], F32) for _ in range(2)]
        macc = A.alloc([512], F32)
        t2 = [A.alloc([512], F32) for _ in range(2)]
        brk = {0: [0, 1], 1: [2, 3, 4, 5], 2: [6, 7], 3: [8, 9]}
        brw = ["w_br_a", "w_br_b", "w_br_c", "w_br_m"]
        tcp = self.tcp
        def p2_load(j2):
            wl_, wb_ = wls[j2 % 2], wbs[j2 % 2]
            twl_, twb_ = T_("p_wl", j2 % 2), T_("p_wb", j2 % 2)
            for br in range(4):
                c0 = OFF["mrg"] + br * 1024 + j2 * 256
                self.wload(wl_[:, :, br, :], self.dr["w_in"][l, WIN_G[c0]], twl_)
            self.wload(wb_[:, 0:6, :], self.dr["w_br"][l, j2, :, 0:6, :], twb_)
            self.wload(wb_[:, 6:10, :], self.dr["w_br"][l, j2, :, 6:10, :], twb_)

        p2_load(0)
        for j2 in range(4):
            if j2 + 1 < 4:
                p2_load(j2 + 1)
            wl, wb = wls[j2 % 2], wbs[j2 % 2]
            twl, twb = T_("p_wl", j2 % 2), T_("p_wb", j2 % 2)
            for jj in range(2):
                j = 2 * j2 + jj
                js = slice(jj * 128, (jj + 1) * 128)
                for blk in range(NB):
                    bs = slice(blk * 512, (blk + 1) * 512)
                    for br in range(4):
                        pl, pp = self.ps[2 * (br % 2)], self.ps[2 * (br % 2) + 1]
                        tpl, tpp = self.pst[2 * (br % 2)], self.pst[2 * (br % 2) + 1]
                        self.mmg(pl[:, :], [(wl[:, kt, br, js], self.hT[:, kt, bs]) for kt in range(8)], [twl], tpl)
                        self.mmg(pp[:, :], [(wb[:, kt, js], ya[:, kt, bs]) for kt in brk[br]], [twb], tpp)
                        gg, tg = g[br % 2], T_("p_g", br % 2)
                        self.act(gg, pl[:, :], AF.Sigmoid, [tpl, tcp], [tg],
                                 bias=self.colp[:, C_BM + br * 8 + j:C_BM + br * 8 + j + 1])
                        tm = T_("p_m")
                        if br == 0:
                            self.tt("dve", macc, pp[:, :], gg, ALU.mult, [tpp, tg], [tm])
                        else:
                            tt2 = T_("p_t2", br % 2)
                            self.tt("dve", t2[br % 2], pp[:, :], gg, ALU.mult, [tpp, tg], [tt2])
                            if br < 3:
                                self.tt("dve", macc, macc, t2[br % 2], ALU.add, [tt2, tm], [tm])
                            else:
                                self.tt("dve", merged[:, j, bs], macc, t2[br % 2], ALU.add, [tt2, tm],
                                        [T_("p_mg", blk)])
        self.dump("merged%d" % l, merged, T_("p_mg", 0))
        P.barrier()
        A.reset(m2)
        wo = A.alloc([8, D], BF16)
        two = T_("p_wo")
        for i in range(4):
            self.wload(wo[:, :, i * 256:(i + 1) * 256],
                       self.dr["w_out"][l, i], two)
        xt = A.alloc([8, 512], F32)
        xo = A.alloc([8, 512], F32)
        txt, txo = T_("p_xt"), T_("p_xo")
        for blk in range(NB):
            bs = slice(blk * 512, (blk + 1) * 512)
            for hh in range(2):
                self.dma(xt[:, hh * 4:(hh + 1) * 4, :], src[blk, :, hh * 4:(hh + 1) * 4, :], [], [txt])
            for d2 in range(8):
                pb = 4 + d2 % 2
                self.mmg(self.ps[pb][:, :], [(wo[:, kt, d2 * 128:(d2 + 1) * 128], merged[:, kt, bs]) for kt in range(8)],
                         [two, T_("p_mg", blk)], self.pst[pb])
                self.tt("dve", xo[:, d2, :], self.ps[pb][:, :], xt[:, d2, :], ALU.add, [self.pst[pb], txt], [txo])
            for hh in range(2):
                op = self.dma(dst[blk, :, hh * 4:(hh + 1) * 4, :], xo[:, hh * 4:(hh + 1) * 4, :], [txo], [], q="act")
                if dst is self.outT:
                    self.finals.append(op)


def make_consts():
    c = np.zeros((128, NCONST), np.float32)
    s = np.arange(128)
    c[:, K_TRI:K_TRI + 128] = (s[:, None] <= s[None, :]).astype(np.float32)
    c[:, K_IOT:K_IOT + 128] = s[None, :].astype(np.float32)
    c[:, K_IOS] = s.astype(np.float32)
    half = 16
    inv = (10000.0 ** (-np.arange(half, dtype=np.float32) / half)).astype(np.float32)
    for r in range(64, 96):
        c[r, K_INVF] = inv[(r - 64) % 16]
        c[r, K_SGN] = -1.0 if (r - 64) < 16 else 1.0
    for r in range(128):
        c[r, K_GM + r // 16] = 1.0
    return c


def host_prep(inp):
    f = lambda k: np.asarray(inp[k], dtype=np.float32)
    perm = np.array(PERM)
    sh = {}

    def rows_t(w):
        r, c = w.shape
        return np.ascontiguousarray(w.reshape(r // 128, 128, c).transpose(1, 0, 2))

    w_in = f("w_in")
    sh["w_in"] = np.stack([np.stack([rows_t(w_in[l][:, c0:c0 + 256]) for c0 in WIN_C0]) for l in range(L)])
    krm = np.zeros((L, D, 96), np.float32)
    krp = np.zeros((L, D, 96), np.float32)
    krm[:, :, 64:96] = w_in[:, :, OFF["kr"]:OFF["kr"] + 32]
    krp[:, :, 64:96] = w_in[:, :, OFF["kr"] + perm]
    sh["w_krm"] = np.stack([rows_t(krm[l]) for l in range(L)])
    sh["w_krp"] = np.stack([rows_t(krp[l]) for l in range(L)])
    wuq = f("b_w_uq").reshape(L, 768, 8, 96)
    wqp = np.zeros((L, 768, 8, 96), np.float32)
    wqp[:, :, :, 64:96] = wuq[:, :, :, 64 + perm]
    sh["wq_m"] = np.stack([np.stack([rows_t(wuq[l][:, h, :]) for h in range(8)]) for l in range(L)])
    sh["wq_p"] = np.stack([np.stack([rows_t(wqp[l][:, h, :]) for h in range(8)]) for l in range(L)])
    ukv = f("b_w_ukv")
    sh["w_ukv"] = np.stack([np.stack([rows_t(ukv[l][:, h * 128:(h + 1) * 128]) for h in range(8)]) for l in range(L)])
    sh["a_w_sT"] = np.ascontiguousarray(f("a_w_s").transpose(0, 3, 1, 2))
    c_re, c_im = f("c_c_re"), f("c_c_im")
    cre = np.zeros((L, 8, 128, 128), np.float32)
    cim = np.zeros((L, 8, 128, 128), np.float32)
    for j in range(8):
        for gl in range(2):
            g = 2 * j + gl
            col0 = 16 * (g % 8)
            cre[:, j, gl * 64:(gl + 1) * 64, col0:col0 + 16] = c_re[:, g].transpose(0, 2, 1)
            cim[:, j, gl * 64:(gl + 1) * 64, col0:col0 + 16] = c_im[:, g].transpose(0, 2, 1)
    sh["c_reT"] = np.ascontiguousarray(cre.reshape(L, 2, 4, 128, 128).transpose(0, 1, 3, 2, 4))
    sh["c_imT"] = np.ascontiguousarray(cim.reshape(L, 2, 4, 128, 128).transpose(0, 1, 3, 2, 4))
    sh["c_w_glu"] = np.stack([rows_t(f("c_w_glu")[l]) for l in range(L)])
    mkv = f("m_w_kv")
    sh["m_w_kv"] = np.stack([np.stack([rows_t(mkv[l][:, i * 256:(i + 1) * 256]) for i in range(2)]) for l in range(L)])
    wbr = np.concatenate([f("w_br_a"), f("w_br_b"), f("w_br_c"), f("w_br_m")], axis=1)
    sh["w_br"] = np.stack([np.stack([rows_t(wbr[l][:, i * 256:(i + 1) * 256]) for i in range(4)]) for l in range(L)])
    wo = f("w_out")
    sh["w_out"] = np.stack([np.stack([rows_t(wo[l][:, i * 256:(i + 1) * 256]) for i in range(4)]) for l in range(L)])
    cp = np.zeros((L, 128, NCOL), np.float32)
    cp[:, :, C_NG:C_NG + 8] = f("norm_g").reshape(L, 8, 128).transpose(0, 2, 1)
    cp[:, :, C_MNG:C_MNG + 8] = f("m_norm_g").reshape(L, 8, 128).transpose(0, 2, 1)
    cp[:, :, C_QNG:C_QNG + 6] = f("b_q_norm_g").reshape(L, 6, 128).transpose(0, 2, 1)
    cp[:, :, C_KVNG:C_KVNG + 2] = f("b_kv_norm_g").reshape(L, 2, 128).transpose(0, 2, 1)
    cp[:, :, C_BM:C_BM + 32] = f("b_merge").reshape(L, 32, 128).transpose(0, 2, 1)
    gq, gk = f("b_qk_g_q"), f("b_qk_g_k")
    cp[:, 0:96, C_GQ] = gq
    cp[:, 64:96, C_GQP] = gq[:, 64 + perm]
    cp[:, 0:96, C_GK] = gk
    cp[:, 64:96, C_GKP] = gk[:, 64 + perm]
    cp[:, :, C_MGQ] = np.tile(f("m_qk_g_q"), (1, 2))
    cp[:, :, C_MGK] = np.tile(f("m_qk_g_k"), (1, 2))
    cp[:, :, C_CD:C_CD + 2] = f("c_d").reshape(L, 2, 128).transpose(0, 2, 1)
    cp[:, :, C_BGLU:C_BGLU + 2] = f("c_b_glu").reshape(L, 2, 128).transpose(0, 2, 1)
    a_re, a_im, ldt = f("c_a_re"), f("c_a_im"), f("c_log_dt")
    cp[:, :, C_SRE:C_SRE + 8] = a_re.reshape(L, 8, 128).transpose(0, 2, 1)
    cp[:, :, C_SIM:C_SIM + 8] = a_im.reshape(L, 8, 128).transpose(0, 2, 1)
    ldt_rep = np.repeat(ldt[:, :, None], 64, axis=2)
    cp[:, :, C_SDT:C_SDT + 8] = ldt_rep.reshape(L, 8, 128).transpose(0, 2, 1)
    abs_ = f("a_b_s")
    for ct in range(2):
        for gl in range(2):
            cp[:, gl * 64:(gl + 1) * 64, C_ABS + ct * 128:C_ABS + (ct + 1) * 128] = abs_[:, 2 * ct + gl][:, None, :]
    b_re, b_im = f("c_b_re"), f("c_b_im")
    for ct in range(2):
        base = C_ROW + ct * 320
        for g8 in range(8):
            g = 8 * ct + g8
            rows = slice(16 * g8, 16 * g8 + 16)
            cp[:, rows, base + 0:base + 64] = a_re[:, g][:, None, :]
            cp[:, rows, base + 64:base + 128] = a_im[:, g][:, None, :]
            cp[:, rows, base + 128:base + 192] = ldt[:, g][:, None, None]
            cp[:, rows, base + 192:base + 256] = b_re[:, g].transpose(0, 2, 1)
            cp[:, rows, base + 256:base + 320] = b_im[:, g].transpose(0, 2, 1)
    sh["colpack"] = cp
    rp = np.zeros((L, 128, NROW), np.float32)
    rp[:, :, R_ANG:R_ANG + 256] = f("a_norm_g")[:, None, :]
    rp[:, :, R_SRE:R_SRE + 1024] = a_re.reshape(L, 1, 1024)
    rp[:, :, R_SIM:R_SIM + 1024] = a_im.reshape(L, 1, 1024)
    rp[:, :, R_SDT:R_SDT + 1024] = ldt_rep.reshape(L, 1, 1024)
    sh["rowpack"] = rp
    sh["consts"] = make_consts()
    x = f("x")
    mem = f("mem")
    pos = np.asarray(inp["positions"]).astype(np.int32)
    per_core = []
    for b in range(8):
        d = dict(sh)
        d["xT"] = tile_x(x[b])
        d["memT"] = np.ascontiguousarray(mem[b].T.reshape(8, 128, 1, 256).transpose(2, 1, 0, 3))
        d["pos"] = np.ascontiguousarray(pos[b][None, :])
        per_core.append(d)
    return per_core


def tile_x(xb):
    return np.ascontiguousarray(xb.T.reshape(8, 128, NB, 512).transpose(2, 1, 0, 3))


def untile_x(t):
    return np.ascontiguousarray(t.transpose(2, 1, 0, 3).reshape(D, T).T)


_CACHE = {}
LAYER_KEYS = ("colpack", "rowpack", "w_in", "w_krm", "w_krp", "wq_m", "wq_p", "w_ukv", "a_w_sT", "c_reT", "c_imT",
              "c_w_glu", "m_w_kv", "w_br", "w_out")
FUSED = True


def kernel(**inputs):
    in_maps = host_prep(inputs)
    if FUSED:
        if "nc" not in _CACHE:
            _CACHE["nc"] = Builder(nlayers=L).build()
        res = run_bass_kernel_spmd(_CACHE["nc"], in_maps, core_ids=list(range(8)))
        return np.stack([untile_x(r["outT"]) for r in res.results], axis=0).astype(np.float32)
    if "nc1" not in _CACHE:
        _CACHE["nc1"] = Builder(nlayers=1).build()
    nc = _CACHE["nc1"]
    xs = [m["xT"] for m in in_maps]
    for l in range(L):
        maps = []
        for c in range(8):
            d = dict(in_maps[c])
            for k in LAYER_KEYS:
                a = in_maps[c][k]
                d[k] = np.ascontiguousarray(np.concatenate([a[l:l + 1], a[l:l + 1]], axis=0))
            d["xT"] = xs[c]
            maps.append(d)
        res = run_bass_kernel_spmd(nc, maps, core_ids=list(range(8)))
        xs = [np.ascontiguousarray(r["outT"]) for r in res.results]
    return np.stack([untile_x(t) for t in xs], axis=0).astype(np.float32)
```

```python
import math
import contextlib
import numpy as np
import concourse.bass as bass
import concourse.mybir as mybir
from concourse.bass_utils import run_bass_kernel_spmd

F32 = mybir.dt.float32
BF16 = mybir.dt.bfloat16
I32 = mybir.dt.int32
AF = mybir.ActivationFunctionType
ALU = mybir.AluOpType

D = 1024
T = 2048
L = 2
NB = 4
EPS = 1e-6
IN_W = 7456
OFF = dict(a_u=0, a_v=256, a_g=512, cq=768, ckv=1536, kr=1792, bg=1824,
           cin=2336, cg=2592, mq=2848, mg=3104, mrg=3360)
PERM = list(range(16, 32)) + list(range(0, 16))
WIN_C0 = [0, 256, 512, 768, 1024, 1280, 1536, 1824, 2080, 2336, 2592, 2848, 3104] + \
         [3360 + br * 1024 + j2 * 256 for br in range(4) for j2 in range(4)]
WIN_G = {c: i for i, c in enumerate(WIN_C0)}
NWG = len(WIN_C0)
TWO_PI = 2.0 * math.pi

C_NG, C_MNG, C_QNG, C_KVNG, C_BM = 0, 8, 16, 22, 24
C_GQ, C_GQP, C_GK, C_GKP, C_MGQ, C_MGK = 56, 57, 58, 59, 60, 61
C_CD, C_BGLU, C_SRE, C_SIM, C_SDT, C_ABS, C_ROW = 62, 64, 66, 74, 82, 90, 346
NCOL = 346 + 640
R_ANG, R_SRE, R_SIM, R_SDT = 0, 256, 1280, 2304
NROW = 3328
K_TRI, K_IOT, K_IOS, K_INVF, K_SGN, K_GM = 0, 128, 256, 257, 258, 259
NCONST = 267


class Tok:
    __slots__ = ("w", "rs", "excl")

    def __init__(self):
        self.w = None
        self.rs = {}
        self.excl = False


class Op:
    __slots__ = ("eng", "fn", "deps", "is_dma", "sig", "users")

    def __init__(self, eng, fn, is_dma):
        self.eng = eng
        self.fn = fn
        self.is_dma = is_dma
        self.deps = []
        self.sig = None
        self.users = 0


ENGS = ("pe", "act", "dve", "pool", "sp")
SYNC_ALL = True
WQ = ("sp",)
DMAQ = ("sp", "act", "pool")


class Prog:
    def __init__(self, nc, n_dma_sems=8):
        self.nc = nc
        self.ops = {e: [] for e in ENGS}
        self.all = []
        self.n_dma_sems = n_dma_sems
        self.toks = {}
        self.dmas_since_bar = []

    def tok(self, *key):
        t = self.toks.get(key)
        if t is None:
            t = self.toks[key] = Tok()
        return t

    def _add(self, eng, fn, reads, writes, is_dma, extra_deps=()):
        op = Op(eng, fn, is_dma)
        deps = list(extra_deps)
        raw = set()
        for t in reads:
            if t.w is not None:
                deps.append(t.w)
                raw.add(id(t.w))
            if t.excl:
                deps.extend(o for o in t.rs.values() if o.eng != eng)
        for t in writes:
            if t.w is not None:
                deps.append(t.w)
            deps.extend(t.rs.values())
        rkey = ("dma", id(op)) if is_dma else eng
        for t in reads:
            t.rs[rkey] = op
        for t in writes:
            t.w = op
            t.rs = {}
        seen = set()
        for d in deps:
            if d is op or id(d) in seen:
                continue
            seen.add(id(d))
            if (not d.is_dma) and (not is_dma) and d.eng == eng:
                if eng == "pe" or (id(d) not in raw and not SYNC_ALL):
                    continue
            if (not d.is_dma) and is_dma and d.eng == eng:
                pass
            op.deps.append(d)
            d.users += 1
        self.ops[eng].append(op)
        self.all.append(op)
        if is_dma:
            self.dmas_since_bar.append(op)
        return op

    def op(self, eng, fn, reads=(), writes=()):
        return self._add(eng, fn, reads, writes, False)

    def dma(self, eng, fn, reads=(), writes=()):
        assert eng in DMAQ
        return self._add(eng, fn, reads, writes, True)

    def barrier(self):
        dm = self.dmas_since_bar
        self.dmas_since_bar = []
        bt = [self.tok("__bar", e) for e in ENGS]
        for i, e in enumerate(ENGS):
            self._add(e, lambda eng: eng.drain(), [], [bt[i]], False,
                      extra_deps=dm if e == "sp" else ())
        for e in ENGS:
            self._add(e, lambda eng: eng.nop(), bt, [], False)

    def emit(self, final_wait_ops=()):
        nc = self.nc
        with contextlib.ExitStack() as es:
            esem = {e: es.enter_context(nc.semaphore("s_" + e)) for e in ENGS}
            dsem = {e: [es.enter_context(nc.semaphore("d_%s%d" % (e, i)))
                        for i in range(self.n_dma_sems)] for e in DMAQ}
            ecount = {e: 0 for e in ENGS}
            dcount = {e: 0 for e in DMAQ}
            duse = {e: [0] * self.n_dma_sems for e in DMAQ}
            fw = set(id(o) for o in final_wait_ops)
            for op in self.all:
                if op.is_dma:
                    j = dcount[op.eng]
                    dcount[op.eng] += 1
                    s = j % self.n_dma_sems
                    duse[op.eng][s] += 1
                    op.sig = (dsem[op.eng][s], 16 * duse[op.eng][s], ("d", op.eng, s))
                elif op.users > 0 or id(op) in fw:
                    ecount[op.eng] += 1
                    op.sig = (esem[op.eng], ecount[op.eng], ("e", op.eng))
            self.stats = dict(ecount=ecount, dcount=dcount,
                              nops={e: len(self.ops[e]) for e in ENGS})
            block = es.enter_context(nc.Block())

            def run_engine(e, engine):
                waited = {}

                def wait(sem, val, key):
                    if waited.get(key, 0) >= val:
                        return
                    waited[key] = val
                    engine.wait_ge(sem, val)

                for op in self.ops[e]:
                    for d in op.deps:
                        wait(*d.sig)
                    if op.is_dma:
                        sem, val, key = op.sig
                        if val > 16:
                            wait(sem, val - 16, key)
                    ins = op.fn(engine)
                    if op.sig is not None:
                        ins.then_inc(op.sig[0], 16 if op.is_dma else 1)
                if e == "sp":
                    for op in final_wait_ops:
                        wait(*op.sig)

            block.tensor(lambda eng: run_engine("pe", eng))
            block.scalar(lambda eng: run_engine("act", eng))
            block.vector(lambda eng: run_engine("dve", eng))
            block.gpsimd(lambda eng: run_engine("pool", eng))
            block.sync(lambda eng: run_engine("sp", eng))


class Arena:
    def __init__(self, ap2d, nfloats):
        self.a = ap2d
        self.n = nfloats
        self.off = 0
        self.peak = 0

    def alloc(self, free_shape, dt):
        free_shape = list(free_shape)
        isz = 2 if dt == BF16 else 4
        n = int(np.prod(free_shape))
        n32 = (n * isz + 3) // 4
        assert self.off + n32 <= self.n, ("arena overflow", self.off, n32, self.n)
        v = self.a[:, self.off:self.off + n32]
        self.off += n32
        self.peak = max(self.peak, self.off)
        if dt == BF16:
            v = v.bitcast(BF16)[:, 0:n]
        elif dt == I32:
            v = v.bitcast(I32)
        if len(free_shape) == 2:
            v = v.rearrange("p (a b) -> p a b", a=free_shape[0])
        elif len(free_shape) == 3:
            v = v.rearrange("p (a b c) -> p a b c", a=free_shape[0], b=free_shape[1])
        elif len(free_shape) == 4:
            v = v.rearrange("p (a b c d) -> p a b c d", a=free_shape[0], b=free_shape[1],
                            c=free_shape[2])
        return v

    def mark(self):
        return self.off

    def reset(self, m):
        self.off = m


class Builder:
    def __init__(self, nlayers=L, stop=None, dumps=()):
        self.nlayers = nlayers
        self.stop = stop
        self.dumps = dict()
        self.want = set(dumps)
        self.nc = nc = bass.Bass("TRN2", target_bir_lowering=False)
        self.P = Prog(nc)
        self.dr = {}
        self.finals = []
        self.dump_specs = []

    def din(self, name, shape, dt=F32):
        self.dr[name] = self.nc.dram_tensor(name, list(shape), dt, kind="ExternalInput").ap()
        return self.dr[name]

    def T(self, *k):
        return self.P.tok(*k)

    def mm(self, out, lhsT, rhs, start, stop, reads, writes):
        return self.P.op("pe", lambda e: e.matmul(out, lhsT=lhsT, rhs=rhs, start=start, stop=stop),
                         reads=reads, writes=writes)

    def mmg(self, out, pairs, reads, wtok):
        n = len(pairs)
        for i, (a, b) in enumerate(pairs):
            self.mm(out, a, b, i == 0, i == n - 1, reads, [wtok])

    def act(self, out, in_, func, reads, writes, bias=0.0, scale=1.0, eng="act", accum_out=None):
        if accum_out is None:
            return self.P.op("act", lambda e: e.activation(out=out, in_=in_, func=func, bias=bias, scale=scale),
                             reads=reads, writes=writes)
        return self.P.op("act", lambda e: e.activation(out=out, in_=in_, func=func, bias=bias, scale=scale,
                                                       accum_out=accum_out),
                         reads=reads, writes=writes)

    def tt(self, eng, out, in0, in1, op, reads, writes):
        return self.P.op(eng, lambda e: e.tensor_tensor(out=out, in0=in0, in1=in1, op=op),
                         reads=reads, writes=writes)

    def ts(self, eng, out, in0, s1, op0, reads, writes, s2=None, op1=None):
        if op1 is None:
            return self.P.op(eng, lambda e: e.tensor_scalar(out=out, in0=in0, scalar1=s1, scalar2=None, op0=op0),
                             reads=reads, writes=writes)
        return self.P.op(eng, lambda e: e.tensor_scalar(out=out, in0=in0, scalar1=s1, scalar2=s2, op0=op0, op1=op1),
                         reads=reads, writes=writes)

    def stt(self, eng, out, in0, scalar, in1, op0, op1, reads, writes):
        return self.P.op(eng, lambda e: e.scalar_tensor_tensor(out=out, in0=in0, scalar=scalar, in1=in1,
                                                                op0=op0, op1=op1),
                         reads=reads, writes=writes)

    def cp(self, eng, out, in_, reads, writes):
        if eng == "act":
            return self.P.op("act", lambda e: e.copy(out=out, in_=in_), reads=reads, writes=writes)
        return self.P.op(eng, lambda e: e.tensor_copy(out=out, in_=in_), reads=reads, writes=writes)

    def memset(self, eng, ap, val, writes):
        return self.P.op(eng, lambda e: e.memset(ap, val), reads=(), writes=writes)

    def dma(self, out, in_, reads, writes, q="sp"):
        def ndesc(ap):
            dims = [(int(st), int(n)) for st, n in ap.ap]
            total = 1
            for st, n in dims:
                total *= n
            run = 1
            for st, n in reversed(dims[1:]):
                if st == run:
                    run *= n
                else:
                    break
            return total // run
        self.desc_count = getattr(self, "desc_count", {})
        self.desc_count[q] = self.desc_count.get(q, 0) + max(ndesc(out), ndesc(in_))
        return self.P.dma(q, lambda e: e.dma_start(out=out, in_=in_), reads=reads, writes=writes)

    def rstd(self, out, in_, n, reads, writes):
        self.act(out, in_, AF.Ln, reads, writes, bias=EPS, scale=1.0 / n)
        self.act(out, out, AF.Exp, writes, writes, scale=-0.5)

    def wload(self, dst, src, dtok, cast_eng="pool", scale=None):
        fs = list(dst.shape[1:])
        n = int(np.prod(fs))
        assert n <= self.stage_n, (n, self.stage_n)
        slot = self.wslot
        self.wslot = (slot + 1) % len(self.stage)
        st = self.stage[slot][:, 0:n]
        if len(fs) == 2:
            st = st.rearrange("p (a b) -> p a b", a=fs[0])
        elif len(fs) == 3:
            st = st.rearrange("p (a b c) -> p a b c", a=fs[0], b=fs[1])
        stok = self.T("stage", slot)
        self.wq_i = getattr(self, "wq_i", 0) + 1
        self.dma(st, src, [], [stok], q=WQ[self.wq_i % len(WQ)])
        if scale is None:
            self.cp(cast_eng, dst, st, [stok], [dtok])
        else:
            self.ts(cast_eng, dst, st, scale, ALU.mult, [stok], [dtok])

    def dump(self, name, ap, tok):
        if name not in self.want:
            return
        self.P.barrier()
        shp = list(ap.shape)
        d = self.nc.dram_tensor("dbg_" + name, shp, ap.dtype, kind="ExternalOutput").ap()
        op = self.dma(d, ap, [tok], [])
        self.finals.append(op)
        self.dump_specs.append(name)

    def build(self):
        nc = self.nc
        P = self.P
        din = self.din
        din("xT", [NB, 128, 8, 512])
        din("memT", [1, 128, 8, 256])
        din("pos", [1, T], I32)
        din("consts", [128, NCONST])
        din("colpack", [L, 128, NCOL])
        din("rowpack", [L, 128, NROW])
        din("w_in", [L, NWG, 128, 8, 256])
        din("w_krm", [L, 128, 8, 96])
        din("w_krp", [L, 128, 8, 96])
        din("wq_m", [L, 8, 128, 6, 96])
        din("wq_p", [L, 8, 128, 6, 96])
        din("w_ukv", [L, 8, 128, 2, 128])
        din("a_w_sT", [L, 128, 4, 128])
        din("c_reT", [L, 2, 128, 4, 128])
        din("c_imT", [L, 2, 128, 4, 128])
        din("c_w_glu", [L, 128, 2, 256])
        din("m_w_kv", [L, 2, 128, 8, 256])
        din("w_br", [L, 4, 128, 10, 256])
        din("w_out", [L, 4, 128, 8, 256])
        self.outT = nc.dram_tensor("outT", [NB, 128, 8, 512], F32, kind="ExternalOutput").ap()
        self.x1T = nc.dram_tensor("x1T", [NB, 128, 8, 512], F32, kind="Internal").ap()

        with contextlib.ExitStack() as es:
            NA = 52000
            arena_t = es.enter_context(nc.sbuf_tensor("arena", [128, NA], F32))
            self.A = A = Arena(arena_t[:, :], NA)
            self.ps = [es.enter_context(nc.psum_tensor("ps%d" % i, [128, 512], F32)) for i in range(8)]
            self.pst = [self.T("ps", i) for i in range(8)]
            for t in self.pst:
                t.excl = True

            self.cst = A.alloc([NCONST], F32)
            self.eps_col = A.alloc([1], F32)
            self.tri = A.alloc([128], BF16)
            self.ntri = A.alloc([128], BF16)
            self.ones = A.alloc([128], BF16)
            self.bd64 = A.alloc([128], BF16)
            self.ones_f = A.alloc([128], F32)
            self.cosT = A.alloc([T], F32)
            self.sinT = A.alloc([T], F32)
            self.colp = A.alloc([NCOL], F32)
            self.hT = A.alloc([8, T], BF16)
            self.yall = A.alloc([10, T], BF16)
            self.kmem = A.alloc([2, 256], BF16)
            self.vmem = A.alloc([2, 256], BF16)
            self.stage_n = 2048
            self.stage = [A.alloc([self.stage_n], F32) for _ in range(2)]
            self.wslot = 0
            self.base_mark = A.mark()

            self.setup_consts()
            if self.stop != "consts":
                for l in range(self.nlayers):
                    if self.run_layer(l):
                        break
            P.barrier()
            P.emit(final_wait_ops=self.finals)
        return nc

    def setup_consts(self):
        A, T_ = self.A, self.T
        tc = T_("cst")
        self.dma(self.cst, self.dr["consts"][:, :], [], [tc])
        self.memset("dve", self.eps_col, EPS, [tc])
        self.cp("dve", self.tri, self.cst[:, K_TRI:K_TRI + 128], [tc], [tc])
        self.ts("dve", self.ntri, self.cst[:, K_TRI:K_TRI + 128], -1.0, ALU.mult, [tc], [tc])
        self.memset("dve", self.ones, 1.0, [tc])
        self.memset("dve", self.ones_f, 1.0, [tc])
        self.memset("dve", self.bd64, 0.0, [tc])
        self.memset("dve", self.bd64[0:64, 0:64], 1.0, [tc])
        self.memset("dve", self.bd64[64:128, 64:128], 1.0, [tc])
        if getattr(self, "skip", None):
            self.memset("pool", self.yall, 0.25, [T_("yall_init")])
        m = A.mark()
        posi = A.alloc([T], I32)
        ang = A.alloc([T], F32)
        kf = A.alloc([T], F32)
        ki = A.alloc([T], I32)
        tr = T_("rope")
        self.dma(posi[0:96, :], self.dr["pos"][0:1, :].partition_broadcast(96), [], [tr])
        R = slice(64, 96)
        self.cp("dve", ang[R, :], posi[R, :], [tr], [tr])
        self.ts("dve", ang[R, :], ang[R, :], self.cst[R, K_INVF:K_INVF + 1], ALU.mult, [tr, tc], [tr])

        def sin_of(dst, shift, post_scale_col):
            self.ts("dve", ki[R, :], ang[R, :], shift, ALU.add, [tr], [tr], s2=1.0 / TWO_PI, op1=ALU.mult)
            self.cp("dve", kf[R, :], ki[R, :], [tr], [tr])
            self.stt("dve", kf[R, :], kf[R, :], -TWO_PI, ang[R, :], ALU.mult, ALU.add, [tr], [tr])
            self.ts("dve", kf[R, :], kf[R, :], shift, ALU.add, [tr], [tr], s2=math.pi, op1=ALU.min)
            self.ts("dve", kf[R, :], kf[R, :], -math.pi, ALU.max, [tr], [tr])
            self.act(dst[R, :], kf[R, :], AF.Sin, [tr], [tr])
            if post_scale_col is not None:
                self.ts("dve", dst[R, :], dst[R, :], post_scale_col, ALU.mult, [tr, tc], [tr])

        sin_of(self.sinT, 0.0, self.cst[R, K_SGN:K_SGN + 1])
        sin_of(self.cosT, math.pi / 2, None)
        self.dump("cosT", self.cosT[R, :], tr)
        self.dump("sinT", self.sinT[R, :], tr)
        self.P.barrier()
        A.reset(m)

    def run_layer(self, l):
        P, A, T_ = self.P, self.A, self.T
        src = self.dr["xT"] if l == 0 else self.x1T
        dst = self.x1T if l == 0 else self.outT
        if self.nlayers == 1:
            dst = self.outT
        tcp = T_("colp")
        self.dma(self.colp, self.dr["colpack"][l, :, :], [], [tcp])
        self.tcp = tcp
        stages = [("N", self.stage_norm), ("MP", self.stage_memprep), ("A", self.stage_a),
                  ("C", self.stage_c), ("M", self.stage_m), ("B", self.stage_b),
                  ("P2", self.stage_p2)]
        for name, fn in stages:
            if name in getattr(self, "skip", ()):
                continue
            m = A.mark()
            if name == "N":
                fn(l, src)
            elif name == "P2":
                fn(l, src, dst)
            else:
                fn(l)
            P.barrier()
            A.reset(m)
            if self.stop == (name, l):
                return True
        return False

    def norm_fm(self, srcT, n_tok, gcol, dst, dst_tok_fn, tag):
        A, T_ = self.A, self.T
        W = min(512, n_tok)
        nblk = n_tok // W
        xb = [A.alloc([8, W], F32) for _ in range(2)]
        sq = A.alloc([8, W], BF16)
        rs = A.alloc([W], F32)
        for b in range(nblk):
            x = xb[b % 2]
            tx = T_(tag + "x", b % 2)
            ts_ = T_(tag + "sq")
            trs = T_(tag + "rs")
            for hh in range(2):
                self.dma(x[:, hh * 4:(hh + 1) * 4, :], srcT[b, :, hh * 4:(hh + 1) * 4, :], [], [tx])
            pb = 0
            for kt in range(8):
                self.act(sq[:, kt, :], x[:, kt, :], AF.Square, [tx], [ts_])
            self.mmg(self.ps[pb][:, 0:W], [(self.ones[:, :], sq[:, kt, :]) for kt in range(8)],
                     [ts_], self.pst[pb])
            self.rstd(rs[:, :], self.ps[pb][:, 0:W], float(D), [self.pst[pb]], [trs])
            for kt in range(8):
                self.stt("dve", dst[:, kt, b * W:(b + 1) * W], x[:, kt, :], gcol[:, kt:kt + 1], rs[:, :],
                         ALU.mult, ALU.mult, [tx, trs, self.tcp], [dst_tok_fn(b)])

    def stage_norm(self, l, src):
        self.norm_fm(src, T, self.colp[:, C_NG:C_NG + 8], self.hT, lambda b: self.T("hT", b), "n")
        for b in range(NB):
            self.dump("hT%d_%d" % (l, b), self.hT[:, :, b * 512:(b + 1) * 512], self.T("hT", b))

    def load_win(self, l, c0, ncols, wt, wtok):
        assert ncols == 256
        self.wload(wt[:, :, 0:ncols], self.dr["w_in"][l, WIN_G[c0]], wtok)

    def zproj_blk(self, wt, ncols, blk, pb, wtok, m_off=0):
        self.mmg(self.ps[pb][m_off:m_off + ncols, :],
                 [(wt[:, kt, 0:ncols], self.hT[:, kt, blk * 512:(blk + 1) * 512]) for kt in range(8)],
                 [wtok, self.T("hT", blk)], self.pst[pb])

    def stage_memprep(self, l):
        A, T_ = self.A, self.T
        hm = A.alloc([8, 256], BF16)
        thm = T_("hm")
        self.norm_fm(self.dr["memT"], 256, self.colp[:, C_MNG:C_MNG + 8], hm, lambda b: thm, "m")
        wkv = A.alloc([8, 512], BF16)
        twk = T_("wkv")
        for half in range(2):
            self.wload(wkv[:, :, half * 256:(half + 1) * 256],
                       self.dr["m_w_kv"][l, half],
                       twk)
        sq = A.alloc([256], BF16)
        rs = A.alloc([256], F32)
        tsq, trs, tkm, tvm = T_("mp_sq"), T_("mp_rs"), T_("kmem"), T_("vmem")
        for ct in range(2):
            self.mmg(self.ps[0][:, 0:256], [(wkv[:, kt, ct * 128:(ct + 1) * 128], hm[:, kt, :]) for kt in range(8)],
                     [twk, thm], self.pst[0])
            self.act(sq[:, :], self.ps[0][:, 0:256], AF.Square, [self.pst[0]], [tsq])
            self.mmg(self.ps[1][:, 0:256], [(self.bd64[:, :], sq[:, :])], [tsq, T_("cst")], self.pst[1])
            self.rstd(rs[:, :], self.ps[1][:, 0:256], 64.0, [self.pst[1]], [trs])
            self.stt("dve", self.kmem[:, ct, :], self.ps[0][:, 0:256], self.colp[:, C_MGK:C_MGK + 1], rs[:, :],
                     ALU.mult, ALU.mult, [self.pst[0], trs, self.tcp], [tkm])
        for mt in range(2):
            self.mmg(self.ps[2][:, 0:256], [(hm[:, kt, mt * 128:(mt + 1) * 128], wkv[:, kt, 256:512]) for kt in range(8)],
                     [twk, thm], self.pst[2])
            self.cp("dve", self.vmem[:, mt, :], self.ps[2][:, 0:256], [self.pst[2]], [tvm])
        self.dump("kmem%d" % l, self.kmem, tkm)
        self.dump("vmem%d" % l, self.vmem, tvm)

    def stage_a(self, l):
        A, T_ = self.A, self.T
        ya = self.yall
        wt = [A.alloc([8, 256], BF16) for _ in range(2)]
        gnb = A.alloc([256], F32)
        wsT = A.alloc([4, 128], BF16)
        tmp = A.alloc([512], BF16)
        tgn, tws, ttmp = T_("a_gn"), T_("a_ws"), T_("a_tmp")
        self.dma(gnb, self.dr["rowpack"][l, :, R_ANG:R_ANG + 256], [], [tgn])
        slot = self.wslot
        self.wslot = (slot + 1) % 2
        st = self.stage[slot][:, 0:512].rearrange("p (g t) -> p g t", g=4)
        stok = T_("stage", slot)
        self.dma(st, self.dr["a_w_sT"][l], [], [stok])
        for g in range(4):
            self.tt("dve", wsT[:, g, :], st[:, g, :], self.cst[:, K_TRI:K_TRI + 128], ALU.mult, [stok, T_("cst")], [tws])
        tw0, tw1 = T_("a_w", 0), T_("a_w", 1)
        self.load_win(l, OFF["a_u"], 256, wt[0], tw0)
        self.load_win(l, OFF["a_g"], 256, wt[1], tw1)
        for ct in range(2):
            for blk in range(NB):
                pb = (ct * NB + blk) % 2
                self.mmg(self.ps[pb][:, :],
                         [(wt[0][:, kt, ct * 128:(ct + 1) * 128], self.hT[:, kt, blk * 512:(blk + 1) * 512])
                          for kt in range(8)], [tw0, T_("hT", blk)], self.pst[pb])
                self.act(ya[:, ct, blk * 512:(blk + 1) * 512], self.ps[pb][:, :], AF.Gelu_apprx_tanh,
                         [self.pst[pb]], [T_("ya", ct, blk)])
        for ct in range(2):
            for blk in range(NB):
                pb = 2 + (ct * NB + blk) % 2
                self.mmg(self.ps[pb][:, :],
                         [(wt[1][:, kt, ct * 128:(ct + 1) * 128], self.hT[:, kt, blk * 512:(blk + 1) * 512])
                          for kt in range(8)], [tw1, T_("hT", blk)], self.pst[pb])
                self.act(tmp[:, :], self.ps[pb][:, :], AF.Silu, [self.pst[pb]], [ttmp])
                sl = ya[:, ct, blk * 512:(blk + 1) * 512]
                self.tt("pool", sl, sl, tmp[:, :], ALU.mult, [ttmp], [T_("ya", ct, blk)])
        twv = T_("a_w", 0)
        self.load_win(l, OFF["a_v"], 256, wt[0], twv)
        gv = A.alloc([256], F32)
        ssq = A.alloc([1], F32)
        vn = [A.alloc([256], BF16) for _ in range(2)]
        sb = A.alloc([2, 128], F32)
        junk = A.alloc([256], BF16)
        tgv, tss, tsb = T_("a_gv"), T_("a_ss"), T_("a_sb")
        absT = self.colp[:, C_ABS:C_ABS + 256].rearrange("p (c t) -> p c t", c=2)
        for tt_ in range(16):
            blk = tt_ // 4
            pb = 4 + tt_ % 2
            self.mmg(self.ps[pb][:, 0:256],
                     [(self.hT[:, kt, tt_ * 128:(tt_ + 1) * 128], wt[0][:, kt, :]) for kt in range(8)],
                     [twv, T_("hT", blk)], self.pst[pb])
            self.act(gv[:, :], self.ps[pb][:, 0:256], AF.Gelu_apprx_tanh, [self.pst[pb]], [tgv])
            self.act(junk[:, :], gv[:, :], AF.Square, [tgv], [tss], accum_out=ssq[:, :])
            self.rstd(ssq[:, :], ssq[:, :], 256.0, [tss], [tss])
            v = vn[tt_ % 2]
            tv = T_("a_vn", tt_ % 2)
            self.stt("dve", v[:, :], gv[:, :], ssq[:, 0:1], gnb[:, :], ALU.mult, ALU.mult, [tgv, tss, tgn], [tv])
            pq = 6 + tt_ % 2
            for g in range(4):
                ct, r0 = g // 2, (g % 2) * 64
                self.mm(self.ps[pq][r0:r0 + 64, ct * 128:(ct + 1) * 128], v[:, g * 64:(g + 1) * 64], wsT[:, g, :],
                        True, True, [tv, tws], [self.pst[pq]])
            psv = self.ps[pq][:, 0:256].rearrange("p (c t) -> p c t", c=2)
            self.tt("dve", sb[:, :, :], psv, absT, ALU.add, [self.pst[pq], self.tcp], [tsb])
            sl = ya[:, 0:2, tt_ * 128:(tt_ + 1) * 128]
            self.tt("pool", sl, sl, sb[:, :, :], ALU.mult, [tsb], [T_("ya", 0, blk), T_("ya", 1, blk)])
        self.dump("ya%d" % l, ya[:, 0:2, :], T_("ya", 0, 0))
        self.dump("a_vn%d" % l, vn[1], T_("a_vn", 1))
        self.dump("a_sb%d" % l, sb, tsb)
        self.dump("a_gv%d" % l, gv, tgv)
        self.dump("a_ssq%d" % l, ssq, tss)

    def sin_of(self, dst, ang, shift, ki, kf, tok, extra=()):
        rd = [tok] + list(extra)
        self.ts("dve", ki, ang, shift, ALU.add, rd, [tok], s2=1.0 / TWO_PI, op1=ALU.mult)
        self.cp("dve", kf, ki, [tok], [tok])
        self.stt("dve", kf, kf, -TWO_PI, ang, ALU.mult, ALU.add, [tok], [tok])
        self.ts("dve", kf, kf, shift, ALU.add, [tok], [tok], s2=math.pi, op1=ALU.min)
        self.ts("dve", kf, kf, -math.pi, ALU.max, [tok], [tok])
        self.act(dst, kf, AF.Sin, [tok], [tok])

    def stage_m(self, l):
        A, T_ = self.A, self.T
        ya = self.yall
        wq = A.alloc([8, 256], BF16)
        wg = A.alloc([8, 256], BF16)
        twq, twg = T_("m_wq"), T_("m_wg")
        self.load_win(l, OFF["mq"], 256, wq, twq)
        self.load_win(l, OFF["mg"], 256, wg, twg)
        for ct in range(2):
            for blk in range(NB):
                pb = (ct * NB + blk) % 2
                self.mmg(self.ps[pb][:, :],
                         [(wg[:, kt, ct * 128:(ct + 1) * 128], self.hT[:, kt, blk * 512:(blk + 1) * 512])
                          for kt in range(8)], [twg], self.pst[pb])
                self.act(ya[:, 8 + ct, blk * 512:(blk + 1) * 512], self.ps[pb][:, :], AF.Silu,
                         [self.pst[pb]], [T_("ym", ct, blk)])
        sq = A.alloc([512], BF16)
        rs = A.alloc([512], F32)
        qn = [A.alloc([2, 512], BF16) for _ in range(2)]
        pT = [A.alloc([512], BF16) for _ in range(4)]
        rc = A.alloc([512], F32)
        ot = A.alloc([512], BF16)
        tsq, trs, trc, tot = T_("m_sq"), T_("m_rs"), T_("m_rc"), T_("m_ot")
        scale = 64.0 ** -0.5
        for blk in range(NB):
            q = qn[blk % 2]
            tq = T_("m_qn", blk % 2)
            for ct in range(2):
                self.mmg(self.ps[2][:, :],
                         [(wq[:, kt, ct * 128:(ct + 1) * 128], self.hT[:, kt, blk * 512:(blk + 1) * 512])
                          for kt in range(8)], [twq], self.pst[2])
                self.act(sq[:, :], self.ps[2][:, :], AF.Square, [self.pst[2]], [tsq])
                self.mmg(self.ps[3][:, :], [(self.bd64[:, :], sq[:, :])], [tsq], self.pst[3])
                self.rstd(rs[:, :], self.ps[3][:, :], 64.0, [self.pst[3]], [trs])
                self.stt("dve", q[:, ct, :], self.ps[2][:, :], self.colp[:, C_MGQ:C_MGQ + 1], rs[:, :],
                         ALU.mult, ALU.mult, [self.pst[2], trs], [tq])
            for h in range(4):
                ct, r0 = h // 2, (h % 2) * 64
                R = slice(r0, r0 + 64)
                for mt in range(2):
                    pb = 4 + mt
                    p = pT[(h % 2) * 2 + mt]
                    tp = T_("m_pT", (h % 2) * 2 + mt)
                    self.mm(self.ps[pb][:, :], self.kmem[R, ct, mt * 128:(mt + 1) * 128], q[R, ct, :],
                            True, True, [tq], [self.pst[pb]])
                    self.act(p[:, :], self.ps[pb][:, :], AF.Exp, [self.pst[pb]], [tp], scale=scale)
                tps = [T_("m_pT", (h % 2) * 2 + mt) for mt in range(2)]
                ps_o, ps_d = self.ps[6], self.ps[7]
                self.mmg(ps_o[R, :], [(self.vmem[:, mt, h * 64:(h + 1) * 64], pT[(h % 2) * 2 + mt][:, :])
                                      for mt in range(2)], tps, self.pst[6])
                self.mmg(ps_d[R, :], [(self.ones[:, 0:64], pT[(h % 2) * 2 + mt][:, :]) for mt in range(2)],
                         tps, self.pst[7])
                self.P.op("dve", lambda e, o=rc[R, :], i=ps_d[R, :]: e.reciprocal(out=o, in_=i),
                          reads=[self.pst[7]], writes=[trc])
                self.tt("dve", ot[R, :], ps_o[R, :], rc[R, :], ALU.mult, [self.pst[6], trc], [tot])
                sl = ya[R, 8 + ct, blk * 512:(blk + 1) * 512]
                self.tt("pool", sl, sl, ot[R, :], ALU.mult, [tot], [T_("ym", ct, blk)])
        self.dump("ym%d" % l, ya[:, 8:10, :], T_("ym", 0, 0))

    def stage_c(self, l):
        A, T_ = self.A, self.T
        ya = self.yall
        uT = A.alloc([2, T], BF16)
        ygT = A.alloc([2, T], BF16)
        TAc = A.alloc([1024], F32)
        TAs = A.alloc([1024], F32)
        Dr = A.alloc([8, 128], F32)
        Di = A.alloc([8, 128], F32)
        a128r = A.alloc([8], F32)
        a128i = A.alloc([8], F32)
        Bm = [A.alloc([2, 8, 64], BF16) for _ in range(2)]
        CreT = A.alloc([8, 128], BF16)
        CreTn = A.alloc([8, 128], BF16)
        CimTn = A.alloc([8, 128], BF16)
        wglu = A.alloc([2, 256], BF16)
        m2 = A.mark()
        wu = A.alloc([8, 256], BF16)
        wg = A.alloc([8, 256], BF16)
        twu, twg = T_("c_wu"), T_("c_wg")
        self.load_win(l, OFF["cin"], 256, wu, twu)
        self.load_win(l, OFF["cg"], 256, wg, twg)
        for ct in range(2):
            for blk in range(NB):
                pb = (ct * NB + blk) % 2
                self.mmg(self.ps[pb][:, :],
                         [(wu[:, kt, ct * 128:(ct + 1) * 128], self.hT[:, kt, blk * 512:(blk + 1) * 512])
                          for kt in range(8)], [twu], self.pst[pb])
                self.cp("act", uT[:, ct, blk * 512:(blk + 1) * 512], self.ps[pb][:, :], [self.pst[pb]],
                        [T_("c_uT", blk)])
        for ct in range(2):
            for blk in range(NB):
                pb = 2 + (ct * NB + blk) % 2
                self.mmg(self.ps[pb][:, :],
                         [(wg[:, kt, ct * 128:(ct + 1) * 128], self.hT[:, kt, blk * 512:(blk + 1) * 512])
                          for kt in range(8)], [twg], self.pst[pb])
                self.act(ya[:, 6 + ct, blk * 512:(blk + 1) * 512], self.ps[pb][:, :], AF.Silu,
                         [self.pst[pb]], [T_("yc", ct, blk)])
        tt_ = T_("c_tab")
        rowp = A.alloc([3, 1024], F32)
        self.dma(rowp, self.dr["rowpack"][l, :, R_SRE:R_SRE + 3072].rearrange("p (a b) -> p a b", a=3), [], [tt_])
        s1 = A.alloc([1024], F32)
        s2 = A.alloc([1024], F32)
        si = A.alloc([1024], I32)
        negs = A.alloc([1], F32)
        tcst = T_("cst")
        self.ts("dve", negs, self.cst[:, K_IOS:K_IOS + 1], -1.0, ALU.mult, [tcst], [tt_])
        self.act(rowp[:, 2, :], rowp[:, 2, :], AF.Exp, [tt_], [tt_])
        self.tt("dve", rowp[:, 0, :], rowp[:, 0, :], rowp[:, 2, :], ALU.mult, [tt_], [tt_])
        self.tt("dve", rowp[:, 1, :], rowp[:, 1, :], rowp[:, 2, :], ALU.mult, [tt_], [tt_])
        self.act(s1, rowp[:, 0, :], AF.Exp, [tt_], [tt_], scale=negs[:, 0:1])
        self.ts("dve", s2, rowp[:, 1, :], self.cst[:, K_IOS:K_IOS + 1], ALU.mult, [tt_, tcst], [tt_])
        self.sin_of(TAs, s2, 0.0, si, rowp[:, 2, :], tt_)
        self.sin_of(TAc, s2, math.pi / 2, si, rowp[:, 2, :], tt_)
        self.tt("dve", TAs, TAs, s1, ALU.mult, [tt_], [tt_])
        self.tt("dve", TAc, TAc, s1, ALU.mult, [tt_], [tt_])
        s1v = s1.rearrange("p (j t) -> p j t", j=8)
        s2v = s2.rearrange("p (j t) -> p j t", j=8)
        siv = si.rearrange("p (j t) -> p j t", j=8)
        kfv = rowp[:, 2, :].rearrange("p (j t) -> p j t", j=8)
        dtj = A.alloc([8], F32)
        thrj = A.alloc([8], F32)
        thij = A.alloc([8], F32)
        e128 = A.alloc([8], F32)
        p128 = A.alloc([8], F32)
        k128 = A.alloc([8], F32)
        i128 = A.alloc([8], I32)
        tcp = self.tcp
        self.act(dtj, self.colp[:, C_SDT:C_SDT + 8], AF.Exp, [tcp, tt_], [tt_])
        self.tt("dve", thrj, self.colp[:, C_SRE:C_SRE + 8], dtj, ALU.mult, [tcp, tt_], [tt_])
        self.tt("dve", thij, self.colp[:, C_SIM:C_SIM + 8], dtj, ALU.mult, [tcp, tt_], [tt_])
        iot = self.cst[:, K_IOT:K_IOT + 128]
        for j in range(8):
            self.act(s1v[:, j, :], iot, AF.Exp, [tt_, tcst], [tt_], scale=thrj[:, j:j + 1])
            self.ts("dve", s2v[:, j, :], iot, thij[:, j:j + 1], ALU.mult, [tt_, tcst], [tt_])
        self.sin_of(Di.rearrange("p j t -> p (j t)"), s2, 0.0, si, rowp[:, 2, :], tt_)
        self.sin_of(Dr.rearrange("p j t -> p (j t)"), s2, math.pi / 2, si, rowp[:, 2, :], tt_)
        self.tt("dve", Di, Di, s1v, ALU.mult, [tt_], [tt_])
        self.tt("dve", Dr, Dr, s1v, ALU.mult, [tt_], [tt_])
        self.act(e128, thrj, AF.Exp, [tt_], [tt_], scale=128.0)
        self.ts("dve", p128, thij, 128.0, ALU.mult, [tt_], [tt_])
        self.sin_of(a128i, p128, 0.0, i128, k128, tt_)
        self.sin_of(a128r, p128, math.pi / 2, i128, k128, tt_)
        self.tt("dve", a128i, a128i, e128, ALU.mult, [tt_], [tt_])
        self.tt("dve", a128r, a128r, e128, ALU.mult, [tt_], [tt_])
        w = [A.alloc([64], F32) for _ in range(8)]
        wi = A.alloc([64], I32)
        gmask = self.cst[:, K_GM:K_GM + 8]
        for ct in range(2):
            base = C_ROW + ct * 320
            are = self.colp[:, base:base + 64]
            aim = self.colp[:, base + 64:base + 128]
            ldt = self.colp[:, base + 128:base + 192]
            bre = self.colp[:, base + 192:base + 256]
            bim = self.colp[:, base + 256:base + 320]
            dt_, thr, thi, ea, abr, abi, t0, t1 = w
            rd = [tt_, tcp]
            self.act(dt_, ldt, AF.Exp, rd, [tt_])
            self.tt("dve", thr, are, dt_, ALU.mult, rd, [tt_])
            self.tt("dve", thi, aim, dt_, ALU.mult, rd, [tt_])
            self.act(ea, thr, AF.Exp, [tt_], [tt_])
            self.sin_of(abi, thi, 0.0, wi, t0, tt_)
            self.sin_of(abr, thi, math.pi / 2, wi, t0, tt_)
            self.tt("dve", abi, abi, ea, ALU.mult, [tt_], [tt_])
            self.tt("dve", abr, abr, ea, ALU.mult, [tt_], [tt_])
            self.ts("dve", abr, abr, -1.0, ALU.add, [tt_], [tt_])
            self.tt("dve", dt_, are, are, ALU.mult, rd, [tt_])
            self.tt("dve", t0, aim, aim, ALU.mult, rd, [tt_])
            self.tt("dve", dt_, dt_, t0, ALU.add, [tt_], [tt_])
            self.P.op("dve", lambda e, o=dt_, i=dt_: e.reciprocal(out=o, in_=i), reads=[tt_], writes=[tt_])
            self.tt("dve", t0, abr, are, ALU.mult, rd, [tt_])
            self.tt("dve", t1, abi, aim, ALU.mult, rd, [tt_])
            self.tt("dve", t0, t0, t1, ALU.add, [tt_], [tt_])
            self.tt("dve", thr, t0, dt_, ALU.mult, [tt_], [tt_])
            self.tt("dve", t0, abi, are, ALU.mult, rd, [tt_])
            self.tt("dve", t1, abr, aim, ALU.mult, rd, [tt_])
            self.tt("dve", t0, t0, t1, ALU.subtract, [tt_], [tt_])
            self.tt("dve", thi, t0, dt_, ALU.mult, [tt_], [tt_])
            self.tt("dve", t0, thr, bre, ALU.mult, rd, [tt_])
            self.tt("dve", t1, thi, bim, ALU.mult, rd, [tt_])
            self.tt("dve", ea, t0, t1, ALU.subtract, [tt_], [tt_])
            self.tt("dve", t0, thr, bim, ALU.mult, rd, [tt_])
            self.tt("dve", t1, thi, bre, ALU.mult, rd, [tt_])
            self.tt("dve", abi, t0, t1, ALU.add, [tt_], [tt_])
            for ri, src_ in enumerate((ea, abi)):
                self.tt("dve", Bm[ct][:, ri, :, :], src_.unsqueeze(1).to_broadcast([128, 8, 64]),
                        gmask.unsqueeze(2).to_broadcast([128, 8, 64]), ALU.mult, [tt_, tcst], [tt_])
        tct = T_("c_ct")
        for j0 in (0, 4):
            srcr = self.dr["c_reT"][l, j0 // 4]
            srci = self.dr["c_imT"][l, j0 // 4]
            self.wload(CreT[:, j0:j0 + 4, :], srcr, tct)
            self.wload(CreTn[:, j0:j0 + 4, :], srcr, tct, scale=-1.0)
            self.wload(CimTn[:, j0:j0 + 4, :], srci, tct, scale=-1.0)
        self.wload(wglu, self.dr["c_w_glu"][l], tct)
        self.dump("c_TAc%d" % l, TAc, tt_)
        self.dump("c_TAs%d" % l, TAs, tt_)
        self.dump("c_Dr%d" % l, Dr, tt_)
        self.dump("c_Di%d" % l, Di, tt_)
        self.dump("c_Bm%d" % l, Bm[0], tt_)
        self.dump("c_a128r%d" % l, a128r, tt_)
        self.P.barrier()
        A.reset(m2)
        aS = A.alloc([16], F32)
        self.memset("dve", aS, 0.0, [T_("c_aS", 0), T_("c_aS", 1)])
        X = [A.alloc([4, 512], BF16) for _ in range(2)]
        Pfs = [A.alloc([8, 128], F32) for _ in range(2)]
        Ys = [A.alloc([4, 4, 128], BF16) for _ in range(2)]
        p127 = A.alloc([8], F32)
        c1 = A.alloc([8], F32)
        c2 = A.alloc([8], F32)
        yss = [A.alloc([128], F32) for _ in range(2)]
        tch = T_("c_ch")
        its = [(n, ct) for n in range(16) for ct in range(2)]

        def front(i):
            n, ct = its[i]
            par = i % 2
            blk = n // 4
            cs = slice(n * 128, (n + 1) * 128)
            pre_, pim_ = self.ps[0], self.ps[1]
            tpre, tpim = self.pst[0], self.pst[1]
            Bre = Bm[ct][:, 0, :, :].rearrange("p g q -> p (g q)")
            Bim = Bm[ct][:, 1, :, :].rearrange("p g q -> p (g q)")
            self.mm(pre_[:, :], uT[:, ct, cs], Bre, True, True, [T_("c_uT", blk), tt_], [tpre])
            self.mm(pim_[:, :], uT[:, ct, cs], Bim, True, True, [T_("c_uT", blk), tt_], [tpim])
            x = X[par]
            tx = T_("c_X", par)
            tc_ = TAc[:, ct * 512:(ct + 1) * 512]
            ts_ = TAs[:, ct * 512:(ct + 1) * 512]
            self.tt("dve", x[:, 0, :], pre_[:, :], tc_, ALU.mult, [tpre, tt_], [tx])
            self.tt("dve", x[:, 3, :], pre_[:, :], ts_, ALU.mult, [tpre, tt_], [tx])
            self.tt("dve", x[:, 1, :], pim_[:, :], ts_, ALU.mult, [tpim, tt_], [tx])
            self.tt("dve", x[:, 2, :], pim_[:, :], tc_, ALU.mult, [tpim, tt_], [tx])
            Pre, Pim = self.ps[2 + 2 * par], self.ps[3 + 2 * par]
            for jl in range(4):
                js = slice(jl * 128, (jl + 1) * 128)
                self.mmg(Pre[:, js], [(x[:, 0, js], self.tri[:, :]), (x[:, 1, js], self.tri[:, :])],
                         [tx], self.pst[2 + 2 * par])
                self.mmg(Pim[:, js], [(x[:, 2, js], self.tri[:, :]), (x[:, 3, js], self.ntri[:, :])],
                         [tx], self.pst[3 + 2 * par])

        def back_a(i):
            n, ct = its[i]
            par = i % 2
            Pre, Pim = self.ps[2 + 2 * par], self.ps[3 + 2 * par]
            tP0, tP1 = self.pst[2 + 2 * par], self.pst[3 + 2 * par]
            Pf, Y = Pfs[par], Ys[par]
            tPf, tY = T_("c_Pf", par), T_("c_Y", par)
            ta = T_("c_aS", ct)
            jr = slice(4 * ct, 4 * ct + 4)
            ji = slice(8 + 4 * ct, 8 + 4 * ct + 4)
            self.tt("dve", Pf[:, 0:4, :], Pre[:, :].rearrange("p (j t) -> p j t", j=4),
                    aS[:, jr].unsqueeze(2).to_broadcast([128, 4, 128]), ALU.add, [tP0, ta], [tPf])
            self.tt("dve", Pf[:, 4:8, :], Pim[:, :].rearrange("p (j t) -> p j t", j=4),
                    aS[:, ji].unsqueeze(2).to_broadcast([128, 4, 128]), ALU.add, [tP1, ta], [tPf])
            self.cp("dve", p127, Pf[:, :, 127], [tPf], [tch])
            self.tt("dve", c1[:, 0:4], a128r[:, jr], p127[:, 0:4], ALU.mult, [tch, tt_], [tch])
            self.tt("dve", c1[:, 4:8], a128i[:, jr], p127[:, 4:8], ALU.mult, [tch, tt_], [tch])
            self.tt("dve", c2[:, 0:4], a128r[:, jr], p127[:, 4:8], ALU.mult, [tch, tt_], [tch])
            self.tt("dve", c2[:, 4:8], a128i[:, jr], p127[:, 0:4], ALU.mult, [tch, tt_], [tch])
            self.tt("dve", aS[:, jr], c1[:, 0:4], c1[:, 4:8], ALU.subtract, [tch], [ta])
            self.tt("dve", aS[:, ji], c2[:, 0:4], c2[:, 4:8], ALU.add, [tch], [ta])
            self.tt("pool", Y[:, 0, :, :], Pf[:, 0:4, :], Dr[:, jr, :], ALU.mult, [tPf, tt_], [tY])
            self.tt("pool", Y[:, 1, :, :], Pf[:, 4:8, :], Di[:, jr, :], ALU.mult, [tPf, tt_], [tY])
            self.tt("pool", Y[:, 2, :, :], Pf[:, 4:8, :], Dr[:, jr, :], ALU.mult, [tPf, tt_], [tY])
            self.tt("pool", Y[:, 3, :, :], Pf[:, 0:4, :], Di[:, jr, :], ALU.mult, [tPf, tt_], [tY])
            py = self.ps[6 + par]
            pairs = []
            for jl in range(4):
                j = 4 * ct + jl
                pairs += [(CreT[:, j, :], Y[:, 0, jl, :]), (CreTn[:, j, :], Y[:, 1, jl, :]),
                          (CimTn[:, j, :], Y[:, 2, jl, :]), (CimTn[:, j, :], Y[:, 3, jl, :])]
            self.mmg(py[:, 0:128], pairs, [tY, tct], self.pst[6 + par])

        def back_b(i):
            n, ct = its[i]
            par = i % 2
            blk = n // 4
            cs = slice(n * 128, (n + 1) * 128)
            py = self.ps[6 + par]
            ys, tys = yss[par], T_("c_ys", par)
            self.stt("dve", ys, uT[:, ct, cs], self.colp[:, C_CD + ct:C_CD + ct + 1], py[:, 0:128],
                     ALU.mult, ALU.add, [self.pst[6 + par], T_("c_uT", blk), tcp], [tys])
            self.act(ygT[:, ct, cs], ys, AF.Gelu_apprx_tanh, [tys], [T_("c_yg", blk)])

        front(0)
        for i in range(len(its)):
            if i + 1 < len(its):
                front(i + 1)
            back_a(i)
            if i >= 1:
                back_b(i - 1)
        back_b(len(its) - 1)
        self.dump("c_yg%d" % l, ygT, T_("c_yg", 0))
        sg = A.alloc([512], BF16)
        tsg = T_("c_sg")
        for blk in range(NB):
            bs = slice(blk * 512, (blk + 1) * 512)
            for cc in range(2):
                pb = 7
                self.mmg(self.ps[pb][:, :], [(wglu[:, ct, cc * 128:(cc + 1) * 128], ygT[:, ct, bs]) for ct in range(2)],
                         [tct, T_("c_yg", blk)], self.pst[pb])
                self.act(sg, self.ps[pb][:, :], AF.Sigmoid, [self.pst[pb], tcp], [tsg],
                         bias=self.colp[:, C_BGLU + cc:C_BGLU + cc + 1])
                sl = ya[:, 6 + cc, bs]
                self.tt("pool", sg, sg, ygT[:, cc, bs], ALU.mult, [tsg, T_("c_yg", blk)], [tsg])
                self.tt("pool", sl, sl, sg, ALU.mult, [tsg], [T_("yc", cc, blk)])
        self.dump("yc%d" % l, ya[:, 6:8, :], T_("yc", 0, 0))

    def stage_b(self, l):
        A, T_, P = self.A, self.T, self.P
        ya = self.yall
        tcp = self.tcp
        tcst = T_("cst")
        cqn = A.alloc([6, T], BF16)
        ckvn = A.alloc([2, T], BF16)
        krr = A.alloc([T], F32)
        sqkr = A.alloc([T], BF16)
        m2 = A.mark()
        wt = A.alloc([8, 512], BF16)
        wq_in = A.alloc([8, 768], BF16)
        wkv_in = A.alloc([8, 256], BF16)
        wkr = [A.alloc([8, 96], BF16) for _ in range(2)]
        sq = A.alloc([512], BF16)
        rs = A.alloc([512], F32)
        t1 = A.alloc([512], F32)
        t2 = A.alloc([512], F32)
        tw = T_("b_w")
        for i in range(2):
            self.load_win(l, OFF["bg"] + i * 256, 256, wt[:, :, i * 256:(i + 1) * 256], tw)
        for i in range(3):
            self.load_win(l, OFF["cq"] + i * 256, 256, wq_in[:, :, i * 256:(i + 1) * 256], tw)
        self.load_win(l, OFF["ckv"], 256, wkv_in, tw)
        self.wload(wkr[0], self.dr["w_krm"][l], tw)
        self.wload(wkr[1], self.dr["w_krp"][l], tw)
        for ct in range(4):
            for blk in range(NB):
                pb = (ct * NB + blk) % 2
                self.zproj_blk(wt[:, :, ct * 128:(ct + 1) * 128], 128, blk, pb, tw)
                self.act(ya[:, 2 + ct, blk * 512:(blk + 1) * 512], self.ps[pb][:, :], AF.Silu,
                         [self.pst[pb]], [T_("yb", ct, blk)])
        tsq, trs = T_("b_sq"), T_("b_rs")
        R = slice(64, 96)
        for blk in range(NB):
            bs = slice(blk * 512, (blk + 1) * 512)
            for i in range(6):
                self.zproj_blk(wq_in[:, :, i * 128:(i + 1) * 128], 128, blk, i, tw)
                self.act(sq, self.ps[i][:, :], AF.Square, [self.pst[i]], [tsq])
                self.mm(self.ps[6][:, :], self.ones[:, :], sq, i == 0, i == 5, [tsq], [self.pst[6]])
            self.rstd(rs, self.ps[6][:, :], 768.0, [self.pst[6]], [trs])
            for i in range(6):
                self.stt("dve", cqn[:, i, bs], self.ps[i][:, :], self.colp[:, C_QNG + i:C_QNG + i + 1], rs,
                         ALU.mult, ALU.mult, [self.pst[i], trs, tcp], [T_("b_cqn", blk)])
            for i in range(2):
                self.zproj_blk(wkv_in[:, :, i * 128:(i + 1) * 128], 128, blk, i, tw)
                self.act(sq, self.ps[i][:, :], AF.Square, [self.pst[i]], [tsq])
                self.mm(self.ps[7][:, :], self.ones[:, :], sq, i == 0, i == 1, [tsq], [self.pst[7]])
            self.rstd(rs, self.ps[7][:, :], 256.0, [self.pst[7]], [trs])
            for i in range(2):
                self.stt("dve", ckvn[:, i, bs], self.ps[i][:, :], self.colp[:, C_KVNG + i:C_KVNG + i + 1], rs,
                         ALU.mult, ALU.mult, [self.pst[i], trs, tcp], [T_("b_ckvn", blk)])
            for i in range(2):
                self.zproj_blk(wkr[i], 96, blk, 2 + i, tw)
            self.act(sqkr[R, bs], self.ps[2][R, :], AF.Square, [self.pst[2]], [T_("b_kr", blk)])
            tt1 = T_("b_t1")
            self.stt("dve", t1[R, :], self.ps[2][R, :], self.colp[R, C_GK:C_GK + 1], self.cosT[R, bs],
                     ALU.mult, ALU.mult, [self.pst[2], tcp], [tt1])
            self.stt("dve", t2[R, :], self.ps[3][R, :], self.colp[R, C_GKP:C_GKP + 1], self.sinT[R, bs],
                     ALU.mult, ALU.mult, [self.pst[3], tcp], [tt1])
            self.tt("pool", krr[R, bs], t1[R, :], t2[R, :], ALU.add, [tt1], [T_("b_kr", blk)])
        self.dump("b_cqn%d" % l, cqn, T_("b_cqn", 0))
        self.dump("b_ckvn%d" % l, ckvn, T_("b_ckvn", 0))
        self.dump("b_krr%d" % l, krr[R, :], T_("b_kr", 0))
        P.barrier()
        A.reset(m2)
        wqm = [A.alloc([6, 96], BF16) for _ in range(2)]
        wqp = [A.alloc([6, 96], BF16) for _ in range(2)]
        wkv = [A.alloc([2, 128], BF16) for _ in range(2)]
        wk = [w_[:, :, 0:64] for w_ in wkv]
        wv = [w_[:, :, 64:128] for w_ in wkv]
        qn = [A.alloc([T], BF16) for _ in range(2)]
        kn = [A.alloc([T], BF16) for _ in range(2)]
        vh = [A.alloc([16, 64], BF16) for _ in range(2)]
        sq = [A.alloc([512], BF16) for _ in range(2)]
        rs = [A.alloc([512], F32) for _ in range(2)]
        t1 = A.alloc([512], F32)
        t2 = A.alloc([512], F32)
        pT = [A.alloc([512], BF16) for _ in range(4)]
        rcs = [A.alloc([512], F32) for _ in range(2)]
        ots = [A.alloc([512], BF16) for _ in range(2)]
        scale = 96.0 ** -0.5
        def b_load(h_):
            s_ = h_ % 2
            twh_ = T_("b_wh", s_)
            self.wload(wqm[s_], self.dr["wq_m"][l, h_], twh_)
            self.wload(wqp[s_], self.dr["wq_p"][l, h_], twh_)
            self.wload(wkv[s_], self.dr["w_ukv"][l, h_], twh_)

        b_load(0)
        for h in range(8):
            s = h % 2
            twh = T_("b_wh", s)
            tq, tk, tv = T_("b_qn", s), T_("b_kn", s), T_("b_vh", s)
            Q = slice(0, 96)
            N_ = slice(0, 64)
            for half in range(2):
                pbv = 6 + half
                for t8 in range(8):
                    tt_ = half * 8 + t8
                    self.mmg(self.ps[pbv][:, t8 * 64:(t8 + 1) * 64],
                             [(ckvn[:, kt, tt_ * 128:(tt_ + 1) * 128], wv[s][:, kt, :]) for kt in range(2)],
                             [twh], self.pst[pbv])
                self.cp("dve", vh[s][:, half * 8:(half + 1) * 8, :],
                        self.ps[pbv][:, :].rearrange("p (a b) -> p a b", a=8), [self.pst[pbv]], [tv])

            def banks(blk):
                o = 0 if blk % 2 == 0 else 4
                return o, o + 1, o + 2, o + 3

            def prep_front(blk):
                bs = slice(blk * 512, (blk + 1) * 512)
                bq, bp, _, bk = banks(blk)
                self.mmg(self.ps[bq][Q, :], [(wqm[s][:, kt, :], cqn[:, kt, bs]) for kt in range(6)], [twh], self.pst[bq])
                self.mmg(self.ps[bp][Q, :], [(wqp[s][:, kt, :], cqn[:, kt, bs]) for kt in range(6)], [twh], self.pst[bp])
                self.mmg(self.ps[bk][N_, :], [(wk[s][:, kt, :], ckvn[:, kt, bs]) for kt in range(2)], [twh], self.pst[bk])

            def prep_back(blk):
                bs = slice(blk * 512, (blk + 1) * 512)
                bq, bp, bsq, bk = banks(blk)
                tsq0, trs0 = T_("b_sq", 0), T_("b_rs", 0)
                tsq1, trs1 = T_("b_sq", 1), T_("b_rs", 1)
                self.act(sq[0][Q, :], self.ps[bq][Q, :], AF.Square, [self.pst[bq]], [tsq0])
                self.act(sq[1][N_, :], self.ps[bk][N_, :], AF.Square, [self.pst[bk]], [tsq1])
                self.cp("pool", sq[1][R, :], sqkr[R, bs], [], [tsq1])
                self.mmg(self.ps[bsq][Q, :], [(self.ones[Q, 0:96], sq[0][Q, :])], [tsq0], self.pst[bsq])
                self.rstd(rs[0][Q, :], self.ps[bsq][Q, :], 96.0, [self.pst[bsq]], [trs0])
                self.mmg(self.ps[bsq][Q, :], [(self.ones[Q, 0:96], sq[1][Q, :])], [tsq1], self.pst[bsq])
                self.rstd(rs[1][Q, :], self.ps[bsq][Q, :], 96.0, [self.pst[bsq]], [trs1])
                tt1 = T_("b_t1")
                self.stt("dve", t1[R, :], self.ps[bq][R, :], self.colp[R, C_GQ:C_GQ + 1], self.cosT[R, bs],
                         ALU.mult, ALU.mult, [self.pst[bq], tcp], [tt1])
                self.stt("dve", t2[R, :], self.ps[bp][R, :], self.colp[R, C_GQP:C_GQP + 1], self.sinT[R, bs],
                         ALU.mult, ALU.mult, [self.pst[bp], tcp], [tt1])
                self.stt("dve", qn[s][N_, bs], self.ps[bq][N_, :], self.colp[N_, C_GQ:C_GQ + 1], rs[0][N_, :],
                         ALU.mult, ALU.mult, [self.pst[bq], trs0, tcp], [tq])
                self.stt("dve", kn[s][N_, bs], self.ps[bk][N_, :], self.colp[N_, C_GK:C_GK + 1], rs[1][N_, :],
                         ALU.mult, ALU.mult, [self.pst[bk], trs1, tcp], [tk])
                self.tt("pool", t1[R, :], t1[R, :], t2[R, :], ALU.add, [tt1], [tt1])
                self.tt("pool", qn[s][R, bs], t1[R, :], rs[0][R, :], ALU.mult, [tt1, trs0], [tq])
                self.tt("pool", kn[s][R, bs], krr[R, bs], rs[1][R, :], ALU.mult, [trs1], [tk])

            prep_front(0)
            for blk in range(NB):
                if blk + 1 < NB:
                    prep_front(blk + 1)
                prep_back(blk)
            if h == 0:
                self.dump("b_qn%d" % l, qn[0][0:96, :], tq)
                self.dump("b_kn%d" % l, kn[0][0:96, :], tk)
                self.dump("b_vh%d" % l, vh[0], tv)
            if h + 1 < 8:
                b_load(h + 1)
            ct, r0 = h // 2, (h % 2) * 64
            RR = slice(r0, r0 + 64)
            seq = [(b, j) for b in range(NB) for j in range(4 * b + 4)]

            def att_s(i):
                b, j = seq[i]
                jj = j - 4 * b
                c0 = 128 * jj if jj > 0 else 0
                pb = 4 + i % 2
                self.mm(self.ps[pb][:, c0:512], kn[s][0:96, j * 128:(j + 1) * 128],
                        qn[s][0:96, b * 512 + c0:(b + 1) * 512], True, True, [tq, tk], [self.pst[pb]])

            def att_rest(i):
                b, j = seq[i]
                nj = 4 * b + 4
                jj = j - 4 * b
                c0 = 128 * jj if jj > 0 else 0
                pb = 4 + i % 2
                p = pT[i % 4]
                tp = T_("b_pT", i % 4)
                po, pd = (6, 7) if b % 2 == 0 else (2, 3)
                self.act(p[:, c0:512], self.ps[pb][:, c0:512], AF.Exp, [self.pst[pb]], [tp], scale=scale)
                if jj >= 0:
                    self.tt("pool", p[:, c0:c0 + 128], p[:, c0:c0 + 128], self.tri[:, :], ALU.mult,
                            [tp, tcst], [tp])
                self.mm(self.ps[po][RR, c0:512], vh[s][:, j, :], p[:, c0:512], j == 0, j == nj - 1,
                        [tv, tp], [self.pst[po]])
                self.mm(self.ps[pd][RR, c0:512], self.ones[:, 0:64], p[:, c0:512], j == 0, j == nj - 1,
                        [tp], [self.pst[pd]])
                if j == nj - 1:
                    trc, tot = T_("b_rc", b % 2), T_("b_ot", b % 2)
                    rc_, ot_ = rcs[b % 2], ots[b % 2]
                    self.P.op("dve", lambda e, o=rc_[RR, :], i_=self.ps[pd][RR, :]: e.reciprocal(out=o, in_=i_),
                              reads=[self.pst[pd]], writes=[trc])
                    self.tt("dve", ot_[RR, :], self.ps[po][RR, :], rc_[RR, :], ALU.mult, [self.pst[po], trc], [tot])
                    sl = ya[RR, 2 + ct, b * 512:(b + 1) * 512]
                    self.tt("pool", sl, sl, ot_[RR, :], ALU.mult, [tot], [T_("yb", ct, b)])

            att_s(0)
            for i in range(len(seq)):
                if i + 1 < len(seq):
                    att_s(i + 1)
                att_rest(i)
        self.dump("yb%d" % l, ya[:, 2:6, :], T_("yb", 0, 0))

    def stage_p2(self, l, src, dst):
        A, T_, P = self.A, self.T, self.P
        ya = self.yall
        merged = A.alloc([8, T], BF16)
        m2 = A.mark()
        wls = [A.alloc([8, 4, 256], BF16) for _ in range(2)]
        wbs = [A.alloc([10, 256], BF16) for _ in range(2)]
        g = [A.alloc([512], F32) for _ in range(2)]
        macc = A.alloc([512], F32)
        t2 = [A.alloc([512], F32) for _ in range(2)]
        brk = {0: [0, 1], 1: [2, 3, 4, 5], 2: [6, 7], 3: [8, 9]}
        brw = ["w_br_a", "w_br_b", "w_br_c", "w_br_m"]
        tcp = self.tcp
        def p2_load(j2):
            wl_, wb_ = wls[j2 % 2], wbs[j2 % 2]
            twl_, twb_ = T_("p_wl", j2 % 2), T_("p_wb", j2 % 2)
            for br in range(4):
                c0 = OFF["mrg"] + br * 1024 + j2 * 256
                self.wload(wl_[:, :, br, :], self.dr["w_in"][l, WIN_G[c0]], twl_)
            self.wload(wb_[:, 0:6, :], self.dr["w_br"][l, j2, :, 0:6, :], twb_)
            self.wload(wb_[:, 6:10, :], self.dr["w_br"][l, j2, :, 6:10, :], twb_)

        p2_load(0)
        for j2 in range(4):
            if j2 + 1 < 4:
                p2_load(j2 + 1)
            wl, wb = wls[j2 % 2], wbs[j2 % 2]
            twl, twb = T_("p_wl", j2 % 2), T_("p_wb", j2 % 2)
            for jj in range(2):
                j = 2 * j2 + jj
                js = slice(jj * 128, (jj + 1) * 128)
                for blk in range(NB):
                    bs = slice(blk * 512, (blk + 1) * 512)
                    for br in range(4):
                        pl, pp = self.ps[2 * (br % 2)], self.ps[2 * (br % 2) + 1]
                        tpl, tpp = self.pst[2 * (br % 2)], self.pst[2 * (br % 2) + 1]
                        self.mmg(pl[:, :], [(wl[:, kt, br, js], self.hT[:, kt, bs]) for kt in range(8)], [twl], tpl)
                        self.mmg(pp[:, :], [(wb[:, kt, js], ya[:, kt, bs]) for kt in brk[br]], [twb], tpp)
                        gg, tg = g[br % 2], T_("p_g", br % 2)
                        self.act(gg, pl[:, :], AF.Sigmoid, [tpl, tcp], [tg],
                                 bias=self.colp[:, C_BM + br * 8 + j:C_BM + br * 8 + j + 1])
                        tm = T_("p_m")
                        if br == 0:
                            self.tt("dve", macc, pp[:, :], gg, ALU.mult, [tpp, tg], [tm])
                        else:
                            tt2 = T_("p_t2", br % 2)
                            self.tt("dve", t2[br % 2], pp[:, :], gg, ALU.mult, [tpp, tg], [tt2])
                            if br < 3:
                                self.tt("dve", macc, macc, t2[br % 2], ALU.add, [tt2, tm], [tm])
                            else:
                                self.tt("dve", merged[:, j, bs], macc, t2[br % 2], ALU.add, [tt2, tm],
                                        [T_("p_mg", blk)])
        self.dump("merged%d" % l, merged, T_("p_mg", 0))
        P.barrier()
        A.reset(m2)
        wo = A.alloc([8, D], BF16)
        two = T_("p_wo")
        for i in range(4):
            self.wload(wo[:, :, i * 256:(i + 1) * 256],
                       self.dr["w_out"][l, i], two)
        xt = A.alloc([8, 512], F32)
        xo = A.alloc([8, 512], F32)
        txt, txo = T_("p_xt"), T_("p_xo")
        for blk in range(NB):
            bs = slice(blk * 512, (blk + 1) * 512)
            for hh in range(2):
                self.dma(xt[:, hh * 4:(hh + 1) * 4, :], src[blk, :, hh * 4:(hh + 1) * 4, :], [], [txt])
            for d2 in range(8):
                pb = 4 + d2 % 2
                self.mmg(self.ps[pb][:, :], [(wo[:, kt, d2 * 128:(d2 + 1) * 128], merged[:, kt, bs]) for kt in range(8)],
                         [two, T_("p_mg", blk)], self.pst[pb])
                self.tt("dve", xo[:, d2, :], self.ps[pb][:, :], xt[:, d2, :], ALU.add, [self.pst[pb], txt], [txo])
            for hh in range(2):
                op = self.dma(dst[blk, :, hh * 4:(hh + 1) * 4, :], xo[:, hh * 4:(hh + 1) * 4, :], [txo], [], q="act")
                if dst is self.outT:
                    self.finals.append(op)


def make_consts():
    c = np.zeros((128, NCONST), np.float32)
    s = np.arange(128)
    c[:, K_TRI:K_TRI + 128] = (s[:, None] <= s[None, :]).astype(np.float32)
    c[:, K_IOT:K_IOT + 128] = s[None, :].astype(np.float32)
    c[:, K_IOS] = s.astype(np.float32)
    half = 16
    inv = (10000.0 ** (-np.arange(half, dtype=np.float32) / half)).astype(np.float32)
    for r in range(64, 96):
        c[r, K_INVF] = inv[(r - 64) % 16]
        c[r, K_SGN] = -1.0 if (r - 64) < 16 else 1.0
    for r in range(128):
        c[r, K_GM + r // 16] = 1.0
    return c


def host_prep(inp):
    f = lambda k: np.asarray(inp[k], dtype=np.float32)
    perm = np.array(PERM)
    sh = {}

    def rows_t(w):
        r, c = w.shape
        return np.ascontiguousarray(w.reshape(r // 128, 128, c).transpose(1, 0, 2))

    w_in = f("w_in")
    sh["w_in"] = np.stack([np.stack([rows_t(w_in[l][:, c0:c0 + 256]) for c0 in WIN_C0]) for l in range(L)])
    krm = np.zeros((L, D, 96), np.float32)
    krp = np.zeros((L, D, 96), np.float32)
    krm[:, :, 64:96] = w_in[:, :, OFF["kr"]:OFF["kr"] + 32]
    krp[:, :, 64:96] = w_in[:, :, OFF["kr"] + perm]
    sh["w_krm"] = np.stack([rows_t(krm[l]) for l in range(L)])
    sh["w_krp"] = np.stack([rows_t(krp[l]) for l in range(L)])
    wuq = f("b_w_uq").reshape(L, 768, 8, 96)
    wqp = np.zeros((L, 768, 8, 96), np.float32)
    wqp[:, :, :, 64:96] = wuq[:, :, :, 64 + perm]
    sh["wq_m"] = np.stack([np.stack([rows_t(wuq[l][:, h, :]) for h in range(8)]) for l in range(L)])
    sh["wq_p"] = np.stack([np.stack([rows_t(wqp[l][:, h, :]) for h in range(8)]) for l in range(L)])
    ukv = f("b_w_ukv")
    sh["w_ukv"] = np.stack([np.stack([rows_t(ukv[l][:, h * 128:(h + 1) * 128]) for h in range(8)]) for l in range(L)])
    sh["a_w_sT"] = np.ascontiguousarray(f("a_w_s").transpose(0, 3, 1, 2))
    c_re, c_im = f("c_c_re"), f("c_c_im")
    cre = np.zeros((L, 8, 128, 128), np.float32)
    cim = np.zeros((L, 8, 128, 128), np.float32)
    for j in range(8):
        for gl in range(2):
            g = 2 * j + gl
            col0 = 16 * (g % 8)
            cre[:, j, gl * 64:(gl + 1) * 64, col0:col0 + 16] = c_re[:, g].transpose(0, 2, 1)
            cim[:, j, gl * 64:(gl + 1) * 64, col0:col0 + 16] = c_im[:, g].transpose(0, 2, 1)
    sh["c_reT"] = np.ascontiguousarray(cre.reshape(L, 2, 4, 128, 128).transpose(0, 1, 3, 2, 4))
    sh["c_imT"] = np.ascontiguousarray(cim.reshape(L, 2, 4, 128, 128).transpose(0, 1, 3, 2, 4))
    sh["c_w_glu"] = np.stack([rows_t(f("c_w_glu")[l]) for l in range(L)])
    mkv = f("m_w_kv")
    sh["m_w_kv"] = np.stack([np.stack([rows_t(mkv[l][:, i * 256:(i + 1) * 256]) for i in range(2)]) for l in range(L)])
    wbr = np.concatenate([f("w_br_a"), f("w_br_b"), f("w_br_c"), f("w_br_m")], axis=1)
    sh["w_br"] = np.stack([np.stack([rows_t(wbr[l][:, i * 256:(i + 1) * 256]) for i in range(4)]) for l in range(L)])
    wo = f("w_out")
    sh["w_out"] = np.stack([np.stack([rows_t(wo[l][:, i * 256:(i + 1) * 256]) for i in range(4)]) for l in range(L)])
    cp = np.zeros((L, 128, NCOL), np.float32)
    cp[:, :, C_NG:C_NG + 8] = f("norm_g").reshape(L, 8, 128).transpose(0, 2, 1)
    cp[:, :, C_MNG:C_MNG + 8] = f("m_norm_g").reshape(L, 8, 128).transpose(0, 2, 1)
    cp[:, :, C_QNG:C_QNG + 6] = f("b_q_norm_g").reshape(L, 6, 128).transpose(0, 2, 1)
    cp[:, :, C_KVNG:C_KVNG + 2] = f("b_kv_norm_g").reshape(L, 2, 128).transpose(0, 2, 1)
    cp[:, :, C_BM:C_BM + 32] = f("b_merge").reshape(L, 32, 128).transpose(0, 2, 1)
    gq, gk = f("b_qk_g_q"), f("b_qk_g_k")
    cp[:, 0:96, C_GQ] = gq
    cp[:, 64:96, C_GQP] = gq[:, 64 + perm]
    cp[:, 0:96, C_GK] = gk
    cp[:, 64:96, C_GKP] = gk[:, 64 + perm]
    cp[:, :, C_MGQ] = np.tile(f("m_qk_g_q"), (1, 2))
    cp[:, :, C_MGK] = np.tile(f("m_qk_g_k"), (1, 2))
    cp[:, :, C_CD:C_CD + 2] = f("c_d").reshape(L, 2, 128).transpose(0, 2, 1)
    cp[:, :, C_BGLU:C_BGLU + 2] = f("c_b_glu").reshape(L, 2, 128).transpose(0, 2, 1)
    a_re, a_im, ldt = f("c_a_re"), f("c_a_im"), f("c_log_dt")
    cp[:, :, C_SRE:C_SRE + 8] = a_re.reshape(L, 8, 128).transpose(0, 2, 1)
    cp[:, :, C_SIM:C_SIM + 8] = a_im.reshape(L, 8, 128).transpose(0, 2, 1)
    ldt_rep = np.repeat(ldt[:, :, None], 64, axis=2)
    cp[:, :, C_SDT:C_SDT + 8] = ldt_rep.reshape(L, 8, 128).transpose(0, 2, 1)
    abs_ = f("a_b_s")
    for ct in range(2):
        for gl in range(2):
            cp[:, gl * 64:(gl + 1) * 64, C_ABS + ct * 128:C_ABS + (ct + 1) * 128] = abs_[:, 2 * ct + gl][:, None, :]
    b_re, b_im = f("c_b_re"), f("c_b_im")
    for ct in range(2):
        base = C_ROW + ct * 320
        for g8 in range(8):
            g = 8 * ct + g8
            rows = slice(16 * g8, 16 * g8 + 16)
            cp[:, rows, base + 0:base + 64] = a_re[:, g][:, None, :]
            cp[:, rows, base + 64:base + 128] = a_im[:, g][:, None, :]
            cp[:, rows, base + 128:base + 192] = ldt[:, g][:, None, None]
            cp[:, rows, base + 192:base + 256] = b_re[:, g].transpose(0, 2, 1)
            cp[:, rows, base + 256:base + 320] = b_im[:, g].transpose(0, 2, 1)
    sh["colpack"] = cp
    rp = np.zeros((L, 128, NROW), np.float32)
    rp[:, :, R_ANG:R_ANG + 256] = f("a_norm_g")[:, None, :]
    rp[:, :, R_SRE:R_SRE + 1024] = a_re.reshape(L, 1, 1024)
    rp[:, :, R_SIM:R_SIM + 1024] = a_im.reshape(L, 1, 1024)
    rp[:, :, R_SDT:R_SDT + 1024] = ldt_rep.reshape(L, 1, 1024)
    sh["rowpack"] = rp
    sh["consts"] = make_consts()
    x = f("x")
    mem = f("mem")
    pos = np.asarray(inp["positions"]).astype(np.int32)
    per_core = []
    for b in range(8):
        d = dict(sh)
        d["xT"] = tile_x(x[b])
        d["memT"] = np.ascontiguousarray(mem[b].T.reshape(8, 128, 1, 256).transpose(2, 1, 0, 3))
        d["pos"] = np.ascontiguousarray(pos[b][None, :])
        per_core.append(d)
    return per_core


def tile_x(xb):
    return np.ascontiguousarray(xb.T.reshape(8, 128, NB, 512).transpose(2, 1, 0, 3))


def untile_x(t):
    return np.ascontiguousarray(t.transpose(2, 1, 0, 3).reshape(D, T).T)


_CACHE = {}
LAYER_KEYS = ("colpack", "rowpack", "w_in", "w_krm", "w_krp", "wq_m", "wq_p", "w_ukv", "a_w_sT", "c_reT", "c_imT",
              "c_w_glu", "m_w_kv", "w_br", "w_out")
FUSED = True


def kernel(**inputs):
    in_maps = host_prep(inputs)
    if FUSED:
        if "nc" not in _CACHE:
            _CACHE["nc"] = Builder(nlayers=L).build()
        res = run_bass_kernel_spmd(_CACHE["nc"], in_maps, core_ids=list(range(8)))
        return np.stack([untile_x(r["outT"]) for r in res.results], axis=0).astype(np.float32)
    if "nc1" not in _CACHE:
        _CACHE["nc1"] = Builder(nlayers=1).build()
    nc = _CACHE["nc1"]
    xs = [m["xT"] for m in in_maps]
    for l in range(L):
        maps = []
        for c in range(8):
            d = dict(in_maps[c])
            for k in LAYER_KEYS:
                a = in_maps[c][k]
                d[k] = np.ascontiguousarray(np.concatenate([a[l:l + 1], a[l:l + 1]], axis=0))
            d["xT"] = xs[c]
            maps.append(d)
        res = run_bass_kernel_spmd(nc, maps, core_ids=list(range(8)))
        xs = [np.ascontiguousarray(r["outT"]) for r in res.results]
    return np.stack([untile_x(t) for t in xs], axis=0).astype(np.float32)
```

```python
import math
import contextlib
import numpy as np
import concourse.bass as bass
import concourse.mybir as mybir
from concourse.bass_utils import run_bass_kernel_spmd

F32 = mybir.dt.float32
BF16 = mybir.dt.bfloat16
I32 = mybir.dt.int32
AF = mybir.ActivationFunctionType
ALU = mybir.AluOpType

D = 1024
T = 2048
L = 2
NB = 4
EPS = 1e-6
IN_W = 7456
OFF = dict(a_u=0, a_v=256, a_g=512, cq=768, ckv=1536, kr=1792, bg=1824,
           cin=2336, cg=2592, mq=2848, mg=3104, mrg=3360)
PERM = list(range(16, 32)) + list(range(0, 16))
WIN_C0 = [0, 256, 512, 768, 1024, 1280, 1536, 1824, 2080, 2336, 2592, 2848, 3104] + \
         [3360 + br * 1024 + j2 * 256 for br in range(4) for j2 in range(4)]
WIN_G = {c: i for i, c in enumerate(WIN_C0)}
NWG = len(WIN_C0)
TWO_PI = 2.0 * math.pi

C_NG, C_MNG, C_QNG, C_KVNG, C_BM = 0, 8, 16, 22, 24
C_GQ, C_GQP, C_GK, C_GKP, C_MGQ, C_MGK = 56, 57, 58, 59, 60, 61
C_CD, C_BGLU, C_SRE, C_SIM, C_SDT, C_ABS, C_ROW = 62, 64, 66, 74, 82, 90, 346
NCOL = 346 + 640
R_ANG, R_SRE, R_SIM, R_SDT = 0, 256, 1280, 2304
NROW = 3328
K_TRI, K_IOT, K_IOS, K_INVF, K_SGN, K_GM = 0, 128, 256, 257, 258, 259
NCONST = 267


class Tok:
    __slots__ = ("w", "rs", "excl")

    def __init__(self):
        self.w = None
        self.rs = {}
        self.excl = False


class Op:
    __slots__ = ("eng", "fn", "deps", "is_dma", "sig", "users")

    def __init__(self, eng, fn, is_dma):
        self.eng = eng
        self.fn = fn
        self.is_dma = is_dma
        self.deps = []
        self.sig = None
        self.users = 0


ENGS = ("pe", "act", "dve", "pool", "sp")
SYNC_ALL = True
WQ = ("sp",)
DMAQ = ("sp", "act", "pool")


class Prog:
    def __init__(self, nc, n_dma_sems=8):
        self.nc = nc
        self.ops = {e: [] for e in ENGS}
        self.all = []
        self.n_dma_sems = n_dma_sems
        self.toks = {}
        self.dmas_since_bar = []

    def tok(self, *key):
        t = self.toks.get(key)
        if t is None:
            t = self.toks[key] = Tok()
        return t

    def _add(self, eng, fn, reads, writes, is_dma, extra_deps=()):
        op = Op(eng, fn, is_dma)
        deps = list(extra_deps)
        raw = set()
        for t in reads:
            if t.w is not None:
                deps.append(t.w)
                raw.add(id(t.w))
            if t.excl:
                deps.extend(o for o in t.rs.values() if o.eng != eng)
        for t in writes:
            if t.w is not None:
                deps.append(t.w)
            deps.extend(t.rs.values())
        rkey = ("dma", id(op)) if is_dma else eng
        for t in reads:
            t.rs[rkey] = op
        for t in writes:
            t.w = op
            t.rs = {}
        seen = set()
        for d in deps:
            if d is op or id(d) in seen:
                continue
            seen.add(id(d))
            if (not d.is_dma) and (not is_dma) and d.eng == eng:
                if eng == "pe" or (id(d) not in raw and not SYNC_ALL):
                    continue
            if (not d.is_dma) and is_dma and d.eng == eng:
                pass
            op.deps.append(d)
            d.users += 1
        self.ops[eng].append(op)
        self.all.append(op)
        if is_dma:
            self.dmas_since_bar.append(op)
        return op

    def op(self, eng, fn, reads=(), writes=()):
        return self._add(eng, fn, reads, writes, False)

    def dma(self, eng, fn, reads=(), writes=()):
        assert eng in DMAQ
        return self._add(eng, fn, reads, writes, True)

    def barrier(self):
        dm = self.dmas_since_bar
        self.dmas_since_bar = []
        bt = [self.tok("__bar", e) for e in ENGS]
        for i, e in enumerate(ENGS):
            self._add(e, lambda eng: eng.drain(), [], [bt[i]], False,
                      extra_deps=dm if e == "sp" else ())
        for e in ENGS:
            self._add(e, lambda eng: eng.nop(), bt, [], False)

    def emit(self, final_wait_ops=()):
        nc = self.nc
        with contextlib.ExitStack() as es:
            esem = {e: es.enter_context(nc.semaphore("s_" + e)) for e in ENGS}
            dsem = {e: [es.enter_context(nc.semaphore("d_%s%d" % (e, i)))
                        for i in range(self.n_dma_sems)] for e in DMAQ}
            ecount = {e: 0 for e in ENGS}
            dcount = {e: 0 for e in DMAQ}
            duse = {e: [0] * self.n_dma_sems for e in DMAQ}
            fw = set(id(o) for o in final_wait_ops)
            for op in self.all:
                if op.is_dma:
                    j = dcount[op.eng]
                    dcount[op.eng] += 1
                    s = j % self.n_dma_sems
                    duse[op.eng][s] += 1
                    op.sig = (dsem[op.eng][s], 16 * duse[op.eng][s], ("d", op.eng, s))
                elif op.users > 0 or id(op) in fw:
                    ecount[op.eng] += 1
                    op.sig = (esem[op.eng], ecount[op.eng], ("e", op.eng))
            self.stats = dict(ecount=ecount, dcount=dcount,
                              nops={e: len(self.ops[e]) for e in ENGS})
            block = es.enter_context(nc.Block())

            def run_engine(e, engine):
                waited = {}

                def wait(sem, val, key):
                    if waited.get(key, 0) >= val:
                        return
                    waited[key] = val
                    engine.wait_ge(sem, val)

                for op in self.ops[e]:
                    for d in op.deps:
                        wait(*d.sig)
                    if op.is_dma:
                        sem, val, key = op.sig
                        if val > 16:
                            wait(sem, val - 16, key)
                    ins = op.fn(engine)
                    if op.sig is not None:
                        ins.then_inc(op.sig[0], 16 if op.is_dma else 1)
                if e == "sp":
                    for op in final_wait_ops:
                        wait(*op.sig)

            block.tensor(lambda eng: run_engine("pe", eng))
            block.scalar(lambda eng: run_engine("act", eng))
            block.vector(lambda eng: run_engine("dve", eng))
            block.gpsimd(lambda eng: run_engine("pool", eng))
            block.sync(lambda eng: run_engine("sp", eng))


class Arena:
    def __init__(self, ap2d, nfloats):
        self.a = ap2d
        self.n = nfloats
        self.off = 0
        self.peak = 0

    def alloc(self, free_shape, dt):
        free_shape = list(free_shape)
        isz = 2 if dt == BF16 else 4
        n = int(np.prod(free_shape))
        n32 = (n * isz + 3) // 4
        assert self.off + n32 <= self.n, ("arena overflow", self.off, n32, self.n)
        v = self.a[:, self.off:self.off + n32]
        self.off += n32
        self.peak = max(self.peak, self.off)
        if dt == BF16:
            v = v.bitcast(BF16)[:, 0:n]
        elif dt == I32:
            v = v.bitcast(I32)
        if len(free_shape) == 2:
            v = v.rearrange("p (a b) -> p a b", a=free_shape[0])
        elif len(free_shape) == 3:
            v = v.rearrange("p (a b c) -> p a b c", a=free_shape[0], b=free_shape[1])
        elif len(free_shape) == 4:
            v = v.rearrange("p (a b c d) -> p a b c d", a=free_shape[0], b=free_shape[1],
                            c=free_shape[2])
        return v

    def mark(self):
        return self.off

    def reset(self, m):
        self.off = m


class Builder:
    def __init__(self, nlayers=L, stop=None, dumps=()):
        self.nlayers = nlayers
        self.stop = stop
        self.dumps = dict()
        self.want = set(dumps)
        self.nc = nc = bass.Bass("TRN2", target_bir_lowering=False)
        self.P = Prog(nc)
        self.dr = {}
        self.finals = []
        self.dump_specs = []

    def din(self, name, shape, dt=F32):
        self.dr[name] = self.nc.dram_tensor(name, list(shape), dt, kind="ExternalInput").ap()
        return self.dr[name]

    def T(self, *k):
        return self.P.tok(*k)

    def mm(self, out, lhsT, rhs, start, stop, reads, writes):
        return self.P.op("pe", lambda e: e.matmul(out, lhsT=lhsT, rhs=rhs, start=start, stop=stop),
                         reads=reads, writes=writes)

    def mmg(self, out, pairs, reads, wtok):
        n = len(pairs)
        for i, (a, b) in enumerate(pairs):
            self.mm(out, a, b, i == 0, i == n - 1, reads, [wtok])

    def act(self, out, in_, func, reads, writes, bias=0.0, scale=1.0, eng="act", accum_out=None):
        if accum_out is None:
            return self.P.op("act", lambda e: e.activation(out=out, in_=in_, func=func, bias=bias, scale=scale),
                             reads=reads, writes=writes)
        return self.P.op("act", lambda e: e.activation(out=out, in_=in_, func=func, bias=bias, scale=scale,
                                                       accum_out=accum_out),
                         reads=reads, writes=writes)

    def tt(self, eng, out, in0, in1, op, reads, writes):
        return self.P.op(eng, lambda e: e.tensor_tensor(out=out, in0=in0, in1=in1, op=op),
                         reads=reads, writes=writes)

    def ts(self, eng, out, in0, s1, op0, reads, writes, s2=None, op1=None):
        if op1 is None:
            return self.P.op(eng, lambda e: e.tensor_scalar(out=out, in0=in0, scalar1=s1, scalar2=None, op0=op0),
                             reads=reads, writes=writes)
        return self.P.op(eng, lambda e: e.tensor_scalar(out=out, in0=in0, scalar1=s1, scalar2=s2, op0=op0, op1=op1),
                         reads=reads, writes=writes)

    def stt(self, eng, out, in0, scalar, in1, op0, op1, reads, writes):
        return self.P.op(eng, lambda e: e.scalar_tensor_tensor(out=out, in0=in0, scalar=scalar, in1=in1,
                                                                op0=op0, op1=op1),
                         reads=reads, writes=writes)

    def cp(self, eng, out, in_, reads, writes):
        if eng == "act":
            return self.P.op("act", lambda e: e.copy(out=out, in_=in_), reads=reads, writes=writes)
        return self.P.op(eng, lambda e: e.tensor_copy(out=out, in_=in_), reads=reads, writes=writes)

    def memset(self, eng, ap, val, writes):
        return self.P.op(eng, lambda e: e.memset(ap, val), reads=(), writes=writes)

    def dma(self, out, in_, reads, writes, q="sp"):
        def ndesc(ap):
            dims = [(int(st), int(n)) for st, n in ap.ap]
            total = 1
            for st, n in dims:
                total *= n
            run = 1
            for st, n in reversed(dims[1:]):
                if st == run:
                    run *= n
                else:
                    break
            return total // run
        self.desc_count = getattr(self, "desc_count", {})
        self.desc_count[q] = self.desc_count.get(q, 0) + max(ndesc(out), ndesc(in_))
        return self.P.dma(q, lambda e: e.dma_start(out=out, in_=in_), reads=reads, writes=writes)

    def rstd(self, out, in_, n, reads, writes):
        self.act(out, in_, AF.Ln, reads, writes, bias=EPS, scale=1.0 / n)
        self.act(out, out, AF.Exp, writes, writes, scale=-0.5)

    def wload(self, dst, src, dtok, cast_eng="pool", scale=None):
        fs = list(dst.shape[1:])
        n = int(np.prod(fs))
        assert n <= self.stage_n, (n, self.stage_n)
        slot = self.wslot
        self.wslot = (slot + 1) % len(self.stage)
        st = self.stage[slot][:, 0:n]
        if len(fs) == 2:
            st = st.rearrange("p (a b) -> p a b", a=fs[0])
        elif len(fs) == 3:
            st = st.rearrange("p (a b c) -> p a b c", a=fs[0], b=fs[1])
        stok = self.T("stage", slot)
        self.wq_i = getattr(self, "wq_i", 0) + 1
        self.dma(st, src, [], [stok], q=WQ[self.wq_i % len(WQ)])
        if scale is None:
            self.cp(cast_eng, dst, st, [stok], [dtok])
        else:
            self.ts(cast_eng, dst, st, scale, ALU.mult, [stok], [dtok])

    def dump(self, name, ap, tok):
        if name not in self.want:
            return
        self.P.barrier()
        shp = list(ap.shape)
        d = self.nc.dram_tensor("dbg_" + name, shp, ap.dtype, kind="ExternalOutput").ap()
        op = self.dma(d, ap, [tok], [])
        self.finals.append(op)
        self.dump_specs.append(name)

    def build(self):
        nc = self.nc
        P = self.P
        din = self.din
        din("xT", [NB, 128, 8, 512])
        din("memT", [1, 128, 8, 256])
        din("pos", [1, T], I32)
        din("consts", [128, NCONST])
        din("colpack", [L, 128, NCOL])
        din("rowpack", [L, 128, NROW])
        din("w_in", [L, NWG, 128, 8, 256])
        din("w_krm", [L, 128, 8, 96])
        din("w_krp", [L, 128, 8, 96])
        din("wq_m", [L, 8, 128, 6, 96])
        din("wq_p", [L, 8, 128, 6, 96])
        din("w_ukv", [L, 8, 128, 2, 128])
        din("a_w_sT", [L, 128, 4, 128])
        din("c_reT", [L, 2, 128, 4, 128])
        din("c_imT", [L, 2, 128, 4, 128])
        din("c_w_glu", [L, 128, 2, 256])
        din("m_w_kv", [L, 2, 128, 8, 256])
        din("w_br", [L, 4, 128, 10, 256])
        din("w_out", [L, 4, 128, 8, 256])
        self.outT = nc.dram_tensor("outT", [NB, 128, 8, 512], F32, kind="ExternalOutput").ap()
        self.x1T = nc.dram_tensor("x1T", [NB, 128, 8, 512], F32, kind="Internal").ap()

        with contextlib.ExitStack() as es:
            NA = 52000
            arena_t = es.enter_context(nc.sbuf_tensor("arena", [128, NA], F32))
            self.A = A = Arena(arena_t[:, :], NA)
            self.ps = [es.enter_context(nc.psum_tensor("ps%d" % i, [128, 512], F32)) for i in range(8)]
            self.pst = [self.T("ps", i) for i in range(8)]
            for t in self.pst:
                t.excl = True

            self.cst = A.alloc([NCONST], F32)
            self.eps_col = A.alloc([1], F32)
            self.tri = A.alloc([128], BF16)
            self.ntri = A.alloc([128], BF16)
            self.ones = A.alloc([128], BF16)
            self.bd64 = A.alloc([128], BF16)
            self.ones_f = A.alloc([128], F32)
            self.cosT = A.alloc([T], F32)
            self.sinT = A.alloc([T], F32)
            self.colp = A.alloc([NCOL], F32)
            self.hT = A.alloc([8, T], BF16)
            self.yall = A.alloc([10, T], BF16)
            self.kmem = A.alloc([2, 256], BF16)
            self.vmem = A.alloc([2, 256], BF16)
            self.stage_n = 2048
            self.stage = [A.alloc([self.stage_n], F32) for _ in range(2)]
            self.wslot = 0
            self.base_mark = A.mark()

            self.setup_consts()
            if self.stop != "consts":
                for l in range(self.nlayers):
                    if self.run_layer(l):
                        break
            P.barrier()
            P.emit(final_wait_ops=self.finals)
        return nc

    def setup_consts(self):
        A, T_ = self.A, self.T
        tc = T_("cst")
        self.dma(self.cst, self.dr["consts"][:, :], [], [tc])
        self.memset("dve", self.eps_col, EPS, [tc])
        self.cp("dve", self.tri, self.cst[:, K_TRI:K_TRI + 128], [tc], [tc])
        self.ts("dve", self.ntri, self.cst[:, K_TRI:K_TRI + 128], -1.0, ALU.mult, [tc], [tc])
        self.memset("dve", self.ones, 1.0, [tc])
        self.memset("dve", self.ones_f, 1.0, [tc])
        self.memset("dve", self.bd64, 0.0, [tc])
        self.memset("dve", self.bd64[0:64, 0:64], 1.0, [tc])
        self.memset("dve", self.bd64[64:128, 64:128], 1.0, [tc])
        if getattr(self, "skip", None):
            self.memset("pool", self.yall, 0.25, [T_("yall_init")])
        m = A.mark()
        posi = A.alloc([T], I32)
        ang = A.alloc([T], F32)
        kf = A.alloc([T], F32)
        ki = A.alloc([T], I32)
        tr = T_("rope")
        self.dma(posi[0:96, :], self.dr["pos"][0:1, :].partition_broadcast(96), [], [tr])
        R = slice(64, 96)
        self.cp("dve", ang[R, :], posi[R, :], [tr], [tr])
        self.ts("dve", ang[R, :], ang[R, :], self.cst[R, K_INVF:K_INVF + 1], ALU.mult, [tr, tc], [tr])

        def sin_of(dst, shift, post_scale_col):
            self.ts("dve", ki[R, :], ang[R, :], shift, ALU.add, [tr], [tr], s2=1.0 / TWO_PI, op1=ALU.mult)
            self.cp("dve", kf[R, :], ki[R, :], [tr], [tr])
            self.stt("dve", kf[R, :], kf[R, :], -TWO_PI, ang[R, :], ALU.mult, ALU.add, [tr], [tr])
            self.ts("dve", kf[R, :], kf[R, :], shift, ALU.add, [tr], [tr], s2=math.pi, op1=ALU.min)
            self.ts("dve", kf[R, :], kf[R, :], -math.pi, ALU.max, [tr], [tr])
            self.act(dst[R, :], kf[R, :], AF.Sin, [tr], [tr])
            if post_scale_col is not None:
                self.ts("dve", dst[R, :], dst[R, :], post_scale_col, ALU.mult, [tr, tc], [tr])

        sin_of(self.sinT, 0.0, self.cst[R, K_SGN:K_SGN + 1])
        sin_of(self.cosT, math.pi / 2, None)
        self.dump("cosT", self.cosT[R, :], tr)
        self.dump("sinT", self.sinT[R, :], tr)
        self.P.barrier()
        A.reset(m)

    def run_layer(self, l):
        P, A, T_ = self.P, self.A, self.T
        src = self.dr["xT"] if l == 0 else self.x1T
        dst = self.x1T if l == 0 else self.outT
        if self.nlayers == 1:
            dst = self.outT
        tcp = T_("colp")
        self.dma(self.colp, self.dr["colpack"][l, :, :], [], [tcp])
        self.tcp = tcp
        stages = [("N", self.stage_norm), ("MP", self.stage_memprep), ("A", self.stage_a),
                  ("C", self.stage_c), ("M", self.stage_m), ("B", self.stage_b),
                  ("P2", self.stage_p2)]
        for name, fn in stages:
            if name in getattr(self, "skip", ()):
                continue
            m = A.mark()
            if name == "N":
                fn(l, src)
            elif name == "P2":
                fn(l, src, dst)
            else:
                fn(l)
            P.barrier()
            A.reset(m)
            if self.stop == (name, l):
                return True
        return False

    def norm_fm(self, srcT, n_tok, gcol, dst, dst_tok_fn, tag):
        A, T_ = self.A, self.T
        W = min(512, n_tok)
        nblk = n_tok // W
        xb = [A.alloc([8, W], F32) for _ in range(2)]
        sq = A.alloc([8, W], BF16)
        rs = A.alloc([W], F32)
        for b in range(nblk):
            x = xb[b % 2]
            tx = T_(tag + "x", b % 2)
            ts_ = T_(tag + "sq")
            trs = T_(tag + "rs")
            for hh in range(2):
                self.dma(x[:, hh * 4:(hh + 1) * 4, :], srcT[b, :, hh * 4:(hh + 1) * 4, :], [], [tx])
            pb = 0
            for kt in range(8):
                self.act(sq[:, kt, :], x[:, kt, :], AF.Square, [tx], [ts_])
            self.mmg(self.ps[pb][:, 0:W], [(self.ones[:, :], sq[:, kt, :]) for kt in range(8)],
                     [ts_], self.pst[pb])
            self.rstd(rs[:, :], self.ps[pb][:, 0:W], float(D), [self.pst[pb]], [trs])
            for kt in range(8):
                self.stt("dve", dst[:, kt, b * W:(b + 1) * W], x[:, kt, :], gcol[:, kt:kt + 1], rs[:, :],
                         ALU.mult, ALU.mult, [tx, trs, self.tcp], [dst_tok_fn(b)])

    def stage_norm(self, l, src):
        self.norm_fm(src, T, self.colp[:, C_NG:C_NG + 8], self.hT, lambda b: self.T("hT", b), "n")
        for b in range(NB):
            self.dump("hT%d_%d" % (l, b), self.hT[:, :, b * 512:(b + 1) * 512], self.T("hT", b))

    def load_win(self, l, c0, ncols, wt, wtok):
        assert ncols == 256
        self.wload(wt[:, :, 0:ncols], self.dr["w_in"][l, WIN_G[c0]], wtok)

    def zproj_blk(self, wt, ncols, blk, pb, wtok, m_off=0):
        self.mmg(self.ps[pb][m_off:m_off + ncols, :],
                 [(wt[:, kt, 0:ncols], self.hT[:, kt, blk * 512:(blk + 1) * 512]) for kt in range(8)],
                 [wtok, self.T("hT", blk)], self.pst[pb])

    def stage_memprep(self, l):
        A, T_ = self.A, self.T
        hm = A.alloc([8, 256], BF16)
        thm = T_("hm")
        self.norm_fm(self.dr["memT"], 256, self.colp[:, C_MNG:C_MNG + 8], hm, lambda b: thm, "m")
        wkv = A.alloc([8, 512], BF16)
        twk = T_("wkv")
        for half in range(2):
            self.wload(wkv[:, :, half * 256:(half + 1) * 256],
                       self.dr["m_w_kv"][l, half],
                       twk)
        sq = A.alloc([256], BF16)
        rs = A.alloc([256], F32)
        tsq, trs, tkm, tvm = T_("mp_sq"), T_("mp_rs"), T_("kmem"), T_("vmem")
        for ct in range(2):
            self.mmg(self.ps[0][:, 0:256], [(wkv[:, kt, ct * 128:(ct + 1) * 128], hm[:, kt, :]) for kt in range(8)],
                     [twk, thm], self.pst[0])
            self.act(sq[:, :], self.ps[0][:, 0:256], AF.Square, [self.pst[0]], [tsq])
            self.mmg(self.ps[1][:, 0:256], [(self.bd64[:, :], sq[:, :])], [tsq, T_("cst")], self.pst[1])
            self.rstd(rs[:, :], self.ps[1][:, 0:256], 64.0, [self.pst[1]], [trs])
            self.stt("dve", self.kmem[:, ct, :], self.ps[0][:, 0:256], self.colp[:, C_MGK:C_MGK + 1], rs[:, :],
                     ALU.mult, ALU.mult, [self.pst[0], trs, self.tcp], [tkm])
        for mt in range(2):
            self.mmg(self.ps[2][:, 0:256], [(hm[:, kt, mt * 128:(mt + 1) * 128], wkv[:, kt, 256:512]) for kt in range(8)],
                     [twk, thm], self.pst[2])
            self.cp("dve", self.vmem[:, mt, :], self.ps[2][:, 0:256], [self.pst[2]], [tvm])
        self.dump("kmem%d" % l, self.kmem, tkm)
        self.dump("vmem%d" % l, self.vmem, tvm)

    def stage_a(self, l):
        A, T_ = self.A, self.T
        ya = self.yall
        wt = [A.alloc([8, 256], BF16) for _ in range(2)]
        gnb = A.alloc([256], F32)
        wsT = A.alloc([4, 128], BF16)
        tmp = A.alloc([512], BF16)
        tgn, tws, ttmp = T_("a_gn"), T_("a_ws"), T_("a_tmp")
        self.dma(gnb, self.dr["rowpack"][l, :, R_ANG:R_ANG + 256], [], [tgn])
        slot = self.wslot
        self.wslot = (slot + 1) % 2
        st = self.stage[slot][:, 0:512].rearrange("p (g t) -> p g t", g=4)
        stok = T_("stage", slot)
        self.dma(st, self.dr["a_w_sT"][l], [], [stok])
        for g in range(4):
            self.tt("dve", wsT[:, g, :], st[:, g, :], self.cst[:, K_TRI:K_TRI + 128], ALU.mult, [stok, T_("cst")], [tws])
        tw0, tw1 = T_("a_w", 0), T_("a_w", 1)
        self.load_win(l, OFF["a_u"], 256, wt[0], tw0)
        self.load_win(l, OFF["a_g"], 256, wt[1], tw1)
        for ct in range(2):
            for blk in range(NB):
                pb = (ct * NB + blk) % 2
                self.mmg(self.ps[pb][:, :],
                         [(wt[0][:, kt, ct * 128:(ct + 1) * 128], self.hT[:, kt, blk * 512:(blk + 1) * 512])
                          for kt in range(8)], [tw0, T_("hT", blk)], self.pst[pb])
                self.act(ya[:, ct, blk * 512:(blk + 1) * 512], self.ps[pb][:, :], AF.Gelu_apprx_tanh,
                         [self.pst[pb]], [T_("ya", ct, blk)])
        for ct in range(2):
            for blk in range(NB):
                pb = 2 + (ct * NB + blk) % 2
                self.mmg(self.ps[pb][:, :],
                         [(wt[1][:, kt, ct * 128:(ct + 1) * 128], self.hT[:, kt, blk * 512:(blk + 1) * 512])
                          for kt in range(8)], [tw1, T_("hT", blk)], self.pst[pb])
                self.act(tmp[:, :], self.ps[pb][:, :], AF.Silu, [self.pst[pb]], [ttmp])
                sl = ya[:, ct, blk * 512:(blk + 1) * 512]
                self.tt("pool", sl, sl, tmp[:, :], ALU.mult, [ttmp], [T_("ya", ct, blk)])
        twv = T_("a_w", 0)
        self.load_win(l, OFF["a_v"], 256, wt[0], twv)
        gvs = [A.alloc([256], F32) for _ in range(2)]
        ssqs = [A.alloc([1], F32) for _ in range(2)]
        vn = [A.alloc([256], BF16) for _ in range(2)]
        sbs = [A.alloc([2, 128], F32) for _ in range(2)]
        junk = A.alloc([256], BF16)
        absT = self.colp[:, C_ABS:C_ABS + 256].rearrange("p (c t) -> p c t", c=2)
        def a_front(tt_):
            blk = tt_ // 4
            pb = 4 + tt_ % 2
            self.mmg(self.ps[pb][:, 0:256],
                     [(self.hT[:, kt, tt_ * 128:(tt_ + 1) * 128], wt[0][:, kt, :]) for kt in range(8)],
                     [twv, T_("hT", blk)], self.pst[pb])

        def a_back(tt_):
            blk = tt_ // 4
            k = tt_ % 2
            pb = 4 + k
            gv, ssq, sb = gvs[k], ssqs[k], sbs[k]
            tgv, tss, tsb = T_("a_gv", k), T_("a_ss", k), T_("a_sb", k)
            self.act(gv[:, :], self.ps[pb][:, 0:256], AF.Gelu_apprx_tanh, [self.pst[pb]], [tgv])
            self.act(junk[:, :], gv[:, :], AF.Square, [tgv], [tss], accum_out=ssq[:, :])
            self.rstd(ssq[:, :], ssq[:, :], 256.0, [tss], [tss])
            v = vn[k]
            tv = T_("a_vn", k)
            self.stt("dve", v[:, :], gv[:, :], ssq[:, 0:1], gnb[:, :], ALU.mult, ALU.mult, [tgv, tss, tgn], [tv])
            pq = 6 + k
            for g in range(4):
                ct, r0 = g // 2, (g % 2) * 64
                self.mm(self.ps[pq][r0:r0 + 64, ct * 128:(ct + 1) * 128], v[:, g * 64:(g + 1) * 64], wsT[:, g, :],
                        True, True, [tv, tws], [self.pst[pq]])
            psv = self.ps[pq][:, 0:256].rearrange("p (c t) -> p c t", c=2)
            self.tt("dve", sb[:, :, :], psv, absT, ALU.add, [self.pst[pq], self.tcp], [tsb])
            sl = ya[:, 0:2, tt_ * 128:(tt_ + 1) * 128]
            self.tt("pool", sl, sl, sb[:, :, :], ALU.mult, [tsb], [T_("ya", 0, blk), T_("ya", 1, blk)])

        a_front(0)
        for tt_ in range(16):
            if tt_ + 1 < 16:
                a_front(tt_ + 1)
            a_back(tt_)
        self.dump("ya%d" % l, ya[:, 0:2, :], T_("ya", 0, 0))

    def sin_of(self, dst, ang, shift, ki, kf, tok, extra=()):
        rd = [tok] + list(extra)
        self.ts("dve", ki, ang, shift, ALU.add, rd, [tok], s2=1.0 / TWO_PI, op1=ALU.mult)
        self.cp("dve", kf, ki, [tok], [tok])
        self.stt("dve", kf, kf, -TWO_PI, ang, ALU.mult, ALU.add, [tok], [tok])
        self.ts("dve", kf, kf, shift, ALU.add, [tok], [tok], s2=math.pi, op1=ALU.min)
        self.ts("dve", kf, kf, -math.pi, ALU.max, [tok], [tok])
        self.act(dst, kf, AF.Sin, [tok], [tok])

    def stage_m(self, l):
        A, T_ = self.A, self.T
        ya = self.yall
        wq = A.alloc([8, 256], BF16)
        wg = A.alloc([8, 256], BF16)
        twq, twg = T_("m_wq"), T_("m_wg")
        self.load_win(l, OFF["mq"], 256, wq, twq)
        self.load_win(l, OFF["mg"], 256, wg, twg)
        for ct in range(2):
            for blk in range(NB):
                pb = (ct * NB + blk) % 2
                self.mmg(self.ps[pb][:, :],
                         [(wg[:, kt, ct * 128:(ct + 1) * 128], self.hT[:, kt, blk * 512:(blk + 1) * 512])
                          for kt in range(8)], [twg], self.pst[pb])
                self.act(ya[:, 8 + ct, blk * 512:(blk + 1) * 512], self.ps[pb][:, :], AF.Silu,
                         [self.pst[pb]], [T_("ym", ct, blk)])
        sq = A.alloc([512], BF16)
        rs = A.alloc([512], F32)
        qn = [A.alloc([2, 512], BF16) for _ in range(2)]
        pT = [A.alloc([512], BF16) for _ in range(4)]
        rc = A.alloc([512], F32)
        ot = A.alloc([512], BF16)
        tsq, trs, trc, tot = T_("m_sq"), T_("m_rs"), T_("m_rc"), T_("m_ot")
        scale = 64.0 ** -0.5

        def m_prep(blk):
            q = qn[blk % 2]
            tq = T_("m_qn", blk % 2)
            for ct in range(2):
                self.mmg(self.ps[2][:, :],
                         [(wq[:, kt, ct * 128:(ct + 1) * 128], self.hT[:, kt, blk * 512:(blk + 1) * 512])
                          for kt in range(8)], [twq], self.pst[2])
                self.act(sq[:, :], self.ps[2][:, :], AF.Square, [self.pst[2]], [tsq])
                self.mmg(self.ps[3][:, :], [(self.bd64[:, :], sq[:, :])], [tsq], self.pst[3])
                self.rstd(rs[:, :], self.ps[3][:, :], 64.0, [self.pst[3]], [trs])
                self.stt("dve", q[:, ct, :], self.ps[2][:, :], self.colp[:, C_MGQ:C_MGQ + 1], rs[:, :],
                         ALU.mult, ALU.mult, [self.pst[2], trs], [tq])

        def m_s(blk, h):
            q = qn[blk % 2]
            tq = T_("m_qn", blk % 2)
            ct, r0 = h // 2, (h % 2) * 64
            R = slice(r0, r0 + 64)
            for mt in range(2):
                pb = (4 + mt) if h % 2 == 0 else mt
                self.mm(self.ps[pb][:, :], self.kmem[R, ct, mt * 128:(mt + 1) * 128], q[R, ct, :],
                        True, True, [tq], [self.pst[pb]])

        def m_rest(blk, h):
            ct, r0 = h // 2, (h % 2) * 64
            R = slice(r0, r0 + 64)
            for mt in range(2):
                pb = (4 + mt) if h % 2 == 0 else mt
                p = pT[(h % 2) * 2 + mt]
                tp = T_("m_pT", (h % 2) * 2 + mt)
                self.act(p[:, :], self.ps[pb][:, :], AF.Exp, [self.pst[pb]], [tp], scale=scale)
            tps = [T_("m_pT", (h % 2) * 2 + mt) for mt in range(2)]
            ps_o, ps_d = self.ps[6], self.ps[7]
            self.mmg(ps_o[R, :], [(self.vmem[:, mt, h * 64:(h + 1) * 64], pT[(h % 2) * 2 + mt][:, :])
                                  for mt in range(2)], tps, self.pst[6])
            self.mmg(ps_d[R, :], [(self.ones[:, 0:64], pT[(h % 2) * 2 + mt][:, :]) for mt in range(2)],
                     tps, self.pst[7])
            self.P.op("dve", lambda e, o=rc[R, :], i=ps_d[R, :]: e.reciprocal(out=o, in_=i),
                      reads=[self.pst[7]], writes=[trc])
            self.tt("dve", ot[R, :], ps_o[R, :], rc[R, :], ALU.mult, [self.pst[6], trc], [tot])
            sl = ya[R, 8 + ct, blk * 512:(blk + 1) * 512]
            self.tt("pool", sl, sl, ot[R, :], ALU.mult, [tot], [T_("ym", ct, blk)])

        m_prep(0)
        for blk in range(NB):
            m_s(blk, 0)
            if blk + 1 < NB:
                m_prep(blk + 1)
            for h in range(4):
                if h + 1 < 4:
                    m_s(blk, h + 1)
                m_rest(blk, h)
        self.dump("ym%d" % l, ya[:, 8:10, :], T_("ym", 0, 0))

    def stage_c(self, l):
        A, T_ = self.A, self.T
        ya = self.yall
        uT = A.alloc([2, T], BF16)
        ygT = A.alloc([2, T], BF16)
        TAc = A.alloc([1024], F32)
        TAs = A.alloc([1024], F32)
        Dr = A.alloc([8, 128], F32)
        Di = A.alloc([8, 128], F32)
        a128r = A.alloc([8], F32)
        a128i = A.alloc([8], F32)
        Bm = [A.alloc([2, 8, 64], BF16) for _ in range(2)]
        CreT = A.alloc([8, 128], BF16)
        CreTn = A.alloc([8, 128], BF16)
        CimTn = A.alloc([8, 128], BF16)
        wglu = A.alloc([2, 256], BF16)
        m2 = A.mark()
        wu = A.alloc([8, 256], BF16)
        wg = A.alloc([8, 256], BF16)
        twu, twg = T_("c_wu"), T_("c_wg")
        self.load_win(l, OFF["cin"], 256, wu, twu)
        self.load_win(l, OFF["cg"], 256, wg, twg)
        for ct in range(2):
            for blk in range(NB):
                pb = (ct * NB + blk) % 2
                self.mmg(self.ps[pb][:, :],
                         [(wu[:, kt, ct * 128:(ct + 1) * 128], self.hT[:, kt, blk * 512:(blk + 1) * 512])
                          for kt in range(8)], [twu], self.pst[pb])
                self.cp("act", uT[:, ct, blk * 512:(blk + 1) * 512], self.ps[pb][:, :], [self.pst[pb]],
                        [T_("c_uT", blk)])
        for ct in range(2):
            for blk in range(NB):
                pb = 2 + (ct * NB + blk) % 2
                self.mmg(self.ps[pb][:, :],
                         [(wg[:, kt, ct * 128:(ct + 1) * 128], self.hT[:, kt, blk * 512:(blk + 1) * 512])
                          for kt in range(8)], [twg], self.pst[pb])
                self.act(ya[:, 6 + ct, blk * 512:(blk + 1) * 512], self.ps[pb][:, :], AF.Silu,
                         [self.pst[pb]], [T_("yc", ct, blk)])
        tt_ = T_("c_tab")
        rowp = A.alloc([3, 1024], F32)
        self.dma(rowp, self.dr["rowpack"][l, :, R_SRE:R_SRE + 3072].rearrange("p (a b) -> p a b", a=3), [], [tt_])
        s1 = A.alloc([1024], F32)
        s2 = A.alloc([1024], F32)
        si = A.alloc([1024], I32)
        negs = A.alloc([1], F32)
        tcst = T_("cst")
        self.ts("dve", negs, self.cst[:, K_IOS:K_IOS + 1], -1.0, ALU.mult, [tcst], [tt_])
        self.act(rowp[:, 2, :], rowp[:, 2, :], AF.Exp, [tt_], [tt_])
        self.tt("dve", rowp[:, 0, :], rowp[:, 0, :], rowp[:, 2, :], ALU.mult, [tt_], [tt_])
        self.tt("dve", rowp[:, 1, :], rowp[:, 1, :], rowp[:, 2, :], ALU.mult, [tt_], [tt_])
        self.act(s1, rowp[:, 0, :], AF.Exp, [tt_], [tt_], scale=negs[:, 0:1])
        self.ts("dve", s2, rowp[:, 1, :], self.cst[:, K_IOS:K_IOS + 1], ALU.mult, [tt_, tcst], [tt_])
        self.sin_of(TAs, s2, 0.0, si, rowp[:, 2, :], tt_)
        self.sin_of(TAc, s2, math.pi / 2, si, rowp[:, 2, :], tt_)
        self.tt("dve", TAs, TAs, s1, ALU.mult, [tt_], [tt_])
        self.tt("dve", TAc, TAc, s1, ALU.mult, [tt_], [tt_])
        s1v = s1.rearrange("p (j t) -> p j t", j=8)
        s2v = s2.rearrange("p (j t) -> p j t", j=8)
        siv = si.rearrange("p (j t) -> p j t", j=8)
        kfv = rowp[:, 2, :].rearrange("p (j t) -> p j t", j=8)
        dtj = A.alloc([8], F32)
        thrj = A.alloc([8], F32)
        thij = A.alloc([8], F32)
        e128 = A.alloc([8], F32)
        p128 = A.alloc([8], F32)
        k128 = A.alloc([8], F32)
        i128 = A.alloc([8], I32)
        tcp = self.tcp
        self.act(dtj, self.colp[:, C_SDT:C_SDT + 8], AF.Exp, [tcp, tt_], [tt_])
        self.tt("dve", thrj, self.colp[:, C_SRE:C_SRE + 8], dtj, ALU.mult, [tcp, tt_], [tt_])
        self.tt("dve", thij, self.colp[:, C_SIM:C_SIM + 8], dtj, ALU.mult, [tcp, tt_], [tt_])
        iot = self.cst[:, K_IOT:K_IOT + 128]
        for j in range(8):
            self.act(s1v[:, j, :], iot, AF.Exp, [tt_, tcst], [tt_], scale=thrj[:, j:j + 1])
            self.ts("dve", s2v[:, j, :], iot, thij[:, j:j + 1], ALU.mult, [tt_, tcst], [tt_])
        self.sin_of(Di.rearrange("p j t -> p (j t)"), s2, 0.0, si, rowp[:, 2, :], tt_)
        self.sin_of(Dr.rearrange("p j t -> p (j t)"), s2, math.pi / 2, si, rowp[:, 2, :], tt_)
        self.tt("dve", Di, Di, s1v, ALU.mult, [tt_], [tt_])
        self.tt("dve", Dr, Dr, s1v, ALU.mult, [tt_], [tt_])
        self.act(e128, thrj, AF.Exp, [tt_], [tt_], scale=128.0)
        self.ts("dve", p128, thij, 128.0, ALU.mult, [tt_], [tt_])
        self.sin_of(a128i, p128, 0.0, i128, k128, tt_)
        self.sin_of(a128r, p128, math.pi / 2, i128, k128, tt_)
        self.tt("dve", a128i, a128i, e128, ALU.mult, [tt_], [tt_])
        self.tt("dve", a128r, a128r, e128, ALU.mult, [tt_], [tt_])
        w = [A.alloc([64], F32) for _ in range(8)]
        wi = A.alloc([64], I32)
        gmask = self.cst[:, K_GM:K_GM + 8]
        for ct in range(2):
            base = C_ROW + ct * 320
            are = self.colp[:, base:base + 64]
            aim = self.colp[:, base + 64:base + 128]
            ldt = self.colp[:, base + 128:base + 192]
            bre = self.colp[:, base + 192:base + 256]
            bim = self.colp[:, base + 256:base + 320]
            dt_, thr, thi, ea, abr, abi, t0, t1 = w
            rd = [tt_, tcp]
            self.act(dt_, ldt, AF.Exp, rd, [tt_])
            self.tt("dve", thr, are, dt_, ALU.mult, rd, [tt_])
            self.tt("dve", thi, aim, dt_, ALU.mult, rd, [tt_])
            self.act(ea, thr, AF.Exp, [tt_], [tt_])
            self.sin_of(abi, thi, 0.0, wi, t0, tt_)
            self.sin_of(abr, thi, math.pi / 2, wi, t0, tt_)
            self.tt("dve", abi, abi, ea, ALU.mult, [tt_], [tt_])
            self.tt("dve", abr, abr, ea, ALU.mult, [tt_], [tt_])
            self.ts("dve", abr, abr, -1.0, ALU.add, [tt_], [tt_])
            self.tt("dve", dt_, are, are, ALU.mult, rd, [tt_])
            self.tt("dve", t0, aim, aim, ALU.mult, rd, [tt_])
            self.tt("dve", dt_, dt_, t0, ALU.add, [tt_], [tt_])
            self.P.op("dve", lambda e, o=dt_, i=dt_: e.reciprocal(out=o, in_=i), reads=[tt_], writes=[tt_])
            self.tt("dve", t0, abr, are, ALU.mult, rd, [tt_])
            self.tt("dve", t1, abi, aim, ALU.mult, rd, [tt_])
            self.tt("dve", t0, t0, t1, ALU.add, [tt_], [tt_])
            self.tt("dve", thr, t0, dt_, ALU.mult, [tt_], [tt_])
            self.tt("dve", t0, abi, are, ALU.mult, rd, [tt_])
            self.tt("dve", t1, abr, aim, ALU.mult, rd, [tt_])
            self.tt("dve", t0, t0, t1, ALU.subtract, [tt_], [tt_])
            self.tt("dve", thi, t0, dt_, ALU.mult, [tt_], [tt_])
            self.tt("dve", t0, thr, bre, ALU.mult, rd, [tt_])
            self.tt("dve", t1, thi, bim, ALU.mult, rd, [tt_])
            self.tt("dve", ea, t0, t1, ALU.subtract, [tt_], [tt_])
            self.tt("dve", t0, thr, bim, ALU.mult, rd, [tt_])
            self.tt("dve", t1, thi, bre, ALU.mult, rd, [tt_])
            self.tt("dve", abi, t0, t1, ALU.add, [tt_], [tt_])
            for ri, src_ in enumerate((ea, abi)):
                self.tt("dve", Bm[ct][:, ri, :, :], src_.unsqueeze(1).to_broadcast([128, 8, 64]),
                        gmask.unsqueeze(2).to_broadcast([128, 8, 64]), ALU.mult, [tt_, tcst], [tt_])
        tct = T_("c_ct")
        for j0 in (0, 4):
            srcr = self.dr["c_reT"][l, j0 // 4]
            srci = self.dr["c_imT"][l, j0 // 4]
            self.wload(CreT[:, j0:j0 + 4, :], srcr, tct)
            self.wload(CreTn[:, j0:j0 + 4, :], srcr, tct, scale=-1.0)
            self.wload(CimTn[:, j0:j0 + 4, :], srci, tct, scale=-1.0)
        self.wload(wglu, self.dr["c_w_glu"][l], tct)
        self.dump("c_TAc%d" % l, TAc, tt_)
        self.dump("c_TAs%d" % l, TAs, tt_)
        self.dump("c_Dr%d" % l, Dr, tt_)
        self.dump("c_Di%d" % l, Di, tt_)
        self.dump("c_Bm%d" % l, Bm[0], tt_)
        self.dump("c_a128r%d" % l, a128r, tt_)
        self.P.barrier()
        A.reset(m2)
        aS = A.alloc([16], F32)
        self.memset("dve", aS, 0.0, [T_("c_aS", 0), T_("c_aS", 1)])
        X = [A.alloc([4, 512], BF16) for _ in range(2)]
        Pfs = [A.alloc([8, 128], F32) for _ in range(2)]
        Ys = [A.alloc([4, 4, 128], BF16) for _ in range(2)]
        p127 = A.alloc([8], F32)
        c1 = A.alloc([8], F32)
        c2 = A.alloc([8], F32)
        yss = [A.alloc([128], F32) for _ in range(2)]
        tch = T_("c_ch")
        its = [(n, ct) for n in range(16) for ct in range(2)]

        def front(i):
            n, ct = its[i]
            par = i % 2
            blk = n // 4
            cs = slice(n * 128, (n + 1) * 128)
            pre_, pim_ = self.ps[0], self.ps[1]
            tpre, tpim = self.pst[0], self.pst[1]
            Bre = Bm[ct][:, 0, :, :].rearrange("p g q -> p (g q)")
            Bim = Bm[ct][:, 1, :, :].rearrange("p g q -> p (g q)")
            self.mm(pre_[:, :], uT[:, ct, cs], Bre, True, True, [T_("c_uT", blk), tt_], [tpre])
            self.mm(pim_[:, :], uT[:, ct, cs], Bim, True, True, [T_("c_uT", blk), tt_], [tpim])
            x = X[par]
            tx = T_("c_X", par)
            tc_ = TAc[:, ct * 512:(ct + 1) * 512]
            ts_ = TAs[:, ct * 512:(ct + 1) * 512]
            self.tt("dve", x[:, 0, :], pre_[:, :], tc_, ALU.mult, [tpre, tt_], [tx])
            self.tt("dve", x[:, 3, :], pre_[:, :], ts_, ALU.mult, [tpre, tt_], [tx])
            self.tt("dve", x[:, 1, :], pim_[:, :], ts_, ALU.mult, [tpim, tt_], [tx])
            self.tt("dve", x[:, 2, :], pim_[:, :], tc_, ALU.mult, [tpim, tt_], [tx])
            Pre, Pim = self.ps[2 + 2 * par], self.ps[3 + 2 * par]
            for jl in range(4):
                js = slice(jl * 128, (jl + 1) * 128)
                self.mmg(Pre[:, js], [(x[:, 0, js], self.tri[:, :]), (x[:, 1, js], self.tri[:, :])],
                         [tx], self.pst[2 + 2 * par])
                self.mmg(Pim[:, js], [(x[:, 2, js], self.tri[:, :]), (x[:, 3, js], self.ntri[:, :])],
                         [tx], self.pst[3 + 2 * par])

        def back_a(i):
            n, ct = its[i]
            par = i % 2
            Pre, Pim = self.ps[2 + 2 * par], self.ps[3 + 2 * par]
            tP0, tP1 = self.pst[2 + 2 * par], self.pst[3 + 2 * par]
            Pf, Y = Pfs[par], Ys[par]
            tPf, tY = T_("c_Pf", par), T_("c_Y", par)
            ta = T_("c_aS", ct)
            jr = slice(4 * ct, 4 * ct + 4)
            ji = slice(8 + 4 * ct, 8 + 4 * ct + 4)
            self.tt("dve", Pf[:, 0:4, :], Pre[:, :].rearrange("p (j t) -> p j t", j=4),
                    aS[:, jr].unsqueeze(2).to_broadcast([128, 4, 128]), ALU.add, [tP0, ta], [tPf])
            self.tt("dve", Pf[:, 4:8, :], Pim[:, :].rearrange("p (j t) -> p j t", j=4),
                    aS[:, ji].unsqueeze(2).to_broadcast([128, 4, 128]), ALU.add, [tP1, ta], [tPf])
            self.cp("dve", p127, Pf[:, :, 127], [tPf], [tch])
            self.tt("dve", c1[:, 0:4], a128r[:, jr], p127[:, 0:4], ALU.mult, [tch, tt_], [tch])
            self.tt("dve", c1[:, 4:8], a128i[:, jr], p127[:, 4:8], ALU.mult, [tch, tt_], [tch])
            self.tt("dve", c2[:, 0:4], a128r[:, jr], p127[:, 4:8], ALU.mult, [tch, tt_], [tch])
            self.tt("dve", c2[:, 4:8], a128i[:, jr], p127[:, 0:4], ALU.mult, [tch, tt_], [tch])
            self.tt("dve", aS[:, jr], c1[:, 0:4], c1[:, 4:8], ALU.subtract, [tch], [ta])
            self.tt("dve", aS[:, ji], c2[:, 0:4], c2[:, 4:8], ALU.add, [tch], [ta])
            self.tt("pool", Y[:, 0, :, :], Pf[:, 0:4, :], Dr[:, jr, :], ALU.mult, [tPf, tt_], [tY])
            self.tt("pool", Y[:, 1, :, :], Pf[:, 4:8, :], Di[:, jr, :], ALU.mult, [tPf, tt_], [tY])
            self.tt("pool", Y[:, 2, :, :], Pf[:, 4:8, :], Dr[:, jr, :], ALU.mult, [tPf, tt_], [tY])
            self.tt("pool", Y[:, 3, :, :], Pf[:, 0:4, :], Di[:, jr, :], ALU.mult, [tPf, tt_], [tY])
            py = self.ps[6 + par]
            pairs = []
            for jl in range(4):
                j = 4 * ct + jl
                pairs += [(CreT[:, j, :], Y[:, 0, jl, :]), (CreTn[:, j, :], Y[:, 1, jl, :]),
                          (CimTn[:, j, :], Y[:, 2, jl, :]), (CimTn[:, j, :], Y[:, 3, jl, :])]
            self.mmg(py[:, 0:128], pairs, [tY, tct], self.pst[6 + par])

        def back_b(i):
            n, ct = its[i]
            par = i % 2
            blk = n // 4
            cs = slice(n * 128, (n + 1) * 128)
            py = self.ps[6 + par]
            ys, tys = yss[par], T_("c_ys", par)
            self.stt("dve", ys, uT[:, ct, cs], self.colp[:, C_CD + ct:C_CD + ct + 1], py[:, 0:128],
                     ALU.mult, ALU.add, [self.pst[6 + par], T_("c_uT", blk), tcp], [tys])
            self.act(ygT[:, ct, cs], ys, AF.Gelu_apprx_tanh, [tys], [T_("c_yg", blk)])

        front(0)
        for i in range(len(its)):
            if i + 1 < len(its):
                front(i + 1)
            back_a(i)
            if i >= 1:
                back_b(i - 1)
        back_b(len(its) - 1)
        self.dump("c_yg%d" % l, ygT, T_("c_yg", 0))
        sg = A.alloc([512], BF16)
        tsg = T_("c_sg")
        for blk in range(NB):
            bs = slice(blk * 512, (blk + 1) * 512)
            for cc in range(2):
                pb = 7
                self.mmg(self.ps[pb][:, :], [(wglu[:, ct, cc * 128:(cc + 1) * 128], ygT[:, ct, bs]) for ct in range(2)],
                         [tct, T_("c_yg", blk)], self.pst[pb])
                self.act(sg, self.ps[pb][:, :], AF.Sigmoid, [self.pst[pb], tcp], [tsg],
                         bias=self.colp[:, C_BGLU + cc:C_BGLU + cc + 1])
                sl = ya[:, 6 + cc, bs]
                self.tt("pool", sg, sg, ygT[:, cc, bs], ALU.mult, [tsg, T_("c_yg", blk)], [tsg])
                self.tt("pool", sl, sl, sg, ALU.mult, [tsg], [T_("yc", cc, blk)])
        self.dump("yc%d" % l, ya[:, 6:8, :], T_("yc", 0, 0))

    def stage_b(self, l):
        A, T_, P = self.A, self.T, self.P
        ya = self.yall
        tcp = self.tcp
        tcst = T_("cst")
        cqn = A.alloc([6, T], BF16)
        ckvn = A.alloc([2, T], BF16)
        krr = A.alloc([T], F32)
        sqkr = A.alloc([T], BF16)
        m2 = A.mark()
        wt = A.alloc([8, 512], BF16)
        wq_in = A.alloc([8, 768], BF16)
        wkv_in = A.alloc([8, 256], BF16)
        wkr = [A.alloc([8, 96], BF16) for _ in range(2)]
        sq = A.alloc([512], BF16)
        rs = A.alloc([512], F32)
        t1 = A.alloc([512], F32)
        t2 = A.alloc([512], F32)
        tw = T_("b_w")
        for i in range(2):
            self.load_win(l, OFF["bg"] + i * 256, 256, wt[:, :, i * 256:(i + 1) * 256], tw)
        for i in range(3):
            self.load_win(l, OFF["cq"] + i * 256, 256, wq_in[:, :, i * 256:(i + 1) * 256], tw)
        self.load_win(l, OFF["ckv"], 256, wkv_in, tw)
        self.wload(wkr[0], self.dr["w_krm"][l], tw)
        self.wload(wkr[1], self.dr["w_krp"][l], tw)
        for ct in range(4):
            for blk in range(NB):
                pb = (ct * NB + blk) % 2
                self.zproj_blk(wt[:, :, ct * 128:(ct + 1) * 128], 128, blk, pb, tw)
                self.act(ya[:, 2 + ct, blk * 512:(blk + 1) * 512], self.ps[pb][:, :], AF.Silu,
                         [self.pst[pb]], [T_("yb", ct, blk)])
        tsq, trs = T_("b_sq"), T_("b_rs")
        R = slice(64, 96)
        for blk in range(NB):
            bs = slice(blk * 512, (blk + 1) * 512)
            for i in range(6):
                self.zproj_blk(wq_in[:, :, i * 128:(i + 1) * 128], 128, blk, i, tw)
                self.act(sq, self.ps[i][:, :], AF.Square, [self.pst[i]], [tsq])
                self.mm(self.ps[6][:, :], self.ones[:, :], sq, i == 0, i == 5, [tsq], [self.pst[6]])
            self.rstd(rs, self.ps[6][:, :], 768.0, [self.pst[6]], [trs])
            for i in range(6):
                self.stt("dve", cqn[:, i, bs], self.ps[i][:, :], self.colp[:, C_QNG + i:C_QNG + i + 1], rs,
                         ALU.mult, ALU.mult, [self.pst[i], trs, tcp], [T_("b_cqn", blk)])
            for i in range(2):
                self.zproj_blk(wkv_in[:, :, i * 128:(i + 1) * 128], 128, blk, i, tw)
                self.act(sq, self.ps[i][:, :], AF.Square, [self.pst[i]], [tsq])
                self.mm(self.ps[7][:, :], self.ones[:, :], sq, i == 0, i == 1, [tsq], [self.pst[7]])
            self.rstd(rs, self.ps[7][:, :], 256.0, [self.pst[7]], [trs])
            for i in range(2):
                self.stt("dve", ckvn[:, i, bs], self.ps[i][:, :], self.colp[:, C_KVNG + i:C_KVNG + i + 1], rs,
                         ALU.mult, ALU.mult, [self.pst[i], trs, tcp], [T_("b_ckvn", blk)])
            for i in range(2):
                self.zproj_blk(wkr[i], 96, blk, 2 + i, tw)
            self.act(sqkr[R, bs], self.ps[2][R, :], AF.Square, [self.pst[2]], [T_("b_kr", blk)])
            tt1 = T_("b_t1")
            self.stt("dve", t1[R, :], self.ps[2][R, :], self.colp[R, C_GK:C_GK + 1], self.cosT[R, bs],
                     ALU.mult, ALU.mult, [self.pst[2], tcp], [tt1])
            self.stt("dve", t2[R, :], self.ps[3][R, :], self.colp[R, C_GKP:C_GKP + 1], self.sinT[R, bs],
                     ALU.mult, ALU.mult, [self.pst[3], tcp], [tt1])
            self.tt("pool", krr[R, bs], t1[R, :], t2[R, :], ALU.add, [tt1], [T_("b_kr", blk)])
        self.dump("b_cqn%d" % l, cqn, T_("b_cqn", 0))
        self.dump("b_ckvn%d" % l, ckvn, T_("b_ckvn", 0))
        self.dump("b_krr%d" % l, krr[R, :], T_("b_kr", 0))
        P.barrier()
        A.reset(m2)
        wqm = [A.alloc([6, 96], BF16) for _ in range(2)]
        wqp = [A.alloc([6, 96], BF16) for _ in range(2)]
        wkv = [A.alloc([2, 128], BF16) for _ in range(2)]
        wk = [w_[:, :, 0:64] for w_ in wkv]
        wv = [w_[:, :, 64:128] for w_ in wkv]
        qn = [A.alloc([T], BF16) for _ in range(2)]
        kn = [A.alloc([T], BF16) for _ in range(2)]
        vh = [A.alloc([16, 64], BF16) for _ in range(2)]
        sq = [A.alloc([512], BF16) for _ in range(2)]
        rs = [A.alloc([512], F32) for _ in range(2)]
        t1 = A.alloc([512], F32)
        t2 = A.alloc([512], F32)
        pT = [A.alloc([512], BF16) for _ in range(4)]
        rcs = [A.alloc([512], F32) for _ in range(2)]
        ots = [A.alloc([512], BF16) for _ in range(2)]
        scale = 96.0 ** -0.5
        def b_load(h_):
            s_ = h_ % 2
            twh_ = T_("b_wh", s_)
            self.wload(wqm[s_], self.dr["wq_m"][l, h_], twh_)
            self.wload(wqp[s_], self.dr["wq_p"][l, h_], twh_)
            self.wload(wkv[s_], self.dr["w_ukv"][l, h_], twh_)

        b_load(0)
        for h in range(8):
            s = h % 2
            twh = T_("b_wh", s)
            tq, tk, tv = T_("b_qn", s), T_("b_kn", s), T_("b_vh", s)
            Q = slice(0, 96)
            N_ = slice(0, 64)
            for half in range(2):
                pbv = 6 + half
                for t8 in range(8):
                    tt_ = half * 8 + t8
                    self.mmg(self.ps[pbv][:, t8 * 64:(t8 + 1) * 64],
                             [(ckvn[:, kt, tt_ * 128:(tt_ + 1) * 128], wv[s][:, kt, :]) for kt in range(2)],
                             [twh], self.pst[pbv])
                self.cp("dve", vh[s][:, half * 8:(half + 1) * 8, :],
                        self.ps[pbv][:, :].rearrange("p (a b) -> p a b", a=8), [self.pst[pbv]], [tv])

            def banks(blk):
                o = 0 if blk % 2 == 0 else 4
                return o, o + 1, o + 2, o + 3

            def prep_front(blk):
                bs = slice(blk * 512, (blk + 1) * 512)
                bq, bp, _, bk = banks(blk)
                self.mmg(self.ps[bq][Q, :], [(wqm[s][:, kt, :], cqn[:, kt, bs]) for kt in range(6)], [twh], self.pst[bq])
                self.mmg(self.ps[bp][Q, :], [(wqp[s][:, kt, :], cqn[:, kt, bs]) for kt in range(6)], [twh], self.pst[bp])
                self.mmg(self.ps[bk][N_, :], [(wk[s][:, kt, :], ckvn[:, kt, bs]) for kt in range(2)], [twh], self.pst[bk])

            def prep_back(blk):
                bs = slice(blk * 512, (blk + 1) * 512)
                bq, bp, bsq, bk = banks(blk)
                tsq0, trs0 = T_("b_sq", 0), T_("b_rs", 0)
                tsq1, trs1 = T_("b_sq", 1), T_("b_rs", 1)
                self.act(sq[0][Q, :], self.ps[bq][Q, :], AF.Square, [self.pst[bq]], [tsq0])
                self.act(sq[1][N_, :], self.ps[bk][N_, :], AF.Square, [self.pst[bk]], [tsq1])
                self.cp("pool", sq[1][R, :], sqkr[R, bs], [], [tsq1])
                self.mmg(self.ps[bsq][Q, :], [(self.ones[Q, 0:96], sq[0][Q, :])], [tsq0], self.pst[bsq])
                self.rstd(rs[0][Q, :], self.ps[bsq][Q, :], 96.0, [self.pst[bsq]], [trs0])
                self.mmg(self.ps[bsq][Q, :], [(self.ones[Q, 0:96], sq[1][Q, :])], [tsq1], self.pst[bsq])
                self.rstd(rs[1][Q, :], self.ps[bsq][Q, :], 96.0, [self.pst[bsq]], [trs1])
                tt1 = T_("b_t1")
                self.stt("dve", t1[R, :], self.ps[bq][R, :], self.colp[R, C_GQ:C_GQ + 1], self.cosT[R, bs],
                         ALU.mult, ALU.mult, [self.pst[bq], tcp], [tt1])
                self.stt("dve", t2[R, :], self.ps[bp][R, :], self.colp[R, C_GQP:C_GQP + 1], self.sinT[R, bs],
                         ALU.mult, ALU.mult, [self.pst[bp], tcp], [tt1])
                self.stt("dve", qn[s][N_, bs], self.ps[bq][N_, :], self.colp[N_, C_GQ:C_GQ + 1], rs[0][N_, :],
                         ALU.mult, ALU.mult, [self.pst[bq], trs0, tcp], [tq])
                self.stt("dve", kn[s][N_, bs], self.ps[bk][N_, :], self.colp[N_, C_GK:C_GK + 1], rs[1][N_, :],
                         ALU.mult, ALU.mult, [self.pst[bk], trs1, tcp], [tk])
                self.tt("pool", t1[R, :], t1[R, :], t2[R, :], ALU.add, [tt1], [tt1])
                self.tt("pool", qn[s][R, bs], t1[R, :], rs[0][R, :], ALU.mult, [tt1, trs0], [tq])
                self.tt("pool", kn[s][R, bs], krr[R, bs], rs[1][R, :], ALU.mult, [trs1], [tk])

            prep_front(0)
            for blk in range(NB):
                if blk + 1 < NB:
                    prep_front(blk + 1)
                prep_back(blk)
            if h == 0:
                self.dump("b_qn%d" % l, qn[0][0:96, :], tq)
                self.dump("b_kn%d" % l, kn[0][0:96, :], tk)
                self.dump("b_vh%d" % l, vh[0], tv)
            if h + 1 < 8:
                b_load(h + 1)
            ct, r0 = h // 2, (h % 2) * 64
            RR = slice(r0, r0 + 64)
            seq = [(b, j) for b in range(NB) for j in range(4 * b + 4)]

            def att_s(i):
                b, j = seq[i]
                jj = j - 4 * b
                c0 = 128 * jj if jj > 0 else 0
                pb = 4 + i % 2
                self.mm(self.ps[pb][:, c0:512], kn[s][0:96, j * 128:(j + 1) * 128],
                        qn[s][0:96, b * 512 + c0:(b + 1) * 512], True, True, [tq, tk], [self.pst[pb]])

            def att_rest(i):
                b, j = seq[i]
                nj = 4 * b + 4
                jj = j - 4 * b
                c0 = 128 * jj if jj > 0 else 0
                pb = 4 + i % 2
                p = pT[i % 4]
                tp = T_("b_pT", i % 4)
                po, pd = (6, 7) if b % 2 == 0 else (2, 3)
                self.act(p[:, c0:512], self.ps[pb][:, c0:512], AF.Exp, [self.pst[pb]], [tp], scale=scale)
                if jj >= 0:
                    self.tt("pool", p[:, c0:c0 + 128], p[:, c0:c0 + 128], self.tri[:, :], ALU.mult,
                            [tp, tcst], [tp])
                self.mm(self.ps[po][RR, c0:512], vh[s][:, j, :], p[:, c0:512], j == 0, j == nj - 1,
                        [tv, tp], [self.pst[po]])
                self.mm(self.ps[pd][RR, c0:512], self.ones[:, 0:64], p[:, c0:512], j == 0, j == nj - 1,
                        [tp], [self.pst[pd]])
                if j == nj - 1:
                    trc, tot = T_("b_rc", b % 2), T_("b_ot", b % 2)
                    rc_, ot_ = rcs[b % 2], ots[b % 2]
                    self.P.op("dve", lambda e, o=rc_[RR, :], i_=self.ps[pd][RR, :]: e.reciprocal(out=o, in_=i_),
                              reads=[self.pst[pd]], writes=[trc])
                    self.tt("dve", ot_[RR, :], self.ps[po][RR, :], rc_[RR, :], ALU.mult, [self.pst[po], trc], [tot])
                    sl = ya[RR, 2 + ct, b * 512:(b + 1) * 512]
                    self.tt("pool", sl, sl, ot_[RR, :], ALU.mult, [tot], [T_("yb", ct, b)])

            att_s(0)
            for i in range(len(seq)):
                if i + 1 < len(seq):
                    att_s(i + 1)
                att_rest(i)
        self.dump("yb%d" % l, ya[:, 2:6, :], T_("yb", 0, 0))

    def stage_p2(self, l, src, dst):
        A, T_, P = self.A, self.T, self.P
        ya = self.yall
        merged = A.alloc([8, T], BF16)
        m2 = A.mark()
        wls = [A.alloc([8, 4, 256], BF16) for _ in range(2)]
        wbs = [A.alloc([10, 256], BF16) for _ in range(2)]
        g = [A.alloc([512], F32) for _ in range(2)]
        macc = A.alloc([512], F32)
        t2 = [A.alloc([512], F32) for _ in range(2)]
        brk = {0: [0, 1], 1: [2, 3, 4, 5], 2: [6, 7], 3: [8, 9]}
        brw = ["w_br_a", "w_br_b", "w_br_c", "w_br_m"]
        tcp = self.tcp
        def p2_load(j2):
            wl_, wb_ = wls[j2 % 2], wbs[j2 % 2]
            twl_, twb_ = T_("p_wl", j2 % 2), T_("p_wb", j2 % 2)
            for br in range(4):
                c0 = OFF["mrg"] + br * 1024 + j2 * 256
                self.wload(wl_[:, :, br, :], self.dr["w_in"][l, WIN_G[c0]], twl_)
            self.wload(wb_[:, 0:6, :], self.dr["w_br"][l, j2, :, 0:6, :], twb_)
            self.wload(wb_[:, 6:10, :], self.dr["w_br"][l, j2, :, 6:10, :], twb_)

        p2_load(0)
        for j2 in range(4):
            if j2 + 1 < 4:
                p2_load(j2 + 1)
            wl, wb = wls[j2 % 2], wbs[j2 % 2]
            twl, twb = T_("p_wl", j2 % 2), T_("p_wb", j2 % 2)
            for jj in range(2):
                j = 2 * j2 + jj
                js = slice(jj * 128, (jj + 1) * 128)
                for blk in range(NB):
                    bs = slice(blk * 512, (blk + 1) * 512)
                    for br in range(4):
                        pl, pp = self.ps[2 * (br % 2)], self.ps[2 * (br % 2) + 1]
                        tpl, tpp = self.pst[2 * (br % 2)], self.pst[2 * (br % 2) + 1]
                        self.mmg(pl[:, :], [(wl[:, kt, br, js], self.hT[:, kt, bs]) for kt in range(8)], [twl], tpl)
                        self.mmg(pp[:, :], [(wb[:, kt, js], ya[:, kt, bs]) for kt in brk[br]], [twb], tpp)
                        gg, tg = g[br % 2], T_("p_g", br % 2)
                        self.act(gg, pl[:, :], AF.Sigmoid, [tpl, tcp], [tg],
                                 bias=self.colp[:, C_BM + br * 8 + j:C_BM + br * 8 + j + 1])
                        tm = T_("p_m")
                        if br == 0:
                            self.tt("dve", macc, pp[:, :], gg, ALU.mult, [tpp, tg], [tm])
                        else:
                            tt2 = T_("p_t2", br % 2)
                            self.tt("dve", t2[br % 2], pp[:, :], gg, ALU.mult, [tpp, tg], [tt2])
                            if br < 3:
                                self.tt("dve", macc, macc, t2[br % 2], ALU.add, [tt2, tm], [tm])
                            else:
                                self.tt("dve", merged[:, j, bs], macc, t2[br % 2], ALU.add, [tt2, tm],
                                        [T_("p_mg", blk)])
        self.dump("merged%d" % l, merged, T_("p_mg", 0))
        P.barrier()
        A.reset(m2)
        wo = A.alloc([8, D], BF16)
        two = T_("p_wo")
        for i in range(4):
            self.wload(wo[:, :, i * 256:(i + 1) * 256],
                       self.dr["w_out"][l, i], two)
        xt = A.alloc([8, 512], F32)
        xo = A.alloc([8, 512], F32)
        txt, txo = T_("p_xt"), T_("p_xo")
        for blk in range(NB):
            bs = slice(blk * 512, (blk + 1) * 512)
            for hh in range(2):
                self.dma(xt[:, hh * 4:(hh + 1) * 4, :], src[blk, :, hh * 4:(hh + 1) * 4, :], [], [txt])
            for d2 in range(8):
                pb = 4 + d2 % 2
                self.mmg(self.ps[pb][:, :], [(wo[:, kt, d2 * 128:(d2 + 1) * 128], merged[:, kt, bs]) for kt in range(8)],
                         [two, T_("p_mg", blk)], self.pst[pb])
                self.tt("dve", xo[:, d2, :], self.ps[pb][:, :], xt[:, d2, :], ALU.add, [self.pst[pb], txt], [txo])
            for hh in range(2):
                op = self.dma(dst[blk, :, hh * 4:(hh + 1) * 4, :], xo[:, hh * 4:(hh + 1) * 4, :], [txo], [], q="act")
                if dst is self.outT:
                    self.finals.append(op)


def make_consts():
    c = np.zeros((128, NCONST), np.float32)
    s = np.arange(128)
    c[:, K_TRI:K_TRI + 128] = (s[:, None] <= s[None, :]).astype(np.float32)
    c[:, K_IOT:K_IOT + 128] = s[None, :].astype(np.float32)
    c[:, K_IOS] = s.astype(np.float32)
    half = 16
    inv = (10000.0 ** (-np.arange(half, dtype=np.float32) / half)).astype(np.float32)
    for r in range(64, 96):
        c[r, K_INVF] = inv[(r - 64) % 16]
        c[r, K_SGN] = -1.0 if (r - 64) < 16 else 1.0
    for r in range(128):
        c[r, K_GM + r // 16] = 1.0
    return c


def host_prep(inp):
    f = lambda k: np.asarray(inp[k], dtype=np.float32)
    perm = np.array(PERM)
    sh = {}

    def rows_t(w):
        r, c = w.shape
        return np.ascontiguousarray(w.reshape(r // 128, 128, c).transpose(1, 0, 2))

    w_in = f("w_in")
    sh["w_in"] = np.stack([np.stack([rows_t(w_in[l][:, c0:c0 + 256]) for c0 in WIN_C0]) for l in range(L)])
    krm = np.zeros((L, D, 96), np.float32)
    krp = np.zeros((L, D, 96), np.float32)
    krm[:, :, 64:96] = w_in[:, :, OFF["kr"]:OFF["kr"] + 32]
    krp[:, :, 64:96] = w_in[:, :, OFF["kr"] + perm]
    sh["w_krm"] = np.stack([rows_t(krm[l]) for l in range(L)])
    sh["w_krp"] = np.stack([rows_t(krp[l]) for l in range(L)])
    wuq = f("b_w_uq").reshape(L, 768, 8, 96)
    wqp = np.zeros((L, 768, 8, 96), np.float32)
    wqp[:, :, :, 64:96] = wuq[:, :, :, 64 + perm]
    sh["wq_m"] = np.stack([np.stack([rows_t(wuq[l][:, h, :]) for h in range(8)]) for l in range(L)])
    sh["wq_p"] = np.stack([np.stack([rows_t(wqp[l][:, h, :]) for h in range(8)]) for l in range(L)])
    ukv = f("b_w_ukv")
    sh["w_ukv"] = np.stack([np.stack([rows_t(ukv[l][:, h * 128:(h + 1) * 128]) for h in range(8)]) for l in range(L)])
    sh["a_w_sT"] = np.ascontiguousarray(f("a_w_s").transpose(0, 3, 1, 2))
    c_re, c_im = f("c_c_re"), f("c_c_im")
    cre = np.zeros((L, 8, 128, 128), np.float32)
    cim = np.zeros((L, 8, 128, 128), np.float32)
    for j in range(8):
        for gl in range(2):
            g = 2 * j + gl
            col0 = 16 * (g % 8)
            cre[:, j, gl * 64:(gl + 1) * 64, col0:col0 + 16] = c_re[:, g].transpose(0, 2, 1)
            cim[:, j, gl * 64:(gl + 1) * 64, col0:col0 + 16] = c_im[:, g].transpose(0, 2, 1)
    sh["c_reT"] = np.ascontiguousarray(cre.reshape(L, 2, 4, 128, 128).transpose(0, 1, 3, 2, 4))
    sh["c_imT"] = np.ascontiguousarray(cim.reshape(L, 2, 4, 128, 128).transpose(0, 1, 3, 2, 4))
    sh["c_w_glu"] = np.stack([rows_t(f("c_w_glu")[l]) for l in range(L)])
    mkv = f("m_w_kv")
    sh["m_w_kv"] = np.stack([np.stack([rows_t(mkv[l][:, i * 256:(i + 1) * 256]) for i in range(2)]) for l in range(L)])
    wbr = np.concatenate([f("w_br_a"), f("w_br_b"), f("w_br_c"), f("w_br_m")], axis=1)
    sh["w_br"] = np.stack([np.stack([rows_t(wbr[l][:, i * 256:(i + 1) * 256]) for i in range(4)]) for l in range(L)])
    wo = f("w_out")
    sh["w_out"] = np.stack([np.stack([rows_t(wo[l][:, i * 256:(i + 1) * 256]) for i in range(4)]) for l in range(L)])
    cp = np.zeros((L, 128, NCOL), np.float32)
    cp[:, :, C_NG:C_NG + 8] = f("norm_g").reshape(L, 8, 128).transpose(0, 2, 1)
    cp[:, :, C_MNG:C_MNG + 8] = f("m_norm_g").reshape(L, 8, 128).transpose(0, 2, 1)
    cp[:, :, C_QNG:C_QNG + 6] = f("b_q_norm_g").reshape(L, 6, 128).transpose(0, 2, 1)
    cp[:, :, C_KVNG:C_KVNG + 2] = f("b_kv_norm_g").reshape(L, 2, 128).transpose(0, 2, 1)
    cp[:, :, C_BM:C_BM + 32] = f("b_merge").reshape(L, 32, 128).transpose(0, 2, 1)
    gq, gk = f("b_qk_g_q"), f("b_qk_g_k")
    cp[:, 0:96, C_GQ] = gq
    cp[:, 64:96, C_GQP] = gq[:, 64 + perm]
    cp[:, 0:96, C_GK] = gk
    cp[:, 64:96, C_GKP] = gk[:, 64 + perm]
    cp[:, :, C_MGQ] = np.tile(f("m_qk_g_q"), (1, 2))
    cp[:, :, C_MGK] = np.tile(f("m_qk_g_k"), (1, 2))
    cp[:, :, C_CD:C_CD + 2] = f("c_d").reshape(L, 2, 128).transpose(0, 2, 1)
    cp[:, :, C_BGLU:C_BGLU + 2] = f("c_b_glu").reshape(L, 2, 128).transpose(0, 2, 1)
    a_re, a_im, ldt = f("c_a_re"), f("c_a_im"), f("c_log_dt")
    cp[:, :, C_SRE:C_SRE + 8] = a_re.reshape(L, 8, 128).transpose(0, 2, 1)
    cp[:, :, C_SIM:C_SIM + 8] = a_im.reshape(L, 8, 128).transpose(0, 2, 1)
    ldt_rep = np.repeat(ldt[:, :, None], 64, axis=2)
    cp[:, :, C_SDT:C_SDT + 8] = ldt_rep.reshape(L, 8, 128).transpose(0, 2, 1)
    abs_ = f("a_b_s")
    for ct in range(2):
        for gl in range(2):
            cp[:, gl * 64:(gl + 1) * 64, C_ABS + ct * 128:C_ABS + (ct + 1) * 128] = abs_[:, 2 * ct + gl][:, None, :]
    b_re, b_im = f("c_b_re"), f("c_b_im")
    for ct in range(2):
        base = C_ROW + ct * 320
        for g8 in range(8):
            g = 8 * ct + g8
            rows = slice(16 * g8, 16 * g8 + 16)
            cp[:, rows, base + 0:base + 64] = a_re[:, g][:, None, :]
            cp[:, rows, base + 64:base + 128] = a_im[:, g][:, None, :]
            cp[:, rows, base + 128:base + 192] = ldt[:, g][:, None, None]
            cp[:, rows, base + 192:base + 256] = b_re[:, g].transpose(0, 2, 1)
            cp[:, rows, base + 256:base + 320] = b_im[:, g].transpose(0, 2, 1)
    sh["colpack"] = cp
    rp = np.zeros((L, 128, NROW), np.float32)
    rp[:, :, R_ANG:R_ANG + 256] = f("a_norm_g")[:, None, :]
    rp[:, :, R_SRE:R_SRE + 1024] = a_re.reshape(L, 1, 1024)
    rp[:, :, R_SIM:R_SIM + 1024] = a_im.reshape(L, 1, 1024)
    rp[:, :, R_SDT:R_SDT + 1024] = ldt_rep.reshape(L, 1, 1024)
    sh["rowpack"] = rp
    sh["consts"] = make_consts()
    x = f("x")
    mem = f("mem")
    pos = np.asarray(inp["positions"]).astype(np.int32)
    per_core = []
    for b in range(8):
        d = dict(sh)
        d["xT"] = tile_x(x[b])
        d["memT"] = np.ascontiguousarray(mem[b].T.reshape(8, 128, 1, 256).transpose(2, 1, 0, 3))
        d["pos"] = np.ascontiguousarray(pos[b][None, :])
        per_core.append(d)
    return per_core


def tile_x(xb):
    return np.ascontiguousarray(xb.T.reshape(8, 128, NB, 512).transpose(2, 1, 0, 3))


def untile_x(t):
    return np.ascontiguousarray(t.transpose(2, 1, 0, 3).reshape(D, T).T)


_CACHE = {}
LAYER_KEYS = ("colpack", "rowpack", "w_in", "w_krm", "w_krp", "wq_m", "wq_p", "w_ukv", "a_w_sT", "c_reT", "c_imT",
              "c_w_glu", "m_w_kv", "w_br", "w_out")
FUSED = True


def kernel(**inputs):
    in_maps = host_prep(inputs)
    if FUSED:
        if "nc" not in _CACHE:
            _CACHE["nc"] = Builder(nlayers=L).build()
        res = run_bass_kernel_spmd(_CACHE["nc"], in_maps, core_ids=list(range(8)))
        return np.stack([untile_x(r["outT"]) for r in res.results], axis=0).astype(np.float32)
    if "nc1" not in _CACHE:
        _CACHE["nc1"] = Builder(nlayers=1).build()
    nc = _CACHE["nc1"]
    xs = [m["xT"] for m in in_maps]
    for l in range(L):
        maps = []
        for c in range(8):
            d = dict(in_maps[c])
            for k in LAYER_KEYS:
                a = in_maps[c][k]
                d[k] = np.ascontiguousarray(np.concatenate([a[l:l + 1], a[l:l + 1]], axis=0))
            d["xT"] = xs[c]
            maps.append(d)
        res = run_bass_kernel_spmd(nc, maps, core_ids=list(range(8)))
        xs = [np.ascontiguousarray(r["outT"]) for r in res.results]
    return np.stack([untile_x(t) for t in xs], axis=0).astype(np.float32)
```

```python
import math
import contextlib
import numpy as np
import concourse.bass as bass
import concourse.mybir as mybir
from concourse.bass_utils import run_bass_kernel_spmd

F32 = mybir.dt.float32
BF16 = mybir.dt.bfloat16
I32 = mybir.dt.int32
AF = mybir.ActivationFunctionType
ALU = mybir.AluOpType

D = 1024
T = 2048
L = 2
NB = 4
EPS = 1e-6
IN_W = 7456
OFF = dict(a_u=0, a_v=256, a_g=512, cq=768, ckv=1536, kr=1792, bg=1824,
           cin=2336, cg=2592, mq=2848, mg=3104, mrg=3360)
PERM = list(range(16, 32)) + list(range(0, 16))
WIN_C0 = [0, 256, 512, 768, 1024, 1280, 1536, 1824, 2080, 2336, 2592, 2848, 3104] + \
         [3360 + br * 1024 + j2 * 256 for br in range(4) for j2 in range(4)]
WIN_G = {c: i for i, c in enumerate(WIN_C0)}
NWG = len(WIN_C0)
TWO_PI = 2.0 * math.pi

C_NG, C_MNG, C_QNG, C_KVNG, C_BM = 0, 8, 16, 22, 24
C_GQ, C_GQP, C_GK, C_GKP, C_MGQ, C_MGK = 56, 57, 58, 59, 60, 61
C_CD, C_BGLU, C_SRE, C_SIM, C_SDT, C_ABS, C_ROW = 62, 64, 66, 74, 82, 90, 346
NCOL = 346 + 640
R_ANG, R_SRE, R_SIM, R_SDT = 0, 256, 1280, 2304
NROW = 3328
K_TRI, K_IOT, K_IOS, K_INVF, K_SGN, K_GM = 0, 128, 256, 257, 258, 259
NCONST = 267


class Tok:
    __slots__ = ("w", "rs", "excl")

    def __init__(self):
        self.w = None
        self.rs = {}
        self.excl = False


class Op:
    __slots__ = ("eng", "fn", "deps", "is_dma", "sig", "users")

    def __init__(self, eng, fn, is_dma):
        self.eng = eng
        self.fn = fn
        self.is_dma = is_dma
        self.deps = []
        self.sig = None
        self.users = 0


ENGS = ("pe", "act", "dve", "pool", "sp")
SYNC_ALL = True
WQ = ("sp",)
DMAQ = ("sp", "act", "pool")


class Prog:
    def __init__(self, nc, n_dma_sems=8):
        self.nc = nc
        self.ops = {e: [] for e in ENGS}
        self.all = []
        self.n_dma_sems = n_dma_sems
        self.toks = {}
        self.dmas_since_bar = []

    def tok(self, *key):
        t = self.toks.get(key)
        if t is None:
            t = self.toks[key] = Tok()
        return t

    def _add(self, eng, fn, reads, writes, is_dma, extra_deps=()):
        op = Op(eng, fn, is_dma)
        deps = list(extra_deps)
        raw = set()
        for t in reads:
            if t.w is not None:
                deps.append(t.w)
                raw.add(id(t.w))
            if t.excl:
                deps.extend(o for o in t.rs.values() if o.eng != eng)
        for t in writes:
            if t.w is not None:
                deps.append(t.w)
            deps.extend(t.rs.values())
        rkey = ("dma", id(op)) if is_dma else eng
        for t in reads:
            t.rs[rkey] = op
        for t in writes:
            t.w = op
            t.rs = {}
        seen = set()
        for d in deps:
            if d is op or id(d) in seen:
                continue
            seen.add(id(d))
            if (not d.is_dma) and (not is_dma) and d.eng == eng:
                if eng == "pe" or (id(d) not in raw and not SYNC_ALL):
                    continue
            if (not d.is_dma) and is_dma and d.eng == eng:
                pass
            op.deps.append(d)
            d.users += 1
        self.ops[eng].append(op)
        self.all.append(op)
        if is_dma:
            self.dmas_since_bar.append(op)
        return op

    def op(self, eng, fn, reads=(), writes=()):
        return self._add(eng, fn, reads, writes, False)

    def dma(self, eng, fn, reads=(), writes=()):
        assert eng in DMAQ
        return self._add(eng, fn, reads, writes, True)

    def barrier(self):
        dm = self.dmas_since_bar
        self.dmas_since_bar = []
        bt = [self.tok("__bar", e) for e in ENGS]
        for i, e in enumerate(ENGS):
            self._add(e, lambda eng: eng.drain(), [], [bt[i]], False,
                      extra_deps=dm if e == "sp" else ())
        for e in ENGS:
            self._add(e, lambda eng: eng.nop(), bt, [], False)

    def emit(self, final_wait_ops=()):
        nc = self.nc
        with contextlib.ExitStack() as es:
            esem = {e: es.enter_context(nc.semaphore("s_" + e)) for e in ENGS}
            dsem = {e: [es.enter_context(nc.semaphore("d_%s%d" % (e, i)))
                        for i in range(self.n_dma_sems)] for e in DMAQ}
            ecount = {e: 0 for e in ENGS}
            dcount = {e: 0 for e in DMAQ}
            duse = {e: [0] * self.n_dma_sems for e in DMAQ}
            fw = set(id(o) for o in final_wait_ops)
            for op in self.all:
                if op.is_dma:
                    j = dcount[op.eng]
                    dcount[op.eng] += 1
                    s = j % self.n_dma_sems
                    duse[op.eng][s] += 1
                    op.sig = (dsem[op.eng][s], 16 * duse[op.eng][s], ("d", op.eng, s))
                elif op.users > 0 or id(op) in fw:
                    ecount[op.eng] += 1
                    op.sig = (esem[op.eng], ecount[op.eng], ("e", op.eng))
            self.stats = dict(ecount=ecount, dcount=dcount,
                              nops={e: len(self.ops[e]) for e in ENGS})
            block = es.enter_context(nc.Block())

            def run_engine(e, engine):
                waited = {}

                def wait(sem, val, key):
                    if waited.get(key, 0) >= val:
                        return
                    waited[key] = val
                    engine.wait_ge(sem, val)

                for op in self.ops[e]:
                    for d in op.deps:
                        wait(*d.sig)
                    if op.is_dma:
                        sem, val, key = op.sig
                        if val > 16:
                            wait(sem, val - 16, key)
                    ins = op.fn(engine)
                    if op.sig is not None:
                        ins.then_inc(op.sig[0], 16 if op.is_dma else 1)
                if e == "sp":
                    for op in final_wait_ops:
                        wait(*op.sig)

            block.tensor(lambda eng: run_engine("pe", eng))
            block.scalar(lambda eng: run_engine("act", eng))
            block.vector(lambda eng: run_engine("dve", eng))
            block.gpsimd(lambda eng: run_engine("pool", eng))
            block.sync(lambda eng: run_engine("sp", eng))


class Arena:
    def __init__(self, ap2d, nfloats):
        self.a = ap2d
        self.n = nfloats
        self.off = 0
        self.peak = 0

    def alloc(self, free_shape, dt):
        free_shape = list(free_shape)
        isz = 2 if dt == BF16 else 4
        n = int(np.prod(free_shape))
        n32 = (n * isz + 3) // 4
        assert self.off + n32 <= self.n, ("arena overflow", self.off, n32, self.n)
        v = self.a[:, self.off:self.off + n32]
        self.off += n32
        self.peak = max(self.peak, self.off)
        if dt == BF16:
            v = v.bitcast(BF16)[:, 0:n]
        elif dt == I32:
            v = v.bitcast(I32)
        if len(free_shape) == 2:
            v = v.rearrange("p (a b) -> p a b", a=free_shape[0])
        elif len(free_shape) == 3:
            v = v.rearrange("p (a b c) -> p a b c", a=free_shape[0], b=free_shape[1])
        elif len(free_shape) == 4:
            v = v.rearrange("p (a b c d) -> p a b c d", a=free_shape[0], b=free_shape[1],
                            c=free_shape[2])
        return v

    def mark(self):
        return self.off

    def reset(self, m):
        self.off = m


class Builder:
    def __init__(self, nlayers=L, stop=None, dumps=()):
        self.nlayers = nlayers
        self.stop = stop
        self.dumps = dict()
        self.want = set(dumps)
        self.nc = nc = bass.Bass("TRN2", target_bir_lowering=False)
        self.P = Prog(nc)
        self.dr = {}
        self.finals = []
        self.dump_specs = []

    def din(self, name, shape, dt=F32):
        self.dr[name] = self.nc.dram_tensor(name, list(shape), dt, kind="ExternalInput").ap()
        return self.dr[name]

    def T(self, *k):
        return self.P.tok(*k)

    def mm(self, out, lhsT, rhs, start, stop, reads, writes):
        return self.P.op("pe", lambda e: e.matmul(out, lhsT=lhsT, rhs=rhs, start=start, stop=stop),
                         reads=reads, writes=writes)

    def mmg(self, out, pairs, reads, wtok):
        n = len(pairs)
        for i, (a, b) in enumerate(pairs):
            self.mm(out, a, b, i == 0, i == n - 1, reads, [wtok])

    def act(self, out, in_, func, reads, writes, bias=0.0, scale=1.0, eng="act", accum_out=None):
        if accum_out is None:
            return self.P.op("act", lambda e: e.activation(out=out, in_=in_, func=func, bias=bias, scale=scale),
                             reads=reads, writes=writes)
        return self.P.op("act", lambda e: e.activation(out=out, in_=in_, func=func, bias=bias, scale=scale,
                                                       accum_out=accum_out),
                         reads=reads, writes=writes)

    def tt(self, eng, out, in0, in1, op, reads, writes):
        return self.P.op(eng, lambda e: e.tensor_tensor(out=out, in0=in0, in1=in1, op=op),
                         reads=reads, writes=writes)

    def ts(self, eng, out, in0, s1, op0, reads, writes, s2=None, op1=None):
        if op1 is None:
            return self.P.op(eng, lambda e: e.tensor_scalar(out=out, in0=in0, scalar1=s1, scalar2=None, op0=op0),
                             reads=reads, writes=writes)
        return self.P.op(eng, lambda e: e.tensor_scalar(out=out, in0=in0, scalar1=s1, scalar2=s2, op0=op0, op1=op1),
                         reads=reads, writes=writes)

    def stt(self, eng, out, in0, scalar, in1, op0, op1, reads, writes):
        return self.P.op(eng, lambda e: e.scalar_tensor_tensor(out=out, in0=in0, scalar=scalar, in1=in1,
                                                                op0=op0, op1=op1),
                         reads=reads, writes=writes)

    def cp(self, eng, out, in_, reads, writes):
        if eng == "act":
            return self.P.op("act", lambda e: e.copy(out=out, in_=in_), reads=reads, writes=writes)
        return self.P.op(eng, lambda e: e.tensor_copy(out=out, in_=in_), reads=reads, writes=writes)

    def memset(self, eng, ap, val, writes):
        return self.P.op(eng, lambda e: e.memset(ap, val), reads=(), writes=writes)

    def dma(self, out, in_, reads, writes, q="sp"):
        def ndesc(ap):
            dims = [(int(st), int(n)) for st, n in ap.ap]
            total = 1
            for st, n in dims:
                total *= n
            run = 1
            for st, n in reversed(dims[1:]):
                if st == run:
                    run *= n
                else:
                    break
            return total // run
        self.desc_count = getattr(self, "desc_count", {})
        self.desc_count[q] = self.desc_count.get(q, 0) + max(ndesc(out), ndesc(in_))
        return self.P.dma(q, lambda e: e.dma_start(out=out, in_=in_), reads=reads, writes=writes)

    def rstd(self, out, in_, n, reads, writes):
        self.act(out, in_, AF.Ln, reads, writes, bias=EPS, scale=1.0 / n)
        self.act(out, out, AF.Exp, writes, writes, scale=-0.5)

    def wload(self, dst, src, dtok, cast_eng="pool", scale=None):
        fs = list(dst.shape[1:])
        n = int(np.prod(fs))
        assert n <= self.stage_n, (n, self.stage_n)
        slot = self.wslot
        self.wslot = (slot + 1) % len(self.stage)
        st = self.stage[slot][:, 0:n]
        if len(fs) == 2:
            st = st.rearrange("p (a b) -> p a b", a=fs[0])
        elif len(fs) == 3:
            st = st.rearrange("p (a b c) -> p a b c", a=fs[0], b=fs[1])
        stok = self.T("stage", slot)
        self.wq_i = getattr(self, "wq_i", 0) + 1
        self.dma(st, src, [], [stok], q=WQ[self.wq_i % len(WQ)])
        if scale is None:
            self.cp(cast_eng, dst, st, [stok], [dtok])
        else:
            self.ts(cast_eng, dst, st, scale, ALU.mult, [stok], [dtok])

    def dump(self, name, ap, tok):
        if name not in self.want:
            return
        self.P.barrier()
        shp = list(ap.shape)
        d = self.nc.dram_tensor("dbg_" + name, shp, ap.dtype, kind="ExternalOutput").ap()
        op = self.dma(d, ap, [tok], [])
        self.finals.append(op)
        self.dump_specs.append(name)

    def build(self):
        nc = self.nc
        P = self.P
        din = self.din
        din("xT", [NB, 128, 8, 512])
        din("memT", [1, 128, 8, 256])
        din("pos", [1, T], I32)
        din("consts", [128, NCONST])
        din("colpack", [L, 128, NCOL])
        din("rowpack", [L, 128, NROW])
        din("w_in", [L, NWG, 128, 8, 256])
        din("w_krm", [L, 128, 8, 96])
        din("w_krp", [L, 128, 8, 96])
        din("wq_m", [L, 8, 128, 6, 96])
        din("wq_p", [L, 8, 128, 6, 96])
        din("w_ukv", [L, 8, 128, 2, 128])
        din("a_w_sT", [L, 128, 4, 128])
        din("c_reT", [L, 2, 128, 4, 128])
        din("c_imT", [L, 2, 128, 4, 128])
        din("c_w_glu", [L, 128, 2, 256])
        din("m_w_kv", [L, 2, 128, 8, 256])
        din("w_br", [L, 4, 128, 10, 256])
        din("w_out", [L, 4, 128, 8, 256])
        self.outT = nc.dram_tensor("outT", [NB, 128, 8, 512], F32, kind="ExternalOutput").ap()
        self.x1T = nc.dram_tensor("x1T", [NB, 128, 8, 512], F32, kind="Internal").ap()

        with contextlib.ExitStack() as es:
            NA = 52000
            arena_t = es.enter_context(nc.sbuf_tensor("arena", [128, NA], F32))
            self.A = A = Arena(arena_t[:, :], NA)
            self.ps = [es.enter_context(nc.psum_tensor("ps%d" % i, [128, 512], F32)) for i in range(8)]
            self.pst = [self.T("ps", i) for i in range(8)]
            for t in self.pst:
                t.excl = True

            self.cst = A.alloc([NCONST], F32)
            self.eps_col = A.alloc([1], F32)
            self.tri = A.alloc([128], BF16)
            self.ntri = A.alloc([128], BF16)
            self.ones = A.alloc([128], BF16)
            self.bd64 = A.alloc([128], BF16)
            self.ones_f = A.alloc([128], F32)
            self.cosT = A.alloc([T], F32)
            self.sinT = A.alloc([T], F32)
            self.colp = A.alloc([NCOL], F32)
            self.hT = A.alloc([8, T], BF16)
            self.yall = A.alloc([10, T], BF16)
            self.kmem = A.alloc([2, 256], BF16)
            self.vmem = A.alloc([2, 256], BF16)
            self.stage_n = 2048
            self.stage = [A.alloc([self.stage_n], F32) for _ in range(2)]
            self.wslot = 0
            self.base_mark = A.mark()

            self.setup_consts()
            if self.stop != "consts":
                for l in range(self.nlayers):
                    if self.run_layer(l):
                        break
            P.barrier()
            P.emit(final_wait_ops=self.finals)
        return nc

    def setup_consts(self):
        A, T_ = self.A, self.T
        tc = T_("cst")
        self.dma(self.cst, self.dr["consts"][:, :], [], [tc])
        self.memset("dve", self.eps_col, EPS, [tc])
        self.cp("dve", self.tri, self.cst[:, K_TRI:K_TRI + 128], [tc], [tc])
        self.ts("dve", self.ntri, self.cst[:, K_TRI:K_TRI + 128], -1.0, ALU.mult, [tc], [tc])
        self.memset("dve", self.ones, 1.0, [tc])
        self.memset("dve", self.ones_f, 1.0, [tc])
        self.memset("dve", self.bd64, 0.0, [tc])
        self.memset("dve", self.bd64[0:64, 0:64], 1.0, [tc])
        self.memset("dve", self.bd64[64:128, 64:128], 1.0, [tc])
        if getattr(self, "skip", None):
            self.memset("pool", self.yall, 0.25, [T_("yall_init")])
        m = A.mark()
        posi = A.alloc([T], I32)
        ang = A.alloc([T], F32)
        kf = A.alloc([T], F32)
        ki = A.alloc([T], I32)
        tr = T_("rope")
        self.dma(posi[0:96, :], self.dr["pos"][0:1, :].partition_broadcast(96), [], [tr])
        R = slice(64, 96)
        self.cp("dve", ang[R, :], posi[R, :], [tr], [tr])
        self.ts("dve", ang[R, :], ang[R, :], self.cst[R, K_INVF:K_INVF + 1], ALU.mult, [tr, tc], [tr])

        def sin_of(dst, shift, post_scale_col):
            self.ts("dve", ki[R, :], ang[R, :], shift, ALU.add, [tr], [tr], s2=1.0 / TWO_PI, op1=ALU.mult)
            self.cp("dve", kf[R, :], ki[R, :], [tr], [tr])
            self.stt("dve", kf[R, :], kf[R, :], -TWO_PI, ang[R, :], ALU.mult, ALU.add, [tr], [tr])
            self.ts("dve", kf[R, :], kf[R, :], shift, ALU.add, [tr], [tr], s2=math.pi, op1=ALU.min)
            self.ts("dve", kf[R, :], kf[R, :], -math.pi, ALU.max, [tr], [tr])
            self.act(dst[R, :], kf[R, :], AF.Sin, [tr], [tr])
            if post_scale_col is not None:
                self.ts("dve", dst[R, :], dst[R, :], post_scale_col, ALU.mult, [tr, tc], [tr])

        sin_of(self.sinT, 0.0, self.cst[R, K_SGN:K_SGN + 1])
        sin_of(self.cosT, math.pi / 2, None)
        self.dump("cosT", self.cosT[R, :], tr)
        self.dump("sinT", self.sinT[R, :], tr)
        self.P.barrier()
        A.reset(m)

    def run_layer(self, l):
        P, A, T_ = self.P, self.A, self.T
        src = self.dr["xT"] if l == 0 else self.x1T
        dst = self.x1T if l == 0 else self.outT
        if self.nlayers == 1:
            dst = self.outT
        tcp = T_("colp")
        self.dma(self.colp, self.dr["colpack"][l, :, :], [], [tcp])
        self.tcp = tcp
        stages = [("N", self.stage_norm), ("MP", self.stage_memprep), ("A", self.stage_a),
                  ("C", self.stage_c), ("M", self.stage_m), ("B", self.stage_b),
                  ("P2", self.stage_p2)]
        for name, fn in stages:
            if name in getattr(self, "skip", ()):
                continue
            m = A.mark()
            if name == "N":
                fn(l, src)
            elif name == "P2":
                fn(l, src, dst)
            else:
                fn(l)
            P.barrier()
            A.reset(m)
            if self.stop == (name, l):
                return True
        return False

    def norm_fm(self, srcT, n_tok, gcol, dst, dst_tok_fn, tag):
        A, T_ = self.A, self.T
        W = min(512, n_tok)
        nblk = n_tok // W
        xb = [A.alloc([8, W], F32) for _ in range(2)]
        sq = A.alloc([8, W], BF16)
        rs = A.alloc([W], F32)
        for b in range(nblk):
            x = xb[b % 2]
            tx = T_(tag + "x", b % 2)
            ts_ = T_(tag + "sq")
            trs = T_(tag + "rs")
            for hh in range(2):
                self.dma(x[:, hh * 4:(hh + 1) * 4, :], srcT[b, :, hh * 4:(hh + 1) * 4, :], [], [tx])
            pb = 0
            for kt in range(8):
                self.act(sq[:, kt, :], x[:, kt, :], AF.Square, [tx], [ts_])
            self.mmg(self.ps[pb][:, 0:W], [(self.ones[:, :], sq[:, kt, :]) for kt in range(8)],
                     [ts_], self.pst[pb])
            self.rstd(rs[:, :], self.ps[pb][:, 0:W], float(D), [self.pst[pb]], [trs])
            for kt in range(8):
                self.stt("dve", dst[:, kt, b * W:(b + 1) * W], x[:, kt, :], gcol[:, kt:kt + 1], rs[:, :],
                         ALU.mult, ALU.mult, [tx, trs, self.tcp], [dst_tok_fn(b)])

    def stage_norm(self, l, src):
        self.norm_fm(src, T, self.colp[:, C_NG:C_NG + 8], self.hT, lambda b: self.T("hT", b), "n")
        for b in range(NB):
            self.dump("hT%d_%d" % (l, b), self.hT[:, :, b * 512:(b + 1) * 512], self.T("hT", b))

    def load_win(self, l, c0, ncols, wt, wtok):
        assert ncols == 256
        self.wload(wt[:, :, 0:ncols], self.dr["w_in"][l, WIN_G[c0]], wtok)

    def zproj_blk(self, wt, ncols, blk, pb, wtok, m_off=0):
        self.mmg(self.ps[pb][m_off:m_off + ncols, :],
                 [(wt[:, kt, 0:ncols], self.hT[:, kt, blk * 512:(blk + 1) * 512]) for kt in range(8)],
                 [wtok, self.T("hT", blk)], self.pst[pb])

    def stage_memprep(self, l):
        A, T_ = self.A, self.T
        hm = A.alloc([8, 256], BF16)
        thm = T_("hm")
        self.norm_fm(self.dr["memT"], 256, self.colp[:, C_MNG:C_MNG + 8], hm, lambda b: thm, "m")
        wkv = A.alloc([8, 512], BF16)
        twk = T_("wkv")
        for half in range(2):
            self.wload(wkv[:, :, half * 256:(half + 1) * 256],
                       self.dr["m_w_kv"][l, half],
                       twk)
        sq = A.alloc([256], BF16)
        rs = A.alloc([256], F32)
        tsq, trs, tkm, tvm = T_("mp_sq"), T_("mp_rs"), T_("kmem"), T_("vmem")
        for ct in range(2):
            self.mmg(self.ps[0][:, 0:256], [(wkv[:, kt, ct * 128:(ct + 1) * 128], hm[:, kt, :]) for kt in range(8)],
                     [twk, thm], self.pst[0])
            self.act(sq[:, :], self.ps[0][:, 0:256], AF.Square, [self.pst[0]], [tsq])
            self.mmg(self.ps[1][:, 0:256], [(self.bd64[:, :], sq[:, :])], [tsq, T_("cst")], self.pst[1])
            self.rstd(rs[:, :], self.ps[1][:, 0:256], 64.0, [self.pst[1]], [trs])
            self.stt("dve", self.kmem[:, ct, :], self.ps[0][:, 0:256], self.colp[:, C_MGK:C_MGK + 1], rs[:, :],
                     ALU.mult, ALU.mult, [self.pst[0], trs, self.tcp], [tkm])
        for mt in range(2):
            self.mmg(self.ps[2][:, 0:256], [(hm[:, kt, mt * 128:(mt + 1) * 128], wkv[:, kt, 256:512]) for kt in range(8)],
                     [twk, thm], self.pst[2])
            self.cp("dve", self.vmem[:, mt, :], self.ps[2][:, 0:256], [self.pst[2]], [tvm])
        self.dump("kmem%d" % l, self.kmem, tkm)
        self.dump("vmem%d" % l, self.vmem, tvm)

    def stage_a(self, l):
        A, T_ = self.A, self.T
        ya = self.yall
        wt = [A.alloc([8, 256], BF16) for _ in range(2)]
        gnb = A.alloc([256], F32)
        wsT = A.alloc([4, 128], BF16)
        tmp = A.alloc([512], BF16)
        tgn, tws, ttmp = T_("a_gn"), T_("a_ws"), T_("a_tmp")
        self.dma(gnb, self.dr["rowpack"][l, :, R_ANG:R_ANG + 256], [], [tgn])
        slot = self.wslot
        self.wslot = (slot + 1) % 2
        st = self.stage[slot][:, 0:512].rearrange("p (g t) -> p g t", g=4)
        stok = T_("stage", slot)
        self.dma(st, self.dr["a_w_sT"][l], [], [stok])
        for g in range(4):
            self.tt("dve", wsT[:, g, :], st[:, g, :], self.cst[:, K_TRI:K_TRI + 128], ALU.mult, [stok, T_("cst")], [tws])
        tw0, tw1 = T_("a_w", 0), T_("a_w", 1)
        self.load_win(l, OFF["a_u"], 256, wt[0], tw0)
        self.load_win(l, OFF["a_g"], 256, wt[1], tw1)
        for ct in range(2):
            for blk in range(NB):
                pb = (ct * NB + blk) % 2
                self.mmg(self.ps[pb][:, :],
                         [(wt[0][:, kt, ct * 128:(ct + 1) * 128], self.hT[:, kt, blk * 512:(blk + 1) * 512])
                          for kt in range(8)], [tw0, T_("hT", blk)], self.pst[pb])
                self.act(ya[:, ct, blk * 512:(blk + 1) * 512], self.ps[pb][:, :], AF.Gelu_apprx_tanh,
                         [self.pst[pb]], [T_("ya", ct, blk)])
        for ct in range(2):
            for blk in range(NB):
                pb = 2 + (ct * NB + blk) % 2
                self.mmg(self.ps[pb][:, :],
                         [(wt[1][:, kt, ct * 128:(ct + 1) * 128], self.hT[:, kt, blk * 512:(blk + 1) * 512])
                          for kt in range(8)], [tw1, T_("hT", blk)], self.pst[pb])
                self.act(tmp[:, :], self.ps[pb][:, :], AF.Silu, [self.pst[pb]], [ttmp])
                sl = ya[:, ct, blk * 512:(blk + 1) * 512]
                self.tt("pool", sl, sl, tmp[:, :], ALU.mult, [ttmp], [T_("ya", ct, blk)])
        twv = T_("a_w", 0)
        self.load_win(l, OFF["a_v"], 256, wt[0], twv)
        gvs = [A.alloc([256], F32) for _ in range(2)]
        ssqs = [A.alloc([1], F32) for _ in range(2)]
        vn = [A.alloc([256], BF16) for _ in range(2)]
        sbs = [A.alloc([2, 128], F32) for _ in range(2)]
        junk = A.alloc([256], BF16)
        absT = self.colp[:, C_ABS:C_ABS + 256].rearrange("p (c t) -> p c t", c=2)
        def a_front(tt_):
            blk = tt_ // 4
            pb = 4 + tt_ % 2
            self.mmg(self.ps[pb][:, 0:256],
                     [(self.hT[:, kt, tt_ * 128:(tt_ + 1) * 128], wt[0][:, kt, :]) for kt in range(8)],
                     [twv, T_("hT", blk)], self.pst[pb])

        def a_back(tt_):
            blk = tt_ // 4
            k = tt_ % 2
            pb = 4 + k
            gv, ssq, sb = gvs[k], ssqs[k], sbs[k]
            tgv, tss, tsb = T_("a_gv", k), T_("a_ss", k), T_("a_sb", k)
            self.act(gv[:, :], self.ps[pb][:, 0:256], AF.Gelu_apprx_tanh, [self.pst[pb]], [tgv])
            self.act(junk[:, :], gv[:, :], AF.Square, [tgv], [tss], accum_out=ssq[:, :])
            self.rstd(ssq[:, :], ssq[:, :], 256.0, [tss], [tss])
            v = vn[k]
            tv = T_("a_vn", k)
            self.stt("dve", v[:, :], gv[:, :], ssq[:, 0:1], gnb[:, :], ALU.mult, ALU.mult, [tgv, tss, tgn], [tv])
            pq = 6 + k
            for g in range(4):
                ct, r0 = g // 2, (g % 2) * 64
                self.mm(self.ps[pq][r0:r0 + 64, ct * 128:(ct + 1) * 128], v[:, g * 64:(g + 1) * 64], wsT[:, g, :],
                        True, True, [tv, tws], [self.pst[pq]])
            psv = self.ps[pq][:, 0:256].rearrange("p (c t) -> p c t", c=2)
            self.tt("dve", sb[:, :, :], psv, absT, ALU.add, [self.pst[pq], self.tcp], [tsb])
            sl = ya[:, 0:2, tt_ * 128:(tt_ + 1) * 128]
            self.tt("pool", sl, sl, sb[:, :, :], ALU.mult, [tsb], [T_("ya", 0, blk), T_("ya", 1, blk)])

        a_front(0)
        for tt_ in range(16):
            if tt_ + 1 < 16:
                a_front(tt_ + 1)
            a_back(tt_)
        self.dump("ya%d" % l, ya[:, 0:2, :], T_("ya", 0, 0))

    def sin_of(self, dst, ang, shift, ki, kf, tok, extra=()):
        rd = [tok] + list(extra)
        self.ts("dve", ki, ang, shift, ALU.add, rd, [tok], s2=1.0 / TWO_PI, op1=ALU.mult)
        self.cp("dve", kf, ki, [tok], [tok])
        self.stt("dve", kf, kf, -TWO_PI, ang, ALU.mult, ALU.add, [tok], [tok])
        self.ts("dve", kf, kf, shift, ALU.add, [tok], [tok], s2=math.pi, op1=ALU.min)
        self.ts("dve", kf, kf, -math.pi, ALU.max, [tok], [tok])
        self.act(dst, kf, AF.Sin, [tok], [tok])

    def stage_m(self, l):
        A, T_ = self.A, self.T
        ya = self.yall
        wq = A.alloc([8, 256], BF16)
        wg = A.alloc([8, 256], BF16)
        twq, twg = T_("m_wq"), T_("m_wg")
        self.load_win(l, OFF["mq"], 256, wq, twq)
        self.load_win(l, OFF["mg"], 256, wg, twg)
        for ct in range(2):
            for blk in range(NB):
                pb = (ct * NB + blk) % 2
                self.mmg(self.ps[pb][:, :],
                         [(wg[:, kt, ct * 128:(ct + 1) * 128], self.hT[:, kt, blk * 512:(blk + 1) * 512])
                          for kt in range(8)], [twg], self.pst[pb])
                self.act(ya[:, 8 + ct, blk * 512:(blk + 1) * 512], self.ps[pb][:, :], AF.Silu,
                         [self.pst[pb]], [T_("ym", ct, blk)])
        sq = A.alloc([512], BF16)
        rs = A.alloc([512], F32)
        qn = [A.alloc([2, 512], BF16) for _ in range(2)]
        pT = [A.alloc([512], BF16) for _ in range(4)]
        rc = A.alloc([512], F32)
        ot = A.alloc([512], BF16)
        tsq, trs, trc, tot = T_("m_sq"), T_("m_rs"), T_("m_rc"), T_("m_ot")
        scale = 64.0 ** -0.5

        def m_prep(blk):
            q = qn[blk % 2]
            tq = T_("m_qn", blk % 2)
            for ct in range(2):
                self.mmg(self.ps[2][:, :],
                         [(wq[:, kt, ct * 128:(ct + 1) * 128], self.hT[:, kt, blk * 512:(blk + 1) * 512])
                          for kt in range(8)], [twq], self.pst[2])
                self.act(sq[:, :], self.ps[2][:, :], AF.Square, [self.pst[2]], [tsq])
                self.mmg(self.ps[3][:, :], [(self.bd64[:, :], sq[:, :])], [tsq], self.pst[3])
                self.rstd(rs[:, :], self.ps[3][:, :], 64.0, [self.pst[3]], [trs])
                self.stt("dve", q[:, ct, :], self.ps[2][:, :], self.colp[:, C_MGQ:C_MGQ + 1], rs[:, :],
                         ALU.mult, ALU.mult, [self.pst[2], trs], [tq])

        def m_s(blk, h):
            q = qn[blk % 2]
            tq = T_("m_qn", blk % 2)
            ct, r0 = h // 2, (h % 2) * 64
            R = slice(r0, r0 + 64)
            for mt in range(2):
                pb = (4 + mt) if h % 2 == 0 else mt
                self.mm(self.ps[pb][:, :], self.kmem[R, ct, mt * 128:(mt + 1) * 128], q[R, ct, :],
                        True, True, [tq], [self.pst[pb]])

        def m_rest(blk, h):
            ct, r0 = h // 2, (h % 2) * 64
            R = slice(r0, r0 + 64)
            for mt in range(2):
                pb = (4 + mt) if h % 2 == 0 else mt
                p = pT[(h % 2) * 2 + mt]
                tp = T_("m_pT", (h % 2) * 2 + mt)
                self.act(p[:, :], self.ps[pb][:, :], AF.Exp, [self.pst[pb]], [tp], scale=scale)
            tps = [T_("m_pT", (h % 2) * 2 + mt) for mt in range(2)]
            ps_o, ps_d = self.ps[6], self.ps[7]
            self.mmg(ps_o[R, :], [(self.vmem[:, mt, h * 64:(h + 1) * 64], pT[(h % 2) * 2 + mt][:, :])
                                  for mt in range(2)], tps, self.pst[6])
            self.mmg(ps_d[R, :], [(self.ones[:, 0:64], pT[(h % 2) * 2 + mt][:, :]) for mt in range(2)],
                     tps, self.pst[7])
            self.P.op("dve", lambda e, o=rc[R, :], i=ps_d[R, :]: e.reciprocal(out=o, in_=i),
                      reads=[self.pst[7]], writes=[trc])
            self.tt("dve", ot[R, :], ps_o[R, :], rc[R, :], ALU.mult, [self.pst[6], trc], [tot])
            sl = ya[R, 8 + ct, blk * 512:(blk + 1) * 512]
            self.tt("pool", sl, sl, ot[R, :], ALU.mult, [tot], [T_("ym", ct, blk)])

        m_prep(0)
        for blk in range(NB):
            m_s(blk, 0)
            if blk + 1 < NB:
                m_prep(blk + 1)
            for h in range(4):
                if h + 1 < 4:
                    m_s(blk, h + 1)
                m_rest(blk, h)
        self.dump("ym%d" % l, ya[:, 8:10, :], T_("ym", 0, 0))

    def stage_c(self, l):
        A, T_ = self.A, self.T
        ya = self.yall
        uT = A.alloc([2, T], BF16)
        ygT = A.alloc([2, T], BF16)
        TAc = A.alloc([1024], F32)
        TAs = A.alloc([1024], F32)
        Dr = A.alloc([8, 128], F32)
        Di = A.alloc([8, 128], F32)
        a128r = A.alloc([8], F32)
        a128i = A.alloc([8], F32)
        Bm = [A.alloc([2, 8, 64], BF16) for _ in range(2)]
        CreT = A.alloc([8, 128], BF16)
        CreTn = A.alloc([8, 128], BF16)
        CimTn = A.alloc([8, 128], BF16)
        wglu = A.alloc([2, 256], BF16)
        m2 = A.mark()
        wu = A.alloc([8, 256], BF16)
        wg = A.alloc([8, 256], BF16)
        twu, twg = T_("c_wu"), T_("c_wg")
        self.load_win(l, OFF["cin"], 256, wu, twu)
        self.load_win(l, OFF["cg"], 256, wg, twg)
        for ct in range(2):
            for blk in range(NB):
                pb = (ct * NB + blk) % 2
                self.mmg(self.ps[pb][:, :],
                         [(wu[:, kt, ct * 128:(ct + 1) * 128], self.hT[:, kt, blk * 512:(blk + 1) * 512])
                          for kt in range(8)], [twu], self.pst[pb])
                self.cp("act", uT[:, ct, blk * 512:(blk + 1) * 512], self.ps[pb][:, :], [self.pst[pb]],
                        [T_("c_uT", blk)])
        for ct in range(2):
            for blk in range(NB):
                pb = 2 + (ct * NB + blk) % 2
                self.mmg(self.ps[pb][:, :],
                         [(wg[:, kt, ct * 128:(ct + 1) * 128], self.hT[:, kt, blk * 512:(blk + 1) * 512])
                          for kt in range(8)], [twg], self.pst[pb])
                self.act(ya[:, 6 + ct, blk * 512:(blk + 1) * 512], self.ps[pb][:, :], AF.Silu,
                         [self.pst[pb]], [T_("yc", ct, blk)])
        tt_ = T_("c_tab")
        rowp = A.alloc([3, 1024], F32)
        self.dma(rowp, self.dr["rowpack"][l, :, R_SRE:R_SRE + 3072].rearrange("p (a b) -> p a b", a=3), [], [tt_])
        s1 = A.alloc([1024], F32)
        s2 = A.alloc([1024], F32)
        si = A.alloc([1024], I32)
        negs = A.alloc([1], F32)
        tcst = T_("cst")
        self.ts("dve", negs, self.cst[:, K_IOS:K_IOS + 1], -1.0, ALU.mult, [tcst], [tt_])
        self.act(rowp[:, 2, :], rowp[:, 2, :], AF.Exp, [tt_], [tt_])
        self.tt("dve", rowp[:, 0, :], rowp[:, 0, :], rowp[:, 2, :], ALU.mult, [tt_], [tt_])
        self.tt("dve", rowp[:, 1, :], rowp[:, 1, :], rowp[:, 2, :], ALU.mult, [tt_], [tt_])
        self.act(s1, rowp[:, 0, :], AF.Exp, [tt_], [tt_], scale=negs[:, 0:1])
        self.ts("dve", s2, rowp[:, 1, :], self.cst[:, K_IOS:K_IOS + 1], ALU.mult, [tt_, tcst], [tt_])
        self.sin_of(TAs, s2, 0.0, si, rowp[:, 2, :], tt_)
        self.sin_of(TAc, s2, math.pi / 2, si, rowp[:, 2, :], tt_)
        self.tt("dve", TAs, TAs, s1, ALU.mult, [tt_], [tt_])
        self.tt("dve", TAc, TAc, s1, ALU.mult, [tt_], [tt_])
        s1v = s1.rearrange("p (j t) -> p j t", j=8)
        s2v = s2.rearrange("p (j t) -> p j t", j=8)
        siv = si.rearrange("p (j t) -> p j t", j=8)
        kfv = rowp[:, 2, :].rearrange("p (j t) -> p j t", j=8)
        dtj = A.alloc([8], F32)
        thrj = A.alloc([8], F32)
        thij = A.alloc([8], F32)
        e128 = A.alloc([8], F32)
        p128 = A.alloc([8], F32)
        k128 = A.alloc([8], F32)
        i128 = A.alloc([8], I32)
        tcp = self.tcp
        self.act(dtj, self.colp[:, C_SDT:C_SDT + 8], AF.Exp, [tcp, tt_], [tt_])
        self.tt("dve", thrj, self.colp[:, C_SRE:C_SRE + 8], dtj, ALU.mult, [tcp, tt_], [tt_])
        self.tt("dve", thij, self.colp[:, C_SIM:C_SIM + 8], dtj, ALU.mult, [tcp, tt_], [tt_])
        iot = self.cst[:, K_IOT:K_IOT + 128]
        for j in range(8):
            self.act(s1v[:, j, :], iot, AF.Exp, [tt_, tcst], [tt_], scale=thrj[:, j:j + 1])
            self.ts("dve", s2v[:, j, :], iot, thij[:, j:j + 1], ALU.mult, [tt_, tcst], [tt_])
        self.sin_of(Di.rearrange("p j t -> p (j t)"), s2, 0.0, si, rowp[:, 2, :], tt_)
        self.sin_of(Dr.rearrange("p j t -> p (j t)"), s2, math.pi / 2, si, rowp[:, 2, :], tt_)
        self.tt("dve", Di, Di, s1v, ALU.mult, [tt_], [tt_])
        self.tt("dve", Dr, Dr, s1v, ALU.mult, [tt_], [tt_])
        self.act(e128, thrj, AF.Exp, [tt_], [tt_], scale=128.0)
        self.ts("dve", p128, thij, 128.0, ALU.mult, [tt_], [tt_])
        self.sin_of(a128i, p128, 0.0, i128, k128, tt_)
        self.sin_of(a128r, p128, math.pi / 2, i128, k128, tt_)
        self.tt("dve", a128i, a128i, e128, ALU.mult, [tt_], [tt_])
        self.tt("dve", a128r, a128r, e128, ALU.mult, [tt_], [tt_])
        w = [A.alloc([64], F32) for _ in range(8)]
        wi = A.alloc([64], I32)
        gmask = self.cst[:, K_GM:K_GM + 8]
        for ct in range(2):
            base = C_ROW + ct * 320
            are = self.colp[:, base:base + 64]
            aim = self.colp[:, base + 64:base + 128]
            ldt = self.colp[:, base + 128:base + 192]
            bre = self.colp[:, base + 192:base + 256]
            bim = self.colp[:, base + 256:base + 320]
            dt_, thr, thi, ea, abr, abi, t0, t1 = w
            rd = [tt_, tcp]
            self.act(dt_, ldt, AF.Exp, rd, [tt_])
            self.tt("dve", thr, are, dt_, ALU.mult, rd, [tt_])
            self.tt("dve", thi, aim, dt_, ALU.mult, rd, [tt_])
            self.act(ea, thr, AF.Exp, [tt_], [tt_])
            self.sin_of(abi, thi, 0.0, wi, t0, tt_)
            self.sin_of(abr, thi, math.pi / 2, wi, t0, tt_)
            self.tt("dve", abi, abi, ea, ALU.mult, [tt_], [tt_])
            self.tt("dve", abr, abr, ea, ALU.mult, [tt_], [tt_])
            self.ts("dve", abr, abr, -1.0, ALU.add, [tt_], [tt_])
            self.tt("dve", dt_, are, are, ALU.mult, rd, [tt_])
            self.tt("dve", t0, aim, aim, ALU.mult, rd, [tt_])
            self.tt("dve", dt_, dt_, t0, ALU.add, [tt_], [tt_])
            self.P.op("dve", lambda e, o=dt_, i=dt_: e.reciprocal(out=o, in_=i), reads=[tt_], writes=[tt_])
            self.tt("dve", t0, abr, are, ALU.mult, rd, [tt_])
            self.tt("dve", t1, abi, aim, ALU.mult, rd, [tt_])
            self.tt("dve", t0, t0, t1, ALU.add, [tt_], [tt_])
            self.tt("dve", thr, t0, dt_, ALU.mult, [tt_], [tt_])
            self.tt("dve", t0, abi, are, ALU.mult, rd, [tt_])
            self.tt("dve", t1, abr, aim, ALU.mult, rd, [tt_])
            self.tt("dve", t0, t0, t1, ALU.subtract, [tt_], [tt_])
            self.tt("dve", thi, t0, dt_, ALU.mult, [tt_], [tt_])
            self.tt("dve", t0, thr, bre, ALU.mult, rd, [tt_])
            self.tt("dve", t1, thi, bim, ALU.mult, rd, [tt_])
            self.tt("dve", ea, t0, t1, ALU.subtract, [tt_], [tt_])
            self.tt("dve", t0, thr, bim, ALU.mult, rd, [tt_])
            self.tt("dve", t1, thi, bre, ALU.mult, rd, [tt_])
            self.tt("dve", abi, t0, t1, ALU.add, [tt_], [tt_])
            for ri, src_ in enumerate((ea, abi)):
                self.tt("dve", Bm[ct][:, ri, :, :], src_.unsqueeze(1).to_broadcast([128, 8, 64]),
                        gmask.unsqueeze(2).to_broadcast([128, 8, 64]), ALU.mult, [tt_, tcst], [tt_])
        tct = T_("c_ct")
        for j0 in (0, 4):
            srcr = self.dr["c_reT"][l, j0 // 4]
            srci = self.dr["c_imT"][l, j0 // 4]
            self.wload(CreT[:, j0:j0 + 4, :], srcr, tct)
            self.wload(CreTn[:, j0:j0 + 4, :], srcr, tct, scale=-1.0)
            self.wload(CimTn[:, j0:j0 + 4, :], srci, tct, scale=-1.0)
        self.wload(wglu, self.dr["c_w_glu"][l], tct)
        self.dump("c_TAc%d" % l, TAc, tt_)
        self.dump("c_TAs%d" % l, TAs, tt_)
        self.dump("c_Dr%d" % l, Dr, tt_)
        self.dump("c_Di%d" % l, Di, tt_)
        self.dump("c_Bm%d" % l, Bm[0], tt_)
        self.dump("c_a128r%d" % l, a128r, tt_)
        self.P.barrier()
        A.reset(m2)
        aS = A.alloc([16], F32)
        self.memset("dve", aS, 0.0, [T_("c_aS", 0), T_("c_aS", 1)])
        X = [A.alloc([4, 512], BF16) for _ in range(2)]
        Pfs = [A.alloc([8, 128], F32) for _ in range(2)]
        Ys = [A.alloc([4, 4, 128], BF16) for _ in range(2)]
        p127 = A.alloc([8], F32)
        c1 = A.alloc([8], F32)
        c2 = A.alloc([8], F32)
        yss = [A.alloc([128], F32) for _ in range(2)]
        tch = T_("c_ch")
        its = [(n, ct) for n in range(16) for ct in range(2)]

        def front(i):
            n, ct = its[i]
            par = i % 2
            blk = n // 4
            cs = slice(n * 128, (n + 1) * 128)
            pre_, pim_ = self.ps[0], self.ps[1]
            tpre, tpim = self.pst[0], self.pst[1]
            Bre = Bm[ct][:, 0, :, :].rearrange("p g q -> p (g q)")
            Bim = Bm[ct][:, 1, :, :].rearrange("p g q -> p (g q)")
            self.mm(pre_[:, :], uT[:, ct, cs], Bre, True, True, [T_("c_uT", blk), tt_], [tpre])
            self.mm(pim_[:, :], uT[:, ct, cs], Bim, True, True, [T_("c_uT", blk), tt_], [tpim])
            x = X[par]
            tx = T_("c_X", par)
            tc_ = TAc[:, ct * 512:(ct + 1) * 512]
            ts_ = TAs[:, ct * 512:(ct + 1) * 512]
            self.tt("dve", x[:, 0, :], pre_[:, :], tc_, ALU.mult, [tpre, tt_], [tx])
            self.tt("dve", x[:, 3, :], pre_[:, :], ts_, ALU.mult, [tpre, tt_], [tx])
            self.tt("dve", x[:, 1, :], pim_[:, :], ts_, ALU.mult, [tpim, tt_], [tx])
            self.tt("dve", x[:, 2, :], pim_[:, :], tc_, ALU.mult, [tpim, tt_], [tx])
            Pre, Pim = self.ps[2 + 2 * par], self.ps[3 + 2 * par]
            for jl in range(4):
                js = slice(jl * 128, (jl + 1) * 128)
                self.mmg(Pre[:, js], [(x[:, 0, js], self.tri[:, :]), (x[:, 1, js], self.tri[:, :])],
                         [tx], self.pst[2 + 2 * par])
                self.mmg(Pim[:, js], [(x[:, 2, js], self.tri[:, :]), (x[:, 3, js], self.ntri[:, :])],
                         [tx], self.pst[3 + 2 * par])

        def back_a(i):
            n, ct = its[i]
            par = i % 2
            Pre, Pim = self.ps[2 + 2 * par], self.ps[3 + 2 * par]
            tP0, tP1 = self.pst[2 + 2 * par], self.pst[3 + 2 * par]
            Pf, Y = Pfs[par], Ys[par]
            tPf, tY = T_("c_Pf", par), T_("c_Y", par)
            ta = T_("c_aS", ct)
            jr = slice(4 * ct, 4 * ct + 4)
            ji = slice(8 + 4 * ct, 8 + 4 * ct + 4)
            for jl in range(4):
                js = slice(jl * 128, (jl + 1) * 128)
                self.act(Pf[:, jl, :], Pre[:, js], AF.Identity, [tP0, ta], [tPf],
                         bias=aS[:, 4 * ct + jl:4 * ct + jl + 1])
                self.act(Pf[:, 4 + jl, :], Pim[:, js], AF.Identity, [tP1, ta], [tPf],
                         bias=aS[:, 8 + 4 * ct + jl:8 + 4 * ct + jl + 1])
            self.cp("dve", p127, Pf[:, :, 127], [tPf], [tch])
            self.tt("dve", c1[:, 0:4], a128r[:, jr], p127[:, 0:4], ALU.mult, [tch, tt_], [tch])
            self.tt("dve", c1[:, 4:8], a128i[:, jr], p127[:, 4:8], ALU.mult, [tch, tt_], [tch])
            self.tt("dve", c2[:, 0:4], a128r[:, jr], p127[:, 4:8], ALU.mult, [tch, tt_], [tch])
            self.tt("dve", c2[:, 4:8], a128i[:, jr], p127[:, 0:4], ALU.mult, [tch, tt_], [tch])
            self.tt("dve", aS[:, jr], c1[:, 0:4], c1[:, 4:8], ALU.subtract, [tch], [ta])
            self.tt("dve", aS[:, ji], c2[:, 0:4], c2[:, 4:8], ALU.add, [tch], [ta])
            self.tt("pool", Y[:, 0, :, :], Pf[:, 0:4, :], Dr[:, jr, :], ALU.mult, [tPf, tt_], [tY])
            self.tt("pool", Y[:, 1, :, :], Pf[:, 4:8, :], Di[:, jr, :], ALU.mult, [tPf, tt_], [tY])
            tY2 = T_("c_Y2", par)
            self.tt("dve", Y[:, 2, :, :], Pf[:, 4:8, :], Dr[:, jr, :], ALU.mult, [tPf, tt_], [tY2])
            self.tt("dve", Y[:, 3, :, :], Pf[:, 0:4, :], Di[:, jr, :], ALU.mult, [tPf, tt_], [tY2])

        def back_y(i):
            n, ct = its[i]
            par = i % 2
            Y = Ys[par]
            tY, tY2 = T_("c_Y", par), T_("c_Y2", par)
            py = self.ps[6 + par]
            pairs = []
            for jl in range(4):
                j = 4 * ct + jl
                pairs += [(CreT[:, j, :], Y[:, 0, jl, :]), (CreTn[:, j, :], Y[:, 1, jl, :]),
                          (CimTn[:, j, :], Y[:, 2, jl, :]), (CimTn[:, j, :], Y[:, 3, jl, :])]
            self.mmg(py[:, 0:128], pairs, [tY, tY2, tct], self.pst[6 + par])

        def back_b(i):
            n, ct = its[i]
            par = i % 2
            blk = n // 4
            cs = slice(n * 128, (n + 1) * 128)
            py = self.ps[6 + par]
            ys, tys = yss[par], T_("c_ys", par)
            self.stt("dve", ys, uT[:, ct, cs], self.colp[:, C_CD + ct:C_CD + ct + 1], py[:, 0:128],
                     ALU.mult, ALU.add, [self.pst[6 + par], T_("c_uT", blk), tcp], [tys])
            self.act(ygT[:, ct, cs], ys, AF.Gelu_apprx_tanh, [tys], [T_("c_yg", blk)])

        import os
        if os.environ.get("C_NOLOOP"):
            its = its[:0]
            self.memset("dve", ygT, 0.0, [T_("c_yg", b_) for b_ in range(NB)])
        if its:
            front(0)
        for i in range(len(its)):
            if i + 1 < len(its):
                front(i + 1)
            back_a(i)
            if i >= 1:
                back_y(i - 1)
            if i >= 2:
                back_b(i - 2)
        if its:
            back_y(len(its) - 1)
            back_b(len(its) - 2)
            back_b(len(its) - 1)
        self.dump("c_yg%d" % l, ygT, T_("c_yg", 0))
        sg = A.alloc([512], BF16)
        tsg = T_("c_sg")
        for blk in range(NB):
            bs = slice(blk * 512, (blk + 1) * 512)
            for cc in range(2):
                pb = 7
                self.mmg(self.ps[pb][:, :], [(wglu[:, ct, cc * 128:(cc + 1) * 128], ygT[:, ct, bs]) for ct in range(2)],
                         [tct, T_("c_yg", blk)], self.pst[pb])
                self.act(sg, self.ps[pb][:, :], AF.Sigmoid, [self.pst[pb], tcp], [tsg],
                         bias=self.colp[:, C_BGLU + cc:C_BGLU + cc + 1])
                sl = ya[:, 6 + cc, bs]
                self.tt("pool", sg, sg, ygT[:, cc, bs], ALU.mult, [tsg, T_("c_yg", blk)], [tsg])
                self.tt("pool", sl, sl, sg, ALU.mult, [tsg], [T_("yc", cc, blk)])
        self.dump("yc%d" % l, ya[:, 6:8, :], T_("yc", 0, 0))

    def stage_b(self, l):
        A, T_, P = self.A, self.T, self.P
        ya = self.yall
        tcp = self.tcp
        tcst = T_("cst")
        cqn = A.alloc([6, T], BF16)
        ckvn = A.alloc([2, T], BF16)
        krr = A.alloc([T], F32)
        sqkr = A.alloc([T], BF16)
        m2 = A.mark()
        wt = A.alloc([8, 512], BF16)
        tw = T_("b_w")
        for i in range(2):
            self.load_win(l, OFF["bg"] + i * 256, 256, wt[:, :, i * 256:(i + 1) * 256], tw)
        for ct in range(4):
            for blk in range(NB):
                pb = (ct * NB + blk) % 2
                self.zproj_blk(wt[:, :, ct * 128:(ct + 1) * 128], 128, blk, pb, tw)
                self.act(ya[:, 2 + ct, blk * 512:(blk + 1) * 512], self.ps[pb][:, :], AF.Silu,
                         [self.pst[pb]], [T_("yb", ct, blk)])
        P.barrier()
        A.reset(m2)
        wq_in = A.alloc([8, 768], BF16)
        wkv_in = A.alloc([8, 256], BF16)
        wkr = [A.alloc([8, 96], BF16) for _ in range(2)]
        sq = A.alloc([512], BF16)
        rs = A.alloc([512], F32)
        t1 = A.alloc([512], F32)
        t2 = A.alloc([512], F32)
        cqf = A.alloc([6, 512], F32)
        ckf = A.alloc([2, 512], F32)
        rs2 = A.alloc([512], F32)
        for i in range(3):
            self.load_win(l, OFF["cq"] + i * 256, 256, wq_in[:, :, i * 256:(i + 1) * 256], tw)
        self.load_win(l, OFF["ckv"], 256, wkv_in, tw)
        self.wload(wkr[0], self.dr["w_krm"][l], tw)
        self.wload(wkr[1], self.dr["w_krp"][l], tw)
        tsq, trs = T_("b_sq"), T_("b_rs")
        R = slice(64, 96)
        for blk in range(NB):
            bs = slice(blk * 512, (blk + 1) * 512)
            tcf = T_("b_cqf")
            for i in range(6):
                pb = i % 2
                self.zproj_blk(wq_in[:, :, i * 128:(i + 1) * 128], 128, blk, pb, tw)
                self.act(sq, self.ps[pb][:, :], AF.Square, [self.pst[pb]], [tsq])
                self.cp("act", cqf[:, i, :], self.ps[pb][:, :], [self.pst[pb]], [tcf])
                self.mm(self.ps[6][:, :], self.ones[:, :], sq, i == 0, i == 5, [tsq], [self.pst[6]])
            tkf = T_("b_ckf")
            for i in range(2):
                pb = 4 + i
                self.zproj_blk(wkv_in[:, :, i * 128:(i + 1) * 128], 128, blk, pb, tw)
                self.act(sq, self.ps[pb][:, :], AF.Square, [self.pst[pb]], [tsq])
                self.cp("act", ckf[:, i, :], self.ps[pb][:, :], [self.pst[pb]], [tkf])
                self.mm(self.ps[7][:, :], self.ones[:, :], sq, i == 0, i == 1, [tsq], [self.pst[7]])
            self.rstd(rs, self.ps[6][:, :], 768.0, [self.pst[6]], [trs])
            for i in range(6):
                self.stt("dve", cqn[:, i, bs], cqf[:, i, :], self.colp[:, C_QNG + i:C_QNG + i + 1], rs,
                         ALU.mult, ALU.mult, [tcf, trs, tcp], [T_("b_cqn", blk)])
            trs2 = T_("b_rs2")
            self.rstd(rs2, self.ps[7][:, :], 256.0, [self.pst[7]], [trs2])
            for i in range(2):
                self.stt("dve", ckvn[:, i, bs], ckf[:, i, :], self.colp[:, C_KVNG + i:C_KVNG + i + 1], rs2,
                         ALU.mult, ALU.mult, [tkf, trs2, tcp], [T_("b_ckvn", blk)])
            for i in range(2):
                self.zproj_blk(wkr[i], 96, blk, 2 + i, tw)
            self.act(sqkr[R, bs], self.ps[2][R, :], AF.Square, [self.pst[2]], [T_("b_kr", blk)])
            tt1 = T_("b_t1")
            self.stt("dve", t1[R, :], self.ps[2][R, :], self.colp[R, C_GK:C_GK + 1], self.cosT[R, bs],
                     ALU.mult, ALU.mult, [self.pst[2], tcp], [tt1])
            self.stt("dve", t2[R, :], self.ps[3][R, :], self.colp[R, C_GKP:C_GKP + 1], self.sinT[R, bs],
                     ALU.mult, ALU.mult, [self.pst[3], tcp], [tt1])
            self.tt("pool", krr[R, bs], t1[R, :], t2[R, :], ALU.add, [tt1], [T_("b_kr", blk)])
        self.dump("b_cqn%d" % l, cqn, T_("b_cqn", 0))
        self.dump("b_ckvn%d" % l, ckvn, T_("b_ckvn", 0))
        self.dump("b_krr%d" % l, krr[R, :], T_("b_kr", 0))
        P.barrier()
        A.reset(m2)
        wqm = [A.alloc([6, 96], BF16) for _ in range(2)]
        wqp = [A.alloc([6, 96], BF16) for _ in range(2)]
        wkv = [A.alloc([2, 128], BF16) for _ in range(2)]
        wk = [w_[:, :, 0:64] for w_ in wkv]
        wv = [w_[:, :, 64:128] for w_ in wkv]
        qn = [A.alloc([T], BF16) for _ in range(2)]
        kn = [A.alloc([T], BF16) for _ in range(2)]
        vh = [A.alloc([16, 64], BF16) for _ in range(2)]
        sq = [A.alloc([512], BF16) for _ in range(2)]
        rs = [A.alloc([512], F32) for _ in range(2)]
        t1 = A.alloc([512], F32)
        t2 = A.alloc([512], F32)
        pT = [A.alloc([512], BF16) for _ in range(4)]
        rcs = [A.alloc([512], F32) for _ in range(2)]
        ots = [A.alloc([512], BF16) for _ in range(2)]
        scale = 96.0 ** -0.5
        def b_load(h_):
            s_ = h_ % 2
            twh_ = T_("b_wh", s_)
            self.wload(wqm[s_], self.dr["wq_m"][l, h_], twh_)
            self.wload(wqp[s_], self.dr["wq_p"][l, h_], twh_)
            self.wload(wkv[s_], self.dr["w_ukv"][l, h_], twh_)

        b_load(0)
        for h in range(8):
            s = h % 2
            twh = T_("b_wh", s)
            tq, tk, tv = T_("b_qn", s), T_("b_kn", s), T_("b_vh", s)
            Q = slice(0, 96)
            N_ = slice(0, 64)
            for half in range(2):
                pbv = 6 + half
                for t8 in range(8):
                    tt_ = half * 8 + t8
                    self.mmg(self.ps[pbv][:, t8 * 64:(t8 + 1) * 64],
                             [(ckvn[:, kt, tt_ * 128:(tt_ + 1) * 128], wv[s][:, kt, :]) for kt in range(2)],
                             [twh], self.pst[pbv])
                self.cp("dve", vh[s][:, half * 8:(half + 1) * 8, :],
                        self.ps[pbv][:, :].rearrange("p (a b) -> p a b", a=8), [self.pst[pbv]], [tv])

            def banks(blk):
                o = 0 if blk % 2 == 0 else 4
                return o, o + 1, o + 2, o + 3

            def prep_f1(blk):
                bs = slice(blk * 512, (blk + 1) * 512)
                bq, bp, _, bk = banks(blk)
                self.mmg(self.ps[bq][Q, :], [(wqm[s][:, kt, :], cqn[:, kt, bs]) for kt in range(6)], [twh], self.pst[bq])

            def prep_f2(blk):
                bs = slice(blk * 512, (blk + 1) * 512)
                bq, bp, _, bk = banks(blk)
                self.mmg(self.ps[bp][Q, :], [(wqp[s][:, kt, :], cqn[:, kt, bs]) for kt in range(6)], [twh], self.pst[bp])

            def prep_f3(blk):
                bs = slice(blk * 512, (blk + 1) * 512)
                bq, bp, _, bk = banks(blk)
                self.mmg(self.ps[bk][N_, :], [(wk[s][:, kt, :], ckvn[:, kt, bs]) for kt in range(2)], [twh], self.pst[bk])

            def prep_back(blk, nxt):
                bs = slice(blk * 512, (blk + 1) * 512)
                bq, bp, bsq, bk = banks(blk)
                tsq0, trs0 = T_("b_sq", 0), T_("b_rs", 0)
                tsq1, trs1 = T_("b_sq", 1), T_("b_rs", 1)
                self.act(sq[0][Q, :], self.ps[bq][Q, :], AF.Square, [self.pst[bq]], [tsq0])
                self.act(sq[1][N_, :], self.ps[bk][N_, :], AF.Square, [self.pst[bk]], [tsq1])
                self.cp("pool", sq[1][R, :], sqkr[R, bs], [], [tsq1])
                if nxt is not None:
                    prep_f1(nxt)
                self.mmg(self.ps[bsq][Q, :], [(self.ones[Q, 0:96], sq[0][Q, :])], [tsq0], self.pst[bsq])
                self.rstd(rs[0][Q, :], self.ps[bsq][Q, :], 96.0, [self.pst[bsq]], [trs0])
                if nxt is not None:
                    prep_f2(nxt)
                self.mmg(self.ps[bsq][Q, :], [(self.ones[Q, 0:96], sq[1][Q, :])], [tsq1], self.pst[bsq])
                self.rstd(rs[1][Q, :], self.ps[bsq][Q, :], 96.0, [self.pst[bsq]], [trs1])
                if nxt is not None:
                    prep_f3(nxt)
                tt1 = T_("b_t1")
                self.stt("dve", t1[R, :], self.ps[bq][R, :], self.colp[R, C_GQ:C_GQ + 1], self.cosT[R, bs],
                         ALU.mult, ALU.mult, [self.pst[bq], tcp], [tt1])
                self.stt("dve", t2[R, :], self.ps[bp][R, :], self.colp[R, C_GQP:C_GQP + 1], self.sinT[R, bs],
                         ALU.mult, ALU.mult, [self.pst[bp], tcp], [tt1])
                self.stt("dve", qn[s][N_, bs], self.ps[bq][N_, :], self.colp[N_, C_GQ:C_GQ + 1], rs[0][N_, :],
                         ALU.mult, ALU.mult, [self.pst[bq], trs0, tcp], [tq])
                self.stt("dve", kn[s][N_, bs], self.ps[bk][N_, :], self.colp[N_, C_GK:C_GK + 1], rs[1][N_, :],
                         ALU.mult, ALU.mult, [self.pst[bk], trs1, tcp], [tk])
                self.tt("pool", t1[R, :], t1[R, :], t2[R, :], ALU.add, [tt1], [tt1])
                self.tt("pool", qn[s][R, bs], t1[R, :], rs[0][R, :], ALU.mult, [tt1, trs0], [tq])
                self.tt("pool", kn[s][R, bs], krr[R, bs], rs[1][R, :], ALU.mult, [trs1], [tk])

            prep_f1(0)
            prep_f2(0)
            prep_f3(0)
            for blk in range(NB):
                prep_back(blk, blk + 1 if blk + 1 < NB else None)
            if h == 0:
                self.dump("b_qn%d" % l, qn[0][0:96, :], tq)
                self.dump("b_kn%d" % l, kn[0][0:96, :], tk)
                self.dump("b_vh%d" % l, vh[0], tv)
            if h + 1 < 8:
                b_load(h + 1)
            ct, r0 = h // 2, (h % 2) * 64
            RR = slice(r0, r0 + 64)
            seq = [(b, j) for b in range(NB) for j in range(4 * b + 4)]

            def att_s(i):
                b, j = seq[i]
                jj = j - 4 * b
                c0 = 128 * jj if jj > 0 else 0
                pb = 4 + i % 2
                self.mm(self.ps[pb][:, c0:512], kn[s][0:96, j * 128:(j + 1) * 128],
                        qn[s][0:96, b * 512 + c0:(b + 1) * 512], True, True, [tq, tk], [self.pst[pb]])

            def att_rest(i):
                b, j = seq[i]
                nj = 4 * b + 4
                jj = j - 4 * b
                c0 = 128 * jj if jj > 0 else 0
                pb = 4 + i % 2
                p = pT[i % 4]
                tp = T_("b_pT", i % 4)
                po, pd = (6, 7) if b % 2 == 0 else (2, 3)
                self.act(p[:, c0:512], self.ps[pb][:, c0:512], AF.Exp, [self.pst[pb]], [tp], scale=scale)
                if jj >= 0:
                    self.tt("pool", p[:, c0:c0 + 128], p[:, c0:c0 + 128], self.tri[:, :], ALU.mult,
                            [tp, tcst], [tp])
                self.mm(self.ps[po][RR, c0:512], vh[s][:, j, :], p[:, c0:512], j == 0, j == nj - 1,
                        [tv, tp], [self.pst[po]])
                self.mm(self.ps[pd][RR, c0:512], self.ones[:, 0:64], p[:, c0:512], j == 0, j == nj - 1,
                        [tp], [self.pst[pd]])
                if j == nj - 1:
                    trc, tot = T_("b_rc", b % 2), T_("b_ot", b % 2)
                    rc_, ot_ = rcs[b % 2], ots[b % 2]
                    self.P.op("dve", lambda e, o=rc_[RR, :], i_=self.ps[pd][RR, :]: e.reciprocal(out=o, in_=i_),
                              reads=[self.pst[pd]], writes=[trc])
                    self.tt("dve", ot_[RR, :], self.ps[po][RR, :], rc_[RR, :], ALU.mult, [self.pst[po], trc], [tot])
                    sl = ya[RR, 2 + ct, b * 512:(b + 1) * 512]
                    self.tt("pool", sl, sl, ot_[RR, :], ALU.mult, [tot], [T_("yb", ct, b)])

            att_s(0)
            for i in range(len(seq)):
                if i + 1 < len(seq):
                    att_s(i + 1)
                att_rest(i)
        self.dump("yb%d" % l, ya[:, 2:6, :], T_("yb", 0, 0))

    def stage_p2(self, l, src, dst):
        A, T_, P = self.A, self.T, self.P
        ya = self.yall
        merged = A.alloc([8, T], BF16)
        m2 = A.mark()
        wls = [A.alloc([8, 4, 256], BF16) for _ in range(2)]
        wbs = [A.alloc([10, 256], BF16) for _ in range(2)]
        g = [A.alloc([512], F32) for _ in range(2)]
        macc = A.alloc([512], F32)
        t2 = [A.alloc([512], F32) for _ in range(2)]
        brk = {0: [0, 1], 1: [2, 3, 4, 5], 2: [6, 7], 3: [8, 9]}
        brw = ["w_br_a", "w_br_b", "w_br_c", "w_br_m"]
        tcp = self.tcp
        def p2_load(j2):
            wl_, wb_ = wls[j2 % 2], wbs[j2 % 2]
            twl_, twb_ = T_("p_wl", j2 % 2), T_("p_wb", j2 % 2)
            for br in range(4):
                c0 = OFF["mrg"] + br * 1024 + j2 * 256
                self.wload(wl_[:, :, br, :], self.dr["w_in"][l, WIN_G[c0]], twl_)
            self.wload(wb_[:, 0:6, :], self.dr["w_br"][l, j2, :, 0:6, :], twb_)
            self.wload(wb_[:, 6:10, :], self.dr["w_br"][l, j2, :, 6:10, :], twb_)

        p2_load(0)
        for j2 in range(4):
            if j2 + 1 < 4:
                p2_load(j2 + 1)
            wl, wb = wls[j2 % 2], wbs[j2 % 2]
            twl, twb = T_("p_wl", j2 % 2), T_("p_wb", j2 % 2)
            for jj in range(2):
                j = 2 * j2 + jj
                js = slice(jj * 128, (jj + 1) * 128)
                for blk in range(NB):
                    bs = slice(blk * 512, (blk + 1) * 512)
                    for br in range(4):
                        pl, pp = self.ps[2 * (br % 2)], self.ps[2 * (br % 2) + 1]
                        tpl, tpp = self.pst[2 * (br % 2)], self.pst[2 * (br % 2) + 1]
                        self.mmg(pl[:, :], [(wl[:, kt, br, js], self.hT[:, kt, bs]) for kt in range(8)], [twl], tpl)
                        self.mmg(pp[:, :], [(wb[:, kt, js], ya[:, kt, bs]) for kt in brk[br]], [twb], tpp)
                        gg, tg = g[br % 2], T_("p_g", br % 2)
                        self.act(gg, pl[:, :], AF.Sigmoid, [tpl, tcp], [tg],
                                 bias=self.colp[:, C_BM + br * 8 + j:C_BM + br * 8 + j + 1])
                        tm = T_("p_m")
                        if br == 0:
                            self.tt("dve", macc, pp[:, :], gg, ALU.mult, [tpp, tg], [tm])
                        else:
                            tt2 = T_("p_t2", br % 2)
                            self.tt("dve", t2[br % 2], pp[:, :], gg, ALU.mult, [tpp, tg], [tt2])
                            if br < 3:
                                self.tt("dve", macc, macc, t2[br % 2], ALU.add, [tt2, tm], [tm])
                            else:
                                self.tt("dve", merged[:, j, bs], macc, t2[br % 2], ALU.add, [tt2, tm],
                                        [T_("p_mg", blk)])
        self.dump("merged%d" % l, merged, T_("p_mg", 0))
        P.barrier()
        A.reset(m2)
        wo = A.alloc([8, D], BF16)
        two = T_("p_wo")
        for i in range(4):
            self.wload(wo[:, :, i * 256:(i + 1) * 256],
                       self.dr["w_out"][l, i], two)
        xt = A.alloc([8, 512], F32)
        xo = A.alloc([8, 512], F32)
        txt, txo = T_("p_xt"), T_("p_xo")
        for blk in range(NB):
            bs = slice(blk * 512, (blk + 1) * 512)
            for hh in range(2):
                self.dma(xt[:, hh * 4:(hh + 1) * 4, :], src[blk, :, hh * 4:(hh + 1) * 4, :], [], [txt])
            for d2 in range(8):
                pb = 4 + d2 % 2
                self.mmg(self.ps[pb][:, :], [(wo[:, kt, d2 * 128:(d2 + 1) * 128], merged[:, kt, bs]) for kt in range(8)],
                         [two, T_("p_mg", blk)], self.pst[pb])
                self.tt("dve", xo[:, d2, :], self.ps[pb][:, :], xt[:, d2, :], ALU.add, [self.pst[pb], txt], [txo])
            for hh in range(2):
                op = self.dma(dst[blk, :, hh * 4:(hh + 1) * 4, :], xo[:, hh * 4:(hh + 1) * 4, :], [txo], [], q="act")
                if dst is self.outT:
                    self.finals.append(op)


def make_consts():
    c = np.zeros((128, NCONST), np.float32)
    s = np.arange(128)
    c[:, K_TRI:K_TRI + 128] = (s[:, None] <= s[None, :]).astype(np.float32)
    c[:, K_IOT:K_IOT + 128] = s[None, :].astype(np.float32)
    c[:, K_IOS] = s.astype(np.float32)
    half = 16
    inv = (10000.0 ** (-np.arange(half, dtype=np.float32) / half)).astype(np.float32)
    for r in range(64, 96):
        c[r, K_INVF] = inv[(r - 64) % 16]
        c[r, K_SGN] = -1.0 if (r - 64) < 16 else 1.0
    for r in range(128):
        c[r, K_GM + r // 16] = 1.0
    return c


def host_prep(inp):
    f = lambda k: np.asarray(inp[k], dtype=np.float32)
    perm = np.array(PERM)
    sh = {}

    def rows_t(w):
        r, c = w.shape
        return np.ascontiguousarray(w.reshape(r // 128, 128, c).transpose(1, 0, 2))

    w_in = f("w_in")
    sh["w_in"] = np.stack([np.stack([rows_t(w_in[l][:, c0:c0 + 256]) for c0 in WIN_C0]) for l in range(L)])
    krm = np.zeros((L, D, 96), np.float32)
    krp = np.zeros((L, D, 96), np.float32)
    krm[:, :, 64:96] = w_in[:, :, OFF["kr"]:OFF["kr"] + 32]
    krp[:, :, 64:96] = w_in[:, :, OFF["kr"] + perm]
    sh["w_krm"] = np.stack([rows_t(krm[l]) for l in range(L)])
    sh["w_krp"] = np.stack([rows_t(krp[l]) for l in range(L)])
    wuq = f("b_w_uq").reshape(L, 768, 8, 96)
    wqp = np.zeros((L, 768, 8, 96), np.float32)
    wqp[:, :, :, 64:96] = wuq[:, :, :, 64 + perm]
    sh["wq_m"] = np.stack([np.stack([rows_t(wuq[l][:, h, :]) for h in range(8)]) for l in range(L)])
    sh["wq_p"] = np.stack([np.stack([rows_t(wqp[l][:, h, :]) for h in range(8)]) for l in range(L)])
    ukv = f("b_w_ukv")
    sh["w_ukv"] = np.stack([np.stack([rows_t(ukv[l][:, h * 128:(h + 1) * 128]) for h in range(8)]) for l in range(L)])
    sh["a_w_sT"] = np.ascontiguousarray(f("a_w_s").transpose(0, 3, 1, 2))
    c_re, c_im = f("c_c_re"), f("c_c_im")
    cre = np.zeros((L, 8, 128, 128), np.float32)
    cim = np.zeros((L, 8, 128, 128), np.float32)
    for j in range(8):
        for gl in range(2):
            g = 2 * j + gl
            col0 = 16 * (g % 8)
            cre[:, j, gl * 64:(gl + 1) * 64, col0:col0 + 16] = c_re[:, g].transpose(0, 2, 1)
            cim[:, j, gl * 64:(gl + 1) * 64, col0:col0 + 16] = c_im[:, g].transpose(0, 2, 1)
    sh["c_reT"] = np.ascontiguousarray(cre.reshape(L, 2, 4, 128, 128).transpose(0, 1, 3, 2, 4))
    sh["c_imT"] = np.ascontiguousarray(cim.reshape(L, 2, 4, 128, 128).transpose(0, 1, 3, 2, 4))
    sh["c_w_glu"] = np.stack([rows_t(f("c_w_glu")[l]) for l in range(L)])
    mkv = f("m_w_kv")
    sh["m_w_kv"] = np.stack([np.stack([rows_t(mkv[l][:, i * 256:(i + 1) * 256]) for i in range(2)]) for l in range(L)])
    wbr = np.concatenate([f("w_br_a"), f("w_br_b"), f("w_br_c"), f("w_br_m")], axis=1)
    sh["w_br"] = np.stack([np.stack([rows_t(wbr[l][:, i * 256:(i + 1) * 256]) for i in range(4)]) for l in range(L)])
    wo = f("w_out")
    sh["w_out"] = np.stack([np.stack([rows_t(wo[l][:, i * 256:(i + 1) * 256]) for i in range(4)]) for l in range(L)])
    cp = np.zeros((L, 128, NCOL), np.float32)
    cp[:, :, C_NG:C_NG + 8] = f("norm_g").reshape(L, 8, 128).transpose(0, 2, 1)
    cp[:, :, C_MNG:C_MNG + 8] = f("m_norm_g").reshape(L, 8, 128).transpose(0, 2, 1)
    cp[:, :, C_QNG:C_QNG + 6] = f("b_q_norm_g").reshape(L, 6, 128).transpose(0, 2, 1)
    cp[:, :, C_KVNG:C_KVNG + 2] = f("b_kv_norm_g").reshape(L, 2, 128).transpose(0, 2, 1)
    cp[:, :, C_BM:C_BM + 32] = f("b_merge").reshape(L, 32, 128).transpose(0, 2, 1)
    gq, gk = f("b_qk_g_q"), f("b_qk_g_k")
    cp[:, 0:96, C_GQ] = gq
    cp[:, 64:96, C_GQP] = gq[:, 64 + perm]
    cp[:, 0:96, C_GK] = gk
    cp[:, 64:96, C_GKP] = gk[:, 64 + perm]
    cp[:, :, C_MGQ] = np.tile(f("m_qk_g_q"), (1, 2))
    cp[:, :, C_MGK] = np.tile(f("m_qk_g_k"), (1, 2))
    cp[:, :, C_CD:C_CD + 2] = f("c_d").reshape(L, 2, 128).transpose(0, 2, 1)
    cp[:, :, C_BGLU:C_BGLU + 2] = f("c_b_glu").reshape(L, 2, 128).transpose(0, 2, 1)
    a_re, a_im, ldt = f("c_a_re"), f("c_a_im"), f("c_log_dt")
    cp[:, :, C_SRE:C_SRE + 8] = a_re.reshape(L, 8, 128).transpose(0, 2, 1)
    cp[:, :, C_SIM:C_SIM + 8] = a_im.reshape(L, 8, 128).transpose(0, 2, 1)
    ldt_rep = np.repeat(ldt[:, :, None], 64, axis=2)
    cp[:, :, C_SDT:C_SDT + 8] = ldt_rep.reshape(L, 8, 128).transpose(0, 2, 1)
    abs_ = f("a_b_s")
    for ct in range(2):
        for gl in range(2):
            cp[:, gl * 64:(gl + 1) * 64, C_ABS + ct * 128:C_ABS + (ct + 1) * 128] = abs_[:, 2 * ct + gl][:, None, :]
    b_re, b_im = f("c_b_re"), f("c_b_im")
    for ct in range(2):
        base = C_ROW + ct * 320
        for g8 in range(8):
            g = 8 * ct + g8
            rows = slice(16 * g8, 16 * g8 + 16)
            cp[:, rows, base + 0:base + 64] = a_re[:, g][:, None, :]
            cp[:, rows, base + 64:base + 128] = a_im[:, g][:, None, :]
            cp[:, rows, base + 128:base + 192] = ldt[:, g][:, None, None]
            cp[:, rows, base + 192:base + 256] = b_re[:, g].transpose(0, 2, 1)
            cp[:, rows, base + 256:base + 320] = b_im[:, g].transpose(0, 2, 1)
    sh["colpack"] = cp
    rp = np.zeros((L, 128, NROW), np.float32)
    rp[:, :, R_ANG:R_ANG + 256] = f("a_norm_g")[:, None, :]
    rp[:, :, R_SRE:R_SRE + 1024] = a_re.reshape(L, 1, 1024)
    rp[:, :, R_SIM:R_SIM + 1024] = a_im.reshape(L, 1, 1024)
    rp[:, :, R_SDT:R_SDT + 1024] = ldt_rep.reshape(L, 1, 1024)
    sh["rowpack"] = rp
    sh["consts"] = make_consts()
    x = f("x")
    mem = f("mem")
    pos = np.asarray(inp["positions"]).astype(np.int32)
    per_core = []
    for b in range(8):
        d = dict(sh)
        d["xT"] = tile_x(x[b])
        d["memT"] = np.ascontiguousarray(mem[b].T.reshape(8, 128, 1, 256).transpose(2, 1, 0, 3))
        d["pos"] = np.ascontiguousarray(pos[b][None, :])
        per_core.append(d)
    return per_core


def tile_x(xb):
    return np.ascontiguousarray(xb.T.reshape(8, 128, NB, 512).transpose(2, 1, 0, 3))


def untile_x(t):
    return np.ascontiguousarray(t.transpose(2, 1, 0, 3).reshape(D, T).T)


_CACHE = {}
LAYER_KEYS = ("colpack", "rowpack", "w_in", "w_krm", "w_krp", "wq_m", "wq_p", "w_ukv", "a_w_sT", "c_reT", "c_imT",
              "c_w_glu", "m_w_kv", "w_br", "w_out")
FUSED = True


def kernel(**inputs):
    in_maps = host_prep(inputs)
    if FUSED:
        if "nc" not in _CACHE:
            _CACHE["nc"] = Builder(nlayers=L).build()
        res = run_bass_kernel_spmd(_CACHE["nc"], in_maps, core_ids=list(range(8)))
        return np.stack([untile_x(r["outT"]) for r in res.results], axis=0).astype(np.float32)
    if "nc1" not in _CACHE:
        _CACHE["nc1"] = Builder(nlayers=1).build()
    nc = _CACHE["nc1"]
    xs = [m["xT"] for m in in_maps]
    for l in range(L):
        maps = []
        for c in range(8):
            d = dict(in_maps[c])
            for k in LAYER_KEYS:
                a = in_maps[c][k]
                d[k] = np.ascontiguousarray(np.concatenate([a[l:l + 1], a[l:l + 1]], axis=0))
            d["xT"] = xs[c]
            maps.append(d)
        res = run_bass_kernel_spmd(nc, maps, core_ids=list(range(8)))
        xs = [np.ascontiguousarray(r["outT"]) for r in res.results]
    return np.stack([untile_x(t) for t in xs], axis=0).astype(np.float32)
```

```python
import math
import contextlib
import numpy as np
import concourse.bass as bass
import concourse.mybir as mybir
from concourse.bass_utils import run_bass_kernel_spmd

F32 = mybir.dt.float32
BF16 = mybir.dt.bfloat16
I32 = mybir.dt.int32
AF = mybir.ActivationFunctionType
ALU = mybir.AluOpType

D = 1024
T = 2048
L = 2
NB = 4
EPS = 1e-6
IN_W = 7456
OFF = dict(a_u=0, a_v=256, a_g=512, cq=768, ckv=1536, kr=1792, bg=1824,
           cin=2336, cg=2592, mq=2848, mg=3104, mrg=3360)
PERM = list(range(16, 32)) + list(range(0, 16))
WIN_C0 = [0, 256, 512, 768, 1024, 1280, 1536, 1824, 2080, 2336, 2592, 2848, 3104] + \
         [3360 + br * 1024 + j2 * 256 for br in range(4) for j2 in range(4)]
WIN_G = {c: i for i, c in enumerate(WIN_C0)}
NWG = len(WIN_C0)
TWO_PI = 2.0 * math.pi

C_NG, C_MNG, C_QNG, C_KVNG, C_BM = 0, 8, 16, 22, 24
C_GQ, C_GQP, C_GK, C_GKP, C_MGQ, C_MGK = 56, 57, 58, 59, 60, 61
C_CD, C_BGLU, C_SRE, C_SIM, C_SDT, C_ABS, C_ROW = 62, 64, 66, 74, 82, 90, 346
NCOL = 346 + 640
R_ANG, R_SRE, R_SIM, R_SDT = 0, 256, 1280, 2304
NROW = 3328
K_TRI, K_IOT, K_IOS, K_INVF, K_SGN, K_GM = 0, 128, 256, 257, 258, 259
NCONST = 267


class Tok:
    __slots__ = ("w", "rs", "excl")

    def __init__(self):
        self.w = None
        self.rs = {}
        self.excl = False


class Op:
    __slots__ = ("eng", "fn", "deps", "is_dma", "sig", "users")

    def __init__(self, eng, fn, is_dma):
        self.eng = eng
        self.fn = fn
        self.is_dma = is_dma
        self.deps = []
        self.sig = None
        self.users = 0


ENGS = ("pe", "act", "dve", "pool", "sp")
SYNC_ALL = True
WQ = ("sp",)
DMAQ = ("sp", "act", "pool")


class Prog:
    def __init__(self, nc, n_dma_sems=8):
        self.nc = nc
        self.ops = {e: [] for e in ENGS}
        self.all = []
        self.n_dma_sems = n_dma_sems
        self.toks = {}
        self.dmas_since_bar = []

    def tok(self, *key):
        t = self.toks.get(key)
        if t is None:
            t = self.toks[key] = Tok()
        return t

    def _add(self, eng, fn, reads, writes, is_dma, extra_deps=()):
        op = Op(eng, fn, is_dma)
        deps = list(extra_deps)
        raw = set()
        for t in reads:
            if t.w is not None:
                deps.append(t.w)
                raw.add(id(t.w))
            if t.excl:
                deps.extend(o for o in t.rs.values() if o.eng != eng)
        for t in writes:
            if t.w is not None:
                deps.append(t.w)
            deps.extend(t.rs.values())
        rkey = ("dma", id(op)) if is_dma else eng
        for t in reads:
            t.rs[rkey] = op
        for t in writes:
            t.w = op
            t.rs = {}
        seen = set()
        for d in deps:
            if d is op or id(d) in seen:
                continue
            seen.add(id(d))
            if (not d.is_dma) and (not is_dma) and d.eng == eng:
                if eng == "pe" or (id(d) not in raw and not SYNC_ALL):
                    continue
            if (not d.is_dma) and is_dma and d.eng == eng:
                pass
            op.deps.append(d)
            d.users += 1
        self.ops[eng].append(op)
        self.all.append(op)
        if is_dma:
            self.dmas_since_bar.append(op)
        return op

    def op(self, eng, fn, reads=(), writes=()):
        return self._add(eng, fn, reads, writes, False)

    def dma(self, eng, fn, reads=(), writes=()):
        assert eng in DMAQ
        return self._add(eng, fn, reads, writes, True)

    def barrier(self):
        dm = self.dmas_since_bar
        self.dmas_since_bar = []
        bt = [self.tok("__bar", e) for e in ENGS]
        for i, e in enumerate(ENGS):
            self._add(e, lambda eng: eng.drain(), [], [bt[i]], False,
                      extra_deps=dm if e == "sp" else ())
        for e in ENGS:
            self._add(e, lambda eng: eng.nop(), bt, [], False)

    def emit(self, final_wait_ops=()):
        nc = self.nc
        with contextlib.ExitStack() as es:
            esem = {e: es.enter_context(nc.semaphore("s_" + e)) for e in ENGS}
            dsem = {e: [es.enter_context(nc.semaphore("d_%s%d" % (e, i)))
                        for i in range(self.n_dma_sems)] for e in DMAQ}
            ecount = {e: 0 for e in ENGS}
            dcount = {e: 0 for e in DMAQ}
            duse = {e: [0] * self.n_dma_sems for e in DMAQ}
            fw = set(id(o) for o in final_wait_ops)
            for op in self.all:
                if op.is_dma:
                    j = dcount[op.eng]
                    dcount[op.eng] += 1
                    s = j % self.n_dma_sems
                    duse[op.eng][s] += 1
                    op.sig = (dsem[op.eng][s], 16 * duse[op.eng][s], ("d", op.eng, s))
                elif op.users > 0 or id(op) in fw:
                    ecount[op.eng] += 1
                    op.sig = (esem[op.eng], ecount[op.eng], ("e", op.eng))
            self.stats = dict(ecount=ecount, dcount=dcount,
                              nops={e: len(self.ops[e]) for e in ENGS})
            block = es.enter_context(nc.Block())

            def run_engine(e, engine):
                waited = {}

                def wait(sem, val, key):
                    if waited.get(key, 0) >= val:
                        return
                    waited[key] = val
                    engine.wait_ge(sem, val)

                for op in self.ops[e]:
                    for d in op.deps:
                        wait(*d.sig)
                    if op.is_dma:
                        sem, val, key = op.sig
                        if val > 16:
                            wait(sem, val - 16, key)
                    ins = op.fn(engine)
                    if op.sig is not None:
                        ins.then_inc(op.sig[0], 16 if op.is_dma else 1)
                if e == "sp":
                    for op in final_wait_ops:
                        wait(*op.sig)

            block.tensor(lambda eng: run_engine("pe", eng))
            block.scalar(lambda eng: run_engine("act", eng))
            block.vector(lambda eng: run_engine("dve", eng))
            block.gpsimd(lambda eng: run_engine("pool", eng))
            block.sync(lambda eng: run_engine("sp", eng))


class Arena:
    def __init__(self, ap2d, nfloats):
        self.a = ap2d
        self.n = nfloats
        self.off = 0
        self.peak = 0

    def alloc(self, free_shape, dt):
        free_shape = list(free_shape)
        isz = 2 if dt == BF16 else 4
        n = int(np.prod(free_shape))
        n32 = (n * isz + 3) // 4
        assert self.off + n32 <= self.n, ("arena overflow", self.off, n32, self.n)
        v = self.a[:, self.off:self.off + n32]
        self.off += n32
        self.peak = max(self.peak, self.off)
        if dt == BF16:
            v = v.bitcast(BF16)[:, 0:n]
        elif dt == I32:
            v = v.bitcast(I32)
        if len(free_shape) == 2:
            v = v.rearrange("p (a b) -> p a b", a=free_shape[0])
        elif len(free_shape) == 3:
            v = v.rearrange("p (a b c) -> p a b c", a=free_shape[0], b=free_shape[1])
        elif len(free_shape) == 4:
            v = v.rearrange("p (a b c d) -> p a b c d", a=free_shape[0], b=free_shape[1],
                            c=free_shape[2])
        return v

    def mark(self):
        return self.off

    def reset(self, m):
        self.off = m


class Builder:
    def __init__(self, nlayers=L, stop=None, dumps=()):
        self.nlayers = nlayers
        self.stop = stop
        self.dumps = dict()
        self.want = set(dumps)
        self.nc = nc = bass.Bass("TRN2", target_bir_lowering=False)
        self.P = Prog(nc)
        self.dr = {}
        self.finals = []
        self.dump_specs = []

    def din(self, name, shape, dt=F32):
        self.dr[name] = self.nc.dram_tensor(name, list(shape), dt, kind="ExternalInput").ap()
        return self.dr[name]

    def T(self, *k):
        return self.P.tok(*k)

    def mm(self, out, lhsT, rhs, start, stop, reads, writes):
        return self.P.op("pe", lambda e: e.matmul(out, lhsT=lhsT, rhs=rhs, start=start, stop=stop),
                         reads=reads, writes=writes)

    def mmg(self, out, pairs, reads, wtok):
        n = len(pairs)
        for i, (a, b) in enumerate(pairs):
            self.mm(out, a, b, i == 0, i == n - 1, reads, [wtok])

    def act(self, out, in_, func, reads, writes, bias=0.0, scale=1.0, eng="act", accum_out=None):
        if accum_out is None:
            return self.P.op("act", lambda e: e.activation(out=out, in_=in_, func=func, bias=bias, scale=scale),
                             reads=reads, writes=writes)
        return self.P.op("act", lambda e: e.activation(out=out, in_=in_, func=func, bias=bias, scale=scale,
                                                       accum_out=accum_out),
                         reads=reads, writes=writes)

    def tt(self, eng, out, in0, in1, op, reads, writes):
        return self.P.op(eng, lambda e: e.tensor_tensor(out=out, in0=in0, in1=in1, op=op),
                         reads=reads, writes=writes)

    def ts(self, eng, out, in0, s1, op0, reads, writes, s2=None, op1=None):
        if op1 is None:
            return self.P.op(eng, lambda e: e.tensor_scalar(out=out, in0=in0, scalar1=s1, scalar2=None, op0=op0),
                             reads=reads, writes=writes)
        return self.P.op(eng, lambda e: e.tensor_scalar(out=out, in0=in0, scalar1=s1, scalar2=s2, op0=op0, op1=op1),
                         reads=reads, writes=writes)

    def stt(self, eng, out, in0, scalar, in1, op0, op1, reads, writes):
        return self.P.op(eng, lambda e: e.scalar_tensor_tensor(out=out, in0=in0, scalar=scalar, in1=in1,
                                                                op0=op0, op1=op1),
                         reads=reads, writes=writes)

    def cp(self, eng, out, in_, reads, writes):
        if eng == "act":
            return self.P.op("act", lambda e: e.copy(out=out, in_=in_), reads=reads, writes=writes)
        return self.P.op(eng, lambda e: e.tensor_copy(out=out, in_=in_), reads=reads, writes=writes)

    def memset(self, eng, ap, val, writes):
        return self.P.op(eng, lambda e: e.memset(ap, val), reads=(), writes=writes)

    def dma(self, out, in_, reads, writes, q="sp"):
        def ndesc(ap):
            dims = [(int(st), int(n)) for st, n in ap.ap]
            total = 1
            for st, n in dims:
                total *= n
            run = 1
            for st, n in reversed(dims[1:]):
                if st == run:
                    run *= n
                else:
                    break
            return total // run
        self.desc_count = getattr(self, "desc_count", {})
        self.desc_count[q] = self.desc_count.get(q, 0) + max(ndesc(out), ndesc(in_))
        return self.P.dma(q, lambda e: e.dma_start(out=out, in_=in_), reads=reads, writes=writes)

    def rstd(self, out, in_, n, reads, writes):
        self.act(out, in_, AF.Ln, reads, writes, bias=EPS, scale=1.0 / n)
        self.act(out, out, AF.Exp, writes, writes, scale=-0.5)

    def wload(self, dst, src, dtok, cast_eng="pool", scale=None):
        fs = list(dst.shape[1:])
        n = int(np.prod(fs))
        assert n <= self.stage_n, (n, self.stage_n)
        slot = self.wslot
        self.wslot = (slot + 1) % len(self.stage)
        st = self.stage[slot][:, 0:n]
        if len(fs) == 2:
            st = st.rearrange("p (a b) -> p a b", a=fs[0])
        elif len(fs) == 3:
            st = st.rearrange("p (a b c) -> p a b c", a=fs[0], b=fs[1])
        stok = self.T("stage", slot)
        self.wq_i = getattr(self, "wq_i", 0) + 1
        self.dma(st, src, [], [stok], q=WQ[self.wq_i % len(WQ)])
        if scale is None:
            self.cp(cast_eng, dst, st, [stok], [dtok])
        else:
            self.ts(cast_eng, dst, st, scale, ALU.mult, [stok], [dtok])

    def dump(self, name, ap, tok):
        if name not in self.want:
            return
        self.P.barrier()
        shp = list(ap.shape)
        d = self.nc.dram_tensor("dbg_" + name, shp, ap.dtype, kind="ExternalOutput").ap()
        op = self.dma(d, ap, [tok], [])
        self.finals.append(op)
        self.dump_specs.append(name)

    def build(self):
        nc = self.nc
        P = self.P
        din = self.din
        din("xT", [NB, 128, 8, 512])
        din("memT", [1, 128, 8, 256])
        din("pos", [1, T], I32)
        din("consts", [128, NCONST])
        din("colpack", [L, 128, NCOL])
        din("rowpack", [L, 128, NROW])
        din("w_in", [L, NWG, 128, 8, 256])
        din("w_krm", [L, 128, 8, 96])
        din("w_krp", [L, 128, 8, 96])
        din("wq_m", [L, 8, 128, 6, 96])
        din("wq_p", [L, 8, 128, 6, 96])
        din("w_ukv", [L, 8, 128, 2, 128])
        din("a_w_sT", [L, 128, 4, 128])
        din("c_reT", [L, 2, 128, 4, 128])
        din("c_imT", [L, 2, 128, 4, 128])
        din("c_w_glu", [L, 128, 2, 256])
        din("m_w_kv", [L, 2, 128, 8, 256])
        din("w_br", [L, 4, 128, 10, 256])
        din("w_out", [L, 4, 128, 8, 256])
        self.outT = nc.dram_tensor("outT", [NB, 128, 8, 512], F32, kind="ExternalOutput").ap()
        self.x1T = nc.dram_tensor("x1T", [NB, 128, 8, 512], F32, kind="Internal").ap()

        with contextlib.ExitStack() as es:
            NA = 52000
            arena_t = es.enter_context(nc.sbuf_tensor("arena", [128, NA], F32))
            self.A = A = Arena(arena_t[:, :], NA)
            self.ps = [es.enter_context(nc.psum_tensor("ps%d" % i, [128, 512], F32)) for i in range(8)]
            self.pst = [self.T("ps", i) for i in range(8)]
            for t in self.pst:
                t.excl = True

            self.cst = A.alloc([NCONST], F32)
            self.eps_col = A.alloc([1], F32)
            self.tri = A.alloc([128], BF16)
            self.ntri = A.alloc([128], BF16)
            self.ones = A.alloc([128], BF16)
            self.bd64 = A.alloc([128], BF16)
            self.ones_f = A.alloc([128], F32)
            self.cosT = A.alloc([T], F32)
            self.sinT = A.alloc([T], F32)
            self.colp = A.alloc([NCOL], F32)
            self.hT = A.alloc([8, T], BF16)
            self.yall = A.alloc([10, T], BF16)
            self.kmem = A.alloc([2, 256], BF16)
            self.vmem = A.alloc([2, 256], BF16)
            self.stage_n = 2048
            self.stage = [A.alloc([self.stage_n], F32) for _ in range(2)]
            self.wslot = 0
            self.base_mark = A.mark()

            self.setup_consts()
            if self.stop != "consts":
                for l in range(self.nlayers):
                    if self.run_layer(l):
                        break
            P.barrier()
            P.emit(final_wait_ops=self.finals)
        return nc

    def setup_consts(self):
        A, T_ = self.A, self.T
        tc = T_("cst")
        self.dma(self.cst, self.dr["consts"][:, :], [], [tc])
        self.memset("dve", self.eps_col, EPS, [tc])
        self.cp("dve", self.tri, self.cst[:, K_TRI:K_TRI + 128], [tc], [tc])
        self.ts("dve", self.ntri, self.cst[:, K_TRI:K_TRI + 128], -1.0, ALU.mult, [tc], [tc])
        self.memset("dve", self.ones, 1.0, [tc])
        self.memset("dve", self.ones_f, 1.0, [tc])
        self.memset("dve", self.bd64, 0.0, [tc])
        self.memset("dve", self.bd64[0:64, 0:64], 1.0, [tc])
        self.memset("dve", self.bd64[64:128, 64:128], 1.0, [tc])
        if getattr(self, "skip", None):
            self.memset("pool", self.yall, 0.25, [T_("yall_init")])
        m = A.mark()
        posi = A.alloc([T], I32)
        ang = A.alloc([T], F32)
        kf = A.alloc([T], F32)
        ki = A.alloc([T], I32)
        tr = T_("rope")
        self.dma(posi[0:96, :], self.dr["pos"][0:1, :].partition_broadcast(96), [], [tr])
        R = slice(64, 96)
        self.cp("dve", ang[R, :], posi[R, :], [tr], [tr])
        self.ts("dve", ang[R, :], ang[R, :], self.cst[R, K_INVF:K_INVF + 1], ALU.mult, [tr, tc], [tr])

        def sin_of(dst, shift, post_scale_col):
            self.ts("dve", ki[R, :], ang[R, :], shift, ALU.add, [tr], [tr], s2=1.0 / TWO_PI, op1=ALU.mult)
            self.cp("dve", kf[R, :], ki[R, :], [tr], [tr])
            self.stt("dve", kf[R, :], kf[R, :], -TWO_PI, ang[R, :], ALU.mult, ALU.add, [tr], [tr])
            self.ts("dve", kf[R, :], kf[R, :], shift, ALU.add, [tr], [tr], s2=math.pi, op1=ALU.min)
            self.ts("dve", kf[R, :], kf[R, :], -math.pi, ALU.max, [tr], [tr])
            self.act(dst[R, :], kf[R, :], AF.Sin, [tr], [tr])
            if post_scale_col is not None:
                self.ts("dve", dst[R, :], dst[R, :], post_scale_col, ALU.mult, [tr, tc], [tr])

        sin_of(self.sinT, 0.0, self.cst[R, K_SGN:K_SGN + 1])
        sin_of(self.cosT, math.pi / 2, None)
        self.dump("cosT", self.cosT[R, :], tr)
        self.dump("sinT", self.sinT[R, :], tr)
        self.P.barrier()
        A.reset(m)

    def run_layer(self, l):
        P, A, T_ = self.P, self.A, self.T
        src = self.dr["xT"] if l == 0 else self.x1T
        dst = self.x1T if l == 0 else self.outT
        if self.nlayers == 1:
            dst = self.outT
        tcp = T_("colp")
        self.dma(self.colp, self.dr["colpack"][l, :, :], [], [tcp])
        self.tcp = tcp
        stages = [("N", self.stage_norm), ("MP", self.stage_memprep), ("A", self.stage_a),
                  ("C", self.stage_c), ("M", self.stage_m), ("B", self.stage_b),
                  ("P2", self.stage_p2)]
        for name, fn in stages:
            if name in getattr(self, "skip", ()):
                continue
            m = A.mark()
            if name == "N":
                fn(l, src)
            elif name == "P2":
                fn(l, src, dst)
            else:
                fn(l)
            P.barrier()
            A.reset(m)
            if self.stop == (name, l):
                return True
        return False

    def norm_fm(self, srcT, n_tok, gcol, dst, dst_tok_fn, tag):
        A, T_ = self.A, self.T
        W = min(512, n_tok)
        nblk = n_tok // W
        xb = [A.alloc([8, W], F32) for _ in range(2)]
        sq = A.alloc([8, W], BF16)
        rs = A.alloc([W], F32)
        for b in range(nblk):
            x = xb[b % 2]
            tx = T_(tag + "x", b % 2)
            ts_ = T_(tag + "sq")
            trs = T_(tag + "rs")
            for hh in range(2):
                self.dma(x[:, hh * 4:(hh + 1) * 4, :], srcT[b, :, hh * 4:(hh + 1) * 4, :], [], [tx])
            pb = 0
            for kt in range(8):
                self.act(sq[:, kt, :], x[:, kt, :], AF.Square, [tx], [ts_])
            self.mmg(self.ps[pb][:, 0:W], [(self.ones[:, :], sq[:, kt, :]) for kt in range(8)],
                     [ts_], self.pst[pb])
            self.rstd(rs[:, :], self.ps[pb][:, 0:W], float(D), [self.pst[pb]], [trs])
            for kt in range(8):
                self.stt("dve", dst[:, kt, b * W:(b + 1) * W], x[:, kt, :], gcol[:, kt:kt + 1], rs[:, :],
                         ALU.mult, ALU.mult, [tx, trs, self.tcp], [dst_tok_fn(b)])

    def stage_norm(self, l, src):
        self.norm_fm(src, T, self.colp[:, C_NG:C_NG + 8], self.hT, lambda b: self.T("hT", b), "n")
        for b in range(NB):
            self.dump("hT%d_%d" % (l, b), self.hT[:, :, b * 512:(b + 1) * 512], self.T("hT", b))

    def load_win(self, l, c0, ncols, wt, wtok):
        assert ncols == 256
        self.wload(wt[:, :, 0:ncols], self.dr["w_in"][l, WIN_G[c0]], wtok)

    def zproj_blk(self, wt, ncols, blk, pb, wtok, m_off=0):
        self.mmg(self.ps[pb][m_off:m_off + ncols, :],
                 [(wt[:, kt, 0:ncols], self.hT[:, kt, blk * 512:(blk + 1) * 512]) for kt in range(8)],
                 [wtok, self.T("hT", blk)], self.pst[pb])

    def stage_memprep(self, l):
        A, T_ = self.A, self.T
        hm = A.alloc([8, 256], BF16)
        thm = T_("hm")
        self.norm_fm(self.dr["memT"], 256, self.colp[:, C_MNG:C_MNG + 8], hm, lambda b: thm, "m")
        wkv = A.alloc([8, 512], BF16)
        twk = T_("wkv")
        for half in range(2):
            self.wload(wkv[:, :, half * 256:(half + 1) * 256],
                       self.dr["m_w_kv"][l, half],
                       twk)
        sq = A.alloc([256], BF16)
        rs = A.alloc([256], F32)
        tsq, trs, tkm, tvm = T_("mp_sq"), T_("mp_rs"), T_("kmem"), T_("vmem")
        for ct in range(2):
            self.mmg(self.ps[0][:, 0:256], [(wkv[:, kt, ct * 128:(ct + 1) * 128], hm[:, kt, :]) for kt in range(8)],
                     [twk, thm], self.pst[0])
            self.act(sq[:, :], self.ps[0][:, 0:256], AF.Square, [self.pst[0]], [tsq])
            self.mmg(self.ps[1][:, 0:256], [(self.bd64[:, :], sq[:, :])], [tsq, T_("cst")], self.pst[1])
            self.rstd(rs[:, :], self.ps[1][:, 0:256], 64.0, [self.pst[1]], [trs])
            self.stt("dve", self.kmem[:, ct, :], self.ps[0][:, 0:256], self.colp[:, C_MGK:C_MGK + 1], rs[:, :],
                     ALU.mult, ALU.mult, [self.pst[0], trs, self.tcp], [tkm])
        for mt in range(2):
            self.mmg(self.ps[2][:, 0:256], [(hm[:, kt, mt * 128:(mt + 1) * 128], wkv[:, kt, 256:512]) for kt in range(8)],
                     [twk, thm], self.pst[2])
            self.cp("dve", self.vmem[:, mt, :], self.ps[2][:, 0:256], [self.pst[2]], [tvm])
        self.dump("kmem%d" % l, self.kmem, tkm)
        self.dump("vmem%d" % l, self.vmem, tvm)

    def stage_a(self, l):
        A, T_ = self.A, self.T
        ya = self.yall
        wt = [A.alloc([8, 256], BF16) for _ in range(2)]
        gnb = A.alloc([256], F32)
        wsT = A.alloc([4, 128], BF16)
        tmp = A.alloc([512], BF16)
        tgn, tws, ttmp = T_("a_gn"), T_("a_ws"), T_("a_tmp")
        self.dma(gnb, self.dr["rowpack"][l, :, R_ANG:R_ANG + 256], [], [tgn])
        slot = self.wslot
        self.wslot = (slot + 1) % 2
        st = self.stage[slot][:, 0:512].rearrange("p (g t) -> p g t", g=4)
        stok = T_("stage", slot)
        self.dma(st, self.dr["a_w_sT"][l], [], [stok])
        for g in range(4):
            self.tt("dve", wsT[:, g, :], st[:, g, :], self.cst[:, K_TRI:K_TRI + 128], ALU.mult, [stok, T_("cst")], [tws])
        tw0, tw1 = T_("a_w", 0), T_("a_w", 1)
        self.load_win(l, OFF["a_u"], 256, wt[0], tw0)
        self.load_win(l, OFF["a_g"], 256, wt[1], tw1)
        for ct in range(2):
            for blk in range(NB):
                pb = (ct * NB + blk) % 2
                self.mmg(self.ps[pb][:, :],
                         [(wt[0][:, kt, ct * 128:(ct + 1) * 128], self.hT[:, kt, blk * 512:(blk + 1) * 512])
                          for kt in range(8)], [tw0, T_("hT", blk)], self.pst[pb])
                self.act(ya[:, ct, blk * 512:(blk + 1) * 512], self.ps[pb][:, :], AF.Gelu_apprx_tanh,
                         [self.pst[pb]], [T_("ya", ct, blk)])
        for ct in range(2):
            for blk in range(NB):
                pb = 2 + (ct * NB + blk) % 2
                self.mmg(self.ps[pb][:, :],
                         [(wt[1][:, kt, ct * 128:(ct + 1) * 128], self.hT[:, kt, blk * 512:(blk + 1) * 512])
                          for kt in range(8)], [tw1, T_("hT", blk)], self.pst[pb])
                self.act(tmp[:, :], self.ps[pb][:, :], AF.Silu, [self.pst[pb]], [ttmp])
                sl = ya[:, ct, blk * 512:(blk + 1) * 512]
                self.tt("pool", sl, sl, tmp[:, :], ALU.mult, [ttmp], [T_("ya", ct, blk)])
        twv = T_("a_w", 0)
        self.load_win(l, OFF["a_v"], 256, wt[0], twv)
        gvs = [A.alloc([256], F32) for _ in range(2)]
        ssqs = [A.alloc([1], F32) for _ in range(2)]
        vn = [A.alloc([256], BF16) for _ in range(2)]
        sbs = [A.alloc([2, 128], F32) for _ in range(2)]
        junk = A.alloc([256], BF16)
        absT = self.colp[:, C_ABS:C_ABS + 256].rearrange("p (c t) -> p c t", c=2)
        def a_front(tt_):
            blk = tt_ // 4
            pb = 4 + tt_ % 2
            self.mmg(self.ps[pb][:, 0:256],
                     [(self.hT[:, kt, tt_ * 128:(tt_ + 1) * 128], wt[0][:, kt, :]) for kt in range(8)],
                     [twv, T_("hT", blk)], self.pst[pb])

        def a_back(tt_):
            blk = tt_ // 4
            k = tt_ % 2
            pb = 4 + k
            gv, ssq, sb = gvs[k], ssqs[k], sbs[k]
            tgv, tss, tsb = T_("a_gv", k), T_("a_ss", k), T_("a_sb", k)
            self.act(gv[:, :], self.ps[pb][:, 0:256], AF.Gelu_apprx_tanh, [self.pst[pb]], [tgv])
            self.act(junk[:, :], gv[:, :], AF.Square, [tgv], [tss], accum_out=ssq[:, :])
            self.rstd(ssq[:, :], ssq[:, :], 256.0, [tss], [tss])
            v = vn[k]
            tv = T_("a_vn", k)
            self.stt("dve", v[:, :], gv[:, :], ssq[:, 0:1], gnb[:, :], ALU.mult, ALU.mult, [tgv, tss, tgn], [tv])
            pq = 6 + k
            for g in range(4):
                ct, r0 = g // 2, (g % 2) * 64
                self.mm(self.ps[pq][r0:r0 + 64, ct * 128:(ct + 1) * 128], v[:, g * 64:(g + 1) * 64], wsT[:, g, :],
                        True, True, [tv, tws], [self.pst[pq]])
            psv = self.ps[pq][:, 0:256].rearrange("p (c t) -> p c t", c=2)
            self.tt("dve", sb[:, :, :], psv, absT, ALU.add, [self.pst[pq], self.tcp], [tsb])
            sl = ya[:, 0:2, tt_ * 128:(tt_ + 1) * 128]
            self.tt("pool", sl, sl, sb[:, :, :], ALU.mult, [tsb], [T_("ya", 0, blk), T_("ya", 1, blk)])

        a_front(0)
        for tt_ in range(16):
            if tt_ + 1 < 16:
                a_front(tt_ + 1)
            a_back(tt_)
        self.dump("ya%d" % l, ya[:, 0:2, :], T_("ya", 0, 0))

    def sin_of(self, dst, ang, shift, ki, kf, tok, extra=()):
        rd = [tok] + list(extra)
        self.ts("dve", ki, ang, shift, ALU.add, rd, [tok], s2=1.0 / TWO_PI, op1=ALU.mult)
        self.cp("dve", kf, ki, [tok], [tok])
        self.stt("dve", kf, kf, -TWO_PI, ang, ALU.mult, ALU.add, [tok], [tok])
        self.ts("dve", kf, kf, shift, ALU.add, [tok], [tok], s2=math.pi, op1=ALU.min)
        self.ts("dve", kf, kf, -math.pi, ALU.max, [tok], [tok])
        self.act(dst, kf, AF.Sin, [tok], [tok])

    def stage_m(self, l):
        A, T_ = self.A, self.T
        ya = self.yall
        wq = A.alloc([8, 256], BF16)
        wg = A.alloc([8, 256], BF16)
        twq, twg = T_("m_wq"), T_("m_wg")
        self.load_win(l, OFF["mq"], 256, wq, twq)
        self.load_win(l, OFF["mg"], 256, wg, twg)
        for ct in range(2):
            for blk in range(NB):
                pb = (ct * NB + blk) % 2
                self.mmg(self.ps[pb][:, :],
                         [(wg[:, kt, ct * 128:(ct + 1) * 128], self.hT[:, kt, blk * 512:(blk + 1) * 512])
                          for kt in range(8)], [twg], self.pst[pb])
                self.act(ya[:, 8 + ct, blk * 512:(blk + 1) * 512], self.ps[pb][:, :], AF.Silu,
                         [self.pst[pb]], [T_("ym", ct, blk)])
        sq = A.alloc([512], BF16)
        rs = A.alloc([512], F32)
        qn = [A.alloc([2, 512], BF16) for _ in range(2)]
        pT = [A.alloc([512], BF16) for _ in range(4)]
        rc = A.alloc([512], F32)
        ot = A.alloc([512], BF16)
        tsq, trs, trc, tot = T_("m_sq"), T_("m_rs"), T_("m_rc"), T_("m_ot")
        scale = 64.0 ** -0.5

        def m_prep(blk):
            q = qn[blk % 2]
            tq = T_("m_qn", blk % 2)
            for ct in range(2):
                self.mmg(self.ps[2][:, :],
                         [(wq[:, kt, ct * 128:(ct + 1) * 128], self.hT[:, kt, blk * 512:(blk + 1) * 512])
                          for kt in range(8)], [twq], self.pst[2])
                self.act(sq[:, :], self.ps[2][:, :], AF.Square, [self.pst[2]], [tsq])
                self.mmg(self.ps[3][:, :], [(self.bd64[:, :], sq[:, :])], [tsq], self.pst[3])
                self.rstd(rs[:, :], self.ps[3][:, :], 64.0, [self.pst[3]], [trs])
                self.stt("dve", q[:, ct, :], self.ps[2][:, :], self.colp[:, C_MGQ:C_MGQ + 1], rs[:, :],
                         ALU.mult, ALU.mult, [self.pst[2], trs], [tq])

        def m_s(blk, h):
            q = qn[blk % 2]
            tq = T_("m_qn", blk % 2)
            ct, r0 = h // 2, (h % 2) * 64
            R = slice(r0, r0 + 64)
            for mt in range(2):
                pb = (4 + mt) if h % 2 == 0 else mt
                self.mm(self.ps[pb][:, :], self.kmem[R, ct, mt * 128:(mt + 1) * 128], q[R, ct, :],
                        True, True, [tq], [self.pst[pb]])

        def m_rest(blk, h):
            ct, r0 = h // 2, (h % 2) * 64
            R = slice(r0, r0 + 64)
            for mt in range(2):
                pb = (4 + mt) if h % 2 == 0 else mt
                p = pT[(h % 2) * 2 + mt]
                tp = T_("m_pT", (h % 2) * 2 + mt)
                self.act(p[:, :], self.ps[pb][:, :], AF.Exp, [self.pst[pb]], [tp], scale=scale)
            tps = [T_("m_pT", (h % 2) * 2 + mt) for mt in range(2)]
            ps_o, ps_d = self.ps[6], self.ps[7]
            self.mmg(ps_o[R, :], [(self.vmem[:, mt, h * 64:(h + 1) * 64], pT[(h % 2) * 2 + mt][:, :])
                                  for mt in range(2)], tps, self.pst[6])
            self.mmg(ps_d[R, :], [(self.ones[:, 0:64], pT[(h % 2) * 2 + mt][:, :]) for mt in range(2)],
                     tps, self.pst[7])
            self.P.op("dve", lambda e, o=rc[R, :], i=ps_d[R, :]: e.reciprocal(out=o, in_=i),
                      reads=[self.pst[7]], writes=[trc])
            self.tt("dve", ot[R, :], ps_o[R, :], rc[R, :], ALU.mult, [self.pst[6], trc], [tot])
            sl = ya[R, 8 + ct, blk * 512:(blk + 1) * 512]
            self.tt("pool", sl, sl, ot[R, :], ALU.mult, [tot], [T_("ym", ct, blk)])

        m_prep(0)
        for blk in range(NB):
            m_s(blk, 0)
            if blk + 1 < NB:
                m_prep(blk + 1)
            for h in range(4):
                if h + 1 < 4:
                    m_s(blk, h + 1)
                m_rest(blk, h)
        self.dump("ym%d" % l, ya[:, 8:10, :], T_("ym", 0, 0))

    def stage_c(self, l):
        A, T_ = self.A, self.T
        ya = self.yall
        uT = A.alloc([2, T], BF16)
        ygT = A.alloc([2, T], BF16)
        TAc = A.alloc([1024], F32)
        TAs = A.alloc([1024], F32)
        Dr = A.alloc([8, 128], F32)
        Di = A.alloc([8, 128], F32)
        a128r = A.alloc([8], F32)
        a128i = A.alloc([8], F32)
        Bm = [A.alloc([2, 8, 64], BF16) for _ in range(2)]
        CreT = A.alloc([8, 128], BF16)
        CreTn = A.alloc([8, 128], BF16)
        CimTn = A.alloc([8, 128], BF16)
        wglu = A.alloc([2, 256], BF16)
        m2 = A.mark()
        wu = A.alloc([8, 256], BF16)
        wg = A.alloc([8, 256], BF16)
        twu, twg = T_("c_wu"), T_("c_wg")
        self.load_win(l, OFF["cin"], 256, wu, twu)
        self.load_win(l, OFF["cg"], 256, wg, twg)
        for ct in range(2):
            for blk in range(NB):
                pb = (ct * NB + blk) % 2
                self.mmg(self.ps[pb][:, :],
                         [(wu[:, kt, ct * 128:(ct + 1) * 128], self.hT[:, kt, blk * 512:(blk + 1) * 512])
                          for kt in range(8)], [twu], self.pst[pb])
                self.cp("act", uT[:, ct, blk * 512:(blk + 1) * 512], self.ps[pb][:, :], [self.pst[pb]],
                        [T_("c_uT", blk)])
        for ct in range(2):
            for blk in range(NB):
                pb = 2 + (ct * NB + blk) % 2
                self.mmg(self.ps[pb][:, :],
                         [(wg[:, kt, ct * 128:(ct + 1) * 128], self.hT[:, kt, blk * 512:(blk + 1) * 512])
                          for kt in range(8)], [twg], self.pst[pb])
                self.act(ya[:, 6 + ct, blk * 512:(blk + 1) * 512], self.ps[pb][:, :], AF.Silu,
                         [self.pst[pb]], [T_("yc", ct, blk)])
        tt_ = T_("c_tab")
        rowp = A.alloc([3, 1024], F32)
        self.dma(rowp, self.dr["rowpack"][l, :, R_SRE:R_SRE + 3072].rearrange("p (a b) -> p a b", a=3), [], [tt_])
        s1 = A.alloc([1024], F32)
        s2 = A.alloc([1024], F32)
        si = A.alloc([1024], I32)
        negs = A.alloc([1], F32)
        tcst = T_("cst")
        self.ts("dve", negs, self.cst[:, K_IOS:K_IOS + 1], -1.0, ALU.mult, [tcst], [tt_])
        self.act(rowp[:, 2, :], rowp[:, 2, :], AF.Exp, [tt_], [tt_])
        self.tt("dve", rowp[:, 0, :], rowp[:, 0, :], rowp[:, 2, :], ALU.mult, [tt_], [tt_])
        self.tt("dve", rowp[:, 1, :], rowp[:, 1, :], rowp[:, 2, :], ALU.mult, [tt_], [tt_])
        self.act(s1, rowp[:, 0, :], AF.Exp, [tt_], [tt_], scale=negs[:, 0:1])
        self.ts("dve", s2, rowp[:, 1, :], self.cst[:, K_IOS:K_IOS + 1], ALU.mult, [tt_, tcst], [tt_])
        self.sin_of(TAs, s2, 0.0, si, rowp[:, 2, :], tt_)
        self.sin_of(TAc, s2, math.pi / 2, si, rowp[:, 2, :], tt_)
        self.tt("dve", TAs, TAs, s1, ALU.mult, [tt_], [tt_])
        self.tt("dve", TAc, TAc, s1, ALU.mult, [tt_], [tt_])
        s1v = s1.rearrange("p (j t) -> p j t", j=8)
        s2v = s2.rearrange("p (j t) -> p j t", j=8)
        siv = si.rearrange("p (j t) -> p j t", j=8)
        kfv = rowp[:, 2, :].rearrange("p (j t) -> p j t", j=8)
        dtj = A.alloc([8], F32)
        thrj = A.alloc([8], F32)
        thij = A.alloc([8], F32)
        e128 = A.alloc([8], F32)
        p128 = A.alloc([8], F32)
        k128 = A.alloc([8], F32)
        i128 = A.alloc([8], I32)
        tcp = self.tcp
        self.act(dtj, self.colp[:, C_SDT:C_SDT + 8], AF.Exp, [tcp, tt_], [tt_])
        self.tt("dve", thrj, self.colp[:, C_SRE:C_SRE + 8], dtj, ALU.mult, [tcp, tt_], [tt_])
        self.tt("dve", thij, self.colp[:, C_SIM:C_SIM + 8], dtj, ALU.mult, [tcp, tt_], [tt_])
        iot = self.cst[:, K_IOT:K_IOT + 128]
        for j in range(8):
            self.act(s1v[:, j, :], iot, AF.Exp, [tt_, tcst], [tt_], scale=thrj[:, j:j + 1])
            self.ts("dve", s2v[:, j, :], iot, thij[:, j:j + 1], ALU.mult, [tt_, tcst], [tt_])
        self.sin_of(Di.rearrange("p j t -> p (j t)"), s2, 0.0, si, rowp[:, 2, :], tt_)
        self.sin_of(Dr.rearrange("p j t -> p (j t)"), s2, math.pi / 2, si, rowp[:, 2, :], tt_)
        self.tt("dve", Di, Di, s1v, ALU.mult, [tt_], [tt_])
        self.tt("dve", Dr, Dr, s1v, ALU.mult, [tt_], [tt_])
        self.act(e128, thrj, AF.Exp, [tt_], [tt_], scale=128.0)
        self.ts("dve", p128, thij, 128.0, ALU.mult, [tt_], [tt_])
        self.sin_of(a128i, p128, 0.0, i128, k128, tt_)
        self.sin_of(a128r, p128, math.pi / 2, i128, k128, tt_)
        self.tt("dve", a128i, a128i, e128, ALU.mult, [tt_], [tt_])
        self.tt("dve", a128r, a128r, e128, ALU.mult, [tt_], [tt_])
        w = [A.alloc([64], F32) for _ in range(8)]
        wi = A.alloc([64], I32)
        gmask = self.cst[:, K_GM:K_GM + 8]
        for ct in range(2):
            base = C_ROW + ct * 320
            are = self.colp[:, base:base + 64]
            aim = self.colp[:, base + 64:base + 128]
            ldt = self.colp[:, base + 128:base + 192]
            bre = self.colp[:, base + 192:base + 256]
            bim = self.colp[:, base + 256:base + 320]
            dt_, thr, thi, ea, abr, abi, t0, t1 = w
            rd = [tt_, tcp]
            self.act(dt_, ldt, AF.Exp, rd, [tt_])
            self.tt("dve", thr, are, dt_, ALU.mult, rd, [tt_])
            self.tt("dve", thi, aim, dt_, ALU.mult, rd, [tt_])
            self.act(ea, thr, AF.Exp, [tt_], [tt_])
            self.sin_of(abi, thi, 0.0, wi, t0, tt_)
            self.sin_of(abr, thi, math.pi / 2, wi, t0, tt_)
            self.tt("dve", abi, abi, ea, ALU.mult, [tt_], [tt_])
            self.tt("dve", abr, abr, ea, ALU.mult, [tt_], [tt_])
            self.ts("dve", abr, abr, -1.0, ALU.add, [tt_], [tt_])
            self.tt("dve", dt_, are, are, ALU.mult, rd, [tt_])
            self.tt("dve", t0, aim, aim, ALU.mult, rd, [tt_])
            self.tt("dve", dt_, dt_, t0, ALU.add, [tt_], [tt_])
            self.P.op("dve", lambda e, o=dt_, i=dt_: e.reciprocal(out=o, in_=i), reads=[tt_], writes=[tt_])
            self.tt("dve", t0, abr, are, ALU.mult, rd, [tt_])
            self.tt("dve", t1, abi, aim, ALU.mult, rd, [tt_])
            self.tt("dve", t0, t0, t1, ALU.add, [tt_], [tt_])
            self.tt("dve", thr, t0, dt_, ALU.mult, [tt_], [tt_])
            self.tt("dve", t0, abi, are, ALU.mult, rd, [tt_])
            self.tt("dve", t1, abr, aim, ALU.mult, rd, [tt_])
            self.tt("dve", t0, t0, t1, ALU.subtract, [tt_], [tt_])
            self.tt("dve", thi, t0, dt_, ALU.mult, [tt_], [tt_])
            self.tt("dve", t0, thr, bre, ALU.mult, rd, [tt_])
            self.tt("dve", t1, thi, bim, ALU.mult, rd, [tt_])
            self.tt("dve", ea, t0, t1, ALU.subtract, [tt_], [tt_])
            self.tt("dve", t0, thr, bim, ALU.mult, rd, [tt_])
            self.tt("dve", t1, thi, bre, ALU.mult, rd, [tt_])
            self.tt("dve", abi, t0, t1, ALU.add, [tt_], [tt_])
            for ri, src_ in enumerate((ea, abi)):
                self.tt("dve", Bm[ct][:, ri, :, :], src_.unsqueeze(1).to_broadcast([128, 8, 64]),
                        gmask.unsqueeze(2).to_broadcast([128, 8, 64]), ALU.mult, [tt_, tcst], [tt_])
        tct = T_("c_ct")
        for j0 in (0, 4):
            srcr = self.dr["c_reT"][l, j0 // 4]
            srci = self.dr["c_imT"][l, j0 // 4]
            self.wload(CreT[:, j0:j0 + 4, :], srcr, tct)
            self.wload(CreTn[:, j0:j0 + 4, :], srcr, tct, scale=-1.0)
            self.wload(CimTn[:, j0:j0 + 4, :], srci, tct, scale=-1.0)
        self.wload(wglu, self.dr["c_w_glu"][l], tct)
        self.dump("c_TAc%d" % l, TAc, tt_)
        self.dump("c_TAs%d" % l, TAs, tt_)
        self.dump("c_Dr%d" % l, Dr, tt_)
        self.dump("c_Di%d" % l, Di, tt_)
        self.dump("c_Bm%d" % l, Bm[0], tt_)
        self.dump("c_a128r%d" % l, a128r, tt_)
        self.P.barrier()
        A.reset(m2)
        aS = A.alloc([16], F32)
        self.memset("dve", aS, 0.0, [T_("c_aS", 0), T_("c_aS", 1)])
        X = [A.alloc([4, 512], BF16) for _ in range(2)]
        Pfs = [A.alloc([8, 128], F32) for _ in range(2)]
        Ys = [A.alloc([4, 4, 128], BF16) for _ in range(2)]
        p127 = A.alloc([8], F32)
        c1 = A.alloc([8], F32)
        c2 = A.alloc([8], F32)
        yss = [A.alloc([128], F32) for _ in range(2)]
        tch = T_("c_ch")
        its = [(n, ct) for n in range(16) for ct in range(2)]

        def front(i):
            n, ct = its[i]
            par = i % 2
            blk = n // 4
            cs = slice(n * 128, (n + 1) * 128)
            pre_, pim_ = self.ps[0], self.ps[1]
            tpre, tpim = self.pst[0], self.pst[1]
            Bre = Bm[ct][:, 0, :, :].rearrange("p g q -> p (g q)")
            Bim = Bm[ct][:, 1, :, :].rearrange("p g q -> p (g q)")
            self.mm(pre_[:, :], uT[:, ct, cs], Bre, True, True, [T_("c_uT", blk), tt_], [tpre])
            self.mm(pim_[:, :], uT[:, ct, cs], Bim, True, True, [T_("c_uT", blk), tt_], [tpim])
            x = X[par]
            tx = T_("c_X", par)
            tc_ = TAc[:, ct * 512:(ct + 1) * 512]
            ts_ = TAs[:, ct * 512:(ct + 1) * 512]
            self.tt("dve", x[:, 0, :], pre_[:, :], tc_, ALU.mult, [tpre, tt_], [tx])
            self.tt("dve", x[:, 3, :], pre_[:, :], ts_, ALU.mult, [tpre, tt_], [tx])
            self.tt("dve", x[:, 1, :], pim_[:, :], ts_, ALU.mult, [tpim, tt_], [tx])
            self.tt("dve", x[:, 2, :], pim_[:, :], tc_, ALU.mult, [tpim, tt_], [tx])
            Pre, Pim = self.ps[2 + 2 * par], self.ps[3 + 2 * par]
            for jl in range(4):
                js = slice(jl * 128, (jl + 1) * 128)
                self.mmg(Pre[:, js], [(x[:, 0, js], self.tri[:, :]), (x[:, 1, js], self.tri[:, :])],
                         [tx], self.pst[2 + 2 * par])
                self.mmg(Pim[:, js], [(x[:, 2, js], self.tri[:, :]), (x[:, 3, js], self.ntri[:, :])],
                         [tx], self.pst[3 + 2 * par])

        def back_a(i):
            n, ct = its[i]
            par = i % 2
            Pre, Pim = self.ps[2 + 2 * par], self.ps[3 + 2 * par]
            tP0, tP1 = self.pst[2 + 2 * par], self.pst[3 + 2 * par]
            Pf, Y = Pfs[par], Ys[par]
            tPf, tY = T_("c_Pf", par), T_("c_Y", par)
            ta = T_("c_aS", ct)
            jr = slice(4 * ct, 4 * ct + 4)
            ji = slice(8 + 4 * ct, 8 + 4 * ct + 4)
            for jl in range(4):
                js = slice(jl * 128, (jl + 1) * 128)
                self.act(Pf[:, jl, :], Pre[:, js], AF.Identity, [tP0, ta], [tPf],
                         bias=aS[:, 4 * ct + jl:4 * ct + jl + 1])
                self.act(Pf[:, 4 + jl, :], Pim[:, js], AF.Identity, [tP1, ta], [tPf],
                         bias=aS[:, 8 + 4 * ct + jl:8 + 4 * ct + jl + 1])
            self.cp("dve", p127, Pf[:, :, 127], [tPf], [tch])
            self.tt("dve", c1[:, 0:4], a128r[:, jr], p127[:, 0:4], ALU.mult, [tch, tt_], [tch])
            self.tt("dve", c1[:, 4:8], a128i[:, jr], p127[:, 4:8], ALU.mult, [tch, tt_], [tch])
            self.tt("dve", c2[:, 0:4], a128r[:, jr], p127[:, 4:8], ALU.mult, [tch, tt_], [tch])
            self.tt("dve", c2[:, 4:8], a128i[:, jr], p127[:, 0:4], ALU.mult, [tch, tt_], [tch])
            self.tt("dve", aS[:, jr], c1[:, 0:4], c1[:, 4:8], ALU.subtract, [tch], [ta])
            self.tt("dve", aS[:, ji], c2[:, 0:4], c2[:, 4:8], ALU.add, [tch], [ta])
            self.tt("pool", Y[:, 0, :, :], Pf[:, 0:4, :], Dr[:, jr, :], ALU.mult, [tPf, tt_], [tY])
            self.tt("pool", Y[:, 1, :, :], Pf[:, 4:8, :], Di[:, jr, :], ALU.mult, [tPf, tt_], [tY])
            tY2 = T_("c_Y2", par)
            self.tt("dve", Y[:, 2, :, :], Pf[:, 4:8, :], Dr[:, jr, :], ALU.mult, [tPf, tt_], [tY2])
            self.tt("dve", Y[:, 3, :, :], Pf[:, 0:4, :], Di[:, jr, :], ALU.mult, [tPf, tt_], [tY2])

        def back_y(i):
            n, ct = its[i]
            par = i % 2
            Y = Ys[par]
            tY, tY2 = T_("c_Y", par), T_("c_Y2", par)
            py = self.ps[6 + par]
            pairs = []
            for jl in range(4):
                j = 4 * ct + jl
                pairs += [(CreT[:, j, :], Y[:, 0, jl, :]), (CreTn[:, j, :], Y[:, 1, jl, :]),
                          (CimTn[:, j, :], Y[:, 2, jl, :]), (CimTn[:, j, :], Y[:, 3, jl, :])]
            self.mmg(py[:, 0:128], pairs, [tY, tY2, tct], self.pst[6 + par])

        def back_b(i):
            n, ct = its[i]
            par = i % 2
            blk = n // 4
            cs = slice(n * 128, (n + 1) * 128)
            py = self.ps[6 + par]
            ys, tys = yss[par], T_("c_ys", par)
            self.stt("dve", ys, uT[:, ct, cs], self.colp[:, C_CD + ct:C_CD + ct + 1], py[:, 0:128],
                     ALU.mult, ALU.add, [self.pst[6 + par], T_("c_uT", blk), tcp], [tys])
            self.act(ygT[:, ct, cs], ys, AF.Gelu_apprx_tanh, [tys], [T_("c_yg", blk)])

        front(0)
        for i in range(len(its)):
            if i + 1 < len(its):
                front(i + 1)
            back_a(i)
            if i >= 1:
                back_y(i - 1)
            if i >= 2:
                back_b(i - 2)
        back_y(len(its) - 1)
        back_b(len(its) - 2)
        back_b(len(its) - 1)
        self.dump("c_yg%d" % l, ygT, T_("c_yg", 0))
        sg = A.alloc([512], BF16)
        tsg = T_("c_sg")
        for blk in range(NB):
            bs = slice(blk * 512, (blk + 1) * 512)
            for cc in range(2):
                pb = 7
                self.mmg(self.ps[pb][:, :], [(wglu[:, ct, cc * 128:(cc + 1) * 128], ygT[:, ct, bs]) for ct in range(2)],
                         [tct, T_("c_yg", blk)], self.pst[pb])
                self.act(sg, self.ps[pb][:, :], AF.Sigmoid, [self.pst[pb], tcp], [tsg],
                         bias=self.colp[:, C_BGLU + cc:C_BGLU + cc + 1])
                sl = ya[:, 6 + cc, bs]
                self.tt("pool", sg, sg, ygT[:, cc, bs], ALU.mult, [tsg, T_("c_yg", blk)], [tsg])
                self.tt("pool", sl, sl, sg, ALU.mult, [tsg], [T_("yc", cc, blk)])
        self.dump("yc%d" % l, ya[:, 6:8, :], T_("yc", 0, 0))

    def stage_b(self, l):
        A, T_, P = self.A, self.T, self.P
        ya = self.yall
        tcp = self.tcp
        tcst = T_("cst")
        cqn = A.alloc([6, T], BF16)
        ckvn = A.alloc([2, T], BF16)
        krr = A.alloc([T], F32)
        sqkr = A.alloc([T], BF16)
        m2 = A.mark()
        wt = A.alloc([8, 512], BF16)
        tw = T_("b_w")
        for i in range(2):
            self.load_win(l, OFF["bg"] + i * 256, 256, wt[:, :, i * 256:(i + 1) * 256], tw)
        for ct in range(4):
            for blk in range(NB):
                pb = (ct * NB + blk) % 2
                self.zproj_blk(wt[:, :, ct * 128:(ct + 1) * 128], 128, blk, pb, tw)
                self.act(ya[:, 2 + ct, blk * 512:(blk + 1) * 512], self.ps[pb][:, :], AF.Silu,
                         [self.pst[pb]], [T_("yb", ct, blk)])
        P.barrier()
        A.reset(m2)
        wq_in = A.alloc([8, 768], BF16)
        wkv_in = A.alloc([8, 256], BF16)
        wkr = [A.alloc([8, 96], BF16) for _ in range(2)]
        sq = A.alloc([512], BF16)
        rs = A.alloc([512], F32)
        t1 = A.alloc([512], F32)
        t2 = A.alloc([512], F32)
        cqf = A.alloc([6, 512], F32)
        ckf = A.alloc([2, 512], F32)
        rs2 = A.alloc([512], F32)
        for i in range(3):
            self.load_win(l, OFF["cq"] + i * 256, 256, wq_in[:, :, i * 256:(i + 1) * 256], tw)
        self.load_win(l, OFF["ckv"], 256, wkv_in, tw)
        self.wload(wkr[0], self.dr["w_krm"][l], tw)
        self.wload(wkr[1], self.dr["w_krp"][l], tw)
        tsq, trs = T_("b_sq"), T_("b_rs")
        R = slice(64, 96)
        for blk in range(NB):
            bs = slice(blk * 512, (blk + 1) * 512)
            tcf = T_("b_cqf")
            for i in range(6):
                pb = i % 2
                self.zproj_blk(wq_in[:, :, i * 128:(i + 1) * 128], 128, blk, pb, tw)
                self.act(sq, self.ps[pb][:, :], AF.Square, [self.pst[pb]], [tsq])
                self.cp("act", cqf[:, i, :], self.ps[pb][:, :], [self.pst[pb]], [tcf])
                self.mm(self.ps[6][:, :], self.ones[:, :], sq, i == 0, i == 5, [tsq], [self.pst[6]])
            tkf = T_("b_ckf")
            for i in range(2):
                pb = 4 + i
                self.zproj_blk(wkv_in[:, :, i * 128:(i + 1) * 128], 128, blk, pb, tw)
                self.act(sq, self.ps[pb][:, :], AF.Square, [self.pst[pb]], [tsq])
                self.cp("act", ckf[:, i, :], self.ps[pb][:, :], [self.pst[pb]], [tkf])
                self.mm(self.ps[7][:, :], self.ones[:, :], sq, i == 0, i == 1, [tsq], [self.pst[7]])
            self.rstd(rs, self.ps[6][:, :], 768.0, [self.pst[6]], [trs])
            for i in range(6):
                self.stt("dve", cqn[:, i, bs], cqf[:, i, :], self.colp[:, C_QNG + i:C_QNG + i + 1], rs,
                         ALU.mult, ALU.mult, [tcf, trs, tcp], [T_("b_cqn", blk)])
            trs2 = T_("b_rs2")
            self.rstd(rs2, self.ps[7][:, :], 256.0, [self.pst[7]], [trs2])
            for i in range(2):
                self.stt("dve", ckvn[:, i, bs], ckf[:, i, :], self.colp[:, C_KVNG + i:C_KVNG + i + 1], rs2,
                         ALU.mult, ALU.mult, [tkf, trs2, tcp], [T_("b_ckvn", blk)])
            for i in range(2):
                self.zproj_blk(wkr[i], 96, blk, 2 + i, tw)
            self.act(sqkr[R, bs], self.ps[2][R, :], AF.Square, [self.pst[2]], [T_("b_kr", blk)])
            tt1 = T_("b_t1")
            self.stt("dve", t1[R, :], self.ps[2][R, :], self.colp[R, C_GK:C_GK + 1], self.cosT[R, bs],
                     ALU.mult, ALU.mult, [self.pst[2], tcp], [tt1])
            self.stt("dve", t2[R, :], self.ps[3][R, :], self.colp[R, C_GKP:C_GKP + 1], self.sinT[R, bs],
                     ALU.mult, ALU.mult, [self.pst[3], tcp], [tt1])
            self.tt("pool", krr[R, bs], t1[R, :], t2[R, :], ALU.add, [tt1], [T_("b_kr", blk)])
        self.dump("b_cqn%d" % l, cqn, T_("b_cqn", 0))
        self.dump("b_ckvn%d" % l, ckvn, T_("b_ckvn", 0))
        self.dump("b_krr%d" % l, krr[R, :], T_("b_kr", 0))
        P.barrier()
        A.reset(m2)
        wqm = [A.alloc([6, 96], BF16) for _ in range(2)]
        wqp = [A.alloc([6, 96], BF16) for _ in range(2)]
        wkv = [A.alloc([2, 128], BF16) for _ in range(2)]
        wk = [w_[:, :, 0:64] for w_ in wkv]
        wv = [w_[:, :, 64:128] for w_ in wkv]
        qn = [A.alloc([T], BF16) for _ in range(2)]
        kn = [A.alloc([T], BF16) for _ in range(2)]
        vh = [A.alloc([16, 64], BF16) for _ in range(2)]
        sq = [A.alloc([512], BF16) for _ in range(2)]
        rs = [A.alloc([512], F32) for _ in range(2)]
        t1 = A.alloc([512], F32)
        t2 = A.alloc([512], F32)
        pT = [A.alloc([512], BF16) for _ in range(4)]
        rcs = [A.alloc([512], F32) for _ in range(2)]
        ots = [A.alloc([512], BF16) for _ in range(2)]
        scale = 96.0 ** -0.5
        def b_load(h_):
            s_ = h_ % 2
            twh_ = T_("b_wh", s_)
            self.wload(wqm[s_], self.dr["wq_m"][l, h_], twh_)
            self.wload(wqp[s_], self.dr["wq_p"][l, h_], twh_)
            self.wload(wkv[s_], self.dr["w_ukv"][l, h_], twh_)

        b_load(0)
        for h in range(8):
            s = h % 2
            twh = T_("b_wh", s)
            tq, tk, tv = T_("b_qn", s), T_("b_kn", s), T_("b_vh", s)
            Q = slice(0, 96)
            N_ = slice(0, 64)
            for half in range(2):
                pbv = 6 + half
                for t8 in range(8):
                    tt_ = half * 8 + t8
                    self.mmg(self.ps[pbv][:, t8 * 64:(t8 + 1) * 64],
                             [(ckvn[:, kt, tt_ * 128:(tt_ + 1) * 128], wv[s][:, kt, :]) for kt in range(2)],
                             [twh], self.pst[pbv])
                self.cp("dve", vh[s][:, half * 8:(half + 1) * 8, :],
                        self.ps[pbv][:, :].rearrange("p (a b) -> p a b", a=8), [self.pst[pbv]], [tv])

            def banks(blk):
                o = 0 if blk % 2 == 0 else 4
                return o, o + 1, o + 2, o + 3

            def prep_f1(blk):
                bs = slice(blk * 512, (blk + 1) * 512)
                bq, bp, _, bk = banks(blk)
                self.mmg(self.ps[bq][Q, :], [(wqm[s][:, kt, :], cqn[:, kt, bs]) for kt in range(6)], [twh], self.pst[bq])

            def prep_f2(blk):
                bs = slice(blk * 512, (blk + 1) * 512)
                bq, bp, _, bk = banks(blk)
                self.mmg(self.ps[bp][Q, :], [(wqp[s][:, kt, :], cqn[:, kt, bs]) for kt in range(6)], [twh], self.pst[bp])

            def prep_f3(blk):
                bs = slice(blk * 512, (blk + 1) * 512)
                bq, bp, _, bk = banks(blk)
                self.mmg(self.ps[bk][N_, :], [(wk[s][:, kt, :], ckvn[:, kt, bs]) for kt in range(2)], [twh], self.pst[bk])

            def prep_back(blk, nxt):
                bs = slice(blk * 512, (blk + 1) * 512)
                bq, bp, bsq, bk = banks(blk)
                tsq0, trs0 = T_("b_sq", 0), T_("b_rs", 0)
                tsq1, trs1 = T_("b_sq", 1), T_("b_rs", 1)
                self.act(sq[0][Q, :], self.ps[bq][Q, :], AF.Square, [self.pst[bq]], [tsq0])
                self.act(sq[1][N_, :], self.ps[bk][N_, :], AF.Square, [self.pst[bk]], [tsq1])
                self.cp("pool", sq[1][R, :], sqkr[R, bs], [], [tsq1])
                if nxt is not None:
                    prep_f1(nxt)
                self.mmg(self.ps[bsq][Q, :], [(self.ones[Q, 0:96], sq[0][Q, :])], [tsq0], self.pst[bsq])
                self.rstd(rs[0][Q, :], self.ps[bsq][Q, :], 96.0, [self.pst[bsq]], [trs0])
                if nxt is not None:
                    prep_f2(nxt)
                self.mmg(self.ps[bsq][Q, :], [(self.ones[Q, 0:96], sq[1][Q, :])], [tsq1], self.pst[bsq])
                self.rstd(rs[1][Q, :], self.ps[bsq][Q, :], 96.0, [self.pst[bsq]], [trs1])
                if nxt is not None:
                    prep_f3(nxt)
                tt1 = T_("b_t1")
                self.stt("dve", t1[R, :], self.ps[bq][R, :], self.colp[R, C_GQ:C_GQ + 1], self.cosT[R, bs],
                         ALU.mult, ALU.mult, [self.pst[bq], tcp], [tt1])
                self.stt("dve", t2[R, :], self.ps[bp][R, :], self.colp[R, C_GQP:C_GQP + 1], self.sinT[R, bs],
                         ALU.mult, ALU.mult, [self.pst[bp], tcp], [tt1])
                self.stt("dve", qn[s][N_, bs], self.ps[bq][N_, :], self.colp[N_, C_GQ:C_GQ + 1], rs[0][N_, :],
                         ALU.mult, ALU.mult, [self.pst[bq], trs0, tcp], [tq])
                self.stt("dve", kn[s][N_, bs], self.ps[bk][N_, :], self.colp[N_, C_GK:C_GK + 1], rs[1][N_, :],
                         ALU.mult, ALU.mult, [self.pst[bk], trs1, tcp], [tk])
                self.tt("pool", t1[R, :], t1[R, :], t2[R, :], ALU.add, [tt1], [tt1])
                self.tt("pool", qn[s][R, bs], t1[R, :], rs[0][R, :], ALU.mult, [tt1, trs0], [tq])
                self.tt("pool", kn[s][R, bs], krr[R, bs], rs[1][R, :], ALU.mult, [trs1], [tk])

            prep_f1(0)
            prep_f2(0)
            prep_f3(0)
            for blk in range(NB):
                prep_back(blk, blk + 1 if blk + 1 < NB else None)
            if h == 0:
                self.dump("b_qn%d" % l, qn[0][0:96, :], tq)
                self.dump("b_kn%d" % l, kn[0][0:96, :], tk)
                self.dump("b_vh%d" % l, vh[0], tv)
            if h + 1 < 8:
                b_load(h + 1)
            ct, r0 = h // 2, (h % 2) * 64
            RR = slice(r0, r0 + 64)
            seq = [(b, j) for b in range(NB) for j in range(4 * b + 4)]

            def att_s(i):
                b, j = seq[i]
                jj = j - 4 * b
                c0 = 128 * jj if jj > 0 else 0
                pb = 4 + i % 2
                self.mm(self.ps[pb][:, c0:512], kn[s][0:96, j * 128:(j + 1) * 128],
                        qn[s][0:96, b * 512 + c0:(b + 1) * 512], True, True, [tq, tk], [self.pst[pb]])

            def att_rest(i):
                b, j = seq[i]
                nj = 4 * b + 4
                jj = j - 4 * b
                c0 = 128 * jj if jj > 0 else 0
                pb = 4 + i % 2
                p = pT[i % 4]
                tp = T_("b_pT", i % 4)
                po, pd = (6, 7) if b % 2 == 0 else (2, 3)
                self.act(p[:, c0:512], self.ps[pb][:, c0:512], AF.Exp, [self.pst[pb]], [tp], scale=scale)
                if jj >= 0:
                    self.tt("pool", p[:, c0:c0 + 128], p[:, c0:c0 + 128], self.tri[:, :], ALU.mult,
                            [tp, tcst], [tp])
                self.mm(self.ps[po][RR, c0:512], vh[s][:, j, :], p[:, c0:512], j == 0, j == nj - 1,
                        [tv, tp], [self.pst[po]])
                self.mm(self.ps[pd][RR, c0:512], self.ones[:, 0:64], p[:, c0:512], j == 0, j == nj - 1,
                        [tp], [self.pst[pd]])
                if j == nj - 1:
                    trc, tot = T_("b_rc", b % 2), T_("b_ot", b % 2)
                    rc_, ot_ = rcs[b % 2], ots[b % 2]
                    self.P.op("dve", lambda e, o=rc_[RR, :], i_=self.ps[pd][RR, :]: e.reciprocal(out=o, in_=i_),
                              reads=[self.pst[pd]], writes=[trc])
                    self.tt("dve", ot_[RR, :], self.ps[po][RR, :], rc_[RR, :], ALU.mult, [self.pst[po], trc], [tot])
                    sl = ya[RR, 2 + ct, b * 512:(b + 1) * 512]
                    self.tt("pool", sl, sl, ot_[RR, :], ALU.mult, [tot], [T_("yb", ct, b)])

            att_s(0)
            for i in range(len(seq)):
                if i + 1 < len(seq):
                    att_s(i + 1)
                att_rest(i)
        self.dump("yb%d" % l, ya[:, 2:6, :], T_("yb", 0, 0))

    def stage_p2(self, l, src, dst):
        A, T_, P = self.A, self.T, self.P
        ya = self.yall
        merged = A.alloc([8, T], BF16)
        m2 = A.mark()
        wls = [A.alloc([8, 4, 256], BF16) for _ in range(2)]
        wbs = [A.alloc([10, 256], BF16) for _ in range(2)]
        g = [A.alloc([512], F32) for _ in range(2)]
        macc = A.alloc([512], F32)
        t2 = [A.alloc([512], F32) for _ in range(2)]
        brk = {0: [0, 1], 1: [2, 3, 4, 5], 2: [6, 7], 3: [8, 9]}
        brw = ["w_br_a", "w_br_b", "w_br_c", "w_br_m"]
        tcp = self.tcp
        def p2_load(j2):
            wl_, wb_ = wls[j2 % 2], wbs[j2 % 2]
            twl_, twb_ = T_("p_wl", j2 % 2), T_("p_wb", j2 % 2)
            for br in range(4):
                c0 = OFF["mrg"] + br * 1024 + j2 * 256
                self.wload(wl_[:, :, br, :], self.dr["w_in"][l, WIN_G[c0]], twl_)
            self.wload(wb_[:, 0:6, :], self.dr["w_br"][l, j2, :, 0:6, :], twb_)
            self.wload(wb_[:, 6:10, :], self.dr["w_br"][l, j2, :, 6:10, :], twb_)

        p2_load(0)
        for j2 in range(4):
            if j2 + 1 < 4:
                p2_load(j2 + 1)
            wl, wb = wls[j2 % 2], wbs[j2 % 2]
            twl, twb = T_("p_wl", j2 % 2), T_("p_wb", j2 % 2)
            for jj in range(2):
                j = 2 * j2 + jj
                js = slice(jj * 128, (jj + 1) * 128)
                for blk in range(NB):
                    bs = slice(blk * 512, (blk + 1) * 512)
                    for br in range(4):
                        pl, pp = self.ps[2 * (br % 2)], self.ps[2 * (br % 2) + 1]
                        tpl, tpp = self.pst[2 * (br % 2)], self.pst[2 * (br % 2) + 1]
                        self.mmg(pl[:, :], [(wl[:, kt, br, js], self.hT[:, kt, bs]) for kt in range(8)], [twl], tpl)
                        self.mmg(pp[:, :], [(wb[:, kt, js], ya[:, kt, bs]) for kt in brk[br]], [twb], tpp)
                        gg, tg = g[br % 2], T_("p_g", br % 2)
                        self.act(gg, pl[:, :], AF.Sigmoid, [tpl, tcp], [tg],
                                 bias=self.colp[:, C_BM + br * 8 + j:C_BM + br * 8 + j + 1])
                        tm = T_("p_m")
                        if br == 0:
                            self.tt("dve", macc, pp[:, :], gg, ALU.mult, [tpp, tg], [tm])
                        else:
                            tt2 = T_("p_t2", br % 2)
                            self.tt("dve", t2[br % 2], pp[:, :], gg, ALU.mult, [tpp, tg], [tt2])
                            if br < 3:
                                self.tt("dve", macc, macc, t2[br % 2], ALU.add, [tt2, tm], [tm])
                            else:
                                self.tt("dve", merged[:, j, bs], macc, t2[br % 2], ALU.add, [tt2, tm],
                                        [T_("p_mg", blk)])
        self.dump("merged%d" % l, merged, T_("p_mg", 0))
        P.barrier()
        A.reset(m2)
        wo = A.alloc([8, D], BF16)
        two = T_("p_wo")
        for i in range(4):
            self.wload(wo[:, :, i * 256:(i + 1) * 256],
                       self.dr["w_out"][l, i], two)
        xt = A.alloc([8, 512], F32)
        xo = A.alloc([8, 512], F32)
        txt, txo = T_("p_xt"), T_("p_xo")
        for blk in range(NB):
            bs = slice(blk * 512, (blk + 1) * 512)
            for hh in range(2):
                self.dma(xt[:, hh * 4:(hh + 1) * 4, :], src[blk, :, hh * 4:(hh + 1) * 4, :], [], [txt])
            for d2 in range(8):
                pb = 4 + d2 % 2
                self.mmg(self.ps[pb][:, :], [(wo[:, kt, d2 * 128:(d2 + 1) * 128], merged[:, kt, bs]) for kt in range(8)],
                         [two, T_("p_mg", blk)], self.pst[pb])
                self.tt("dve", xo[:, d2, :], self.ps[pb][:, :], xt[:, d2, :], ALU.add, [self.pst[pb], txt], [txo])
            for hh in range(2):
                op = self.dma(dst[blk, :, hh * 4:(hh + 1) * 4, :], xo[:, hh * 4:(hh + 1) * 4, :], [txo], [], q="act")
                if dst is self.outT:
                    self.finals.append(op)


def make_consts():
    c = np.zeros((128, NCONST), np.float32)
    s = np.arange(128)
    c[:, K_TRI:K_TRI + 128] = (s[:, None] <= s[None, :]).astype(np.float32)
    c[:, K_IOT:K_IOT + 128] = s[None, :].astype(np.float32)
    c[:, K_IOS] = s.astype(np.float32)
    half = 16
    inv = (10000.0 ** (-np.arange(half, dtype=np.float32) / half)).astype(np.float32)
    for r in range(64, 96):
        c[r, K_INVF] = inv[(r - 64) % 16]
        c[r, K_SGN] = -1.0 if (r - 64) < 16 else 1.0
    for r in range(128):
        c[r, K_GM + r // 16] = 1.0
    return c


def host_prep(inp):
    f = lambda k: np.asarray(inp[k], dtype=np.float32)
    perm = np.array(PERM)
    sh = {}

    def rows_t(w):
        r, c = w.shape
        return np.ascontiguousarray(w.reshape(r // 128, 128, c).transpose(1, 0, 2))

    w_in = f("w_in")
    sh["w_in"] = np.stack([np.stack([rows_t(w_in[l][:, c0:c0 + 256]) for c0 in WIN_C0]) for l in range(L)])
    krm = np.zeros((L, D, 96), np.float32)
    krp = np.zeros((L, D, 96), np.float32)
    krm[:, :, 64:96] = w_in[:, :, OFF["kr"]:OFF["kr"] + 32]
    krp[:, :, 64:96] = w_in[:, :, OFF["kr"] + perm]
    sh["w_krm"] = np.stack([rows_t(krm[l]) for l in range(L)])
    sh["w_krp"] = np.stack([rows_t(krp[l]) for l in range(L)])
    wuq = f("b_w_uq").reshape(L, 768, 8, 96)
    wqp = np.zeros((L, 768, 8, 96), np.float32)
    wqp[:, :, :, 64:96] = wuq[:, :, :, 64 + perm]
    sh["wq_m"] = np.stack([np.stack([rows_t(wuq[l][:, h, :]) for h in range(8)]) for l in range(L)])
    sh["wq_p"] = np.stack([np.stack([rows_t(wqp[l][:, h, :]) for h in range(8)]) for l in range(L)])
    ukv = f("b_w_ukv")
    sh["w_ukv"] = np.stack([np.stack([rows_t(ukv[l][:, h * 128:(h + 1) * 128]) for h in range(8)]) for l in range(L)])
    sh["a_w_sT"] = np.ascontiguousarray(f("a_w_s").transpose(0, 3, 1, 2))
    c_re, c_im = f("c_c_re"), f("c_c_im")
    cre = np.zeros((L, 8, 128, 128), np.float32)
    cim = np.zeros((L, 8, 128, 128), np.float32)
    for j in range(8):
        for gl in range(2):
            g = 2 * j + gl
            col0 = 16 * (g % 8)
            cre[:, j, gl * 64:(gl + 1) * 64, col0:col0 + 16] = c_re[:, g].transpose(0, 2, 1)
            cim[:, j, gl * 64:(gl + 1) * 64, col0:col0 + 16] = c_im[:, g].transpose(0, 2, 1)
    sh["c_reT"] = np.ascontiguousarray(cre.reshape(L, 2, 4, 128, 128).transpose(0, 1, 3, 2, 4))
    sh["c_imT"] = np.ascontiguousarray(cim.reshape(L, 2, 4, 128, 128).transpose(0, 1, 3, 2, 4))
    sh["c_w_glu"] = np.stack([rows_t(f("c_w_glu")[l]) for l in range(L)])
    mkv = f("m_w_kv")
    sh["m_w_kv"] = np.stack([np.stack([rows_t(mkv[l][:, i * 256:(i + 1) * 256]) for i in range(2)]) for l in range(L)])
    wbr = np.concatenate([f("w_br_a"), f("w_br_b"), f("w_br_c"), f("w_br_m")], axis=1)
    sh["w_br"] = np.stack([np.stack([rows_t(wbr[l][:, i * 256:(i + 1) * 256]) for i in range(4)]) for l in range(L)])
    wo = f("w_out")
    sh["w_out"] = np.stack([np.stack([rows_t(wo[l][:, i * 256:(i + 1) * 256]) for i in range(4)]) for l in range(L)])
    cp = np.zeros((L, 128, NCOL), np.float32)
    cp[:, :, C_NG:C_NG + 8] = f("norm_g").reshape(L, 8, 128).transpose(0, 2, 1)
    cp[:, :, C_MNG:C_MNG + 8] = f("m_norm_g").reshape(L, 8, 128).transpose(0, 2, 1)
    cp[:, :, C_QNG:C_QNG + 6] = f("b_q_norm_g").reshape(L, 6, 128).transpose(0, 2, 1)
    cp[:, :, C_KVNG:C_KVNG + 2] = f("b_kv_norm_g").reshape(L, 2, 128).transpose(0, 2, 1)
    cp[:, :, C_BM:C_BM + 32] = f("b_merge").reshape(L, 32, 128).transpose(0, 2, 1)
    gq, gk = f("b_qk_g_q"), f("b_qk_g_k")
    cp[:, 0:96, C_GQ] = gq
    cp[:, 64:96, C_GQP] = gq[:, 64 + perm]
    cp[:, 0:96, C_GK] = gk
    cp[:, 64:96, C_GKP] = gk[:, 64 + perm]
    cp[:, :, C_MGQ] = np.tile(f("m_qk_g_q"), (1, 2))
    cp[:, :, C_MGK] = np.tile(f("m_qk_g_k"), (1, 2))
    cp[:, :, C_CD:C_CD + 2] = f("c_d").reshape(L, 2, 128).transpose(0, 2, 1)
    cp[:, :, C_BGLU:C_BGLU + 2] = f("c_b_glu").reshape(L, 2, 128).transpose(0, 2, 1)
    a_re, a_im, ldt = f("c_a_re"), f("c_a_im"), f("c_log_dt")
    cp[:, :, C_SRE:C_SRE + 8] = a_re.reshape(L, 8, 128).transpose(0, 2, 1)
    cp[:, :, C_SIM:C_SIM + 8] = a_im.reshape(L, 8, 128).transpose(0, 2, 1)
    ldt_rep = np.repeat(ldt[:, :, None], 64, axis=2)
    cp[:, :, C_SDT:C_SDT + 8] = ldt_rep.reshape(L, 8, 128).transpose(0, 2, 1)
    abs_ = f("a_b_s")
    for ct in range(2):
        for gl in range(2):
            cp[:, gl * 64:(gl + 1) * 64, C_ABS + ct * 128:C_ABS + (ct + 1) * 128] = abs_[:, 2 * ct + gl][:, None, :]
    b_re, b_im = f("c_b_re"), f("c_b_im")
    for ct in range(2):
        base = C_ROW + ct * 320
        for g8 in range(8):
            g = 8 * ct + g8
            rows = slice(16 * g8, 16 * g8 + 16)
            cp[:, rows, base + 0:base + 64] = a_re[:, g][:, None, :]
            cp[:, rows, base + 64:base + 128] = a_im[:, g][:, None, :]
            cp[:, rows, base + 128:base + 192] = ldt[:, g][:, None, None]
            cp[:, rows, base + 192:base + 256] = b_re[:, g].transpose(0, 2, 1)
            cp[:, rows, base + 256:base + 320] = b_im[:, g].transpose(0, 2, 1)
    sh["colpack"] = cp
    rp = np.zeros((L, 128, NROW), np.float32)
    rp[:, :, R_ANG:R_ANG + 256] = f("a_norm_g")[:, None, :]
    rp[:, :, R_SRE:R_SRE + 1024] = a_re.reshape(L, 1, 1024)
    rp[:, :, R_SIM:R_SIM + 1024] = a_im.reshape(L, 1, 1024)
    rp[:, :, R_SDT:R_SDT + 1024] = ldt_rep.reshape(L, 1, 1024)
    sh["rowpack"] = rp
    sh["consts"] = make_consts()
    x = f("x")
    mem = f("mem")
    pos = np.asarray(inp["positions"]).astype(np.int32)
    per_core = []
    for b in range(8):
        d = dict(sh)
        d["xT"] = tile_x(x[b])
        d["memT"] = np.ascontiguousarray(mem[b].T.reshape(8, 128, 1, 256).transpose(2, 1, 0, 3))
        d["pos"] = np.ascontiguousarray(pos[b][None, :])
        per_core.append(d)
    return per_core


def tile_x(xb):
    return np.ascontiguousarray(xb.T.reshape(8, 128, NB, 512).transpose(2, 1, 0, 3))


def untile_x(t):
    return np.ascontiguousarray(t.transpose(2, 1, 0, 3).reshape(D, T).T)


_CACHE = {}
LAYER_KEYS = ("colpack", "rowpack", "w_in", "w_krm", "w_krp", "wq_m", "wq_p", "w_ukv", "a_w_sT", "c_reT", "c_imT",
              "c_w_glu", "m_w_kv", "w_br", "w_out")
FUSED = True


def kernel(**inputs):
    in_maps = host_prep(inputs)
    if FUSED:
        if "nc" not in _CACHE:
            _CACHE["nc"] = Builder(nlayers=L).build()
        res = run_bass_kernel_spmd(_CACHE["nc"], in_maps, core_ids=list(range(8)))
        return np.stack([untile_x(r["outT"]) for r in res.results], axis=0).astype(np.float32)
    if "nc1" not in _CACHE:
        _CACHE["nc1"] = Builder(nlayers=1).build()
    nc = _CACHE["nc1"]
    xs = [m["xT"] for m in in_maps]
    for l in range(L):
        maps = []
        for c in range(8):
            d = dict(in_maps[c])
            for k in LAYER_KEYS:
                a = in_maps[c][k]
                d[k] = np.ascontiguousarray(np.concatenate([a[l:l + 1], a[l:l + 1]], axis=0))
            d["xT"] = xs[c]
            maps.append(d)
        res = run_bass_kernel_spmd(nc, maps, core_ids=list(range(8)))
        xs = [np.ascontiguousarray(r["outT"]) for r in res.results]
    return np.stack([untile_x(t) for t in xs], axis=0).astype(np.float32)
```

```python
import math
import contextlib
import numpy as np
import concourse.bass as bass
import concourse.mybir as mybir
from concourse.bass_utils import run_bass_kernel_spmd

F32 = mybir.dt.float32
BF16 = mybir.dt.bfloat16
I32 = mybir.dt.int32
AF = mybir.ActivationFunctionType
ALU = mybir.AluOpType

D = 1024
T = 2048
L = 2
NB = 4
EPS = 1e-6
IN_W = 7456
OFF = dict(a_u=0, a_v=256, a_g=512, cq=768, ckv=1536, kr=1792, bg=1824,
           cin=2336, cg=2592, mq=2848, mg=3104, mrg=3360)
PERM = list(range(16, 32)) + list(range(0, 16))
WIN_C0 = [0, 256, 512, 768, 1024, 1280, 1536, 1824, 2080, 2336, 2592, 2848, 3104] + \
         [3360 + br * 1024 + j2 * 256 for br in range(4) for j2 in range(4)]
WIN_G = {c: i for i, c in enumerate(WIN_C0)}
NWG = len(WIN_C0)
TWO_PI = 2.0 * math.pi

C_NG, C_MNG, C_QNG, C_KVNG, C_BM = 0, 8, 16, 22, 24
C_GQ, C_GQP, C_GK, C_GKP, C_MGQ, C_MGK = 56, 57, 58, 59, 60, 61
C_CD, C_BGLU, C_SRE, C_SIM, C_SDT, C_ABS, C_ROW = 62, 64, 66, 74, 82, 90, 346
NCOL = 346 + 640
R_ANG, R_SRE, R_SIM, R_SDT = 0, 256, 1280, 2304
NROW = 3328
K_TRI, K_IOT, K_IOS, K_INVF, K_SGN, K_GM = 0, 128, 256, 257, 258, 259
NCONST = 267


class Tok:
    __slots__ = ("w", "rs", "excl")

    def __init__(self):
        self.w = None
        self.rs = {}
        self.excl = False


class Op:
    __slots__ = ("eng", "fn", "deps", "is_dma", "sig", "users")

    def __init__(self, eng, fn, is_dma):
        self.eng = eng
        self.fn = fn
        self.is_dma = is_dma
        self.deps = []
        self.sig = None
        self.users = 0


ENGS = ("pe", "act", "dve", "pool", "sp")
SYNC_ALL = True
WQ = ("sp",)
DMAQ = ("sp", "act", "pool")


class Prog:
    def __init__(self, nc, n_dma_sems=8):
        self.nc = nc
        self.ops = {e: [] for e in ENGS}
        self.all = []
        self.n_dma_sems = n_dma_sems
        self.toks = {}
        self.dmas_since_bar = []

    def tok(self, *key):
        t = self.toks.get(key)
        if t is None:
            t = self.toks[key] = Tok()
        return t

    def _add(self, eng, fn, reads, writes, is_dma, extra_deps=()):
        op = Op(eng, fn, is_dma)
        deps = list(extra_deps)
        raw = set()
        for t in reads:
            if t.w is not None:
                deps.append(t.w)
                raw.add(id(t.w))
            if t.excl:
                deps.extend(o for o in t.rs.values() if o.eng != eng)
        for t in writes:
            if t.w is not None:
                deps.append(t.w)
            deps.extend(t.rs.values())
        rkey = ("dma", id(op)) if is_dma else eng
        for t in reads:
            t.rs[rkey] = op
        for t in writes:
            t.w = op
            t.rs = {}
        seen = set()
        for d in deps:
            if d is op or id(d) in seen:
                continue
            seen.add(id(d))
            if (not d.is_dma) and (not is_dma) and d.eng == eng:
                if eng == "pe" or (id(d) not in raw and not SYNC_ALL):
                    continue
            if (not d.is_dma) and is_dma and d.eng == eng:
                pass
            op.deps.append(d)
            d.users += 1
        self.ops[eng].append(op)
        self.all.append(op)
        if is_dma:
            self.dmas_since_bar.append(op)
        return op

    def op(self, eng, fn, reads=(), writes=()):
        return self._add(eng, fn, reads, writes, False)

    def dma(self, eng, fn, reads=(), writes=()):
        assert eng in DMAQ
        return self._add(eng, fn, reads, writes, True)

    def barrier(self):
        dm = self.dmas_since_bar
        self.dmas_since_bar = []
        bt = [self.tok("__bar", e) for e in ENGS]
        for i, e in enumerate(ENGS):
            self._add(e, lambda eng: eng.drain(), [], [bt[i]], False,
                      extra_deps=dm if e == "sp" else ())
        for e in ENGS:
            self._add(e, lambda eng: eng.nop(), bt, [], False)

    def emit(self, final_wait_ops=()):
        nc = self.nc
        with contextlib.ExitStack() as es:
            esem = {e: es.enter_context(nc.semaphore("s_" + e)) for e in ENGS}
            dsem = {e: [es.enter_context(nc.semaphore("d_%s%d" % (e, i)))
                        for i in range(self.n_dma_sems)] for e in DMAQ}
            ecount = {e: 0 for e in ENGS}
            dcount = {e: 0 for e in DMAQ}
            duse = {e: [0] * self.n_dma_sems for e in DMAQ}
            fw = set(id(o) for o in final_wait_ops)
            for op in self.all:
                if op.is_dma:
                    j = dcount[op.eng]
                    dcount[op.eng] += 1
                    s = j % self.n_dma_sems
                    duse[op.eng][s] += 1
                    op.sig = (dsem[op.eng][s], 16 * duse[op.eng][s], ("d", op.eng, s))
                elif op.users > 0 or id(op) in fw:
                    ecount[op.eng] += 1
                    op.sig = (esem[op.eng], ecount[op.eng], ("e", op.eng))
            self.stats = dict(ecount=ecount, dcount=dcount,
                              nops={e: len(self.ops[e]) for e in ENGS})
            block = es.enter_context(nc.Block())

            def run_engine(e, engine):
                waited = {}

                def wait(sem, val, key):
                    if waited.get(key, 0) >= val:
                        return
                    waited[key] = val
                    engine.wait_ge(sem, val)

                for op in self.ops[e]:
                    for d in op.deps:
                        wait(*d.sig)
                    if op.is_dma:
                        sem, val, key = op.sig
                        if val > 16:
                            wait(sem, val - 16, key)
                    ins = op.fn(engine)
                    if op.sig is not None:
                        ins.then_inc(op.sig[0], 16 if op.is_dma else 1)
                if e == "sp":
                    for op in final_wait_ops:
                        wait(*op.sig)

            block.tensor(lambda eng: run_engine("pe", eng))
            block.scalar(lambda eng: run_engine("act", eng))
            block.vector(lambda eng: run_engine("dve", eng))
            block.gpsimd(lambda eng: run_engine("pool", eng))
            block.sync(lambda eng: run_engine("sp", eng))


class Arena:
    def __init__(self, ap2d, nfloats):
        self.a = ap2d
        self.n = nfloats
        self.off = 0
        self.peak = 0

    def alloc(self, free_shape, dt):
        free_shape = list(free_shape)
        isz = 2 if dt == BF16 else 4
        n = int(np.prod(free_shape))
        n32 = (n * isz + 3) // 4
        assert self.off + n32 <= self.n, ("arena overflow", self.off, n32, self.n)
        v = self.a[:, self.off:self.off + n32]
        self.off += n32
        self.peak = max(self.peak, self.off)
        if dt == BF16:
            v = v.bitcast(BF16)[:, 0:n]
        elif dt == I32:
            v = v.bitcast(I32)
        if len(free_shape) == 2:
            v = v.rearrange("p (a b) -> p a b", a=free_shape[0])
        elif len(free_shape) == 3:
            v = v.rearrange("p (a b c) -> p a b c", a=free_shape[0], b=free_shape[1])
        elif len(free_shape) == 4:
            v = v.rearrange("p (a b c d) -> p a b c d", a=free_shape[0], b=free_shape[1],
                            c=free_shape[2])
        return v

    def mark(self):
        return self.off

    def reset(self, m):
        self.off = m


class Builder:
    def __init__(self, nlayers=L, stop=None, dumps=()):
        self.nlayers = nlayers
        self.stop = stop
        self.dumps = dict()
        self.want = set(dumps)
        self.nc = nc = bass.Bass("TRN2", target_bir_lowering=False)
        self.P = Prog(nc)
        self.dr = {}
        self.finals = []
        self.dump_specs = []

    def din(self, name, shape, dt=F32):
        self.dr[name] = self.nc.dram_tensor(name, list(shape), dt, kind="ExternalInput").ap()
        return self.dr[name]

    def T(self, *k):
        return self.P.tok(*k)

    def mm(self, out, lhsT, rhs, start, stop, reads, writes):
        return self.P.op("pe", lambda e: e.matmul(out, lhsT=lhsT, rhs=rhs, start=start, stop=stop),
                         reads=reads, writes=writes)

    def mmg(self, out, pairs, reads, wtok):
        n = len(pairs)
        for i, (a, b) in enumerate(pairs):
            self.mm(out, a, b, i == 0, i == n - 1, reads, [wtok])

    def act(self, out, in_, func, reads, writes, bias=0.0, scale=1.0, eng="act", accum_out=None):
        if accum_out is None:
            return self.P.op("act", lambda e: e.activation(out=out, in_=in_, func=func, bias=bias, scale=scale),
                             reads=reads, writes=writes)
        return self.P.op("act", lambda e: e.activation(out=out, in_=in_, func=func, bias=bias, scale=scale,
                                                       accum_out=accum_out),
                         reads=reads, writes=writes)

    def tt(self, eng, out, in0, in1, op, reads, writes):
        return self.P.op(eng, lambda e: e.tensor_tensor(out=out, in0=in0, in1=in1, op=op),
                         reads=reads, writes=writes)

    def ts(self, eng, out, in0, s1, op0, reads, writes, s2=None, op1=None):
        if op1 is None:
            return self.P.op(eng, lambda e: e.tensor_scalar(out=out, in0=in0, scalar1=s1, scalar2=None, op0=op0),
                             reads=reads, writes=writes)
        return self.P.op(eng, lambda e: e.tensor_scalar(out=out, in0=in0, scalar1=s1, scalar2=s2, op0=op0, op1=op1),
                         reads=reads, writes=writes)

    def stt(self, eng, out, in0, scalar, in1, op0, op1, reads, writes):
        return self.P.op(eng, lambda e: e.scalar_tensor_tensor(out=out, in0=in0, scalar=scalar, in1=in1,
                                                                op0=op0, op1=op1),
                         reads=reads, writes=writes)

    def cp(self, eng, out, in_, reads, writes):
        if eng == "act":
            return self.P.op("act", lambda e: e.copy(out=out, in_=in_), reads=reads, writes=writes)
        return self.P.op(eng, lambda e: e.tensor_copy(out=out, in_=in_), reads=reads, writes=writes)

    def memset(self, eng, ap, val, writes):
        return self.P.op(eng, lambda e: e.memset(ap, val), reads=(), writes=writes)

    def dma(self, out, in_, reads, writes, q="sp"):
        def ndesc(ap):
            dims = [(int(st), int(n)) for st, n in ap.ap]
            total = 1
            for st, n in dims:
                total *= n
            run = 1
            for st, n in reversed(dims[1:]):
                if st == run:
                    run *= n
                else:
                    break
            return total // run
        self.desc_count = getattr(self, "desc_count", {})
        self.desc_count[q] = self.desc_count.get(q, 0) + max(ndesc(out), ndesc(in_))
        return self.P.dma(q, lambda e: e.dma_start(out=out, in_=in_), reads=reads, writes=writes)

    def rstd(self, out, in_, n, reads, writes):
        self.act(out, in_, AF.Ln, reads, writes, bias=EPS, scale=1.0 / n)
        self.act(out, out, AF.Exp, writes, writes, scale=-0.5)

    def wload(self, dst, src, dtok, cast_eng="pool", scale=None):
        fs = list(dst.shape[1:])
        n = int(np.prod(fs))
        assert n <= self.stage_n, (n, self.stage_n)
        slot = self.wslot
        self.wslot = (slot + 1) % len(self.stage)
        st = self.stage[slot][:, 0:n]
        if len(fs) == 2:
            st = st.rearrange("p (a b) -> p a b", a=fs[0])
        elif len(fs) == 3:
            st = st.rearrange("p (a b c) -> p a b c", a=fs[0], b=fs[1])
        stok = self.T("stage", slot)
        self.wq_i = getattr(self, "wq_i", 0) + 1
        self.dma(st, src, [], [stok], q=WQ[self.wq_i % len(WQ)])
        if scale is None:
            self.cp(cast_eng, dst, st, [stok], [dtok])
        else:
            self.ts(cast_eng, dst, st, scale, ALU.mult, [stok], [dtok])

    def dump(self, name, ap, tok):
        if name not in self.want:
            return
        self.P.barrier()
        shp = list(ap.shape)
        d = self.nc.dram_tensor("dbg_" + name, shp, ap.dtype, kind="ExternalOutput").ap()
        op = self.dma(d, ap, [tok], [])
        self.finals.append(op)
        self.dump_specs.append(name)

    def build(self):
        nc = self.nc
        P = self.P
        din = self.din
        din("xT", [NB, 128, 8, 512])
        din("memT", [1, 128, 8, 256])
        din("pos", [1, T], I32)
        din("consts", [128, NCONST])
        din("colpack", [L, 128, NCOL])
        din("rowpack", [L, 128, NROW])
        din("w_in", [L, NWG, 128, 8, 256])
        din("w_krm", [L, 128, 8, 96])
        din("w_krp", [L, 128, 8, 96])
        din("wq_m", [L, 8, 128, 6, 96])
        din("wq_p", [L, 8, 128, 6, 96])
        din("w_ukv", [L, 8, 128, 2, 128])
        din("a_w_sT", [L, 128, 4, 128])
        din("c_reT", [L, 2, 128, 4, 128])
        din("c_imT", [L, 2, 128, 4, 128])
        din("c_w_glu", [L, 128, 2, 256])
        din("m_w_kv", [L, 2, 128, 8, 256])
        din("w_br", [L, 4, 128, 10, 256])
        din("w_out", [L, 4, 128, 8, 256])
        self.outT = nc.dram_tensor("outT", [NB, 128, 8, 512], F32, kind="ExternalOutput").ap()
        self.x1T = nc.dram_tensor("x1T", [NB, 128, 8, 512], F32, kind="Internal").ap()

        with contextlib.ExitStack() as es:
            NA = 52000
            arena_t = es.enter_context(nc.sbuf_tensor("arena", [128, NA], F32))
            self.A = A = Arena(arena_t[:, :], NA)
            self.ps = [es.enter_context(nc.psum_tensor("ps%d" % i, [128, 512], F32)) for i in range(8)]
            self.pst = [self.T("ps", i) for i in range(8)]
            for t in self.pst:
                t.excl = True

            self.cst = A.alloc([NCONST], F32)
            self.eps_col = A.alloc([1], F32)
            self.tri = A.alloc([128], BF16)
            self.ntri = A.alloc([128], BF16)
            self.ones = A.alloc([128], BF16)
            self.bd64 = A.alloc([128], BF16)
            self.ones_f = A.alloc([128], F32)
            self.cosT = A.alloc([T], F32)
            self.sinT = A.alloc([T], F32)
            self.colp = A.alloc([NCOL], F32)
            self.hT = A.alloc([8, T], BF16)
            self.yall = A.alloc([10, T], BF16)
            self.kmem = A.alloc([2, 256], BF16)
            self.vmem = A.alloc([2, 256], BF16)
            self.stage_n = 2048
            self.stage = [A.alloc([self.stage_n], F32) for _ in range(2)]
            self.wslot = 0
            self.base_mark = A.mark()

            self.setup_consts()
            if self.stop != "consts":
                for l in range(self.nlayers):
                    if self.run_layer(l):
                        break
            P.barrier()
            P.emit(final_wait_ops=self.finals)
        return nc

    def setup_consts(self):
        A, T_ = self.A, self.T
        tc = T_("cst")
        self.dma(self.cst, self.dr["consts"][:, :], [], [tc])
        self.memset("dve", self.eps_col, EPS, [tc])
        self.cp("dve", self.tri, self.cst[:, K_TRI:K_TRI + 128], [tc], [tc])
        self.ts("dve", self.ntri, self.cst[:, K_TRI:K_TRI + 128], -1.0, ALU.mult, [tc], [tc])
        self.memset("dve", self.ones, 1.0, [tc])
        self.memset("dve", self.ones_f, 1.0, [tc])
        self.memset("dve", self.bd64, 0.0, [tc])
        self.memset("dve", self.bd64[0:64, 0:64], 1.0, [tc])
        self.memset("dve", self.bd64[64:128, 64:128], 1.0, [tc])
        if getattr(self, "skip", None):
            self.memset("pool", self.yall, 0.25, [T_("yall_init")])
        m = A.mark()
        posi = A.alloc([T], I32)
        ang = A.alloc([T], F32)
        kf = A.alloc([T], F32)
        ki = A.alloc([T], I32)
        tr = T_("rope")
        self.dma(posi[0:96, :], self.dr["pos"][0:1, :].partition_broadcast(96), [], [tr])
        R = slice(64, 96)
        self.cp("dve", ang[R, :], posi[R, :], [tr], [tr])
        self.ts("dve", ang[R, :], ang[R, :], self.cst[R, K_INVF:K_INVF + 1], ALU.mult, [tr, tc], [tr])

        def sin_of(dst, shift, post_scale_col):
            self.ts("dve", ki[R, :], ang[R, :], shift, ALU.add, [tr], [tr], s2=1.0 / TWO_PI, op1=ALU.mult)
            self.cp("dve", kf[R, :], ki[R, :], [tr], [tr])
            self.stt("dve", kf[R, :], kf[R, :], -TWO_PI, ang[R, :], ALU.mult, ALU.add, [tr], [tr])
            self.ts("dve", kf[R, :], kf[R, :], shift, ALU.add, [tr], [tr], s2=math.pi, op1=ALU.min)
            self.ts("dve", kf[R, :], kf[R, :], -math.pi, ALU.max, [tr], [tr])
            self.act(dst[R, :], kf[R, :], AF.Sin, [tr], [tr])
            if post_scale_col is not None:
                self.ts("dve", dst[R, :], dst[R, :], post_scale_col, ALU.mult, [tr, tc], [tr])

        sin_of(self.sinT, 0.0, self.cst[R, K_SGN:K_SGN + 1])
        sin_of(self.cosT, math.pi / 2, None)
        self.dump("cosT", self.cosT[R, :], tr)
        self.dump("sinT", self.sinT[R, :], tr)
        self.P.barrier()
        A.reset(m)

    def run_layer(self, l):
        P, A, T_ = self.P, self.A, self.T
        src = self.dr["xT"] if l == 0 else self.x1T
        dst = self.x1T if l == 0 else self.outT
        if self.nlayers == 1:
            dst = self.outT
        tcp = T_("colp")
        self.dma(self.colp, self.dr["colpack"][l, :, :], [], [tcp])
        self.tcp = tcp
        stages = [("N", self.stage_norm), ("MP", self.stage_memprep), ("A", self.stage_a),
                  ("C", self.stage_c), ("M", self.stage_m), ("B", self.stage_b),
                  ("P2", self.stage_p2)]
        for name, fn in stages:
            if name in getattr(self, "skip", ()):
                continue
            m = A.mark()
            if name == "N":
                fn(l, src)
            elif name == "P2":
                fn(l, src, dst)
            else:
                fn(l)
            P.barrier()
            A.reset(m)
            if self.stop == (name, l):
                return True
        return False

    def norm_fm(self, srcT, n_tok, gcol, dst, dst_tok_fn, tag):
        A, T_ = self.A, self.T
        W = min(512, n_tok)
        nblk = n_tok // W
        xb = [A.alloc([8, W], F32) for _ in range(2)]
        sq = A.alloc([8, W], BF16)
        rs = A.alloc([W], F32)
        for b in range(nblk):
            x = xb[b % 2]
            tx = T_(tag + "x", b % 2)
            ts_ = T_(tag + "sq")
            trs = T_(tag + "rs")
            for hh in range(2):
                self.dma(x[:, hh * 4:(hh + 1) * 4, :], srcT[b, :, hh * 4:(hh + 1) * 4, :], [], [tx])
            pb = 0
            for kt in range(8):
                self.act(sq[:, kt, :], x[:, kt, :], AF.Square, [tx], [ts_])
            self.mmg(self.ps[pb][:, 0:W], [(self.ones[:, :], sq[:, kt, :]) for kt in range(8)],
                     [ts_], self.pst[pb])
            self.rstd(rs[:, :], self.ps[pb][:, 0:W], float(D), [self.pst[pb]], [trs])
            for kt in range(8):
                self.stt("dve", dst[:, kt, b * W:(b + 1) * W], x[:, kt, :], gcol[:, kt:kt + 1], rs[:, :],
                         ALU.mult, ALU.mult, [tx, trs, self.tcp], [dst_tok_fn(b)])

    def stage_norm(self, l, src):
        self.norm_fm(src, T, self.colp[:, C_NG:C_NG + 8], self.hT, lambda b: self.T("hT", b), "n")
        for b in range(NB):
            self.dump("hT%d_%d" % (l, b), self.hT[:, :, b * 512:(b + 1) * 512], self.T("hT", b))

    def load_win(self, l, c0, ncols, wt, wtok):
        assert ncols == 256
        self.wload(wt[:, :, 0:ncols], self.dr["w_in"][l, WIN_G[c0]], wtok)

    def zproj_blk(self, wt, ncols, blk, pb, wtok, m_off=0):
        self.mmg(self.ps[pb][m_off:m_off + ncols, :],
                 [(wt[:, kt, 0:ncols], self.hT[:, kt, blk * 512:(blk + 1) * 512]) for kt in range(8)],
                 [wtok, self.T("hT", blk)], self.pst[pb])

    def stage_memprep(self, l):
        A, T_ = self.A, self.T
        hm = A.alloc([8, 256], BF16)
        thm = T_("hm")
        self.norm_fm(self.dr["memT"], 256, self.colp[:, C_MNG:C_MNG + 8], hm, lambda b: thm, "m")
        wkv = A.alloc([8, 512], BF16)
        twk = T_("wkv")
        for half in range(2):
            self.wload(wkv[:, :, half * 256:(half + 1) * 256],
                       self.dr["m_w_kv"][l, half],
                       twk)
        sq = A.alloc([256], BF16)
        rs = A.alloc([256], F32)
        tsq, trs, tkm, tvm = T_("mp_sq"), T_("mp_rs"), T_("kmem"), T_("vmem")
        for ct in range(2):
            self.mmg(self.ps[0][:, 0:256], [(wkv[:, kt, ct * 128:(ct + 1) * 128], hm[:, kt, :]) for kt in range(8)],
                     [twk, thm], self.pst[0])
            self.act(sq[:, :], self.ps[0][:, 0:256], AF.Square, [self.pst[0]], [tsq])
            self.mmg(self.ps[1][:, 0:256], [(self.bd64[:, :], sq[:, :])], [tsq, T_("cst")], self.pst[1])
            self.rstd(rs[:, :], self.ps[1][:, 0:256], 64.0, [self.pst[1]], [trs])
            self.stt("dve", self.kmem[:, ct, :], self.ps[0][:, 0:256], self.colp[:, C_MGK:C_MGK + 1], rs[:, :],
                     ALU.mult, ALU.mult, [self.pst[0], trs, self.tcp], [tkm])
        for mt in range(2):
            self.mmg(self.ps[2][:, 0:256], [(hm[:, kt, mt * 128:(mt + 1) * 128], wkv[:, kt, 256:512]) for kt in range(8)],
                     [twk, thm], self.pst[2])
            self.cp("dve", self.vmem[:, mt, :], self.ps[2][:, 0:256], [self.pst[2]], [tvm])
        self.dump("kmem%d" % l, self.kmem, tkm)
        self.dump("vmem%d" % l, self.vmem, tvm)

    def stage_a(self, l):
        A, T_ = self.A, self.T
        ya = self.yall
        wt = [A.alloc([8, 256], BF16) for _ in range(2)]
        gnb = A.alloc([256], F32)
        wsT = A.alloc([4, 128], BF16)
        tmp = A.alloc([512], BF16)
        tgn, tws, ttmp = T_("a_gn"), T_("a_ws"), T_("a_tmp")
        self.dma(gnb, self.dr["rowpack"][l, :, R_ANG:R_ANG + 256], [], [tgn])
        slot = self.wslot
        self.wslot = (slot + 1) % 2
        st = self.stage[slot][:, 0:512].rearrange("p (g t) -> p g t", g=4)
        stok = T_("stage", slot)
        self.dma(st, self.dr["a_w_sT"][l], [], [stok])
        for g in range(4):
            self.tt("dve", wsT[:, g, :], st[:, g, :], self.cst[:, K_TRI:K_TRI + 128], ALU.mult, [stok, T_("cst")], [tws])
        tw0, tw1 = T_("a_w", 0), T_("a_w", 1)
        self.load_win(l, OFF["a_u"], 256, wt[0], tw0)
        self.load_win(l, OFF["a_g"], 256, wt[1], tw1)
        for ct in range(2):
            for blk in range(NB):
                pb = (ct * NB + blk) % 2
                self.mmg(self.ps[pb][:, :],
                         [(wt[0][:, kt, ct * 128:(ct + 1) * 128], self.hT[:, kt, blk * 512:(blk + 1) * 512])
                          for kt in range(8)], [tw0, T_("hT", blk)], self.pst[pb])
                self.act(ya[:, ct, blk * 512:(blk + 1) * 512], self.ps[pb][:, :], AF.Gelu_apprx_tanh,
                         [self.pst[pb]], [T_("ya", ct, blk)])
        for ct in range(2):
            for blk in range(NB):
                pb = 2 + (ct * NB + blk) % 2
                self.mmg(self.ps[pb][:, :],
                         [(wt[1][:, kt, ct * 128:(ct + 1) * 128], self.hT[:, kt, blk * 512:(blk + 1) * 512])
                          for kt in range(8)], [tw1, T_("hT", blk)], self.pst[pb])
                self.act(tmp[:, :], self.ps[pb][:, :], AF.Silu, [self.pst[pb]], [ttmp])
                sl = ya[:, ct, blk * 512:(blk + 1) * 512]
                self.tt("pool", sl, sl, tmp[:, :], ALU.mult, [ttmp], [T_("ya", ct, blk)])
        twv = T_("a_w", 0)
        self.load_win(l, OFF["a_v"], 256, wt[0], twv)
        gvs = [A.alloc([256], F32) for _ in range(2)]
        ssqs = [A.alloc([1], F32) for _ in range(2)]
        vn = [A.alloc([256], BF16) for _ in range(2)]
        sbs = [A.alloc([2, 128], F32) for _ in range(2)]
        junk = A.alloc([256], BF16)
        absT = self.colp[:, C_ABS:C_ABS + 256].rearrange("p (c t) -> p c t", c=2)
        def a_front(tt_):
            blk = tt_ // 4
            pb = 4 + tt_ % 2
            self.mmg(self.ps[pb][:, 0:256],
                     [(self.hT[:, kt, tt_ * 128:(tt_ + 1) * 128], wt[0][:, kt, :]) for kt in range(8)],
                     [twv, T_("hT", blk)], self.pst[pb])

        def a_back(tt_):
            blk = tt_ // 4
            k = tt_ % 2
            pb = 4 + k
            gv, ssq, sb = gvs[k], ssqs[k], sbs[k]
            tgv, tss, tsb = T_("a_gv", k), T_("a_ss", k), T_("a_sb", k)
            self.act(gv[:, :], self.ps[pb][:, 0:256], AF.Gelu_apprx_tanh, [self.pst[pb]], [tgv])
            self.act(junk[:, :], gv[:, :], AF.Square, [tgv], [tss], accum_out=ssq[:, :])
            self.rstd(ssq[:, :], ssq[:, :], 256.0, [tss], [tss])
            v = vn[k]
            tv = T_("a_vn", k)
            self.stt("dve", v[:, :], gv[:, :], ssq[:, 0:1], gnb[:, :], ALU.mult, ALU.mult, [tgv, tss, tgn], [tv])
            pq = 6 + k
            for g in range(4):
                ct, r0 = g // 2, (g % 2) * 64
                self.mm(self.ps[pq][r0:r0 + 64, ct * 128:(ct + 1) * 128], v[:, g * 64:(g + 1) * 64], wsT[:, g, :],
                        True, True, [tv, tws], [self.pst[pq]])
            psv = self.ps[pq][:, 0:256].rearrange("p (c t) -> p c t", c=2)
            self.tt("dve", sb[:, :, :], psv, absT, ALU.add, [self.pst[pq], self.tcp], [tsb])
            sl = ya[:, 0:2, tt_ * 128:(tt_ + 1) * 128]
            self.tt("pool", sl, sl, sb[:, :, :], ALU.mult, [tsb], [T_("ya", 0, blk), T_("ya", 1, blk)])

        a_front(0)
        for tt_ in range(16):
            if tt_ + 1 < 16:
                a_front(tt_ + 1)
            a_back(tt_)
        self.dump("ya%d" % l, ya[:, 0:2, :], T_("ya", 0, 0))

    def sin_of(self, dst, ang, shift, ki, kf, tok, extra=()):
        rd = [tok] + list(extra)
        self.ts("dve", ki, ang, shift, ALU.add, rd, [tok], s2=1.0 / TWO_PI, op1=ALU.mult)
        self.cp("dve", kf, ki, [tok], [tok])
        self.stt("dve", kf, kf, -TWO_PI, ang, ALU.mult, ALU.add, [tok], [tok])
        self.ts("dve", kf, kf, shift, ALU.add, [tok], [tok], s2=math.pi, op1=ALU.min)
        self.ts("dve", kf, kf, -math.pi, ALU.max, [tok], [tok])
        self.act(dst, kf, AF.Sin, [tok], [tok])

    def stage_m(self, l):
        A, T_ = self.A, self.T
        ya = self.yall
        wq = A.alloc([8, 256], BF16)
        wg = A.alloc([8, 256], BF16)
        twq, twg = T_("m_wq"), T_("m_wg")
        self.load_win(l, OFF["mq"], 256, wq, twq)
        self.load_win(l, OFF["mg"], 256, wg, twg)
        for ct in range(2):
            for blk in range(NB):
                pb = (ct * NB + blk) % 2
                self.mmg(self.ps[pb][:, :],
                         [(wg[:, kt, ct * 128:(ct + 1) * 128], self.hT[:, kt, blk * 512:(blk + 1) * 512])
                          for kt in range(8)], [twg], self.pst[pb])
                self.act(ya[:, 8 + ct, blk * 512:(blk + 1) * 512], self.ps[pb][:, :], AF.Silu,
                         [self.pst[pb]], [T_("ym", ct, blk)])
        sq = A.alloc([512], BF16)
        rs = A.alloc([512], F32)
        qn = [A.alloc([2, 512], BF16) for _ in range(2)]
        pT = [A.alloc([512], BF16) for _ in range(4)]
        rc = A.alloc([512], F32)
        ot = A.alloc([512], BF16)
        tsq, trs, trc, tot = T_("m_sq"), T_("m_rs"), T_("m_rc"), T_("m_ot")
        scale = 64.0 ** -0.5

        def m_prep(blk):
            q = qn[blk % 2]
            tq = T_("m_qn", blk % 2)
            for ct in range(2):
                self.mmg(self.ps[2][:, :],
                         [(wq[:, kt, ct * 128:(ct + 1) * 128], self.hT[:, kt, blk * 512:(blk + 1) * 512])
                          for kt in range(8)], [twq], self.pst[2])
                self.act(sq[:, :], self.ps[2][:, :], AF.Square, [self.pst[2]], [tsq])
                self.mmg(self.ps[3][:, :], [(self.bd64[:, :], sq[:, :])], [tsq], self.pst[3])
                self.rstd(rs[:, :], self.ps[3][:, :], 64.0, [self.pst[3]], [trs])
                self.stt("dve", q[:, ct, :], self.ps[2][:, :], self.colp[:, C_MGQ:C_MGQ + 1], rs[:, :],
                         ALU.mult, ALU.mult, [self.pst[2], trs], [tq])

        def m_s(blk, h):
            q = qn[blk % 2]
            tq = T_("m_qn", blk % 2)
            ct, r0 = h // 2, (h % 2) * 64
            R = slice(r0, r0 + 64)
            for mt in range(2):
                pb = (4 + mt) if h % 2 == 0 else mt
                self.mm(self.ps[pb][:, :], self.kmem[R, ct, mt * 128:(mt + 1) * 128], q[R, ct, :],
                        True, True, [tq], [self.pst[pb]])

        def m_rest(blk, h):
            ct, r0 = h // 2, (h % 2) * 64
            R = slice(r0, r0 + 64)
            for mt in range(2):
                pb = (4 + mt) if h % 2 == 0 else mt
                p = pT[(h % 2) * 2 + mt]
                tp = T_("m_pT", (h % 2) * 2 + mt)
                self.act(p[:, :], self.ps[pb][:, :], AF.Exp, [self.pst[pb]], [tp], scale=scale)
            tps = [T_("m_pT", (h % 2) * 2 + mt) for mt in range(2)]
            ps_o, ps_d = self.ps[6], self.ps[7]
            self.mmg(ps_o[R, :], [(self.vmem[:, mt, h * 64:(h + 1) * 64], pT[(h % 2) * 2 + mt][:, :])
                                  for mt in range(2)], tps, self.pst[6])
            self.mmg(ps_d[R, :], [(self.ones[:, 0:64], pT[(h % 2) * 2 + mt][:, :]) for mt in range(2)],
                     tps, self.pst[7])
            self.P.op("dve", lambda e, o=rc[R, :], i=ps_d[R, :]: e.reciprocal(out=o, in_=i),
                      reads=[self.pst[7]], writes=[trc])
            self.tt("dve", ot[R, :], ps_o[R, :], rc[R, :], ALU.mult, [self.pst[6], trc], [tot])
            sl = ya[R, 8 + ct, blk * 512:(blk + 1) * 512]
            self.tt("pool", sl, sl, ot[R, :], ALU.mult, [tot], [T_("ym", ct, blk)])

        m_prep(0)
        for blk in range(NB):
            m_s(blk, 0)
            if blk + 1 < NB:
                m_prep(blk + 1)
            for h in range(4):
                if h + 1 < 4:
                    m_s(blk, h + 1)
                m_rest(blk, h)
        self.dump("ym%d" % l, ya[:, 8:10, :], T_("ym", 0, 0))

    def stage_c(self, l):
        A, T_ = self.A, self.T
        ya = self.yall
        uT = A.alloc([2, T], BF16)
        ygT = A.alloc([2, T], BF16)
        TAc = A.alloc([1024], F32)
        TAs = A.alloc([1024], F32)
        Dr = A.alloc([8, 128], F32)
        Di = A.alloc([8, 128], F32)
        a128r = A.alloc([8], F32)
        a128i = A.alloc([8], F32)
        Bm = [A.alloc([2, 8, 64], BF16) for _ in range(2)]
        CreT = A.alloc([8, 128], BF16)
        CreTn = A.alloc([8, 128], BF16)
        CimTn = A.alloc([8, 128], BF16)
        wglu = A.alloc([2, 256], BF16)
        m2 = A.mark()
        wu = A.alloc([8, 256], BF16)
        wg = A.alloc([8, 256], BF16)
        twu, twg = T_("c_wu"), T_("c_wg")
        self.load_win(l, OFF["cin"], 256, wu, twu)
        self.load_win(l, OFF["cg"], 256, wg, twg)
        for ct in range(2):
            for blk in range(NB):
                pb = (ct * NB + blk) % 2
                self.mmg(self.ps[pb][:, :],
                         [(wu[:, kt, ct * 128:(ct + 1) * 128], self.hT[:, kt, blk * 512:(blk + 1) * 512])
                          for kt in range(8)], [twu], self.pst[pb])
                self.cp("act", uT[:, ct, blk * 512:(blk + 1) * 512], self.ps[pb][:, :], [self.pst[pb]],
                        [T_("c_uT", blk)])
        for ct in range(2):
            for blk in range(NB):
                pb = 2 + (ct * NB + blk) % 2
                self.mmg(self.ps[pb][:, :],
                         [(wg[:, kt, ct * 128:(ct + 1) * 128], self.hT[:, kt, blk * 512:(blk + 1) * 512])
                          for kt in range(8)], [twg], self.pst[pb])
                self.act(ya[:, 6 + ct, blk * 512:(blk + 1) * 512], self.ps[pb][:, :], AF.Silu,
                         [self.pst[pb]], [T_("yc", ct, blk)])
        tt_ = T_("c_tab")
        rowp = A.alloc([3, 1024], F32)
        self.dma(rowp, self.dr["rowpack"][l, :, R_SRE:R_SRE + 3072].rearrange("p (a b) -> p a b", a=3), [], [tt_])
        s1 = A.alloc([1024], F32)
        s2 = A.alloc([1024], F32)
        si = A.alloc([1024], I32)
        negs = A.alloc([1], F32)
        tcst = T_("cst")
        self.ts("dve", negs, self.cst[:, K_IOS:K_IOS + 1], -1.0, ALU.mult, [tcst], [tt_])
        self.act(rowp[:, 2, :], rowp[:, 2, :], AF.Exp, [tt_], [tt_])
        self.tt("dve", rowp[:, 0, :], rowp[:, 0, :], rowp[:, 2, :], ALU.mult, [tt_], [tt_])
        self.tt("dve", rowp[:, 1, :], rowp[:, 1, :], rowp[:, 2, :], ALU.mult, [tt_], [tt_])
        self.act(s1, rowp[:, 0, :], AF.Exp, [tt_], [tt_], scale=negs[:, 0:1])
        self.ts("dve", s2, rowp[:, 1, :], self.cst[:, K_IOS:K_IOS + 1], ALU.mult, [tt_, tcst], [tt_])
        self.sin_of(TAs, s2, 0.0, si, rowp[:, 2, :], tt_)
        self.sin_of(TAc, s2, math.pi / 2, si, rowp[:, 2, :], tt_)
        self.tt("dve", TAs, TAs, s1, ALU.mult, [tt_], [tt_])
        self.tt("dve", TAc, TAc, s1, ALU.mult, [tt_], [tt_])
        s1v = s1.rearrange("p (j t) -> p j t", j=8)
        s2v = s2.rearrange("p (j t) -> p j t", j=8)
        siv = si.rearrange("p (j t) -> p j t", j=8)
        kfv = rowp[:, 2, :].rearrange("p (j t) -> p j t", j=8)
        dtj = A.alloc([8], F32)
        thrj = A.alloc([8], F32)
        thij = A.alloc([8], F32)
        e128 = A.alloc([8], F32)
        p128 = A.alloc([8], F32)
        k128 = A.alloc([8], F32)
        i128 = A.alloc([8], I32)
        tcp = self.tcp
        self.act(dtj, self.colp[:, C_SDT:C_SDT + 8], AF.Exp, [tcp, tt_], [tt_])
        self.tt("dve", thrj, self.colp[:, C_SRE:C_SRE + 8], dtj, ALU.mult, [tcp, tt_], [tt_])
        self.tt("dve", thij, self.colp[:, C_SIM:C_SIM + 8], dtj, ALU.mult, [tcp, tt_], [tt_])
        iot = self.cst[:, K_IOT:K_IOT + 128]
        for j in range(8):
            self.act(s1v[:, j, :], iot, AF.Exp, [tt_, tcst], [tt_], scale=thrj[:, j:j + 1])
            self.ts("dve", s2v[:, j, :], iot, thij[:, j:j + 1], ALU.mult, [tt_, tcst], [tt_])
        self.sin_of(Di.rearrange("p j t -> p (j t)"), s2, 0.0, si, rowp[:, 2, :], tt_)
        self.sin_of(Dr.rearrange("p j t -> p (j t)"), s2, math.pi / 2, si, rowp[:, 2, :], tt_)
        self.tt("dve", Di, Di, s1v, ALU.mult, [tt_], [tt_])
        self.tt("dve", Dr, Dr, s1v, ALU.mult, [tt_], [tt_])
        self.act(e128, thrj, AF.Exp, [tt_], [tt_], scale=128.0)
        self.ts("dve", p128, thij, 128.0, ALU.mult, [tt_], [tt_])
        self.sin_of(a128i, p128, 0.0, i128, k128, tt_)
        self.sin_of(a128r, p128, math.pi / 2, i128, k128, tt_)
        self.tt("dve", a128i, a128i, e128, ALU.mult, [tt_], [tt_])
        self.tt("dve", a128r, a128r, e128, ALU.mult, [tt_], [tt_])
        w = [A.alloc([64], F32) for _ in range(8)]
        wi = A.alloc([64], I32)
        gmask = self.cst[:, K_GM:K_GM + 8]
        for ct in range(2):
            base = C_ROW + ct * 320
            are = self.colp[:, base:base + 64]
            aim = self.colp[:, base + 64:base + 128]
            ldt = self.colp[:, base + 128:base + 192]
            bre = self.colp[:, base + 192:base + 256]
            bim = self.colp[:, base + 256:base + 320]
            dt_, thr, thi, ea, abr, abi, t0, t1 = w
            rd = [tt_, tcp]
            self.act(dt_, ldt, AF.Exp, rd, [tt_])
            self.tt("dve", thr, are, dt_, ALU.mult, rd, [tt_])
            self.tt("dve", thi, aim, dt_, ALU.mult, rd, [tt_])
            self.act(ea, thr, AF.Exp, [tt_], [tt_])
            self.sin_of(abi, thi, 0.0, wi, t0, tt_)
            self.sin_of(abr, thi, math.pi / 2, wi, t0, tt_)
            self.tt("dve", abi, abi, ea, ALU.mult, [tt_], [tt_])
            self.tt("dve", abr, abr, ea, ALU.mult, [tt_], [tt_])
            self.ts("dve", abr, abr, -1.0, ALU.add, [tt_], [tt_])
            self.tt("dve", dt_, are, are, ALU.mult, rd, [tt_])
            self.tt("dve", t0, aim, aim, ALU.mult, rd, [tt_])
            self.tt("dve", dt_, dt_, t0, ALU.add, [tt_], [tt_])
            self.P.op("dve", lambda e, o=dt_, i=dt_: e.reciprocal(out=o, in_=i), reads=[tt_], writes=[tt_])
            self.tt("dve", t0, abr, are, ALU.mult, rd, [tt_])
            self.tt("dve", t1, abi, aim, ALU.mult, rd, [tt_])
            self.tt("dve", t0, t0, t1, ALU.add, [tt_], [tt_])
            self.tt("dve", thr, t0, dt_, ALU.mult, [tt_], [tt_])
            self.tt("dve", t0, abi, are, ALU.mult, rd, [tt_])
            self.tt("dve", t1, abr, aim, ALU.mult, rd, [tt_])
            self.tt("dve", t0, t0, t1, ALU.subtract, [tt_], [tt_])
            self.tt("dve", thi, t0, dt_, ALU.mult, [tt_], [tt_])
            self.tt("dve", t0, thr, bre, ALU.mult, rd, [tt_])
            self.tt("dve", t1, thi, bim, ALU.mult, rd, [tt_])
            self.tt("dve", ea, t0, t1, ALU.subtract, [tt_], [tt_])
            self.tt("dve", t0, thr, bim, ALU.mult, rd, [tt_])
            self.tt("dve", t1, thi, bre, ALU.mult, rd, [tt_])
            self.tt("dve", abi, t0, t1, ALU.add, [tt_], [tt_])
            for ri, src_ in enumerate((ea, abi)):
                self.tt("dve", Bm[ct][:, ri, :, :], src_.unsqueeze(1).to_broadcast([128, 8, 64]),
                        gmask.unsqueeze(2).to_broadcast([128, 8, 64]), ALU.mult, [tt_, tcst], [tt_])
        tct = T_("c_ct")
        for j0 in (0, 4):
            srcr = self.dr["c_reT"][l, j0 // 4]
            srci = self.dr["c_imT"][l, j0 // 4]
            self.wload(CreT[:, j0:j0 + 4, :], srcr, tct)
            self.wload(CreTn[:, j0:j0 + 4, :], srcr, tct, scale=-1.0)
            self.wload(CimTn[:, j0:j0 + 4, :], srci, tct, scale=-1.0)
        self.wload(wglu, self.dr["c_w_glu"][l], tct)
        self.dump("c_TAc%d" % l, TAc, tt_)
        self.dump("c_TAs%d" % l, TAs, tt_)
        self.dump("c_Dr%d" % l, Dr, tt_)
        self.dump("c_Di%d" % l, Di, tt_)
        self.dump("c_Bm%d" % l, Bm[0], tt_)
        self.dump("c_a128r%d" % l, a128r, tt_)
        self.P.barrier()
        A.reset(m2)
        aS = A.alloc([16], F32)
        self.memset("dve", aS, 0.0, [T_("c_aS", 0), T_("c_aS", 1)])
        X = [A.alloc([4, 512], BF16) for _ in range(2)]
        Pfs = [A.alloc([8, 128], F32) for _ in range(2)]
        Ys = [A.alloc([4, 4, 128], BF16) for _ in range(2)]
        p127 = A.alloc([8], F32)
        c1 = A.alloc([8], F32)
        c2 = A.alloc([8], F32)
        yss = [A.alloc([128], F32) for _ in range(2)]
        tch = T_("c_ch")
        its = [(n, ct) for n in range(16) for ct in range(2)]

        def front(i):
            n, ct = its[i]
            par = i % 2
            blk = n // 4
            cs = slice(n * 128, (n + 1) * 128)
            pre_, pim_ = self.ps[0], self.ps[1]
            tpre, tpim = self.pst[0], self.pst[1]
            Bre = Bm[ct][:, 0, :, :].rearrange("p g q -> p (g q)")
            Bim = Bm[ct][:, 1, :, :].rearrange("p g q -> p (g q)")
            self.mm(pre_[:, :], uT[:, ct, cs], Bre, True, True, [T_("c_uT", blk), tt_], [tpre])
            self.mm(pim_[:, :], uT[:, ct, cs], Bim, True, True, [T_("c_uT", blk), tt_], [tpim])
            x = X[par]
            tx = T_("c_X", par)
            tc_ = TAc[:, ct * 512:(ct + 1) * 512]
            ts_ = TAs[:, ct * 512:(ct + 1) * 512]
            self.tt("dve", x[:, 0, :], pre_[:, :], tc_, ALU.mult, [tpre, tt_], [tx])
            self.tt("dve", x[:, 3, :], pre_[:, :], ts_, ALU.mult, [tpre, tt_], [tx])
            self.tt("dve", x[:, 1, :], pim_[:, :], ts_, ALU.mult, [tpim, tt_], [tx])
            self.tt("dve", x[:, 2, :], pim_[:, :], tc_, ALU.mult, [tpim, tt_], [tx])
            Pre, Pim = self.ps[2 + 2 * par], self.ps[3 + 2 * par]
            for jl in range(4):
                js = slice(jl * 128, (jl + 1) * 128)
                self.mmg(Pre[:, js], [(x[:, 0, js], self.tri[:, :]), (x[:, 1, js], self.tri[:, :])],
                         [tx], self.pst[2 + 2 * par])
                self.mmg(Pim[:, js], [(x[:, 2, js], self.tri[:, :]), (x[:, 3, js], self.ntri[:, :])],
                         [tx], self.pst[3 + 2 * par])

        def back_a(i):
            n, ct = its[i]
            par = i % 2
            Pre, Pim = self.ps[2 + 2 * par], self.ps[3 + 2 * par]
            tP0, tP1 = self.pst[2 + 2 * par], self.pst[3 + 2 * par]
            Pf, Y = Pfs[par], Ys[par]
            tPf, tY = T_("c_Pf", par), T_("c_Y", par)
            ta = T_("c_aS", ct)
            jr = slice(4 * ct, 4 * ct + 4)
            ji = slice(8 + 4 * ct, 8 + 4 * ct + 4)
            for jl in range(4):
                js = slice(jl * 128, (jl + 1) * 128)
                self.act(Pf[:, jl, :], Pre[:, js], AF.Identity, [tP0, ta], [tPf],
                         bias=aS[:, 4 * ct + jl:4 * ct + jl + 1])
                self.act(Pf[:, 4 + jl, :], Pim[:, js], AF.Identity, [tP1, ta], [tPf],
                         bias=aS[:, 8 + 4 * ct + jl:8 + 4 * ct + jl + 1])
            self.cp("dve", p127, Pf[:, :, 127], [tPf], [tch])
            self.tt("dve", c1[:, 0:4], a128r[:, jr], p127[:, 0:4], ALU.mult, [tch, tt_], [tch])
            self.tt("dve", c1[:, 4:8], a128i[:, jr], p127[:, 4:8], ALU.mult, [tch, tt_], [tch])
            self.tt("dve", c2[:, 0:4], a128r[:, jr], p127[:, 4:8], ALU.mult, [tch, tt_], [tch])
            self.tt("dve", c2[:, 4:8], a128i[:, jr], p127[:, 0:4], ALU.mult, [tch, tt_], [tch])
            self.tt("dve", aS[:, jr], c1[:, 0:4], c1[:, 4:8], ALU.subtract, [tch], [ta])
            self.tt("dve", aS[:, ji], c2[:, 0:4], c2[:, 4:8], ALU.add, [tch], [ta])
            self.tt("pool", Y[:, 0, :, :], Pf[:, 0:4, :], Dr[:, jr, :], ALU.mult, [tPf, tt_], [tY])
            self.tt("pool", Y[:, 1, :, :], Pf[:, 4:8, :], Di[:, jr, :], ALU.mult, [tPf, tt_], [tY])
            tY2 = T_("c_Y2", par)
            self.tt("dve", Y[:, 2, :, :], Pf[:, 4:8, :], Dr[:, jr, :], ALU.mult, [tPf, tt_], [tY2])
            self.tt("dve", Y[:, 3, :, :], Pf[:, 0:4, :], Di[:, jr, :], ALU.mult, [tPf, tt_], [tY2])

        def back_y(i):
            n, ct = its[i]
            par = i % 2
            Y = Ys[par]
            tY, tY2 = T_("c_Y", par), T_("c_Y2", par)
            py = self.ps[6 + par]
            pairs = []
            for jl in range(4):
                j = 4 * ct + jl
                pairs += [(CreT[:, j, :], Y[:, 0, jl, :]), (CreTn[:, j, :], Y[:, 1, jl, :]),
                          (CimTn[:, j, :], Y[:, 2, jl, :]), (CimTn[:, j, :], Y[:, 3, jl, :])]
            self.mmg(py[:, 0:128], pairs, [tY, tY2, tct], self.pst[6 + par])

        def back_b(i):
            n, ct = its[i]
            par = i % 2
            blk = n // 4
            cs = slice(n * 128, (n + 1) * 128)
            py = self.ps[6 + par]
            ys, tys = yss[par], T_("c_ys", par)
            self.stt("dve", ys, uT[:, ct, cs], self.colp[:, C_CD + ct:C_CD + ct + 1], py[:, 0:128],
                     ALU.mult, ALU.add, [self.pst[6 + par], T_("c_uT", blk), tcp], [tys])
            self.act(ygT[:, ct, cs], ys, AF.Gelu_apprx_tanh, [tys], [T_("c_yg", blk)])

        front(0)
        for i in range(len(its)):
            if i + 1 < len(its):
                front(i + 1)
            back_a(i)
            if i >= 1:
                back_y(i - 1)
            if i >= 2:
                back_b(i - 2)
        back_y(len(its) - 1)
        back_b(len(its) - 2)
        back_b(len(its) - 1)
        self.dump("c_yg%d" % l, ygT, T_("c_yg", 0))
        sg = A.alloc([512], BF16)
        tsg = T_("c_sg")
        for blk in range(NB):
            bs = slice(blk * 512, (blk + 1) * 512)
            for cc in range(2):
                pb = 7
                self.mmg(self.ps[pb][:, :], [(wglu[:, ct, cc * 128:(cc + 1) * 128], ygT[:, ct, bs]) for ct in range(2)],
                         [tct, T_("c_yg", blk)], self.pst[pb])
                self.act(sg, self.ps[pb][:, :], AF.Sigmoid, [self.pst[pb], tcp], [tsg],
                         bias=self.colp[:, C_BGLU + cc:C_BGLU + cc + 1])
                sl = ya[:, 6 + cc, bs]
                self.tt("pool", sg, sg, ygT[:, cc, bs], ALU.mult, [tsg, T_("c_yg", blk)], [tsg])
                self.tt("pool", sl, sl, sg, ALU.mult, [tsg], [T_("yc", cc, blk)])
        self.dump("yc%d" % l, ya[:, 6:8, :], T_("yc", 0, 0))

    def stage_b(self, l):
        A, T_, P = self.A, self.T, self.P
        ya = self.yall
        tcp = self.tcp
        tcst = T_("cst")
        cqn = A.alloc([6, T], BF16)
        ckvn = A.alloc([2, T], BF16)
        krr = A.alloc([T], F32)
        sqkr = A.alloc([T], BF16)
        m2 = A.mark()
        wq_in = A.alloc([8, 768], BF16)
        wkv_in = A.alloc([8, 256], BF16)
        wkr = [A.alloc([8, 96], BF16) for _ in range(2)]
        m3 = A.mark()
        wt = A.alloc([8, 512], BF16)
        tw = T_("b_w")
        tw1 = T_("b_w1")
        for i in range(2):
            self.load_win(l, OFF["bg"] + i * 256, 256, wt[:, :, i * 256:(i + 1) * 256], tw)
        for i in range(3):
            self.load_win(l, OFF["cq"] + i * 256, 256, wq_in[:, :, i * 256:(i + 1) * 256], tw1)
        self.load_win(l, OFF["ckv"], 256, wkv_in, tw1)
        self.wload(wkr[0], self.dr["w_krm"][l], tw1)
        self.wload(wkr[1], self.dr["w_krp"][l], tw1)
        for ct in range(4):
            for blk in range(NB):
                pb = (ct * NB + blk) % 2
                self.zproj_blk(wt[:, :, ct * 128:(ct + 1) * 128], 128, blk, pb, tw)
                self.act(ya[:, 2 + ct, blk * 512:(blk + 1) * 512], self.ps[pb][:, :], AF.Silu,
                         [self.pst[pb]], [T_("yb", ct, blk)])
        P.barrier()
        A.reset(m3)
        tw = tw1
        sq = A.alloc([512], BF16)
        rs = A.alloc([512], F32)
        t1 = A.alloc([512], F32)
        t2 = A.alloc([512], F32)
        cqf = A.alloc([6, 512], F32)
        ckf = A.alloc([2, 512], F32)
        rs2 = A.alloc([512], F32)
        tsq, trs = T_("b_sq"), T_("b_rs")
        R = slice(64, 96)
        for blk in range(NB):
            bs = slice(blk * 512, (blk + 1) * 512)
            tcf = T_("b_cqf")
            for i in range(6):
                pb = i % 2
                self.zproj_blk(wq_in[:, :, i * 128:(i + 1) * 128], 128, blk, pb, tw)
                self.act(sq, self.ps[pb][:, :], AF.Square, [self.pst[pb]], [tsq])
                self.cp("act", cqf[:, i, :], self.ps[pb][:, :], [self.pst[pb]], [tcf])
                self.mm(self.ps[6][:, :], self.ones[:, :], sq, i == 0, i == 5, [tsq], [self.pst[6]])
            tkf = T_("b_ckf")
            for i in range(2):
                pb = 4 + i
                self.zproj_blk(wkv_in[:, :, i * 128:(i + 1) * 128], 128, blk, pb, tw)
                self.act(sq, self.ps[pb][:, :], AF.Square, [self.pst[pb]], [tsq])
                self.cp("act", ckf[:, i, :], self.ps[pb][:, :], [self.pst[pb]], [tkf])
                self.mm(self.ps[7][:, :], self.ones[:, :], sq, i == 0, i == 1, [tsq], [self.pst[7]])
            self.rstd(rs, self.ps[6][:, :], 768.0, [self.pst[6]], [trs])
            for i in range(6):
                self.stt("dve", cqn[:, i, bs], cqf[:, i, :], self.colp[:, C_QNG + i:C_QNG + i + 1], rs,
                         ALU.mult, ALU.mult, [tcf, trs, tcp], [T_("b_cqn", blk)])
            trs2 = T_("b_rs2")
            self.rstd(rs2, self.ps[7][:, :], 256.0, [self.pst[7]], [trs2])
            for i in range(2):
                self.stt("dve", ckvn[:, i, bs], ckf[:, i, :], self.colp[:, C_KVNG + i:C_KVNG + i + 1], rs2,
                         ALU.mult, ALU.mult, [tkf, trs2, tcp], [T_("b_ckvn", blk)])
            for i in range(2):
                self.zproj_blk(wkr[i], 96, blk, 2 + i, tw)
            self.act(sqkr[R, bs], self.ps[2][R, :], AF.Square, [self.pst[2]], [T_("b_kr", blk)])
            tt1 = T_("b_t1")
            self.stt("dve", t1[R, :], self.ps[2][R, :], self.colp[R, C_GK:C_GK + 1], self.cosT[R, bs],
                     ALU.mult, ALU.mult, [self.pst[2], tcp], [tt1])
            self.stt("dve", t2[R, :], self.ps[3][R, :], self.colp[R, C_GKP:C_GKP + 1], self.sinT[R, bs],
                     ALU.mult, ALU.mult, [self.pst[3], tcp], [tt1])
            self.tt("pool", krr[R, bs], t1[R, :], t2[R, :], ALU.add, [tt1], [T_("b_kr", blk)])
        self.dump("b_cqn%d" % l, cqn, T_("b_cqn", 0))
        self.dump("b_ckvn%d" % l, ckvn, T_("b_ckvn", 0))
        self.dump("b_krr%d" % l, krr[R, :], T_("b_kr", 0))
        P.barrier()
        A.reset(m2)
        wqm = [A.alloc([6, 96], BF16) for _ in range(2)]
        wqp = [A.alloc([6, 96], BF16) for _ in range(2)]
        wkv = [A.alloc([2, 128], BF16) for _ in range(2)]
        wk = [w_[:, :, 0:64] for w_ in wkv]
        wv = [w_[:, :, 64:128] for w_ in wkv]
        qn = [A.alloc([T], BF16) for _ in range(2)]
        kn = [A.alloc([T], BF16) for _ in range(2)]
        vh = [A.alloc([16, 64], BF16) for _ in range(2)]
        sq = [A.alloc([512], BF16) for _ in range(2)]
        rs = [A.alloc([512], F32) for _ in range(2)]
        t1 = A.alloc([512], F32)
        t2 = A.alloc([512], F32)
        pT = [A.alloc([512], BF16) for _ in range(4)]
        rcs = [A.alloc([512], F32) for _ in range(2)]
        ots = [A.alloc([512], BF16) for _ in range(2)]
        scale = 96.0 ** -0.5
        def b_load(h_):
            s_ = h_ % 2
            twh_ = T_("b_wh", s_)
            self.wload(wqm[s_], self.dr["wq_m"][l, h_], twh_)
            self.wload(wqp[s_], self.dr["wq_p"][l, h_], twh_)
            self.wload(wkv[s_], self.dr["w_ukv"][l, h_], twh_)

        b_load(0)
        for h in range(8):
            s = h % 2
            twh = T_("b_wh", s)
            tq, tk, tv = T_("b_qn", s), T_("b_kn", s), T_("b_vh", s)
            Q = slice(0, 96)
            N_ = slice(0, 64)
            for half in range(2):
                pbv = 6 + half
                for t8 in range(8):
                    tt_ = half * 8 + t8
                    self.mmg(self.ps[pbv][:, t8 * 64:(t8 + 1) * 64],
                             [(ckvn[:, kt, tt_ * 128:(tt_ + 1) * 128], wv[s][:, kt, :]) for kt in range(2)],
                             [twh], self.pst[pbv])
                self.cp("dve", vh[s][:, half * 8:(half + 1) * 8, :],
                        self.ps[pbv][:, :].rearrange("p (a b) -> p a b", a=8), [self.pst[pbv]], [tv])

            def banks(blk):
                o = 0 if blk % 2 == 0 else 4
                return o, o + 1, o + 2, o + 3

            def prep_f1(blk):
                bs = slice(blk * 512, (blk + 1) * 512)
                bq, bp, _, bk = banks(blk)
                self.mmg(self.ps[bq][Q, :], [(wqm[s][:, kt, :], cqn[:, kt, bs]) for kt in range(6)], [twh], self.pst[bq])

            def prep_f2(blk):
                bs = slice(blk * 512, (blk + 1) * 512)
                bq, bp, _, bk = banks(blk)
                self.mmg(self.ps[bp][Q, :], [(wqp[s][:, kt, :], cqn[:, kt, bs]) for kt in range(6)], [twh], self.pst[bp])

            def prep_f3(blk):
                bs = slice(blk * 512, (blk + 1) * 512)
                bq, bp, _, bk = banks(blk)
                self.mmg(self.ps[bk][N_, :], [(wk[s][:, kt, :], ckvn[:, kt, bs]) for kt in range(2)], [twh], self.pst[bk])

            def prep_back(blk, nxt):
                bs = slice(blk * 512, (blk + 1) * 512)
                bq, bp, bsq, bk = banks(blk)
                tsq0, trs0 = T_("b_sq", 0), T_("b_rs", 0)
                tsq1, trs1 = T_("b_sq", 1), T_("b_rs", 1)
                self.act(sq[0][Q, :], self.ps[bq][Q, :], AF.Square, [self.pst[bq]], [tsq0])
                self.act(sq[1][N_, :], self.ps[bk][N_, :], AF.Square, [self.pst[bk]], [tsq1])
                self.cp("pool", sq[1][R, :], sqkr[R, bs], [], [tsq1])
                if nxt is not None:
                    prep_f1(nxt)
                self.mmg(self.ps[bsq][Q, :], [(self.ones[Q, 0:96], sq[0][Q, :])], [tsq0], self.pst[bsq])
                self.rstd(rs[0][Q, :], self.ps[bsq][Q, :], 96.0, [self.pst[bsq]], [trs0])
                if nxt is not None:
                    prep_f2(nxt)
                self.mmg(self.ps[bsq][Q, :], [(self.ones[Q, 0:96], sq[1][Q, :])], [tsq1], self.pst[bsq])
                self.rstd(rs[1][Q, :], self.ps[bsq][Q, :], 96.0, [self.pst[bsq]], [trs1])
                if nxt is not None:
                    prep_f3(nxt)
                tt1 = T_("b_t1")
                self.stt("dve", t1[R, :], self.ps[bq][R, :], self.colp[R, C_GQ:C_GQ + 1], self.cosT[R, bs],
                         ALU.mult, ALU.mult, [self.pst[bq], tcp], [tt1])
                self.stt("dve", t2[R, :], self.ps[bp][R, :], self.colp[R, C_GQP:C_GQP + 1], self.sinT[R, bs],
                         ALU.mult, ALU.mult, [self.pst[bp], tcp], [tt1])
                self.stt("dve", qn[s][N_, bs], self.ps[bq][N_, :], self.colp[N_, C_GQ:C_GQ + 1], rs[0][N_, :],
                         ALU.mult, ALU.mult, [self.pst[bq], trs0, tcp], [tq])
                self.stt("dve", kn[s][N_, bs], self.ps[bk][N_, :], self.colp[N_, C_GK:C_GK + 1], rs[1][N_, :],
                         ALU.mult, ALU.mult, [self.pst[bk], trs1, tcp], [tk])
                self.tt("pool", t1[R, :], t1[R, :], t2[R, :], ALU.add, [tt1], [tt1])
                self.tt("pool", qn[s][R, bs], t1[R, :], rs[0][R, :], ALU.mult, [tt1, trs0], [tq])
                self.tt("pool", kn[s][R, bs], krr[R, bs], rs[1][R, :], ALU.mult, [trs1], [tk])

            prep_f1(0)
            prep_f2(0)
            prep_f3(0)
            for blk in range(NB):
                prep_back(blk, blk + 1 if blk + 1 < NB else None)
            if h == 0:
                self.dump("b_qn%d" % l, qn[0][0:96, :], tq)
                self.dump("b_kn%d" % l, kn[0][0:96, :], tk)
                self.dump("b_vh%d" % l, vh[0], tv)
            if h + 1 < 8:
                b_load(h + 1)
            ct, r0 = h // 2, (h % 2) * 64
            RR = slice(r0, r0 + 64)
            seq = [(b, j) for b in range(NB) for j in range(4 * b + 4)]

            def att_s(i):
                b, j = seq[i]
                jj = j - 4 * b
                c0 = 128 * jj if jj > 0 else 0
                pb = 4 + i % 2
                self.mm(self.ps[pb][:, c0:512], kn[s][0:96, j * 128:(j + 1) * 128],
                        qn[s][0:96, b * 512 + c0:(b + 1) * 512], True, True, [tq, tk], [self.pst[pb]])

            def att_rest(i):
                b, j = seq[i]
                nj = 4 * b + 4
                jj = j - 4 * b
                c0 = 128 * jj if jj > 0 else 0
                pb = 4 + i % 2
                p = pT[i % 4]
                tp = T_("b_pT", i % 4)
                po, pd = (6, 7) if b % 2 == 0 else (2, 3)
                self.act(p[:, c0:512], self.ps[pb][:, c0:512], AF.Exp, [self.pst[pb]], [tp], scale=scale)
                if jj >= 0:
                    self.tt("pool", p[:, c0:c0 + 128], p[:, c0:c0 + 128], self.tri[:, :], ALU.mult,
                            [tp, tcst], [tp])
                self.mm(self.ps[po][RR, c0:512], vh[s][:, j, :], p[:, c0:512], j == 0, j == nj - 1,
                        [tv, tp], [self.pst[po]])
                self.mm(self.ps[pd][RR, c0:512], self.ones[:, 0:64], p[:, c0:512], j == 0, j == nj - 1,
                        [tp], [self.pst[pd]])
                if j == nj - 1:
                    trc, tot = T_("b_rc", b % 2), T_("b_ot", b % 2)
                    rc_, ot_ = rcs[b % 2], ots[b % 2]
                    self.P.op("dve", lambda e, o=rc_[RR, :], i_=self.ps[pd][RR, :]: e.reciprocal(out=o, in_=i_),
                              reads=[self.pst[pd]], writes=[trc])
                    self.tt("dve", ot_[RR, :], self.ps[po][RR, :], rc_[RR, :], ALU.mult, [self.pst[po], trc], [tot])
                    sl = ya[RR, 2 + ct, b * 512:(b + 1) * 512]
                    self.tt("pool", sl, sl, ot_[RR, :], ALU.mult, [tot], [T_("yb", ct, b)])

            att_s(0)
            for i in range(len(seq)):
                if i + 1 < len(seq):
                    att_s(i + 1)
                att_rest(i)
        self.dump("yb%d" % l, ya[:, 2:6, :], T_("yb", 0, 0))

    def stage_p2(self, l, src, dst):
        A, T_, P = self.A, self.T, self.P
        ya = self.yall
        merged = A.alloc([8, T], BF16)
        m2 = A.mark()
        wls = [A.alloc([8, 4, 256], BF16) for _ in range(2)]
        wbs = [A.alloc([10, 256], BF16) for _ in range(2)]
        g = [A.alloc([512], F32) for _ in range(2)]
        macc = A.alloc([512], F32)
        t2 = [A.alloc([512], F32) for _ in range(2)]
        brk = {0: [0, 1], 1: [2, 3, 4, 5], 2: [6, 7], 3: [8, 9]}
        brw = ["w_br_a", "w_br_b", "w_br_c", "w_br_m"]
        tcp = self.tcp
        def p2_load(j2):
            wl_, wb_ = wls[j2 % 2], wbs[j2 % 2]
            twl_, twb_ = T_("p_wl", j2 % 2), T_("p_wb", j2 % 2)
            for br in range(4):
                c0 = OFF["mrg"] + br * 1024 + j2 * 256
                self.wload(wl_[:, :, br, :], self.dr["w_in"][l, WIN_G[c0]], twl_)
            self.wload(wb_[:, 0:6, :], self.dr["w_br"][l, j2, :, 0:6, :], twb_)
            self.wload(wb_[:, 6:10, :], self.dr["w_br"][l, j2, :, 6:10, :], twb_)

        p2_load(0)
        for j2 in range(4):
            if j2 + 1 < 4:
                p2_load(j2 + 1)
            wl, wb = wls[j2 % 2], wbs[j2 % 2]
            twl, twb = T_("p_wl", j2 % 2), T_("p_wb", j2 % 2)
            for jj in range(2):
                j = 2 * j2 + jj
                js = slice(jj * 128, (jj + 1) * 128)
                for blk in range(NB):
                    bs = slice(blk * 512, (blk + 1) * 512)
                    for br in range(4):
                        pl, pp = self.ps[2 * (br % 2)], self.ps[2 * (br % 2) + 1]
                        tpl, tpp = self.pst[2 * (br % 2)], self.pst[2 * (br % 2) + 1]
                        self.mmg(pl[:, :], [(wl[:, kt, br, js], self.hT[:, kt, bs]) for kt in range(8)], [twl], tpl)
                        self.mmg(pp[:, :], [(wb[:, kt, js], ya[:, kt, bs]) for kt in brk[br]], [twb], tpp)
                        gg, tg = g[br % 2], T_("p_g", br % 2)
                        self.act(gg, pl[:, :], AF.Sigmoid, [tpl, tcp], [tg],
                                 bias=self.colp[:, C_BM + br * 8 + j:C_BM + br * 8 + j + 1])
                        tm = T_("p_m")
                        if br == 0:
                            self.tt("dve", macc, pp[:, :], gg, ALU.mult, [tpp, tg], [tm])
                        else:
                            tt2 = T_("p_t2", br % 2)
                            self.tt("dve", t2[br % 2], pp[:, :], gg, ALU.mult, [tpp, tg], [tt2])
                            if br < 3:
                                self.tt("dve", macc, macc, t2[br % 2], ALU.add, [tt2, tm], [tm])
                            else:
                                self.tt("dve", merged[:, j, bs], macc, t2[br % 2], ALU.add, [tt2, tm],
                                        [T_("p_mg", blk)])
        self.dump("merged%d" % l, merged, T_("p_mg", 0))
        P.barrier()
        A.reset(m2)
        wo = A.alloc([8, D], BF16)
        two = T_("p_wo")
        for i in range(4):
            self.wload(wo[:, :, i * 256:(i + 1) * 256],
                       self.dr["w_out"][l, i], two)
        xt = A.alloc([8, 512], F32)
        xo = A.alloc([8, 512], F32)
        txt, txo = T_("p_xt"), T_("p_xo")
        for blk in range(NB):
            bs = slice(blk * 512, (blk + 1) * 512)
            for hh in range(2):
                self.dma(xt[:, hh * 4:(hh + 1) * 4, :], src[blk, :, hh * 4:(hh + 1) * 4, :], [], [txt])
            for d2 in range(8):
                pb = 4 + d2 % 2
                self.mmg(self.ps[pb][:, :], [(wo[:, kt, d2 * 128:(d2 + 1) * 128], merged[:, kt, bs]) for kt in range(8)],
                         [two, T_("p_mg", blk)], self.pst[pb])
                self.tt("dve", xo[:, d2, :], self.ps[pb][:, :], xt[:, d2, :], ALU.add, [self.pst[pb], txt], [txo])
            for hh in range(2):
                op = self.dma(dst[blk, :, hh * 4:(hh + 1) * 4, :], xo[:, hh * 4:(hh + 1) * 4, :], [txo], [], q="act")
                if dst is self.outT:
                    self.finals.append(op)


def make_consts():
    c = np.zeros((128, NCONST), np.float32)
    s = np.arange(128)
    c[:, K_TRI:K_TRI + 128] = (s[:, None] <= s[None, :]).astype(np.float32)
    c[:, K_IOT:K_IOT + 128] = s[None, :].astype(np.float32)
    c[:, K_IOS] = s.astype(np.float32)
    half = 16
    inv = (10000.0 ** (-np.arange(half, dtype=np.float32) / half)).astype(np.float32)
    for r in range(64, 96):
        c[r, K_INVF] = inv[(r - 64) % 16]
        c[r, K_SGN] = -1.0 if (r - 64) < 16 else 1.0
    for r in range(128):
        c[r, K_GM + r // 16] = 1.0
    return c


def host_prep(inp):
    f = lambda k: np.asarray(inp[k], dtype=np.float32)
    perm = np.array(PERM)
    sh = {}

    def rows_t(w):
        r, c = w.shape
        return np.ascontiguousarray(w.reshape(r // 128, 128, c).transpose(1, 0, 2))

    w_in = f("w_in")
    sh["w_in"] = np.stack([np.stack([rows_t(w_in[l][:, c0:c0 + 256]) for c0 in WIN_C0]) for l in range(L)])
    krm = np.zeros((L, D, 96), np.float32)
    krp = np.zeros((L, D, 96), np.float32)
    krm[:, :, 64:96] = w_in[:, :, OFF["kr"]:OFF["kr"] + 32]
    krp[:, :, 64:96] = w_in[:, :, OFF["kr"] + perm]
    sh["w_krm"] = np.stack([rows_t(krm[l]) for l in range(L)])
    sh["w_krp"] = np.stack([rows_t(krp[l]) for l in range(L)])
    wuq = f("b_w_uq").reshape(L, 768, 8, 96)
    wqp = np.zeros((L, 768, 8, 96), np.float32)
    wqp[:, :, :, 64:96] = wuq[:, :, :, 64 + perm]
    sh["wq_m"] = np.stack([np.stack([rows_t(wuq[l][:, h, :]) for h in range(8)]) for l in range(L)])
    sh["wq_p"] = np.stack([np.stack([rows_t(wqp[l][:, h, :]) for h in range(8)]) for l in range(L)])
    ukv = f("b_w_ukv")
    sh["w_ukv"] = np.stack([np.stack([rows_t(ukv[l][:, h * 128:(h + 1) * 128]) for h in range(8)]) for l in range(L)])
    sh["a_w_sT"] = np.ascontiguousarray(f("a_w_s").transpose(0, 3, 1, 2))
    c_re, c_im = f("c_c_re"), f("c_c_im")
    cre = np.zeros((L, 8, 128, 128), np.float32)
    cim = np.zeros((L, 8, 128, 128), np.float32)
    for j in range(8):
        for gl in range(2):
            g = 2 * j + gl
            col0 = 16 * (g % 8)
            cre[:, j, gl * 64:(gl + 1) * 64, col0:col0 + 16] = c_re[:, g].transpose(0, 2, 1)
            cim[:, j, gl * 64:(gl + 1) * 64, col0:col0 + 16] = c_im[:, g].transpose(0, 2, 1)
    sh["c_reT"] = np.ascontiguousarray(cre.reshape(L, 2, 4, 128, 128).transpose(0, 1, 3, 2, 4))
    sh["c_imT"] = np.ascontiguousarray(cim.reshape(L, 2, 4, 128, 128).transpose(0, 1, 3, 2, 4))
    sh["c_w_glu"] = np.stack([rows_t(f("c_w_glu")[l]) for l in range(L)])
    mkv = f("m_w_kv")
    sh["m_w_kv"] = np.stack([np.stack([rows_t(mkv[l][:, i * 256:(i + 1) * 256]) for i in range(2)]) for l in range(L)])
    wbr = np.concatenate([f("w_br_a"), f("w_br_b"), f("w_br_c"), f("w_br_m")], axis=1)
    sh["w_br"] = np.stack([np.stack([rows_t(wbr[l][:, i * 256:(i + 1) * 256]) for i in range(4)]) for l in range(L)])
    wo = f("w_out")
    sh["w_out"] = np.stack([np.stack([rows_t(wo[l][:, i * 256:(i + 1) * 256]) for i in range(4)]) for l in range(L)])
    cp = np.zeros((L, 128, NCOL), np.float32)
    cp[:, :, C_NG:C_NG + 8] = f("norm_g").reshape(L, 8, 128).transpose(0, 2, 1)
    cp[:, :, C_MNG:C_MNG + 8] = f("m_norm_g").reshape(L, 8, 128).transpose(0, 2, 1)
    cp[:, :, C_QNG:C_QNG + 6] = f("b_q_norm_g").reshape(L, 6, 128).transpose(0, 2, 1)
    cp[:, :, C_KVNG:C_KVNG + 2] = f("b_kv_norm_g").reshape(L, 2, 128).transpose(0, 2, 1)
    cp[:, :, C_BM:C_BM + 32] = f("b_merge").reshape(L, 32, 128).transpose(0, 2, 1)
    gq, gk = f("b_qk_g_q"), f("b_qk_g_k")
    cp[:, 0:96, C_GQ] = gq
    cp[:, 64:96, C_GQP] = gq[:, 64 + perm]
    cp[:, 0:96, C_GK] = gk
    cp[:, 64:96, C_GKP] = gk[:, 64 + perm]
    cp[:, :, C_MGQ] = np.tile(f("m_qk_g_q"), (1, 2))
    cp[:, :, C_MGK] = np.tile(f("m_qk_g_k"), (1, 2))
    cp[:, :, C_CD:C_CD + 2] = f("c_d").reshape(L, 2, 128).transpose(0, 2, 1)
    cp[:, :, C_BGLU:C_BGLU + 2] = f("c_b_glu").reshape(L, 2, 128).transpose(0, 2, 1)
    a_re, a_im, ldt = f("c_a_re"), f("c_a_im"), f("c_log_dt")
    cp[:, :, C_SRE:C_SRE + 8] = a_re.reshape(L, 8, 128).transpose(0, 2, 1)
    cp[:, :, C_SIM:C_SIM + 8] = a_im.reshape(L, 8, 128).transpose(0, 2, 1)
    ldt_rep = np.repeat(ldt[:, :, None], 64, axis=2)
    cp[:, :, C_SDT:C_SDT + 8] = ldt_rep.reshape(L, 8, 128).transpose(0, 2, 1)
    abs_ = f("a_b_s")
    for ct in range(2):
        for gl in range(2):
            cp[:, gl * 64:(gl + 1) * 64, C_ABS + ct * 128:C_ABS + (ct + 1) * 128] = abs_[:, 2 * ct + gl][:, None, :]
    b_re, b_im = f("c_b_re"), f("c_b_im")
    for ct in range(2):
        base = C_ROW + ct * 320
        for g8 in range(8):
            g = 8 * ct + g8
            rows = slice(16 * g8, 16 * g8 + 16)
            cp[:, rows, base + 0:base + 64] = a_re[:, g][:, None, :]
            cp[:, rows, base + 64:base + 128] = a_im[:, g][:, None, :]
            cp[:, rows, base + 128:base + 192] = ldt[:, g][:, None, None]
            cp[:, rows, base + 192:base + 256] = b_re[:, g].transpose(0, 2, 1)
            cp[:, rows, base + 256:base + 320] = b_im[:, g].transpose(0, 2, 1)
    sh["colpack"] = cp
    rp = np.zeros((L, 128, NROW), np.float32)
    rp[:, :, R_ANG:R_ANG + 256] = f("a_norm_g")[:, None, :]
    rp[:, :, R_SRE:R_SRE + 1024] = a_re.reshape(L, 1, 1024)
    rp[:, :, R_SIM:R_SIM + 1024] = a_im.reshape(L, 1, 1024)
    rp[:, :, R_SDT:R_SDT + 1024] = ldt_rep.reshape(L, 1, 1024)
    sh["rowpack"] = rp
    sh["consts"] = make_consts()
    x = f("x")
    mem = f("mem")
    pos = np.asarray(inp["positions"]).astype(np.int32)
    per_core = []
    for b in range(8):
        d = dict(sh)
        d["xT"] = tile_x(x[b])
        d["memT"] = np.ascontiguousarray(mem[b].T.reshape(8, 128, 1, 256).transpose(2, 1, 0, 3))
        d["pos"] = np.ascontiguousarray(pos[b][None, :])
        per_core.append(d)
    return per_core


def tile_x(xb):
    return np.ascontiguousarray(xb.T.reshape(8, 128, NB, 512).transpose(2, 1, 0, 3))


def untile_x(t):
    return np.ascontiguousarray(t.transpose(2, 1, 0, 3).reshape(D, T).T)


_CACHE = {}
LAYER_KEYS = ("colpack", "rowpack", "w_in", "w_krm", "w_krp", "wq_m", "wq_p", "w_ukv", "a_w_sT", "c_reT", "c_imT",
              "c_w_glu", "m_w_kv", "w_br", "w_out")
FUSED = True


def kernel(**inputs):
    in_maps = host_prep(inputs)
    if FUSED:
        if "nc" not in _CACHE:
            _CACHE["nc"] = Builder(nlayers=L).build()
        res = run_bass_kernel_spmd(_CACHE["nc"], in_maps, core_ids=list(range(8)))
        return np.stack([untile_x(r["outT"]) for r in res.results], axis=0).astype(np.float32)
    if "nc1" not in _CACHE:
        _CACHE["nc1"] = Builder(nlayers=1).build()
    nc = _CACHE["nc1"]
    xs = [m["xT"] for m in in_maps]
    for l in range(L):
        maps = []
        for c in range(8):
            d = dict(in_maps[c])
            for k in LAYER_KEYS:
                a = in_maps[c][k]
                d[k] = np.ascontiguousarray(np.concatenate([a[l:l + 1], a[l:l + 1]], axis=0))
            d["xT"] = xs[c]
            maps.append(d)
        res = run_bass_kernel_spmd(nc, maps, core_ids=list(range(8)))
        xs = [np.ascontiguousarray(r["outT"]) for r in res.results]
    return np.stack([untile_x(t) for t in xs], axis=0).astype(np.float32)
```

```python
import math
import contextlib
import numpy as np
import concourse.bass as bass
import concourse.mybir as mybir
from concourse.bass_utils import run_bass_kernel_spmd

F32 = mybir.dt.float32
BF16 = mybir.dt.bfloat16
I32 = mybir.dt.int32
AF = mybir.ActivationFunctionType
ALU = mybir.AluOpType

D = 1024
T = 2048
L = 2
NB = 4
EPS = 1e-6
IN_W = 7456
OFF = dict(a_u=0, a_v=256, a_g=512, cq=768, ckv=1536, kr=1792, bg=1824,
           cin=2336, cg=2592, mq=2848, mg=3104, mrg=3360)
PERM = list(range(16, 32)) + list(range(0, 16))
WIN_C0 = [0, 256, 512, 768, 1024, 1280, 1536, 1824, 2080, 2336, 2592, 2848, 3104] + \
         [3360 + br * 1024 + j2 * 256 for br in range(4) for j2 in range(4)]
WIN_G = {c: i for i, c in enumerate(WIN_C0)}
NWG = len(WIN_C0)
TWO_PI = 2.0 * math.pi

C_NG, C_MNG, C_QNG, C_KVNG, C_BM = 0, 8, 16, 22, 24
C_GQ, C_GQP, C_GK, C_GKP, C_MGQ, C_MGK = 56, 57, 58, 59, 60, 61
C_CD, C_BGLU, C_SRE, C_SIM, C_SDT, C_ABS, C_ROW = 62, 64, 66, 74, 82, 90, 346
NCOL = 346 + 640
R_ANG, R_SRE, R_SIM, R_SDT = 0, 256, 1280, 2304
NROW = 3328
K_TRI, K_IOT, K_IOS, K_INVF, K_SGN, K_GM = 0, 128, 256, 257, 258, 259
NCONST = 267


class Tok:
    __slots__ = ("w", "rs", "excl")

    def __init__(self):
        self.w = None
        self.rs = {}
        self.excl = False


class Op:
    __slots__ = ("eng", "fn", "deps", "is_dma", "sig", "users")

    def __init__(self, eng, fn, is_dma):
        self.eng = eng
        self.fn = fn
        self.is_dma = is_dma
        self.deps = []
        self.sig = None
        self.users = 0


ENGS = ("pe", "act", "dve", "pool", "sp")
SYNC_ALL = True
WQ = ("sp",)
DMAQ = ("sp", "act", "pool")


class Prog:
    def __init__(self, nc, n_dma_sems=8):
        self.nc = nc
        self.ops = {e: [] for e in ENGS}
        self.all = []
        self.n_dma_sems = n_dma_sems
        self.toks = {}
        self.dmas_since_bar = []

    def tok(self, *key):
        t = self.toks.get(key)
        if t is None:
            t = self.toks[key] = Tok()
        return t

    def _add(self, eng, fn, reads, writes, is_dma, extra_deps=()):
        op = Op(eng, fn, is_dma)
        deps = list(extra_deps)
        raw = set()
        for t in reads:
            if t.w is not None:
                deps.append(t.w)
                raw.add(id(t.w))
            if t.excl:
                deps.extend(o for o in t.rs.values() if o.eng != eng)
        for t in writes:
            if t.w is not None:
                deps.append(t.w)
            deps.extend(t.rs.values())
        rkey = ("dma", id(op)) if is_dma else eng
        for t in reads:
            t.rs[rkey] = op
        for t in writes:
            t.w = op
            t.rs = {}
        seen = set()
        for d in deps:
            if d is op or id(d) in seen:
                continue
            seen.add(id(d))
            if (not d.is_dma) and (not is_dma) and d.eng == eng:
                if eng == "pe" or (id(d) not in raw and not SYNC_ALL):
                    continue
            if (not d.is_dma) and is_dma and d.eng == eng:
                pass
            op.deps.append(d)
            d.users += 1
        self.ops[eng].append(op)
        self.all.append(op)
        if is_dma:
            self.dmas_since_bar.append(op)
        return op

    def op(self, eng, fn, reads=(), writes=()):
        return self._add(eng, fn, reads, writes, False)

    def dma(self, eng, fn, reads=(), writes=()):
        assert eng in DMAQ
        return self._add(eng, fn, reads, writes, True)

    def barrier(self):
        dm = self.dmas_since_bar
        self.dmas_since_bar = []
        bt = [self.tok("__bar", e) for e in ENGS]
        for i, e in enumerate(ENGS):
            self._add(e, lambda eng: eng.drain(), [], [bt[i]], False,
                      extra_deps=dm if e == "sp" else ())
        for e in ENGS:
            self._add(e, lambda eng: eng.nop(), bt, [], False)

    def emit(self, final_wait_ops=()):
        nc = self.nc
        with contextlib.ExitStack() as es:
            esem = {e: es.enter_context(nc.semaphore("s_" + e)) for e in ENGS}
            dsem = {e: [es.enter_context(nc.semaphore("d_%s%d" % (e, i)))
                        for i in range(self.n_dma_sems)] for e in DMAQ}
            ecount = {e: 0 for e in ENGS}
            dcount = {e: 0 for e in DMAQ}
            duse = {e: [0] * self.n_dma_sems for e in DMAQ}
            fw = set(id(o) for o in final_wait_ops)
            for op in self.all:
                if op.is_dma:
                    j = dcount[op.eng]
                    dcount[op.eng] += 1
                    s = j % self.n_dma_sems
                    duse[op.eng][s] += 1
                    op.sig = (dsem[op.eng][s], 16 * duse[op.eng][s], ("d", op.eng, s))
                elif op.users > 0 or id(op) in fw:
                    ecount[op.eng] += 1
                    op.sig = (esem[op.eng], ecount[op.eng], ("e", op.eng))
            self.stats = dict(ecount=ecount, dcount=dcount,
                              nops={e: len(self.ops[e]) for e in ENGS})
            block = es.enter_context(nc.Block())

            def run_engine(e, engine):
                waited = {}

                def wait(sem, val, key):
                    if waited.get(key, 0) >= val:
                        return
                    waited[key] = val
                    engine.wait_ge(sem, val)

                for op in self.ops[e]:
                    for d in op.deps:
                        wait(*d.sig)
                    if op.is_dma:
                        sem, val, key = op.sig
                        if val > 16:
                            wait(sem, val - 16, key)
                    ins = op.fn(engine)
                    if op.sig is not None:
                        ins.then_inc(op.sig[0], 16 if op.is_dma else 1)
                if e == "sp":
                    for op in final_wait_ops:
                        wait(*op.sig)

            block.tensor(lambda eng: run_engine("pe", eng))
            block.scalar(lambda eng: run_engine("act", eng))
            block.vector(lambda eng: run_engine("dve", eng))
            block.gpsimd(lambda eng: run_engine("pool", eng))
            block.sync(lambda eng: run_engine("sp", eng))


class Arena:
    def __init__(self, ap2d, nfloats):
        self.a = ap2d
        self.n = nfloats
        self.off = 0
        self.peak = 0

    def alloc(self, free_shape, dt):
        free_shape = list(free_shape)
        isz = 2 if dt == BF16 else 4
        n = int(np.prod(free_shape))
        n32 = (n * isz + 3) // 4
        assert self.off + n32 <= self.n, ("arena overflow", self.off, n32, self.n)
        v = self.a[:, self.off:self.off + n32]
        self.off += n32
        self.peak = max(self.peak, self.off)
        if dt == BF16:
            v = v.bitcast(BF16)[:, 0:n]
        elif dt == I32:
            v = v.bitcast(I32)
        if len(free_shape) == 2:
            v = v.rearrange("p (a b) -> p a b", a=free_shape[0])
        elif len(free_shape) == 3:
            v = v.rearrange("p (a b c) -> p a b c", a=free_shape[0], b=free_shape[1])
        elif len(free_shape) == 4:
            v = v.rearrange("p (a b c d) -> p a b c d", a=free_shape[0], b=free_shape[1],
                            c=free_shape[2])
        return v

    def mark(self):
        return self.off

    def reset(self, m):
        self.off = m


class Builder:
    def __init__(self, nlayers=L, stop=None, dumps=()):
        self.nlayers = nlayers
        self.stop = stop
        self.dumps = dict()
        self.want = set(dumps)
        self.nc = nc = bass.Bass("TRN2", target_bir_lowering=False)
        self.P = Prog(nc)
        self.dr = {}
        self.finals = []
        self.dump_specs = []

    def din(self, name, shape, dt=F32):
        self.dr[name] = self.nc.dram_tensor(name, list(shape), dt, kind="ExternalInput").ap()
        return self.dr[name]

    def T(self, *k):
        return self.P.tok(*k)

    def mm(self, out, lhsT, rhs, start, stop, reads, writes):
        return self.P.op("pe", lambda e: e.matmul(out, lhsT=lhsT, rhs=rhs, start=start, stop=stop),
                         reads=reads, writes=writes)

    def mmg(self, out, pairs, reads, wtok):
        n = len(pairs)
        for i, (a, b) in enumerate(pairs):
            self.mm(out, a, b, i == 0, i == n - 1, reads, [wtok])

    def act(self, out, in_, func, reads, writes, bias=0.0, scale=1.0, eng="act", accum_out=None):
        if accum_out is None:
            return self.P.op("act", lambda e: e.activation(out=out, in_=in_, func=func, bias=bias, scale=scale),
                             reads=reads, writes=writes)
        return self.P.op("act", lambda e: e.activation(out=out, in_=in_, func=func, bias=bias, scale=scale,
                                                       accum_out=accum_out),
                         reads=reads, writes=writes)

    def tt(self, eng, out, in0, in1, op, reads, writes):
        return self.P.op(eng, lambda e: e.tensor_tensor(out=out, in0=in0, in1=in1, op=op),
                         reads=reads, writes=writes)

    def ts(self, eng, out, in0, s1, op0, reads, writes, s2=None, op1=None):
        if op1 is None:
            return self.P.op(eng, lambda e: e.tensor_scalar(out=out, in0=in0, scalar1=s1, scalar2=None, op0=op0),
                             reads=reads, writes=writes)
        return self.P.op(eng, lambda e: e.tensor_scalar(out=out, in0=in0, scalar1=s1, scalar2=s2, op0=op0, op1=op1),
                         reads=reads, writes=writes)

    def stt(self, eng, out, in0, scalar, in1, op0, op1, reads, writes):
        return self.P.op(eng, lambda e: e.scalar_tensor_tensor(out=out, in0=in0, scalar=scalar, in1=in1,
                                                                op0=op0, op1=op1),
                         reads=reads, writes=writes)

    def cp(self, eng, out, in_, reads, writes):
        if eng == "act":
            return self.P.op("act", lambda e: e.copy(out=out, in_=in_), reads=reads, writes=writes)
        return self.P.op(eng, lambda e: e.tensor_copy(out=out, in_=in_), reads=reads, writes=writes)

    def memset(self, eng, ap, val, writes):
        return self.P.op(eng, lambda e: e.memset(ap, val), reads=(), writes=writes)

    def dma(self, out, in_, reads, writes, q="sp"):
        def ndesc(ap):
            dims = [(int(st), int(n)) for st, n in ap.ap]
            total = 1
            for st, n in dims:
                total *= n
            run = 1
            for st, n in reversed(dims[1:]):
                if st == run:
                    run *= n
                else:
                    break
            return total // run
        self.desc_count = getattr(self, "desc_count", {})
        self.desc_count[q] = self.desc_count.get(q, 0) + max(ndesc(out), ndesc(in_))
        return self.P.dma(q, lambda e: e.dma_start(out=out, in_=in_), reads=reads, writes=writes)

    def rstd(self, out, in_, n, reads, writes):
        self.act(out, in_, AF.Ln, reads, writes, bias=EPS, scale=1.0 / n)
        self.act(out, out, AF.Exp, writes, writes, scale=-0.5)

    def wload(self, dst, src, dtok, cast_eng="pool", scale=None):
        fs = list(dst.shape[1:])
        n = int(np.prod(fs))
        assert n <= self.stage_n, (n, self.stage_n)
        slot = self.wslot
        self.wslot = (slot + 1) % len(self.stage)
        st = self.stage[slot][:, 0:n]
        if len(fs) == 2:
            st = st.rearrange("p (a b) -> p a b", a=fs[0])
        elif len(fs) == 3:
            st = st.rearrange("p (a b c) -> p a b c", a=fs[0], b=fs[1])
        stok = self.T("stage", slot)
        self.wq_i = getattr(self, "wq_i", 0) + 1
        self.dma(st, src, [], [stok], q=WQ[self.wq_i % len(WQ)])
        if scale is None:
            self.cp(cast_eng, dst, st, [stok], [dtok])
        else:
            self.ts(cast_eng, dst, st, scale, ALU.mult, [stok], [dtok])

    def dump(self, name, ap, tok):
        if name not in self.want:
            return
        self.P.barrier()
        shp = list(ap.shape)
        d = self.nc.dram_tensor("dbg_" + name, shp, ap.dtype, kind="ExternalOutput").ap()
        op = self.dma(d, ap, [tok], [])
        self.finals.append(op)
        self.dump_specs.append(name)

    def build(self):
        nc = self.nc
        P = self.P
        din = self.din
        din("xT", [NB, 128, 8, 512])
        din("memT", [1, 128, 8, 256])
        din("pos", [1, T], I32)
        din("consts", [128, NCONST])
        din("colpack", [L, 128, NCOL])
        din("rowpack", [L, 128, NROW])
        din("w_in", [L, NWG, 128, 8, 256])
        din("w_krm", [L, 128, 8, 96])
        din("w_krp", [L, 128, 8, 96])
        din("wq_m", [L, 8, 128, 6, 96])
        din("wq_p", [L, 8, 128, 6, 96])
        din("w_ukv", [L, 8, 128, 2, 128])
        din("a_w_sT", [L, 128, 4, 128])
        din("c_reT", [L, 2, 128, 4, 128])
        din("c_imT", [L, 2, 128, 4, 128])
        din("c_w_glu", [L, 128, 2, 256])
        din("m_w_kv", [L, 2, 128, 8, 256])
        din("w_br", [L, 4, 128, 10, 256])
        din("w_out", [L, 4, 128, 8, 256])
        self.outT = nc.dram_tensor("outT", [NB, 128, 8, 512], F32, kind="ExternalOutput").ap()
        self.x1T = nc.dram_tensor("x1T", [NB, 128, 8, 512], F32, kind="Internal").ap()

        with contextlib.ExitStack() as es:
            NA = 52000
            arena_t = es.enter_context(nc.sbuf_tensor("arena", [128, NA], F32))
            self.A = A = Arena(arena_t[:, :], NA)
            self.ps = [es.enter_context(nc.psum_tensor("ps%d" % i, [128, 512], F32)) for i in range(8)]
            self.pst = [self.T("ps", i) for i in range(8)]
            for t in self.pst:
                t.excl = True

            self.cst = A.alloc([NCONST], F32)
            self.eps_col = A.alloc([1], F32)
            self.tri = A.alloc([128], BF16)
            self.ntri = A.alloc([128], BF16)
            self.ones = A.alloc([128], BF16)
            self.bd64 = A.alloc([128], BF16)
            self.negI = A.alloc([128], BF16)
            self.slow = A.alloc([128], BF16)
            self.ones_f = A.alloc([128], F32)
            self.cosT = A.alloc([T], F32)
            self.sinT = A.alloc([T], F32)
            self.colp = A.alloc([NCOL], F32)
            self.hT = A.alloc([8, T], BF16)
            self.yall = A.alloc([10, T], BF16)
            self.kmem = A.alloc([2, 256], BF16)
            self.vmem = A.alloc([2, 256], BF16)
            self.stage_n = 2048
            self.stage = [A.alloc([self.stage_n], F32) for _ in range(2)]
            self.wslot = 0
            self.base_mark = A.mark()

            self.setup_consts()
            if self.stop != "consts":
                for l in range(self.nlayers):
                    if self.run_layer(l):
                        break
            P.barrier()
            P.emit(final_wait_ops=self.finals)
        return nc

    def setup_consts(self):
        A, T_ = self.A, self.T
        tc = T_("cst")
        self.dma(self.cst, self.dr["consts"][:, :], [], [tc])
        self.memset("dve", self.eps_col, EPS, [tc])
        self.cp("dve", self.tri, self.cst[:, K_TRI:K_TRI + 128], [tc], [tc])
        self.ts("dve", self.ntri, self.cst[:, K_TRI:K_TRI + 128], -1.0, ALU.mult, [tc], [tc])
        self.memset("dve", self.ones, 1.0, [tc])
        self.memset("dve", self.ones_f, 1.0, [tc])
        self.ts("dve", self.negI, self.cst[:, K_IOT:K_IOT + 128], self.cst[:, K_IOS:K_IOS + 1], ALU.is_equal,
                [tc], [tc], s2=-30000.0, op1=ALU.mult)
        self.ts("dve", self.slow, self.cst[:, K_TRI:K_TRI + 128], -1.0, ALU.mult, [tc], [tc], s2=1.0, op1=ALU.add)
        self.memset("dve", self.bd64, 0.0, [tc])
        self.memset("dve", self.bd64[0:64, 0:64], 1.0, [tc])
        self.memset("dve", self.bd64[64:128, 64:128], 1.0, [tc])
        if getattr(self, "skip", None):
            self.memset("pool", self.yall, 0.25, [T_("yall_init")])
        m = A.mark()
        posi = A.alloc([T], I32)
        ang = A.alloc([T], F32)
        kf = A.alloc([T], F32)
        ki = A.alloc([T], I32)
        tr = T_("rope")
        self.dma(posi[0:96, :], self.dr["pos"][0:1, :].partition_broadcast(96), [], [tr])
        R = slice(64, 96)
        self.cp("dve", ang[R, :], posi[R, :], [tr], [tr])
        self.ts("dve", ang[R, :], ang[R, :], self.cst[R, K_INVF:K_INVF + 1], ALU.mult, [tr, tc], [tr])

        def sin_of(dst, shift, post_scale_col):
            self.ts("dve", ki[R, :], ang[R, :], shift, ALU.add, [tr], [tr], s2=1.0 / TWO_PI, op1=ALU.mult)
            self.cp("dve", kf[R, :], ki[R, :], [tr], [tr])
            self.stt("dve", kf[R, :], kf[R, :], -TWO_PI, ang[R, :], ALU.mult, ALU.add, [tr], [tr])
            self.ts("dve", kf[R, :], kf[R, :], shift, ALU.add, [tr], [tr], s2=math.pi, op1=ALU.min)
            self.ts("dve", kf[R, :], kf[R, :], -math.pi, ALU.max, [tr], [tr])
            self.act(dst[R, :], kf[R, :], AF.Sin, [tr], [tr])
            if post_scale_col is not None:
                self.ts("dve", dst[R, :], dst[R, :], post_scale_col, ALU.mult, [tr, tc], [tr])

        sin_of(self.sinT, 0.0, self.cst[R, K_SGN:K_SGN + 1])
        sin_of(self.cosT, math.pi / 2, None)
        self.dump("cosT", self.cosT[R, :], tr)
        self.dump("sinT", self.sinT[R, :], tr)
        self.P.barrier()
        A.reset(m)

    def run_layer(self, l):
        P, A, T_ = self.P, self.A, self.T
        src = self.dr["xT"] if l == 0 else self.x1T
        dst = self.x1T if l == 0 else self.outT
        if self.nlayers == 1:
            dst = self.outT
        tcp = T_("colp")
        self.dma(self.colp, self.dr["colpack"][l, :, :], [], [tcp])
        self.tcp = tcp
        stages = [("N", self.stage_norm), ("MP", self.stage_memprep), ("A", self.stage_a),
                  ("C", self.stage_c), ("M", self.stage_m), ("B", self.stage_b),
                  ("P2", self.stage_p2)]
        full = not getattr(self, "skip", None)
        marks = {}
        self.preA = self.preM = None
        for name, fn in stages:
            if name in getattr(self, "skip", ()):
                continue
            if full and name == "N":
                marks["A"] = A.mark()
                self.preA = [A.alloc([8, 256], BF16) for _ in range(2)]
                self.load_win(l, OFF["a_u"], 256, self.preA[0], T_("a_w", 0))
                self.load_win(l, OFF["a_g"], 256, self.preA[1], T_("a_w", 1))
            if full and name == "C":
                marks["M"] = A.mark()
                self.preM = [A.alloc([8, 256], BF16) for _ in range(2)]
            m = A.mark()
            if name == "N":
                fn(l, src)
            elif name == "P2":
                fn(l, src, dst)
            else:
                fn(l)
            P.barrier()
            A.reset(m)
            if name in marks:
                A.reset(marks.pop(name))
                if name == "A":
                    self.preA = None
                else:
                    self.preM = None
            if self.stop == (name, l):
                return True
        return False

    def norm_fm(self, srcT, n_tok, gcol, dst, dst_tok_fn, tag):
        A, T_ = self.A, self.T
        W = min(512, n_tok)
        nblk = n_tok // W
        xb = [A.alloc([8, W], F32) for _ in range(2)]
        sq = A.alloc([8, W], BF16)
        rs = A.alloc([W], F32)
        for b in range(nblk):
            x = xb[b % 2]
            tx = T_(tag + "x", b % 2)
            ts_ = T_(tag + "sq")
            trs = T_(tag + "rs")
            for hh in range(2):
                self.dma(x[:, hh * 4:(hh + 1) * 4, :], srcT[b, :, hh * 4:(hh + 1) * 4, :], [], [tx])
            pb = 0
            for kt in range(8):
                self.act(sq[:, kt, :], x[:, kt, :], AF.Square, [tx], [ts_])
            self.mmg(self.ps[pb][:, 0:W], [(self.ones[:, :], sq[:, kt, :]) for kt in range(8)],
                     [ts_], self.pst[pb])
            self.rstd(rs[:, :], self.ps[pb][:, 0:W], float(D), [self.pst[pb]], [trs])
            for kt in range(8):
                self.stt("dve", dst[:, kt, b * W:(b + 1) * W], x[:, kt, :], gcol[:, kt:kt + 1], rs[:, :],
                         ALU.mult, ALU.mult, [tx, trs, self.tcp], [dst_tok_fn(b)])

    def stage_norm(self, l, src):
        self.norm_fm(src, T, self.colp[:, C_NG:C_NG + 8], self.hT, lambda b: self.T("hT", b), "n")
        for b in range(NB):
            self.dump("hT%d_%d" % (l, b), self.hT[:, :, b * 512:(b + 1) * 512], self.T("hT", b))

    def load_win(self, l, c0, ncols, wt, wtok):
        assert ncols == 256
        self.wload(wt[:, :, 0:ncols], self.dr["w_in"][l, WIN_G[c0]], wtok)

    def zproj_blk(self, wt, ncols, blk, pb, wtok, m_off=0):
        self.mmg(self.ps[pb][m_off:m_off + ncols, :],
                 [(wt[:, kt, 0:ncols], self.hT[:, kt, blk * 512:(blk + 1) * 512]) for kt in range(8)],
                 [wtok, self.T("hT", blk)], self.pst[pb])

    def stage_memprep(self, l):
        A, T_ = self.A, self.T
        hm = A.alloc([8, 256], BF16)
        thm = T_("hm")
        self.norm_fm(self.dr["memT"], 256, self.colp[:, C_MNG:C_MNG + 8], hm, lambda b: thm, "m")
        wkv = A.alloc([8, 512], BF16)
        twk = T_("wkv")
        for half in range(2):
            self.wload(wkv[:, :, half * 256:(half + 1) * 256],
                       self.dr["m_w_kv"][l, half],
                       twk)
        sq = A.alloc([256], BF16)
        rs = A.alloc([256], F32)
        tsq, trs, tkm, tvm = T_("mp_sq"), T_("mp_rs"), T_("kmem"), T_("vmem")
        for ct in range(2):
            self.mmg(self.ps[0][:, 0:256], [(wkv[:, kt, ct * 128:(ct + 1) * 128], hm[:, kt, :]) for kt in range(8)],
                     [twk, thm], self.pst[0])
            self.act(sq[:, :], self.ps[0][:, 0:256], AF.Square, [self.pst[0]], [tsq])
            self.mmg(self.ps[1][:, 0:256], [(self.bd64[:, :], sq[:, :])], [tsq, T_("cst")], self.pst[1])
            self.rstd(rs[:, :], self.ps[1][:, 0:256], 64.0, [self.pst[1]], [trs])
            self.stt("dve", self.kmem[:, ct, :], self.ps[0][:, 0:256], self.colp[:, C_MGK:C_MGK + 1], rs[:, :],
                     ALU.mult, ALU.mult, [self.pst[0], trs, self.tcp], [tkm])
        for mt in range(2):
            self.mmg(self.ps[2][:, 0:256], [(hm[:, kt, mt * 128:(mt + 1) * 128], wkv[:, kt, 256:512]) for kt in range(8)],
                     [twk, thm], self.pst[2])
            self.cp("dve", self.vmem[:, mt, :], self.ps[2][:, 0:256], [self.pst[2]], [tvm])
        self.dump("kmem%d" % l, self.kmem, tkm)
        self.dump("vmem%d" % l, self.vmem, tvm)

    def stage_a(self, l):
        A, T_ = self.A, self.T
        ya = self.yall
        pre = self.preA is not None
        wt = self.preA if pre else [A.alloc([8, 256], BF16) for _ in range(2)]
        gnb = A.alloc([256], F32)
        wsT = A.alloc([4, 128], BF16)
        tmp = A.alloc([512], BF16)
        tgn, tws, ttmp = T_("a_gn"), T_("a_ws"), T_("a_tmp")
        self.dma(gnb, self.dr["rowpack"][l, :, R_ANG:R_ANG + 256], [], [tgn])
        slot = self.wslot
        self.wslot = (slot + 1) % 2
        st = self.stage[slot][:, 0:512].rearrange("p (g t) -> p g t", g=4)
        stok = T_("stage", slot)
        self.dma(st, self.dr["a_w_sT"][l], [], [stok])
        for g in range(4):
            self.tt("dve", wsT[:, g, :], st[:, g, :], self.cst[:, K_TRI:K_TRI + 128], ALU.mult, [stok, T_("cst")], [tws])
        tw0, tw1 = T_("a_w", 0), T_("a_w", 1)
        if not pre:
            self.load_win(l, OFF["a_u"], 256, wt[0], tw0)
            self.load_win(l, OFF["a_g"], 256, wt[1], tw1)
        for ct in range(2):
            for blk in range(NB):
                pb = (ct * NB + blk) % 2
                self.mmg(self.ps[pb][:, :],
                         [(wt[0][:, kt, ct * 128:(ct + 1) * 128], self.hT[:, kt, blk * 512:(blk + 1) * 512])
                          for kt in range(8)], [tw0, T_("hT", blk)], self.pst[pb])
                self.act(ya[:, ct, blk * 512:(blk + 1) * 512], self.ps[pb][:, :], AF.Gelu_apprx_tanh,
                         [self.pst[pb]], [T_("ya", ct, blk)])
        for ct in range(2):
            for blk in range(NB):
                pb = 2 + (ct * NB + blk) % 2
                self.mmg(self.ps[pb][:, :],
                         [(wt[1][:, kt, ct * 128:(ct + 1) * 128], self.hT[:, kt, blk * 512:(blk + 1) * 512])
                          for kt in range(8)], [tw1, T_("hT", blk)], self.pst[pb])
                self.act(tmp[:, :], self.ps[pb][:, :], AF.Silu, [self.pst[pb]], [ttmp])
                sl = ya[:, ct, blk * 512:(blk + 1) * 512]
                self.tt("pool", sl, sl, tmp[:, :], ALU.mult, [ttmp], [T_("ya", ct, blk)])
        twv = T_("a_w", 0)
        self.load_win(l, OFF["a_v"], 256, wt[0], twv)
        gvs = [A.alloc([256], F32) for _ in range(2)]
        ssqs = [A.alloc([1], F32) for _ in range(2)]
        vn = [A.alloc([256], BF16) for _ in range(2)]
        sbs = [A.alloc([2, 128], F32) for _ in range(2)]
        junk = A.alloc([256], BF16)
        absT = self.colp[:, C_ABS:C_ABS + 256].rearrange("p (c t) -> p c t", c=2)
        def a_front(tt_):
            blk = tt_ // 4
            pb = 4 + tt_ % 2
            self.mmg(self.ps[pb][:, 0:256],
                     [(self.hT[:, kt, tt_ * 128:(tt_ + 1) * 128], wt[0][:, kt, :]) for kt in range(8)],
                     [twv, T_("hT", blk)], self.pst[pb])

        def a_back(tt_):
            blk = tt_ // 4
            k = tt_ % 2
            pb = 4 + k
            gv, ssq, sb = gvs[k], ssqs[k], sbs[k]
            tgv, tss, tsb = T_("a_gv", k), T_("a_ss", k), T_("a_sb", k)
            self.act(gv[:, :], self.ps[pb][:, 0:256], AF.Gelu_apprx_tanh, [self.pst[pb]], [tgv])
            self.act(junk[:, :], gv[:, :], AF.Square, [tgv], [tss], accum_out=ssq[:, :])
            self.rstd(ssq[:, :], ssq[:, :], 256.0, [tss], [tss])
            v = vn[k]
            tv = T_("a_vn", k)
            self.stt("dve", v[:, :], gv[:, :], ssq[:, 0:1], gnb[:, :], ALU.mult, ALU.mult, [tgv, tss, tgn], [tv])
            pq = 6 + k
            for g in range(4):
                ct, r0 = g // 2, (g % 2) * 64
                self.mm(self.ps[pq][r0:r0 + 64, ct * 128:(ct + 1) * 128], v[:, g * 64:(g + 1) * 64], wsT[:, g, :],
                        True, True, [tv, tws], [self.pst[pq]])
            psv = self.ps[pq][:, 0:256].rearrange("p (c t) -> p c t", c=2)
            self.tt("dve", sb[:, :, :], psv, absT, ALU.add, [self.pst[pq], self.tcp], [tsb])
            sl = ya[:, 0:2, tt_ * 128:(tt_ + 1) * 128]
            self.tt("pool", sl, sl, sb[:, :, :], ALU.mult, [tsb], [T_("ya", 0, blk), T_("ya", 1, blk)])

        a_front(0)
        for tt_ in range(16):
            if tt_ + 1 < 16:
                a_front(tt_ + 1)
            a_back(tt_)
        self.dump("ya%d" % l, ya[:, 0:2, :], T_("ya", 0, 0))

    def sin_of(self, dst, ang, shift, ki, kf, tok, extra=()):
        rd = [tok] + list(extra)
        self.ts("dve", ki, ang, shift, ALU.add, rd, [tok], s2=1.0 / TWO_PI, op1=ALU.mult)
        self.cp("dve", kf, ki, [tok], [tok])
        self.stt("dve", kf, kf, -TWO_PI, ang, ALU.mult, ALU.add, [tok], [tok])
        self.ts("dve", kf, kf, shift, ALU.add, [tok], [tok], s2=math.pi, op1=ALU.min)
        self.ts("dve", kf, kf, -math.pi, ALU.max, [tok], [tok])
        self.act(dst, kf, AF.Sin, [tok], [tok])

    def stage_m(self, l):
        A, T_ = self.A, self.T
        ya = self.yall
        twq, twg = T_("m_wq"), T_("m_wg")
        if self.preM is not None:
            wq, wg = self.preM
        else:
            wq = A.alloc([8, 256], BF16)
            wg = A.alloc([8, 256], BF16)
            self.load_win(l, OFF["mq"], 256, wq, twq)
            self.load_win(l, OFF["mg"], 256, wg, twg)
        for ct in range(2):
            for blk in range(NB):
                pb = (ct * NB + blk) % 2
                self.mmg(self.ps[pb][:, :],
                         [(wg[:, kt, ct * 128:(ct + 1) * 128], self.hT[:, kt, blk * 512:(blk + 1) * 512])
                          for kt in range(8)], [twg], self.pst[pb])
                self.act(ya[:, 8 + ct, blk * 512:(blk + 1) * 512], self.ps[pb][:, :], AF.Silu,
                         [self.pst[pb]], [T_("ym", ct, blk)])
        sq = A.alloc([512], BF16)
        rs = A.alloc([512], F32)
        qn = [A.alloc([2, 512], BF16) for _ in range(2)]
        pT = [A.alloc([512], BF16) for _ in range(4)]
        rc = A.alloc([512], F32)
        ot = A.alloc([512], BF16)
        tsq, trs, trc, tot = T_("m_sq"), T_("m_rs"), T_("m_rc"), T_("m_ot")
        scale = 64.0 ** -0.5

        def m_prep(blk):
            q = qn[blk % 2]
            tq = T_("m_qn", blk % 2)
            for ct in range(2):
                self.mmg(self.ps[2][:, :],
                         [(wq[:, kt, ct * 128:(ct + 1) * 128], self.hT[:, kt, blk * 512:(blk + 1) * 512])
                          for kt in range(8)], [twq], self.pst[2])
                self.act(sq[:, :], self.ps[2][:, :], AF.Square, [self.pst[2]], [tsq])
                self.mmg(self.ps[3][:, :], [(self.bd64[:, :], sq[:, :])], [tsq], self.pst[3])
                self.rstd(rs[:, :], self.ps[3][:, :], 64.0, [self.pst[3]], [trs])
                self.stt("dve", q[:, ct, :], self.ps[2][:, :], self.colp[:, C_MGQ:C_MGQ + 1], rs[:, :],
                         ALU.mult, ALU.mult, [self.pst[2], trs], [tq])

        def m_s(blk, h):
            q = qn[blk % 2]
            tq = T_("m_qn", blk % 2)
            ct, r0 = h // 2, (h % 2) * 64
            R = slice(r0, r0 + 64)
            for mt in range(2):
                pb = (4 + mt) if h % 2 == 0 else mt
                self.mm(self.ps[pb][:, :], self.kmem[R, ct, mt * 128:(mt + 1) * 128], q[R, ct, :],
                        True, True, [tq], [self.pst[pb]])

        def m_rest(blk, h):
            ct, r0 = h // 2, (h % 2) * 64
            R = slice(r0, r0 + 64)
            for mt in range(2):
                pb = (4 + mt) if h % 2 == 0 else mt
                p = pT[(h % 2) * 2 + mt]
                tp = T_("m_pT", (h % 2) * 2 + mt)
                self.act(p[:, :], self.ps[pb][:, :], AF.Exp, [self.pst[pb]], [tp], scale=scale)
            tps = [T_("m_pT", (h % 2) * 2 + mt) for mt in range(2)]
            ps_o, ps_d = self.ps[6], self.ps[7]
            self.mmg(ps_o[R, :], [(self.vmem[:, mt, h * 64:(h + 1) * 64], pT[(h % 2) * 2 + mt][:, :])
                                  for mt in range(2)], tps, self.pst[6])
            self.mmg(ps_d[R, :], [(self.ones[:, 0:64], pT[(h % 2) * 2 + mt][:, :]) for mt in range(2)],
                     tps, self.pst[7])
            self.P.op("dve", lambda e, o=rc[R, :], i=ps_d[R, :]: e.reciprocal(out=o, in_=i),
                      reads=[self.pst[7]], writes=[trc])
            self.tt("dve", ot[R, :], ps_o[R, :], rc[R, :], ALU.mult, [self.pst[6], trc], [tot])
            sl = ya[R, 8 + ct, blk * 512:(blk + 1) * 512]
            self.tt("pool", sl, sl, ot[R, :], ALU.mult, [tot], [T_("ym", ct, blk)])

        m_prep(0)
        for blk in range(NB):
            m_s(blk, 0)
            if blk + 1 < NB:
                m_prep(blk + 1)
            for h in range(4):
                if h + 1 < 4:
                    m_s(blk, h + 1)
                m_rest(blk, h)
        self.dump("ym%d" % l, ya[:, 8:10, :], T_("ym", 0, 0))

    def stage_c(self, l):
        A, T_ = self.A, self.T
        ya = self.yall
        uT = A.alloc([2, T], BF16)
        ygT = A.alloc([2, T], BF16)
        TAc = A.alloc([1024], F32)
        TAs = A.alloc([1024], F32)
        Dr = A.alloc([8, 128], F32)
        Di = A.alloc([8, 128], F32)
        a128r = A.alloc([8], F32)
        a128i = A.alloc([8], F32)
        Bm = [A.alloc([2, 8, 64], BF16) for _ in range(2)]
        CreT = A.alloc([8, 128], BF16)
        CreTn = A.alloc([8, 128], BF16)
        CimTn = A.alloc([8, 128], BF16)
        wglu = A.alloc([2, 256], BF16)
        m2 = A.mark()
        wu = A.alloc([8, 256], BF16)
        wg = A.alloc([8, 256], BF16)
        twu, twg = T_("c_wu"), T_("c_wg")
        self.load_win(l, OFF["cin"], 256, wu, twu)
        self.load_win(l, OFF["cg"], 256, wg, twg)
        for ct in range(2):
            for blk in range(NB):
                pb = (ct * NB + blk) % 2
                self.mmg(self.ps[pb][:, :],
                         [(wu[:, kt, ct * 128:(ct + 1) * 128], self.hT[:, kt, blk * 512:(blk + 1) * 512])
                          for kt in range(8)], [twu], self.pst[pb])
                self.cp("act", uT[:, ct, blk * 512:(blk + 1) * 512], self.ps[pb][:, :], [self.pst[pb]],
                        [T_("c_uT", blk)])
        for ct in range(2):
            for blk in range(NB):
                pb = 2 + (ct * NB + blk) % 2
                self.mmg(self.ps[pb][:, :],
                         [(wg[:, kt, ct * 128:(ct + 1) * 128], self.hT[:, kt, blk * 512:(blk + 1) * 512])
                          for kt in range(8)], [twg], self.pst[pb])
                self.act(ya[:, 6 + ct, blk * 512:(blk + 1) * 512], self.ps[pb][:, :], AF.Silu,
                         [self.pst[pb]], [T_("yc", ct, blk)])
        tt_ = T_("c_tab")
        rowp = A.alloc([3, 1024], F32)
        self.dma(rowp, self.dr["rowpack"][l, :, R_SRE:R_SRE + 3072].rearrange("p (a b) -> p a b", a=3), [], [tt_])
        s1 = A.alloc([1024], F32)
        s2 = A.alloc([1024], F32)
        si = A.alloc([1024], I32)
        negs = A.alloc([1], F32)
        tcst = T_("cst")
        self.ts("dve", negs, self.cst[:, K_IOS:K_IOS + 1], -1.0, ALU.mult, [tcst], [tt_])
        self.act(rowp[:, 2, :], rowp[:, 2, :], AF.Exp, [tt_], [tt_])
        self.tt("dve", rowp[:, 0, :], rowp[:, 0, :], rowp[:, 2, :], ALU.mult, [tt_], [tt_])
        self.tt("dve", rowp[:, 1, :], rowp[:, 1, :], rowp[:, 2, :], ALU.mult, [tt_], [tt_])
        self.act(s1, rowp[:, 0, :], AF.Exp, [tt_], [tt_], scale=negs[:, 0:1])
        self.ts("dve", s2, rowp[:, 1, :], self.cst[:, K_IOS:K_IOS + 1], ALU.mult, [tt_, tcst], [tt_])
        self.sin_of(TAs, s2, 0.0, si, rowp[:, 2, :], tt_)
        self.sin_of(TAc, s2, math.pi / 2, si, rowp[:, 2, :], tt_)
        self.tt("dve", TAs, TAs, s1, ALU.mult, [tt_], [tt_])
        self.tt("dve", TAc, TAc, s1, ALU.mult, [tt_], [tt_])
        s1v = s1.rearrange("p (j t) -> p j t", j=8)
        s2v = s2.rearrange("p (j t) -> p j t", j=8)
        siv = si.rearrange("p (j t) -> p j t", j=8)
        kfv = rowp[:, 2, :].rearrange("p (j t) -> p j t", j=8)
        dtj = A.alloc([8], F32)
        thrj = A.alloc([8], F32)
        thij = A.alloc([8], F32)
        e128 = A.alloc([8], F32)
        p128 = A.alloc([8], F32)
        k128 = A.alloc([8], F32)
        i128 = A.alloc([8], I32)
        tcp = self.tcp
        self.act(dtj, self.colp[:, C_SDT:C_SDT + 8], AF.Exp, [tcp, tt_], [tt_])
        self.tt("dve", thrj, self.colp[:, C_SRE:C_SRE + 8], dtj, ALU.mult, [tcp, tt_], [tt_])
        self.tt("dve", thij, self.colp[:, C_SIM:C_SIM + 8], dtj, ALU.mult, [tcp, tt_], [tt_])
        iot = self.cst[:, K_IOT:K_IOT + 128]
        for j in range(8):
            self.act(s1v[:, j, :], iot, AF.Exp, [tt_, tcst], [tt_], scale=thrj[:, j:j + 1])
            self.ts("dve", s2v[:, j, :], iot, thij[:, j:j + 1], ALU.mult, [tt_, tcst], [tt_])
        self.sin_of(Di.rearrange("p j t -> p (j t)"), s2, 0.0, si, rowp[:, 2, :], tt_)
        self.sin_of(Dr.rearrange("p j t -> p (j t)"), s2, math.pi / 2, si, rowp[:, 2, :], tt_)
        self.tt("dve", Di, Di, s1v, ALU.mult, [tt_], [tt_])
        self.tt("dve", Dr, Dr, s1v, ALU.mult, [tt_], [tt_])
        self.act(e128, thrj, AF.Exp, [tt_], [tt_], scale=128.0)
        self.ts("dve", p128, thij, 128.0, ALU.mult, [tt_], [tt_])
        self.sin_of(a128i, p128, 0.0, i128, k128, tt_)
        self.sin_of(a128r, p128, math.pi / 2, i128, k128, tt_)
        self.tt("dve", a128i, a128i, e128, ALU.mult, [tt_], [tt_])
        self.tt("dve", a128r, a128r, e128, ALU.mult, [tt_], [tt_])
        w = [A.alloc([64], F32) for _ in range(8)]
        wi = A.alloc([64], I32)
        gmask = self.cst[:, K_GM:K_GM + 8]
        for ct in range(2):
            base = C_ROW + ct * 320
            are = self.colp[:, base:base + 64]
            aim = self.colp[:, base + 64:base + 128]
            ldt = self.colp[:, base + 128:base + 192]
            bre = self.colp[:, base + 192:base + 256]
            bim = self.colp[:, base + 256:base + 320]
            dt_, thr, thi, ea, abr, abi, t0, t1 = w
            rd = [tt_, tcp]
            self.act(dt_, ldt, AF.Exp, rd, [tt_])
            self.tt("dve", thr, are, dt_, ALU.mult, rd, [tt_])
            self.tt("dve", thi, aim, dt_, ALU.mult, rd, [tt_])
            self.act(ea, thr, AF.Exp, [tt_], [tt_])
            self.sin_of(abi, thi, 0.0, wi, t0, tt_)
            self.sin_of(abr, thi, math.pi / 2, wi, t0, tt_)
            self.tt("dve", abi, abi, ea, ALU.mult, [tt_], [tt_])
            self.tt("dve", abr, abr, ea, ALU.mult, [tt_], [tt_])
            self.ts("dve", abr, abr, -1.0, ALU.add, [tt_], [tt_])
            self.tt("dve", dt_, are, are, ALU.mult, rd, [tt_])
            self.tt("dve", t0, aim, aim, ALU.mult, rd, [tt_])
            self.tt("dve", dt_, dt_, t0, ALU.add, [tt_], [tt_])
            self.P.op("dve", lambda e, o=dt_, i=dt_: e.reciprocal(out=o, in_=i), reads=[tt_], writes=[tt_])
            self.tt("dve", t0, abr, are, ALU.mult, rd, [tt_])
            self.tt("dve", t1, abi, aim, ALU.mult, rd, [tt_])
            self.tt("dve", t0, t0, t1, ALU.add, [tt_], [tt_])
            self.tt("dve", thr, t0, dt_, ALU.mult, [tt_], [tt_])
            self.tt("dve", t0, abi, are, ALU.mult, rd, [tt_])
            self.tt("dve", t1, abr, aim, ALU.mult, rd, [tt_])
            self.tt("dve", t0, t0, t1, ALU.subtract, [tt_], [tt_])
            self.tt("dve", thi, t0, dt_, ALU.mult, [tt_], [tt_])
            self.tt("dve", t0, thr, bre, ALU.mult, rd, [tt_])
            self.tt("dve", t1, thi, bim, ALU.mult, rd, [tt_])
            self.tt("dve", ea, t0, t1, ALU.subtract, [tt_], [tt_])
            self.tt("dve", t0, thr, bim, ALU.mult, rd, [tt_])
            self.tt("dve", t1, thi, bre, ALU.mult, rd, [tt_])
            self.tt("dve", abi, t0, t1, ALU.add, [tt_], [tt_])
            for ri, src_ in enumerate((ea, abi)):
                self.tt("dve", Bm[ct][:, ri, :, :], src_.unsqueeze(1).to_broadcast([128, 8, 64]),
                        gmask.unsqueeze(2).to_broadcast([128, 8, 64]), ALU.mult, [tt_, tcst], [tt_])
        tct = T_("c_ct")
        for j0 in (0, 4):
            srcr = self.dr["c_reT"][l, j0 // 4]
            srci = self.dr["c_imT"][l, j0 // 4]
            self.wload(CreT[:, j0:j0 + 4, :], srcr, tct)
            self.wload(CreTn[:, j0:j0 + 4, :], srcr, tct, scale=-1.0)
            self.wload(CimTn[:, j0:j0 + 4, :], srci, tct, scale=-1.0)
        self.wload(wglu, self.dr["c_w_glu"][l], tct)
        self.dump("c_TAc%d" % l, TAc, tt_)
        self.dump("c_TAs%d" % l, TAs, tt_)
        self.dump("c_Dr%d" % l, Dr, tt_)
        self.dump("c_Di%d" % l, Di, tt_)
        self.dump("c_Bm%d" % l, Bm[0], tt_)
        self.dump("c_a128r%d" % l, a128r, tt_)
        self.P.barrier()
        A.reset(m2)
        if self.preM is not None:
            self.load_win(l, OFF["mq"], 256, self.preM[0], T_("m_wq"))
            self.load_win(l, OFF["mg"], 256, self.preM[1], T_("m_wg"))
        aS = A.alloc([16], F32)
        self.memset("dve", aS, 0.0, [T_("c_aS", 0), T_("c_aS", 1)])
        X = [A.alloc([4, 512], BF16) for _ in range(2)]
        Pfs = [A.alloc([8, 128], F32) for _ in range(2)]
        Ys = [A.alloc([4, 4, 128], BF16) for _ in range(2)]
        p127 = A.alloc([8], F32)
        c1 = A.alloc([8], F32)
        c2 = A.alloc([8], F32)
        yss = [A.alloc([128], F32) for _ in range(2)]
        tch = T_("c_ch")
        its = [(n, ct) for n in range(16) for ct in range(2)]

        def front(i):
            n, ct = its[i]
            par = i % 2
            blk = n // 4
            cs = slice(n * 128, (n + 1) * 128)
            pre_, pim_ = self.ps[0], self.ps[1]
            tpre, tpim = self.pst[0], self.pst[1]
            Bre = Bm[ct][:, 0, :, :].rearrange("p g q -> p (g q)")
            Bim = Bm[ct][:, 1, :, :].rearrange("p g q -> p (g q)")
            self.mm(pre_[:, :], uT[:, ct, cs], Bre, True, True, [T_("c_uT", blk), tt_], [tpre])
            self.mm(pim_[:, :], uT[:, ct, cs], Bim, True, True, [T_("c_uT", blk), tt_], [tpim])
            x = X[par]
            tx = T_("c_X", par)
            tc_ = TAc[:, ct * 512:(ct + 1) * 512]
            ts_ = TAs[:, ct * 512:(ct + 1) * 512]
            self.tt("dve", x[:, 0, :], pre_[:, :], tc_, ALU.mult, [tpre, tt_], [tx])
            self.tt("dve", x[:, 3, :], pre_[:, :], ts_, ALU.mult, [tpre, tt_], [tx])
            self.tt("dve", x[:, 1, :], pim_[:, :], ts_, ALU.mult, [tpim, tt_], [tx])
            self.tt("dve", x[:, 2, :], pim_[:, :], tc_, ALU.mult, [tpim, tt_], [tx])
            Pre, Pim = self.ps[2 + 2 * par], self.ps[3 + 2 * par]
            for jl in range(4):
                js = slice(jl * 128, (jl + 1) * 128)
                self.mmg(Pre[:, js], [(x[:, 0, js], self.tri[:, :]), (x[:, 1, js], self.tri[:, :])],
                         [tx], self.pst[2 + 2 * par])
                self.mmg(Pim[:, js], [(x[:, 2, js], self.tri[:, :]), (x[:, 3, js], self.ntri[:, :])],
                         [tx], self.pst[3 + 2 * par])

        def back_a(i):
            n, ct = its[i]
            par = i % 2
            Pre, Pim = self.ps[2 + 2 * par], self.ps[3 + 2 * par]
            tP0, tP1 = self.pst[2 + 2 * par], self.pst[3 + 2 * par]
            Pf, Y = Pfs[par], Ys[par]
            tPf, tY = T_("c_Pf", par), T_("c_Y", par)
            ta = T_("c_aS", ct)
            jr = slice(4 * ct, 4 * ct + 4)
            ji = slice(8 + 4 * ct, 8 + 4 * ct + 4)
            for jl in range(4):
                js = slice(jl * 128, (jl + 1) * 128)
                self.act(Pf[:, jl, :], Pre[:, js], AF.Identity, [tP0, ta], [tPf],
                         bias=aS[:, 4 * ct + jl:4 * ct + jl + 1])
                self.act(Pf[:, 4 + jl, :], Pim[:, js], AF.Identity, [tP1, ta], [tPf],
                         bias=aS[:, 8 + 4 * ct + jl:8 + 4 * ct + jl + 1])
            self.cp("dve", p127, Pf[:, :, 127], [tPf], [tch])
            self.tt("dve", c1[:, 0:4], a128r[:, jr], p127[:, 0:4], ALU.mult, [tch, tt_], [tch])
            self.tt("dve", c1[:, 4:8], a128i[:, jr], p127[:, 4:8], ALU.mult, [tch, tt_], [tch])
            self.tt("dve", c2[:, 0:4], a128r[:, jr], p127[:, 4:8], ALU.mult, [tch, tt_], [tch])
            self.tt("dve", c2[:, 4:8], a128i[:, jr], p127[:, 0:4], ALU.mult, [tch, tt_], [tch])
            self.tt("dve", aS[:, jr], c1[:, 0:4], c1[:, 4:8], ALU.subtract, [tch], [ta])
            self.tt("dve", aS[:, ji], c2[:, 0:4], c2[:, 4:8], ALU.add, [tch], [ta])
            self.tt("pool", Y[:, 0, :, :], Pf[:, 0:4, :], Dr[:, jr, :], ALU.mult, [tPf, tt_], [tY])
            self.tt("pool", Y[:, 1, :, :], Pf[:, 4:8, :], Di[:, jr, :], ALU.mult, [tPf, tt_], [tY])
            tY2 = T_("c_Y2", par)
            self.tt("dve", Y[:, 2, :, :], Pf[:, 4:8, :], Dr[:, jr, :], ALU.mult, [tPf, tt_], [tY2])
            self.tt("dve", Y[:, 3, :, :], Pf[:, 0:4, :], Di[:, jr, :], ALU.mult, [tPf, tt_], [tY2])

        def back_y(i):
            n, ct = its[i]
            par = i % 2
            Y = Ys[par]
            tY, tY2 = T_("c_Y", par), T_("c_Y2", par)
            py = self.ps[6 + par]
            pairs = []
            for jl in range(4):
                j = 4 * ct + jl
                pairs += [(CreT[:, j, :], Y[:, 0, jl, :]), (CreTn[:, j, :], Y[:, 1, jl, :]),
                          (CimTn[:, j, :], Y[:, 2, jl, :]), (CimTn[:, j, :], Y[:, 3, jl, :])]
            self.mmg(py[:, 0:128], pairs, [tY, tY2, tct], self.pst[6 + par])

        def back_b(i):
            n, ct = its[i]
            par = i % 2
            blk = n // 4
            cs = slice(n * 128, (n + 1) * 128)
            py = self.ps[6 + par]
            ys, tys = yss[par], T_("c_ys", par)
            self.stt("dve", ys, uT[:, ct, cs], self.colp[:, C_CD + ct:C_CD + ct + 1], py[:, 0:128],
                     ALU.mult, ALU.add, [self.pst[6 + par], T_("c_uT", blk), tcp], [tys])
            self.act(ygT[:, ct, cs], ys, AF.Gelu_apprx_tanh, [tys], [T_("c_yg", blk)])

        front(0)
        for i in range(len(its)):
            if i + 1 < len(its):
                front(i + 1)
            back_a(i)
            if i >= 1:
                back_y(i - 1)
            if i >= 2:
                back_b(i - 2)
        back_y(len(its) - 1)
        back_b(len(its) - 2)
        back_b(len(its) - 1)
        self.dump("c_yg%d" % l, ygT, T_("c_yg", 0))
        sg = A.alloc([512], BF16)
        tsg = T_("c_sg")
        for blk in range(NB):
            bs = slice(blk * 512, (blk + 1) * 512)
            for cc in range(2):
                pb = 7
                self.mmg(self.ps[pb][:, :], [(wglu[:, ct, cc * 128:(cc + 1) * 128], ygT[:, ct, bs]) for ct in range(2)],
                         [tct, T_("c_yg", blk)], self.pst[pb])
                self.act(sg, self.ps[pb][:, :], AF.Sigmoid, [self.pst[pb], tcp], [tsg],
                         bias=self.colp[:, C_BGLU + cc:C_BGLU + cc + 1])
                sl = ya[:, 6 + cc, bs]
                self.tt("pool", sg, sg, ygT[:, cc, bs], ALU.mult, [tsg, T_("c_yg", blk)], [tsg])
                self.tt("pool", sl, sl, sg, ALU.mult, [tsg], [T_("yc", cc, blk)])
        self.dump("yc%d" % l, ya[:, 6:8, :], T_("yc", 0, 0))

    def stage_b(self, l):
        A, T_, P = self.A, self.T, self.P
        ya = self.yall
        tcp = self.tcp
        tcst = T_("cst")
        cqn = A.alloc([6, T], BF16)
        ckvn = A.alloc([2, T], BF16)
        krr = A.alloc([T], F32)
        sqkr = A.alloc([T], BF16)
        m2 = A.mark()
        wq_in = A.alloc([8, 768], BF16)
        wkv_in = A.alloc([8, 256], BF16)
        wkr = [A.alloc([8, 96], BF16) for _ in range(2)]
        m3 = A.mark()
        wt = A.alloc([8, 512], BF16)
        tw = T_("b_w")
        tw1 = T_("b_w1")
        for i in range(2):
            self.load_win(l, OFF["bg"] + i * 256, 256, wt[:, :, i * 256:(i + 1) * 256], tw)
        for i in range(3):
            self.load_win(l, OFF["cq"] + i * 256, 256, wq_in[:, :, i * 256:(i + 1) * 256], tw1)
        self.load_win(l, OFF["ckv"], 256, wkv_in, tw1)
        self.wload(wkr[0], self.dr["w_krm"][l], tw1)
        self.wload(wkr[1], self.dr["w_krp"][l], tw1)
        for ct in range(4):
            for blk in range(NB):
                pb = (ct * NB + blk) % 2
                self.zproj_blk(wt[:, :, ct * 128:(ct + 1) * 128], 128, blk, pb, tw)
                self.act(ya[:, 2 + ct, blk * 512:(blk + 1) * 512], self.ps[pb][:, :], AF.Silu,
                         [self.pst[pb]], [T_("yb", ct, blk)])
        P.barrier()
        A.reset(m3)
        tw = tw1
        sq = A.alloc([512], BF16)
        rs = A.alloc([512], F32)
        t1 = A.alloc([512], F32)
        t2 = A.alloc([512], F32)
        cqf = A.alloc([6, 512], F32)
        ckf = A.alloc([2, 512], F32)
        rs2 = A.alloc([512], F32)
        tsq, trs = T_("b_sq"), T_("b_rs")
        R = slice(64, 96)
        for blk in range(NB):
            bs = slice(blk * 512, (blk + 1) * 512)
            tcf = T_("b_cqf")
            for i in range(6):
                pb = i % 2
                self.zproj_blk(wq_in[:, :, i * 128:(i + 1) * 128], 128, blk, pb, tw)
                self.act(sq, self.ps[pb][:, :], AF.Square, [self.pst[pb]], [tsq])
                self.cp("act", cqf[:, i, :], self.ps[pb][:, :], [self.pst[pb]], [tcf])
                self.mm(self.ps[6][:, :], self.ones[:, :], sq, i == 0, i == 5, [tsq], [self.pst[6]])
            tkf = T_("b_ckf")
            for i in range(2):
                pb = 4 + i
                self.zproj_blk(wkv_in[:, :, i * 128:(i + 1) * 128], 128, blk, pb, tw)
                self.act(sq, self.ps[pb][:, :], AF.Square, [self.pst[pb]], [tsq])
                self.cp("act", ckf[:, i, :], self.ps[pb][:, :], [self.pst[pb]], [tkf])
                self.mm(self.ps[7][:, :], self.ones[:, :], sq, i == 0, i == 1, [tsq], [self.pst[7]])
            self.rstd(rs, self.ps[6][:, :], 768.0, [self.pst[6]], [trs])
            for i in range(6):
                self.stt("dve", cqn[:, i, bs], cqf[:, i, :], self.colp[:, C_QNG + i:C_QNG + i + 1], rs,
                         ALU.mult, ALU.mult, [tcf, trs, tcp], [T_("b_cqn", blk)])
            trs2 = T_("b_rs2")
            self.rstd(rs2, self.ps[7][:, :], 256.0, [self.pst[7]], [trs2])
            for i in range(2):
                self.stt("dve", ckvn[:, i, bs], ckf[:, i, :], self.colp[:, C_KVNG + i:C_KVNG + i + 1], rs2,
                         ALU.mult, ALU.mult, [tkf, trs2, tcp], [T_("b_ckvn", blk)])
            for i in range(2):
                self.zproj_blk(wkr[i], 96, blk, 2 + i, tw)
            self.act(sqkr[R, bs], self.ps[2][R, :], AF.Square, [self.pst[2]], [T_("b_kr", blk)])
            tt1 = T_("b_t1")
            self.stt("dve", t1[R, :], self.ps[2][R, :], self.colp[R, C_GK:C_GK + 1], self.cosT[R, bs],
                     ALU.mult, ALU.mult, [self.pst[2], tcp], [tt1])
            self.stt("dve", t2[R, :], self.ps[3][R, :], self.colp[R, C_GKP:C_GKP + 1], self.sinT[R, bs],
                     ALU.mult, ALU.mult, [self.pst[3], tcp], [tt1])
            self.tt("pool", krr[R, bs], t1[R, :], t2[R, :], ALU.add, [tt1], [T_("b_kr", blk)])
        self.dump("b_cqn%d" % l, cqn, T_("b_cqn", 0))
        self.dump("b_ckvn%d" % l, ckvn, T_("b_ckvn", 0))
        self.dump("b_krr%d" % l, krr[R, :], T_("b_kr", 0))
        P.barrier()
        A.reset(m2)
        wqm = [A.alloc([6, 96], BF16) for _ in range(2)]
        wqp = [A.alloc([6, 96], BF16) for _ in range(2)]
        wkv = [A.alloc([2, 128], BF16) for _ in range(2)]
        wk = [w_[:, :, 0:64] for w_ in wkv]
        wv = [w_[:, :, 64:128] for w_ in wkv]
        qn = [A.alloc([T], BF16) for _ in range(2)]
        kn = [A.alloc([T], BF16) for _ in range(2)]
        vh = [A.alloc([16, 64], BF16) for _ in range(2)]
        sq = [A.alloc([512], BF16) for _ in range(2)]
        rs = [A.alloc([512], F32) for _ in range(2)]
        t1 = A.alloc([512], F32)
        t2 = A.alloc([512], F32)
        pT = [A.alloc([512], BF16) for _ in range(4)]
        rcs = [A.alloc([512], F32) for _ in range(2)]
        ots = [A.alloc([512], BF16) for _ in range(2)]
        scale = 96.0 ** -0.5
        def b_load(h_):
            s_ = h_ % 2
            twh_ = T_("b_wh", s_)
            self.wload(wqm[s_], self.dr["wq_m"][l, h_], twh_)
            self.wload(wqp[s_], self.dr["wq_p"][l, h_], twh_)
            self.wload(wkv[s_], self.dr["w_ukv"][l, h_], twh_)

        b_load(0)
        for h in range(8):
            s = h % 2
            twh = T_("b_wh", s)
            tq, tk, tv = T_("b_qn", s), T_("b_kn", s), T_("b_vh", s)
            Q = slice(0, 96)
            N_ = slice(0, 64)
            for half in range(2):
                pbv = 6 + half
                for t8 in range(8):
                    tt_ = half * 8 + t8
                    self.mmg(self.ps[pbv][:, t8 * 64:(t8 + 1) * 64],
                             [(ckvn[:, kt, tt_ * 128:(tt_ + 1) * 128], wv[s][:, kt, :]) for kt in range(2)],
                             [twh], self.pst[pbv])
                self.cp("dve", vh[s][:, half * 8:(half + 1) * 8, :],
                        self.ps[pbv][:, :].rearrange("p (a b) -> p a b", a=8), [self.pst[pbv]], [tv])

            def banks(blk):
                o = 0 if blk % 2 == 0 else 4
                return o, o + 1, o + 2, o + 3

            def prep_f1(blk):
                bs = slice(blk * 512, (blk + 1) * 512)
                bq, bp, _, bk = banks(blk)
                self.mmg(self.ps[bq][Q, :], [(wqm[s][:, kt, :], cqn[:, kt, bs]) for kt in range(6)], [twh], self.pst[bq])

            def prep_f2(blk):
                bs = slice(blk * 512, (blk + 1) * 512)
                bq, bp, _, bk = banks(blk)
                self.mmg(self.ps[bp][Q, :], [(wqp[s][:, kt, :], cqn[:, kt, bs]) for kt in range(6)], [twh], self.pst[bp])

            def prep_f3(blk):
                bs = slice(blk * 512, (blk + 1) * 512)
                bq, bp, _, bk = banks(blk)
                self.mmg(self.ps[bk][N_, :], [(wk[s][:, kt, :], ckvn[:, kt, bs]) for kt in range(2)], [twh], self.pst[bk])

            def prep_back(blk, nxt):
                bs = slice(blk * 512, (blk + 1) * 512)
                bq, bp, bsq, bk = banks(blk)
                tsq0, trs0 = T_("b_sq", 0), T_("b_rs", 0)
                tsq1, trs1 = T_("b_sq", 1), T_("b_rs", 1)
                self.act(sq[0][Q, :], self.ps[bq][Q, :], AF.Square, [self.pst[bq]], [tsq0])
                self.act(sq[1][N_, :], self.ps[bk][N_, :], AF.Square, [self.pst[bk]], [tsq1])
                self.cp("pool", sq[1][R, :], sqkr[R, bs], [], [tsq1])
                if nxt is not None:
                    prep_f1(nxt)
                self.mmg(self.ps[bsq][Q, :], [(self.ones[Q, 0:96], sq[0][Q, :])], [tsq0], self.pst[bsq])
                self.rstd(rs[0][Q, :], self.ps[bsq][Q, :], 96.0, [self.pst[bsq]], [trs0])
                if nxt is not None:
                    prep_f2(nxt)
                self.mmg(self.ps[bsq][Q, :], [(self.ones[Q, 0:96], sq[1][Q, :])], [tsq1], self.pst[bsq])
                self.rstd(rs[1][Q, :], self.ps[bsq][Q, :], 96.0, [self.pst[bsq]], [trs1])
                if nxt is not None:
                    prep_f3(nxt)
                tt1 = T_("b_t1")
                self.stt("dve", t1[R, :], self.ps[bq][R, :], self.colp[R, C_GQ:C_GQ + 1], self.cosT[R, bs],
                         ALU.mult, ALU.mult, [self.pst[bq], tcp], [tt1])
                self.stt("dve", t2[R, :], self.ps[bp][R, :], self.colp[R, C_GQP:C_GQP + 1], self.sinT[R, bs],
                         ALU.mult, ALU.mult, [self.pst[bp], tcp], [tt1])
                self.stt("dve", qn[s][N_, bs], self.ps[bq][N_, :], self.colp[N_, C_GQ:C_GQ + 1], rs[0][N_, :],
                         ALU.mult, ALU.mult, [self.pst[bq], trs0, tcp], [tq])
                self.stt("dve", kn[s][N_, bs], self.ps[bk][N_, :], self.colp[N_, C_GK:C_GK + 1], rs[1][N_, :],
                         ALU.mult, ALU.mult, [self.pst[bk], trs1, tcp], [tk])
                self.tt("pool", t1[R, :], t1[R, :], t2[R, :], ALU.add, [tt1], [tt1])
                self.tt("pool", qn[s][R, bs], t1[R, :], rs[0][R, :], ALU.mult, [tt1, trs0], [tq])
                self.tt("pool", kn[s][R, bs], krr[R, bs], rs[1][R, :], ALU.mult, [trs1], [tk])

            prep_f1(0)
            prep_f2(0)
            prep_f3(0)
            for blk in range(NB):
                prep_back(blk, blk + 1 if blk + 1 < NB else None)
            if h == 0:
                self.dump("b_qn%d" % l, qn[0][0:96, :], tq)
                self.dump("b_kn%d" % l, kn[0][0:96, :], tk)
                self.dump("b_vh%d" % l, vh[0], tv)
            if h + 1 < 8:
                b_load(h + 1)
            ct, r0 = h // 2, (h % 2) * 64
            RR = slice(r0, r0 + 64)
            seq = [(b, j) for b in range(NB) for j in range(4 * b + 4)]

            def att_s(i):
                b, j = seq[i]
                jj = j - 4 * b
                c0 = 128 * jj if jj > 0 else 0
                pb = 4 + i % 2
                self.mm(self.ps[pb][:, c0:512], kn[s][0:96, j * 128:(j + 1) * 128],
                        qn[s][0:96, b * 512 + c0:(b + 1) * 512], True, jj < 0, [tq, tk], [self.pst[pb]])
                if jj >= 0:
                    self.mm(self.ps[pb][:, c0:c0 + 128], self.negI[:, :], self.slow[:, :], False, True,
                            [tcst], [self.pst[pb]])

            def att_rest(i):
                b, j = seq[i]
                nj = 4 * b + 4
                jj = j - 4 * b
                c0 = 128 * jj if jj > 0 else 0
                pb = 4 + i % 2
                p = pT[i % 4]
                tp = T_("b_pT", i % 4)
                po, pd = (6, 7) if b % 2 == 0 else (2, 3)
                self.act(p[:, c0:512], self.ps[pb][:, c0:512], AF.Exp, [self.pst[pb]], [tp], scale=scale)
                self.mm(self.ps[po][RR, c0:512], vh[s][:, j, :], p[:, c0:512], j == 0, j == nj - 1,
                        [tv, tp], [self.pst[po]])
                self.mm(self.ps[pd][RR, c0:512], self.ones[:, 0:64], p[:, c0:512], j == 0, j == nj - 1,
                        [tp], [self.pst[pd]])
                if j == nj - 1:
                    trc, tot = T_("b_rc", b % 2), T_("b_ot", b % 2)
                    rc_, ot_ = rcs[b % 2], ots[b % 2]
                    self.P.op("dve", lambda e, o=rc_[RR, :], i_=self.ps[pd][RR, :]: e.reciprocal(out=o, in_=i_),
                              reads=[self.pst[pd]], writes=[trc])
                    self.tt("dve", ot_[RR, :], self.ps[po][RR, :], rc_[RR, :], ALU.mult, [self.pst[po], trc], [tot])
                    sl = ya[RR, 2 + ct, b * 512:(b + 1) * 512]
                    self.tt("pool", sl, sl, ot_[RR, :], ALU.mult, [tot], [T_("yb", ct, b)])

            att_s(0)
            for i in range(len(seq)):
                if i + 1 < len(seq):
                    att_s(i + 1)
                att_rest(i)
        self.dump("yb%d" % l, ya[:, 2:6, :], T_("yb", 0, 0))

    def stage_p2(self, l, src, dst):
        A, T_, P = self.A, self.T, self.P
        ya = self.yall
        merged = A.alloc([8, T], BF16)
        m2 = A.mark()
        wls = [A.alloc([8, 4, 256], BF16) for _ in range(2)]
        wbs = [A.alloc([10, 256], BF16) for _ in range(2)]
        g = [A.alloc([512], F32) for _ in range(2)]
        macc = A.alloc([512], F32)
        t2 = [A.alloc([512], F32) for _ in range(2)]
        brk = {0: [0, 1], 1: [2, 3, 4, 5], 2: [6, 7], 3: [8, 9]}
        brw = ["w_br_a", "w_br_b", "w_br_c", "w_br_m"]
        tcp = self.tcp
        def p2_load(j2):
            wl_, wb_ = wls[j2 % 2], wbs[j2 % 2]
            twl_, twb_ = T_("p_wl", j2 % 2), T_("p_wb", j2 % 2)
            for br in range(4):
                c0 = OFF["mrg"] + br * 1024 + j2 * 256
                self.wload(wl_[:, :, br, :], self.dr["w_in"][l, WIN_G[c0]], twl_)
            self.wload(wb_[:, 0:6, :], self.dr["w_br"][l, j2, :, 0:6, :], twb_)
            self.wload(wb_[:, 6:10, :], self.dr["w_br"][l, j2, :, 6:10, :], twb_)

        p2_load(0)
        for j2 in range(4):
            if j2 + 1 < 4:
                p2_load(j2 + 1)
            wl, wb = wls[j2 % 2], wbs[j2 % 2]
            twl, twb = T_("p_wl", j2 % 2), T_("p_wb", j2 % 2)
            for jj in range(2):
                j = 2 * j2 + jj
                js = slice(jj * 128, (jj + 1) * 128)
                for blk in range(NB):
                    bs = slice(blk * 512, (blk + 1) * 512)
                    for br in range(4):
                        pl, pp = self.ps[2 * (br % 2)], self.ps[2 * (br % 2) + 1]
                        tpl, tpp = self.pst[2 * (br % 2)], self.pst[2 * (br % 2) + 1]
                        self.mmg(pl[:, :], [(wl[:, kt, br, js], self.hT[:, kt, bs]) for kt in range(8)], [twl], tpl)
                        self.mmg(pp[:, :], [(wb[:, kt, js], ya[:, kt, bs]) for kt in brk[br]], [twb], tpp)
                        gg, tg = g[br % 2], T_("p_g", br % 2)
                        self.act(gg, pl[:, :], AF.Sigmoid, [tpl, tcp], [tg],
                                 bias=self.colp[:, C_BM + br * 8 + j:C_BM + br * 8 + j + 1])
                        tm = T_("p_m")
                        if br == 0:
                            self.tt("dve", macc, pp[:, :], gg, ALU.mult, [tpp, tg], [tm])
                        else:
                            tt2 = T_("p_t2", br % 2)
                            self.tt("dve", t2[br % 2], pp[:, :], gg, ALU.mult, [tpp, tg], [tt2])
                            if br < 3:
                                self.tt("dve", macc, macc, t2[br % 2], ALU.add, [tt2, tm], [tm])
                            else:
                                self.tt("dve", merged[:, j, bs], macc, t2[br % 2], ALU.add, [tt2, tm],
                                        [T_("p_mg", blk)])
        self.dump("merged%d" % l, merged, T_("p_mg", 0))
        P.barrier()
        A.reset(m2)
        wo = A.alloc([8, D], BF16)
        two = T_("p_wo")
        for i in range(4):
            self.wload(wo[:, :, i * 256:(i + 1) * 256],
                       self.dr["w_out"][l, i], two)
        xt = A.alloc([8, 512], F32)
        xo = A.alloc([8, 512], F32)
        txt, txo = T_("p_xt"), T_("p_xo")
        for blk in range(NB):
            bs = slice(blk * 512, (blk + 1) * 512)
            for hh in range(2):
                self.dma(xt[:, hh * 4:(hh + 1) * 4, :], src[blk, :, hh * 4:(hh + 1) * 4, :], [], [txt])
            for d2 in range(8):
                pb = 4 + d2 % 2
                self.mmg(self.ps[pb][:, :], [(wo[:, kt, d2 * 128:(d2 + 1) * 128], merged[:, kt, bs]) for kt in range(8)],
                         [two, T_("p_mg", blk)], self.pst[pb])
                self.tt("dve", xo[:, d2, :], self.ps[pb][:, :], xt[:, d2, :], ALU.add, [self.pst[pb], txt], [txo])
            for hh in range(2):
                op = self.dma(dst[blk, :, hh * 4:(hh + 1) * 4, :], xo[:, hh * 4:(hh + 1) * 4, :], [txo], [], q="act")
                if dst is self.outT:
                    self.finals.append(op)


def make_consts():
    c = np.zeros((128, NCONST), np.float32)
    s = np.arange(128)
    c[:, K_TRI:K_TRI + 128] = (s[:, None] <= s[None, :]).astype(np.float32)
    c[:, K_IOT:K_IOT + 128] = s[None, :].astype(np.float32)
    c[:, K_IOS] = s.astype(np.float32)
    half = 16
    inv = (10000.0 ** (-np.arange(half, dtype=np.float32) / half)).astype(np.float32)
    for r in range(64, 96):
        c[r, K_INVF] = inv[(r - 64) % 16]
        c[r, K_SGN] = -1.0 if (r - 64) < 16 else 1.0
    for r in range(128):
        c[r, K_GM + r // 16] = 1.0
    return c


def host_prep(inp):
    f = lambda k: np.asarray(inp[k], dtype=np.float32)
    perm = np.array(PERM)
    sh = {}

    def rows_t(w):
        r, c = w.shape
        return np.ascontiguousarray(w.reshape(r // 128, 128, c).transpose(1, 0, 2))

    w_in = f("w_in")
    sh["w_in"] = np.stack([np.stack([rows_t(w_in[l][:, c0:c0 + 256]) for c0 in WIN_C0]) for l in range(L)])
    krm = np.zeros((L, D, 96), np.float32)
    krp = np.zeros((L, D, 96), np.float32)
    krm[:, :, 64:96] = w_in[:, :, OFF["kr"]:OFF["kr"] + 32]
    krp[:, :, 64:96] = w_in[:, :, OFF["kr"] + perm]
    sh["w_krm"] = np.stack([rows_t(krm[l]) for l in range(L)])
    sh["w_krp"] = np.stack([rows_t(krp[l]) for l in range(L)])
    wuq = f("b_w_uq").reshape(L, 768, 8, 96)
    wqp = np.zeros((L, 768, 8, 96), np.float32)
    wqp[:, :, :, 64:96] = wuq[:, :, :, 64 + perm]
    sh["wq_m"] = np.stack([np.stack([rows_t(wuq[l][:, h, :]) for h in range(8)]) for l in range(L)])
    sh["wq_p"] = np.stack([np.stack([rows_t(wqp[l][:, h, :]) for h in range(8)]) for l in range(L)])
    ukv = f("b_w_ukv")
    sh["w_ukv"] = np.stack([np.stack([rows_t(ukv[l][:, h * 128:(h + 1) * 128]) for h in range(8)]) for l in range(L)])
    sh["a_w_sT"] = np.ascontiguousarray(f("a_w_s").transpose(0, 3, 1, 2))
    c_re, c_im = f("c_c_re"), f("c_c_im")
    cre = np.zeros((L, 8, 128, 128), np.float32)
    cim = np.zeros((L, 8, 128, 128), np.float32)
    for j in range(8):
        for gl in range(2):
            g = 2 * j + gl
            col0 = 16 * (g % 8)
            cre[:, j, gl * 64:(gl + 1) * 64, col0:col0 + 16] = c_re[:, g].transpose(0, 2, 1)
            cim[:, j, gl * 64:(gl + 1) * 64, col0:col0 + 16] = c_im[:, g].transpose(0, 2, 1)
    sh["c_reT"] = np.ascontiguousarray(cre.reshape(L, 2, 4, 128, 128).transpose(0, 1, 3, 2, 4))
    sh["c_imT"] = np.ascontiguousarray(cim.reshape(L, 2, 4, 128, 128).transpose(0, 1, 3, 2, 4))
    sh["c_w_glu"] = np.stack([rows_t(f("c_w_glu")[l]) for l in range(L)])
    mkv = f("m_w_kv")
    sh["m_w_kv"] = np.stack([np.stack([rows_t(mkv[l][:, i * 256:(i + 1) * 256]) for i in range(2)]) for l in range(L)])
    wbr = np.concatenate([f("w_br_a"), f("w_br_b"), f("w_br_c"), f("w_br_m")], axis=1)
    sh["w_br"] = np.stack([np.stack([rows_t(wbr[l][:, i * 256:(i + 1) * 256]) for i in range(4)]) for l in range(L)])
    wo = f("w_out")
    sh["w_out"] = np.stack([np.stack([rows_t(wo[l][:, i * 256:(i + 1) * 256]) for i in range(4)]) for l in range(L)])
    cp = np.zeros((L, 128, NCOL), np.float32)
    cp[:, :, C_NG:C_NG + 8] = f("norm_g").reshape(L, 8, 128).transpose(0, 2, 1)
    cp[:, :, C_MNG:C_MNG + 8] = f("m_norm_g").reshape(L, 8, 128).transpose(0, 2, 1)
    cp[:, :, C_QNG:C_QNG + 6] = f("b_q_norm_g").reshape(L, 6, 128).transpose(0, 2, 1)
    cp[:, :, C_KVNG:C_KVNG + 2] = f("b_kv_norm_g").reshape(L, 2, 128).transpose(0, 2, 1)
    cp[:, :, C_BM:C_BM + 32] = f("b_merge").reshape(L, 32, 128).transpose(0, 2, 1)
    gq, gk = f("b_qk_g_q"), f("b_qk_g_k")
    cp[:, 0:96, C_GQ] = gq
    cp[:, 64:96, C_GQP] = gq[:, 64 + perm]
    cp[:, 0:96, C_GK] = gk
    cp[:, 64:96, C_GKP] = gk[:, 64 + perm]
    cp[:, :, C_MGQ] = np.tile(f("m_qk_g_q"), (1, 2))
    cp[:, :, C_MGK] = np.tile(f("m_qk_g_k"), (1, 2))
    cp[:, :, C_CD:C_CD + 2] = f("c_d").reshape(L, 2, 128).transpose(0, 2, 1)
    cp[:, :, C_BGLU:C_BGLU + 2] = f("c_b_glu").reshape(L, 2, 128).transpose(0, 2, 1)
    a_re, a_im, ldt = f("c_a_re"), f("c_a_im"), f("c_log_dt")
    cp[:, :, C_SRE:C_SRE + 8] = a_re.reshape(L, 8, 128).transpose(0, 2, 1)
    cp[:, :, C_SIM:C_SIM + 8] = a_im.reshape(L, 8, 128).transpose(0, 2, 1)
    ldt_rep = np.repeat(ldt[:, :, None], 64, axis=2)
    cp[:, :, C_SDT:C_SDT + 8] = ldt_rep.reshape(L, 8, 128).transpose(0, 2, 1)
    abs_ = f("a_b_s")
    for ct in range(2):
        for gl in range(2):
            cp[:, gl * 64:(gl + 1) * 64, C_ABS + ct * 128:C_ABS + (ct + 1) * 128] = abs_[:, 2 * ct + gl][:, None, :]
    b_re, b_im = f("c_b_re"), f("c_b_im")
    for ct in range(2):
        base = C_ROW + ct * 320
        for g8 in range(8):
            g = 8 * ct + g8
            rows = slice(16 * g8, 16 * g8 + 16)
            cp[:, rows, base + 0:base + 64] = a_re[:, g][:, None, :]
            cp[:, rows, base + 64:base + 128] = a_im[:, g][:, None, :]
            cp[:, rows, base + 128:base + 192] = ldt[:, g][:, None, None]
            cp[:, rows, base + 192:base + 256] = b_re[:, g].transpose(0, 2, 1)
            cp[:, rows, base + 256:base + 320] = b_im[:, g].transpose(0, 2, 1)
    sh["colpack"] = cp
    rp = np.zeros((L, 128, NROW), np.float32)
    rp[:, :, R_ANG:R_ANG + 256] = f("a_norm_g")[:, None, :]
    rp[:, :, R_SRE:R_SRE + 1024] = a_re.reshape(L, 1, 1024)
    rp[:, :, R_SIM:R_SIM + 1024] = a_im.reshape(L, 1, 1024)
    rp[:, :, R_SDT:R_SDT + 1024] = ldt_rep.reshape(L, 1, 1024)
    sh["rowpack"] = rp
    sh["consts"] = make_consts()
    x = f("x")
    mem = f("mem")
    pos = np.asarray(inp["positions"]).astype(np.int32)
    per_core = []
    for b in range(8):
        d = dict(sh)
        d["xT"] = tile_x(x[b])
        d["memT"] = np.ascontiguousarray(mem[b].T.reshape(8, 128, 1, 256).transpose(2, 1, 0, 3))
        d["pos"] = np.ascontiguousarray(pos[b][None, :])
        per_core.append(d)
    return per_core


def tile_x(xb):
    return np.ascontiguousarray(xb.T.reshape(8, 128, NB, 512).transpose(2, 1, 0, 3))


def untile_x(t):
    return np.ascontiguousarray(t.transpose(2, 1, 0, 3).reshape(D, T).T)


_CACHE = {}
LAYER_KEYS = ("colpack", "rowpack", "w_in", "w_krm", "w_krp", "wq_m", "wq_p", "w_ukv", "a_w_sT", "c_reT", "c_imT",
              "c_w_glu", "m_w_kv", "w_br", "w_out")
FUSED = True


def kernel(**inputs):
    in_maps = host_prep(inputs)
    if FUSED:
        if "nc" not in _CACHE:
            _CACHE["nc"] = Builder(nlayers=L).build()
        res = run_bass_kernel_spmd(_CACHE["nc"], in_maps, core_ids=list(range(8)))
        return np.stack([untile_x(r["outT"]) for r in res.results], axis=0).astype(np.float32)
    if "nc1" not in _CACHE:
        _CACHE["nc1"] = Builder(nlayers=1).build()
    nc = _CACHE["nc1"]
    xs = [m["xT"] for m in in_maps]
    for l in range(L):
        maps = []
        for c in range(8):
            d = dict(in_maps[c])
            for k in LAYER_KEYS:
                a = in_maps[c][k]
                d[k] = np.ascontiguousarray(np.concatenate([a[l:l + 1], a[l:l + 1]], axis=0))
            d["xT"] = xs[c]
            maps.append(d)
        res = run_bass_kernel_spmd(nc, maps, core_ids=list(range(8)))
        xs = [np.ascontiguousarray(r["outT"]) for r in res.results]
    return np.stack([untile_x(t) for t in xs], axis=0).astype(np.float32)
```

```python
import math
import contextlib
import numpy as np
import concourse.bass as bass
import concourse.mybir as mybir
from concourse.bass_utils import run_bass_kernel_spmd

F32 = mybir.dt.float32
BF16 = mybir.dt.bfloat16
I32 = mybir.dt.int32
AF = mybir.ActivationFunctionType
ALU = mybir.AluOpType

D = 1024
T = 2048
L = 2
NB = 4
EPS = 1e-6
IN_W = 7456
OFF = dict(a_u=0, a_v=256, a_g=512, cq=768, ckv=1536, kr=1792, bg=1824,
           cin=2336, cg=2592, mq=2848, mg=3104, mrg=3360)
PERM = list(range(16, 32)) + list(range(0, 16))
WIN_C0 = [0, 256, 512, 768, 1024, 1280, 1536, 1824, 2080, 2336, 2592, 2848, 3104] + \
         [3360 + br * 1024 + j2 * 256 for br in range(4) for j2 in range(4)]
WIN_G = {c: i for i, c in enumerate(WIN_C0)}
NWG = len(WIN_C0)
TWO_PI = 2.0 * math.pi

C_NG, C_MNG, C_QNG, C_KVNG, C_BM = 0, 8, 16, 22, 24
C_GQ, C_GQP, C_GK, C_GKP, C_MGQ, C_MGK = 56, 57, 58, 59, 60, 61
C_CD, C_BGLU, C_SRE, C_SIM, C_SDT, C_ABS, C_ROW = 62, 64, 66, 74, 82, 90, 346
NCOL = 346 + 640
R_ANG, R_SRE, R_SIM, R_SDT = 0, 256, 1280, 2304
NROW = 3328
K_TRI, K_IOT, K_IOS, K_INVF, K_SGN, K_GM = 0, 128, 256, 257, 258, 259
NCONST = 267


class Tok:
    __slots__ = ("w", "rs", "excl")

    def __init__(self):
        self.w = None
        self.rs = {}
        self.excl = False


class Op:
    __slots__ = ("eng", "fn", "deps", "is_dma", "sig", "users")

    def __init__(self, eng, fn, is_dma):
        self.eng = eng
        self.fn = fn
        self.is_dma = is_dma
        self.deps = []
        self.sig = None
        self.users = 0


ENGS = ("pe", "act", "dve", "pool", "sp")
SYNC_ALL = True
WQ = ("sp",)
DMAQ = ("sp", "act", "pool")


class Prog:
    def __init__(self, nc, n_dma_sems=8):
        self.nc = nc
        self.ops = {e: [] for e in ENGS}
        self.all = []
        self.n_dma_sems = n_dma_sems
        self.toks = {}
        self.dmas_since_bar = []

    def tok(self, *key):
        t = self.toks.get(key)
        if t is None:
            t = self.toks[key] = Tok()
        return t

    def _add(self, eng, fn, reads, writes, is_dma, extra_deps=()):
        op = Op(eng, fn, is_dma)
        deps = list(extra_deps)
        raw = set()
        for t in reads:
            if t.w is not None:
                deps.append(t.w)
                raw.add(id(t.w))
            if t.excl:
                deps.extend(o for o in t.rs.values() if o.eng != eng)
        for t in writes:
            if t.w is not None:
                deps.append(t.w)
            deps.extend(t.rs.values())
        rkey = ("dma", id(op)) if is_dma else eng
        for t in reads:
            t.rs[rkey] = op
        for t in writes:
            t.w = op
            t.rs = {}
        seen = set()
        for d in deps:
            if d is op or id(d) in seen:
                continue
            seen.add(id(d))
            if (not d.is_dma) and (not is_dma) and d.eng == eng:
                if eng == "pe" or (id(d) not in raw and not SYNC_ALL):
                    continue
            if (not d.is_dma) and is_dma and d.eng == eng:
                pass
            op.deps.append(d)
            d.users += 1
        self.ops[eng].append(op)
        self.all.append(op)
        if is_dma:
            self.dmas_since_bar.append(op)
        return op

    def op(self, eng, fn, reads=(), writes=()):
        return self._add(eng, fn, reads, writes, False)

    def dma(self, eng, fn, reads=(), writes=()):
        assert eng in DMAQ
        return self._add(eng, fn, reads, writes, True)

    def barrier(self):
        dm = self.dmas_since_bar
        self.dmas_since_bar = []
        bt = [self.tok("__bar", e) for e in ENGS]
        for i, e in enumerate(ENGS):
            self._add(e, lambda eng: eng.drain(), [], [bt[i]], False,
                      extra_deps=dm if e == "sp" else ())
        for e in ENGS:
            self._add(e, lambda eng: eng.nop(), bt, [], False)

    def emit(self, final_wait_ops=()):
        nc = self.nc
        with contextlib.ExitStack() as es:
            esem = {e: es.enter_context(nc.semaphore("s_" + e)) for e in ENGS}
            dsem = {e: [es.enter_context(nc.semaphore("d_%s%d" % (e, i)))
                        for i in range(self.n_dma_sems)] for e in DMAQ}
            ecount = {e: 0 for e in ENGS}
            dcount = {e: 0 for e in DMAQ}
            duse = {e: [0] * self.n_dma_sems for e in DMAQ}
            fw = set(id(o) for o in final_wait_ops)
            for op in self.all:
                if op.is_dma:
                    j = dcount[op.eng]
                    dcount[op.eng] += 1
                    s = j % self.n_dma_sems
                    duse[op.eng][s] += 1
                    op.sig = (dsem[op.eng][s], 16 * duse[op.eng][s], ("d", op.eng, s))
                elif op.users > 0 or id(op) in fw:
                    ecount[op.eng] += 1
                    op.sig = (esem[op.eng], ecount[op.eng], ("e", op.eng))
            self.stats = dict(ecount=ecount, dcount=dcount,
                              nops={e: len(self.ops[e]) for e in ENGS})
            block = es.enter_context(nc.Block())

            def run_engine(e, engine):
                waited = {}

                def wait(sem, val, key):
                    if waited.get(key, 0) >= val:
                        return
                    waited[key] = val
                    engine.wait_ge(sem, val)

                for op in self.ops[e]:
                    for d in op.deps:
                        wait(*d.sig)
                    if op.is_dma:
                        sem, val, key = op.sig
                        if val > 16:
                            wait(sem, val - 16, key)
                    ins = op.fn(engine)
                    if op.sig is not None:
                        ins.then_inc(op.sig[0], 16 if op.is_dma else 1)
                if e == "sp":
                    for op in final_wait_ops:
                        wait(*op.sig)

            block.tensor(lambda eng: run_engine("pe", eng))
            block.scalar(lambda eng: run_engine("act", eng))
            block.vector(lambda eng: run_engine("dve", eng))
            block.gpsimd(lambda eng: run_engine("pool", eng))
            block.sync(lambda eng: run_engine("sp", eng))


class Arena:
    def __init__(self, ap2d, nfloats):
        self.a = ap2d
        self.n = nfloats
        self.off = 0
        self.peak = 0

    def alloc(self, free_shape, dt):
        free_shape = list(free_shape)
        isz = 2 if dt == BF16 else 4
        n = int(np.prod(free_shape))
        n32 = (n * isz + 3) // 4
        assert self.off + n32 <= self.n, ("arena overflow", self.off, n32, self.n)
        v = self.a[:, self.off:self.off + n32]
        self.off += n32
        self.peak = max(self.peak, self.off)
        if dt == BF16:
            v = v.bitcast(BF16)[:, 0:n]
        elif dt == I32:
            v = v.bitcast(I32)
        if len(free_shape) == 2:
            v = v.rearrange("p (a b) -> p a b", a=free_shape[0])
        elif len(free_shape) == 3:
            v = v.rearrange("p (a b c) -> p a b c", a=free_shape[0], b=free_shape[1])
        elif len(free_shape) == 4:
            v = v.rearrange("p (a b c d) -> p a b c d", a=free_shape[0], b=free_shape[1],
                            c=free_shape[2])
        return v

    def mark(self):
        return self.off

    def reset(self, m):
        self.off = m


class Builder:
    def __init__(self, nlayers=L, stop=None, dumps=()):
        self.nlayers = nlayers
        self.stop = stop
        self.dumps = dict()
        self.want = set(dumps)
        self.nc = nc = bass.Bass("TRN2", target_bir_lowering=False)
        self.P = Prog(nc)
        self.dr = {}
        self.finals = []
        self.dump_specs = []

    def din(self, name, shape, dt=F32):
        self.dr[name] = self.nc.dram_tensor(name, list(shape), dt, kind="ExternalInput").ap()
        return self.dr[name]

    def T(self, *k):
        return self.P.tok(*k)

    def mm(self, out, lhsT, rhs, start, stop, reads, writes):
        return self.P.op("pe", lambda e: e.matmul(out, lhsT=lhsT, rhs=rhs, start=start, stop=stop),
                         reads=reads, writes=writes)

    def mmg(self, out, pairs, reads, wtok):
        n = len(pairs)
        for i, (a, b) in enumerate(pairs):
            self.mm(out, a, b, i == 0, i == n - 1, reads, [wtok])

    def act(self, out, in_, func, reads, writes, bias=0.0, scale=1.0, eng="act", accum_out=None):
        if accum_out is None:
            return self.P.op("act", lambda e: e.activation(out=out, in_=in_, func=func, bias=bias, scale=scale),
                             reads=reads, writes=writes)
        return self.P.op("act", lambda e: e.activation(out=out, in_=in_, func=func, bias=bias, scale=scale,
                                                       accum_out=accum_out),
                         reads=reads, writes=writes)

    def tt(self, eng, out, in0, in1, op, reads, writes):
        return self.P.op(eng, lambda e: e.tensor_tensor(out=out, in0=in0, in1=in1, op=op),
                         reads=reads, writes=writes)

    def ts(self, eng, out, in0, s1, op0, reads, writes, s2=None, op1=None):
        if op1 is None:
            return self.P.op(eng, lambda e: e.tensor_scalar(out=out, in0=in0, scalar1=s1, scalar2=None, op0=op0),
                             reads=reads, writes=writes)
        return self.P.op(eng, lambda e: e.tensor_scalar(out=out, in0=in0, scalar1=s1, scalar2=s2, op0=op0, op1=op1),
                         reads=reads, writes=writes)

    def stt(self, eng, out, in0, scalar, in1, op0, op1, reads, writes):
        return self.P.op(eng, lambda e: e.scalar_tensor_tensor(out=out, in0=in0, scalar=scalar, in1=in1,
                                                                op0=op0, op1=op1),
                         reads=reads, writes=writes)

    def cp(self, eng, out, in_, reads, writes):
        if eng == "act":
            return self.P.op("act", lambda e: e.copy(out=out, in_=in_), reads=reads, writes=writes)
        return self.P.op(eng, lambda e: e.tensor_copy(out=out, in_=in_), reads=reads, writes=writes)

    def memset(self, eng, ap, val, writes):
        return self.P.op(eng, lambda e: e.memset(ap, val), reads=(), writes=writes)

    def dma(self, out, in_, reads, writes, q="sp"):
        def ndesc(ap):
            dims = [(int(st), int(n)) for st, n in ap.ap]
            total = 1
            for st, n in dims:
                total *= n
            run = 1
            for st, n in reversed(dims[1:]):
                if st == run:
                    run *= n
                else:
                    break
            return total // run
        self.desc_count = getattr(self, "desc_count", {})
        self.desc_count[q] = self.desc_count.get(q, 0) + max(ndesc(out), ndesc(in_))
        return self.P.dma(q, lambda e: e.dma_start(out=out, in_=in_), reads=reads, writes=writes)

    def rstd(self, out, in_, n, reads, writes):
        self.act(out, in_, AF.Ln, reads, writes, bias=EPS, scale=1.0 / n)
        self.act(out, out, AF.Exp, writes, writes, scale=-0.5)

    def wload(self, dst, src, dtok, cast_eng="pool", scale=None):
        fs = list(dst.shape[1:])
        n = int(np.prod(fs))
        assert n <= self.stage_n, (n, self.stage_n)
        slot = self.wslot
        self.wslot = (slot + 1) % len(self.stage)
        st = self.stage[slot][:, 0:n]
        if len(fs) == 2:
            st = st.rearrange("p (a b) -> p a b", a=fs[0])
        elif len(fs) == 3:
            st = st.rearrange("p (a b c) -> p a b c", a=fs[0], b=fs[1])
        stok = self.T("stage", slot)
        self.wq_i = getattr(self, "wq_i", 0) + 1
        self.dma(st, src, [], [stok], q=WQ[self.wq_i % len(WQ)])
        if scale is None:
            self.cp(cast_eng, dst, st, [stok], [dtok])
        else:
            self.ts(cast_eng, dst, st, scale, ALU.mult, [stok], [dtok])

    def dump(self, name, ap, tok):
        if name not in self.want:
            return
        self.P.barrier()
        shp = list(ap.shape)
        d = self.nc.dram_tensor("dbg_" + name, shp, ap.dtype, kind="ExternalOutput").ap()
        op = self.dma(d, ap, [tok], [])
        self.finals.append(op)
        self.dump_specs.append(name)

    def build(self):
        nc = self.nc
        P = self.P
        din = self.din
        din("xT", [NB, 128, 8, 512])
        din("memT", [1, 128, 8, 256])
        din("pos", [1, T], I32)
        din("consts", [128, NCONST])
        din("colpack", [L, 128, NCOL])
        din("rowpack", [L, 128, NROW])
        din("w_in", [L, NWG, 128, 8, 256])
        din("w_krm", [L, 128, 8, 96])
        din("w_krp", [L, 128, 8, 96])
        din("wq_m", [L, 8, 128, 6, 96])
        din("wq_p", [L, 8, 128, 6, 96])
        din("w_ukv", [L, 8, 128, 2, 128])
        din("a_w_sT", [L, 128, 4, 128])
        din("c_reT", [L, 2, 128, 4, 128])
        din("c_imT", [L, 2, 128, 4, 128])
        din("c_w_glu", [L, 128, 2, 256])
        din("m_w_kv", [L, 2, 128, 8, 256])
        din("w_br", [L, 4, 128, 10, 256])
        din("w_out", [L, 4, 128, 8, 256])
        self.outT = nc.dram_tensor("outT", [NB, 128, 8, 512], F32, kind="ExternalOutput").ap()
        self.x1T = nc.dram_tensor("x1T", [NB, 128, 8, 512], F32, kind="Internal").ap()

        with contextlib.ExitStack() as es:
            NA = 52000
            arena_t = es.enter_context(nc.sbuf_tensor("arena", [128, NA], F32))
            self.A = A = Arena(arena_t[:, :], NA)
            self.ps = [es.enter_context(nc.psum_tensor("ps%d" % i, [128, 512], F32)) for i in range(8)]
            self.pst = [self.T("ps", i) for i in range(8)]
            for t in self.pst:
                t.excl = True

            self.cst = A.alloc([NCONST], F32)
            self.eps_col = A.alloc([1], F32)
            self.tri = A.alloc([128], BF16)
            self.ntri = A.alloc([128], BF16)
            self.ones = A.alloc([128], BF16)
            self.bd64 = A.alloc([128], BF16)
            self.negI = A.alloc([128], BF16)
            self.slow = A.alloc([128], BF16)
            self.ones_f = A.alloc([128], F32)
            self.cosT = A.alloc([T], F32)
            self.sinT = A.alloc([T], F32)
            self.colp = A.alloc([NCOL], F32)
            self.hT = A.alloc([8, T], BF16)
            self.yall = A.alloc([10, T], BF16)
            self.kmem = A.alloc([2, 256], BF16)
            self.vmem = A.alloc([2, 256], BF16)
            self.stage_n = 2048
            self.stage = [A.alloc([self.stage_n], F32) for _ in range(2)]
            self.wslot = 0
            self.base_mark = A.mark()

            self.setup_consts()
            if self.stop != "consts":
                for l in range(self.nlayers):
                    if self.run_layer(l):
                        break
            P.barrier()
            P.emit(final_wait_ops=self.finals)
        return nc

    def setup_consts(self):
        A, T_ = self.A, self.T
        tc = T_("cst")
        self.dma(self.cst, self.dr["consts"][:, :], [], [tc])
        self.memset("dve", self.eps_col, EPS, [tc])
        self.cp("dve", self.tri, self.cst[:, K_TRI:K_TRI + 128], [tc], [tc])
        self.ts("dve", self.ntri, self.cst[:, K_TRI:K_TRI + 128], -1.0, ALU.mult, [tc], [tc])
        self.memset("dve", self.ones, 1.0, [tc])
        self.memset("dve", self.ones_f, 1.0, [tc])
        self.ts("dve", self.negI, self.cst[:, K_IOT:K_IOT + 128], self.cst[:, K_IOS:K_IOS + 1], ALU.is_equal,
                [tc], [tc], s2=-30000.0, op1=ALU.mult)
        self.ts("dve", self.slow, self.cst[:, K_TRI:K_TRI + 128], -1.0, ALU.mult, [tc], [tc], s2=1.0, op1=ALU.add)
        self.memset("dve", self.bd64, 0.0, [tc])
        self.memset("dve", self.bd64[0:64, 0:64], 1.0, [tc])
        self.memset("dve", self.bd64[64:128, 64:128], 1.0, [tc])
        if getattr(self, "skip", None):
            self.memset("pool", self.yall, 0.25, [T_("yall_init")])
        m = A.mark()
        posi = A.alloc([T], I32)
        ang = A.alloc([T], F32)
        kf = A.alloc([T], F32)
        ki = A.alloc([T], I32)
        tr = T_("rope")
        self.dma(posi[0:96, :], self.dr["pos"][0:1, :].partition_broadcast(96), [], [tr])
        R = slice(64, 96)
        self.cp("dve", ang[R, :], posi[R, :], [tr], [tr])
        self.ts("dve", ang[R, :], ang[R, :], self.cst[R, K_INVF:K_INVF + 1], ALU.mult, [tr, tc], [tr])

        def sin_of(dst, shift, post_scale_col):
            self.ts("dve", ki[R, :], ang[R, :], shift, ALU.add, [tr], [tr], s2=1.0 / TWO_PI, op1=ALU.mult)
            self.cp("dve", kf[R, :], ki[R, :], [tr], [tr])
            self.stt("dve", kf[R, :], kf[R, :], -TWO_PI, ang[R, :], ALU.mult, ALU.add, [tr], [tr])
            self.ts("dve", kf[R, :], kf[R, :], shift, ALU.add, [tr], [tr], s2=math.pi, op1=ALU.min)
            self.ts("dve", kf[R, :], kf[R, :], -math.pi, ALU.max, [tr], [tr])
            self.act(dst[R, :], kf[R, :], AF.Sin, [tr], [tr])
            if post_scale_col is not None:
                self.ts("dve", dst[R, :], dst[R, :], post_scale_col, ALU.mult, [tr, tc], [tr])

        sin_of(self.sinT, 0.0, self.cst[R, K_SGN:K_SGN + 1])
        sin_of(self.cosT, math.pi / 2, None)
        self.dump("cosT", self.cosT[R, :], tr)
        self.dump("sinT", self.sinT[R, :], tr)
        self.P.barrier()
        A.reset(m)

    def run_layer(self, l):
        P, A, T_ = self.P, self.A, self.T
        src = self.dr["xT"] if l == 0 else self.x1T
        dst = self.x1T if l == 0 else self.outT
        if self.nlayers == 1:
            dst = self.outT
        tcp = T_("colp")
        self.dma(self.colp, self.dr["colpack"][l, :, :], [], [tcp])
        self.tcp = tcp
        stages = [("N", self.stage_norm), ("MP", self.stage_memprep), ("A", self.stage_a),
                  ("C", self.stage_c), ("M", self.stage_m), ("B", self.stage_b),
                  ("P2", self.stage_p2)]
        full = not getattr(self, "skip", None)
        marks = {}
        self.preA = self.preM = None
        for name, fn in stages:
            if name in getattr(self, "skip", ()):
                continue
            if full and name == "N":
                marks["A"] = A.mark()
                self.preA = [A.alloc([8, 256], BF16) for _ in range(2)]
                self.load_win(l, OFF["a_u"], 256, self.preA[0], T_("a_w", 0))
                self.load_win(l, OFF["a_g"], 256, self.preA[1], T_("a_w", 1))
            if full and name == "C":
                marks["M"] = A.mark()
                self.preM = [A.alloc([8, 256], BF16) for _ in range(2)]
            m = A.mark()
            if name == "N":
                fn(l, src)
            elif name == "P2":
                fn(l, src, dst)
            else:
                fn(l)
            P.barrier()
            A.reset(m)
            if name in marks:
                A.reset(marks.pop(name))
                if name == "A":
                    self.preA = None
                else:
                    self.preM = None
            if self.stop == (name, l):
                return True
        return False

    def norm_fm(self, srcT, n_tok, gcol, dst, dst_tok_fn, tag):
        A, T_ = self.A, self.T
        W = min(512, n_tok)
        nblk = n_tok // W
        xb = [A.alloc([8, W], F32) for _ in range(2)]
        sq = A.alloc([8, W], BF16)
        rs = A.alloc([W], F32)
        for b in range(nblk):
            x = xb[b % 2]
            tx = T_(tag + "x", b % 2)
            ts_ = T_(tag + "sq")
            trs = T_(tag + "rs")
            for hh in range(2):
                self.dma(x[:, hh * 4:(hh + 1) * 4, :], srcT[b, :, hh * 4:(hh + 1) * 4, :], [], [tx])
            pb = 0
            for kt in range(8):
                self.act(sq[:, kt, :], x[:, kt, :], AF.Square, [tx], [ts_])
            self.mmg(self.ps[pb][:, 0:W], [(self.ones[:, :], sq[:, kt, :]) for kt in range(8)],
                     [ts_], self.pst[pb])
            self.rstd(rs[:, :], self.ps[pb][:, 0:W], float(D), [self.pst[pb]], [trs])
            for kt in range(8):
                self.stt("dve", dst[:, kt, b * W:(b + 1) * W], x[:, kt, :], gcol[:, kt:kt + 1], rs[:, :],
                         ALU.mult, ALU.mult, [tx, trs, self.tcp], [dst_tok_fn(b)])

    def stage_norm(self, l, src):
        self.norm_fm(src, T, self.colp[:, C_NG:C_NG + 8], self.hT, lambda b: self.T("hT", b), "n")
        for b in range(NB):
            self.dump("hT%d_%d" % (l, b), self.hT[:, :, b * 512:(b + 1) * 512], self.T("hT", b))

    def load_win(self, l, c0, ncols, wt, wtok):
        assert ncols == 256
        self.wload(wt[:, :, 0:ncols], self.dr["w_in"][l, WIN_G[c0]], wtok)

    def zproj_blk(self, wt, ncols, blk, pb, wtok, m_off=0):
        self.mmg(self.ps[pb][m_off:m_off + ncols, :],
                 [(wt[:, kt, 0:ncols], self.hT[:, kt, blk * 512:(blk + 1) * 512]) for kt in range(8)],
                 [wtok, self.T("hT", blk)], self.pst[pb])

    def stage_memprep(self, l):
        A, T_ = self.A, self.T
        hm = A.alloc([8, 256], BF16)
        thm = T_("hm")
        self.norm_fm(self.dr["memT"], 256, self.colp[:, C_MNG:C_MNG + 8], hm, lambda b: thm, "m")
        wkv = A.alloc([8, 512], BF16)
        twk = T_("wkv")
        for half in range(2):
            self.wload(wkv[:, :, half * 256:(half + 1) * 256],
                       self.dr["m_w_kv"][l, half],
                       twk)
        sq = A.alloc([256], BF16)
        rs = A.alloc([256], F32)
        tsq, trs, tkm, tvm = T_("mp_sq"), T_("mp_rs"), T_("kmem"), T_("vmem")
        for ct in range(2):
            self.mmg(self.ps[0][:, 0:256], [(wkv[:, kt, ct * 128:(ct + 1) * 128], hm[:, kt, :]) for kt in range(8)],
                     [twk, thm], self.pst[0])
            self.act(sq[:, :], self.ps[0][:, 0:256], AF.Square, [self.pst[0]], [tsq])
            self.mmg(self.ps[1][:, 0:256], [(self.bd64[:, :], sq[:, :])], [tsq, T_("cst")], self.pst[1])
            self.rstd(rs[:, :], self.ps[1][:, 0:256], 64.0, [self.pst[1]], [trs])
            self.stt("dve", self.kmem[:, ct, :], self.ps[0][:, 0:256], self.colp[:, C_MGK:C_MGK + 1], rs[:, :],
                     ALU.mult, ALU.mult, [self.pst[0], trs, self.tcp], [tkm])
        for mt in range(2):
            self.mmg(self.ps[2][:, 0:256], [(hm[:, kt, mt * 128:(mt + 1) * 128], wkv[:, kt, 256:512]) for kt in range(8)],
                     [twk, thm], self.pst[2])
            self.cp("dve", self.vmem[:, mt, :], self.ps[2][:, 0:256], [self.pst[2]], [tvm])
        self.dump("kmem%d" % l, self.kmem, tkm)
        self.dump("vmem%d" % l, self.vmem, tvm)

    def stage_a(self, l):
        A, T_ = self.A, self.T
        ya = self.yall
        pre = self.preA is not None
        wt = self.preA if pre else [A.alloc([8, 256], BF16) for _ in range(2)]
        gnb = A.alloc([256], F32)
        wsT = A.alloc([4, 128], BF16)
        tmp = A.alloc([512], BF16)
        tgn, tws, ttmp = T_("a_gn"), T_("a_ws"), T_("a_tmp")
        self.dma(gnb, self.dr["rowpack"][l, :, R_ANG:R_ANG + 256], [], [tgn])
        slot = self.wslot
        self.wslot = (slot + 1) % 2
        st = self.stage[slot][:, 0:512].rearrange("p (g t) -> p g t", g=4)
        stok = T_("stage", slot)
        self.dma(st, self.dr["a_w_sT"][l], [], [stok])
        for g in range(4):
            self.tt("dve", wsT[:, g, :], st[:, g, :], self.cst[:, K_TRI:K_TRI + 128], ALU.mult, [stok, T_("cst")], [tws])
        tw0, tw1 = T_("a_w", 0), T_("a_w", 1)
        if not pre:
            self.load_win(l, OFF["a_u"], 256, wt[0], tw0)
            self.load_win(l, OFF["a_g"], 256, wt[1], tw1)
        for ct in range(2):
            for blk in range(NB):
                pb = (ct * NB + blk) % 2
                self.mmg(self.ps[pb][:, :],
                         [(wt[0][:, kt, ct * 128:(ct + 1) * 128], self.hT[:, kt, blk * 512:(blk + 1) * 512])
                          for kt in range(8)], [tw0, T_("hT", blk)], self.pst[pb])
                self.act(ya[:, ct, blk * 512:(blk + 1) * 512], self.ps[pb][:, :], AF.Gelu_apprx_tanh,
                         [self.pst[pb]], [T_("ya", ct, blk)])
        for ct in range(2):
            for blk in range(NB):
                pb = 2 + (ct * NB + blk) % 2
                self.mmg(self.ps[pb][:, :],
                         [(wt[1][:, kt, ct * 128:(ct + 1) * 128], self.hT[:, kt, blk * 512:(blk + 1) * 512])
                          for kt in range(8)], [tw1, T_("hT", blk)], self.pst[pb])
                self.act(tmp[:, :], self.ps[pb][:, :], AF.Silu, [self.pst[pb]], [ttmp])
                sl = ya[:, ct, blk * 512:(blk + 1) * 512]
                self.tt("pool", sl, sl, tmp[:, :], ALU.mult, [ttmp], [T_("ya", ct, blk)])
        twv = T_("a_w", 0)
        self.load_win(l, OFF["a_v"], 256, wt[0], twv)
        gvs = [A.alloc([256], F32) for _ in range(2)]
        ssqs = [A.alloc([1], F32) for _ in range(2)]
        vn = [A.alloc([256], BF16) for _ in range(2)]
        sbs = [A.alloc([2, 128], F32) for _ in range(2)]
        junk = A.alloc([256], BF16)
        absT = self.colp[:, C_ABS:C_ABS + 256].rearrange("p (c t) -> p c t", c=2)
        def a_front(tt_):
            blk = tt_ // 4
            pb = 4 + tt_ % 2
            self.mmg(self.ps[pb][:, 0:256],
                     [(self.hT[:, kt, tt_ * 128:(tt_ + 1) * 128], wt[0][:, kt, :]) for kt in range(8)],
                     [twv, T_("hT", blk)], self.pst[pb])

        def a_back(tt_):
            blk = tt_ // 4
            k = tt_ % 2
            pb = 4 + k
            gv, ssq, sb = gvs[k], ssqs[k], sbs[k]
            tgv, tss, tsb = T_("a_gv", k), T_("a_ss", k), T_("a_sb", k)
            self.act(gv[:, :], self.ps[pb][:, 0:256], AF.Gelu_apprx_tanh, [self.pst[pb]], [tgv])
            self.act(junk[:, :], gv[:, :], AF.Square, [tgv], [tss], accum_out=ssq[:, :])
            self.rstd(ssq[:, :], ssq[:, :], 256.0, [tss], [tss])
            v = vn[k]
            tv = T_("a_vn", k)
            self.stt("dve", v[:, :], gv[:, :], ssq[:, 0:1], gnb[:, :], ALU.mult, ALU.mult, [tgv, tss, tgn], [tv])
            pq = 6 + k
            for g in range(4):
                ct, r0 = g // 2, (g % 2) * 64
                self.mm(self.ps[pq][r0:r0 + 64, ct * 128:(ct + 1) * 128], v[:, g * 64:(g + 1) * 64], wsT[:, g, :],
                        True, True, [tv, tws], [self.pst[pq]])
            psv = self.ps[pq][:, 0:256].rearrange("p (c t) -> p c t", c=2)
            self.tt("dve", sb[:, :, :], psv, absT, ALU.add, [self.pst[pq], self.tcp], [tsb])
            sl = ya[:, 0:2, tt_ * 128:(tt_ + 1) * 128]
            self.tt("pool", sl, sl, sb[:, :, :], ALU.mult, [tsb], [T_("ya", 0, blk), T_("ya", 1, blk)])

        a_front(0)
        for tt_ in range(16):
            if tt_ + 1 < 16:
                a_front(tt_ + 1)
            a_back(tt_)
        self.dump("ya%d" % l, ya[:, 0:2, :], T_("ya", 0, 0))

    def sin_of(self, dst, ang, shift, ki, kf, tok, extra=()):
        rd = [tok] + list(extra)
        self.ts("dve", ki, ang, shift, ALU.add, rd, [tok], s2=1.0 / TWO_PI, op1=ALU.mult)
        self.cp("dve", kf, ki, [tok], [tok])
        self.stt("dve", kf, kf, -TWO_PI, ang, ALU.mult, ALU.add, [tok], [tok])
        self.ts("dve", kf, kf, shift, ALU.add, [tok], [tok], s2=math.pi, op1=ALU.min)
        self.ts("dve", kf, kf, -math.pi, ALU.max, [tok], [tok])
        self.act(dst, kf, AF.Sin, [tok], [tok])

    def stage_m(self, l):
        A, T_ = self.A, self.T
        ya = self.yall
        twq, twg = T_("m_wq"), T_("m_wg")
        if self.preM is not None:
            wq, wg = self.preM
        else:
            wq = A.alloc([8, 256], BF16)
            wg = A.alloc([8, 256], BF16)
            self.load_win(l, OFF["mq"], 256, wq, twq)
            self.load_win(l, OFF["mg"], 256, wg, twg)
        for ct in range(2):
            for blk in range(NB):
                pb = (ct * NB + blk) % 2
                self.mmg(self.ps[pb][:, :],
                         [(wg[:, kt, ct * 128:(ct + 1) * 128], self.hT[:, kt, blk * 512:(blk + 1) * 512])
                          for kt in range(8)], [twg], self.pst[pb])
                self.act(ya[:, 8 + ct, blk * 512:(blk + 1) * 512], self.ps[pb][:, :], AF.Silu,
                         [self.pst[pb]], [T_("ym", ct, blk)])
        sq = A.alloc([512], BF16)
        rs = A.alloc([512], F32)
        qn = [A.alloc([2, 512], BF16) for _ in range(2)]
        pT = [A.alloc([512], BF16) for _ in range(4)]
        rc = A.alloc([512], F32)
        ot = A.alloc([512], BF16)
        tsq, trs, trc, tot = T_("m_sq"), T_("m_rs"), T_("m_rc"), T_("m_ot")
        scale = 64.0 ** -0.5

        def m_prep(blk):
            q = qn[blk % 2]
            tq = T_("m_qn", blk % 2)
            for ct in range(2):
                self.mmg(self.ps[2][:, :],
                         [(wq[:, kt, ct * 128:(ct + 1) * 128], self.hT[:, kt, blk * 512:(blk + 1) * 512])
                          for kt in range(8)], [twq], self.pst[2])
                self.act(sq[:, :], self.ps[2][:, :], AF.Square, [self.pst[2]], [tsq])
                self.mmg(self.ps[3][:, :], [(self.bd64[:, :], sq[:, :])], [tsq], self.pst[3])
                self.rstd(rs[:, :], self.ps[3][:, :], 64.0, [self.pst[3]], [trs])
                self.stt("dve", q[:, ct, :], self.ps[2][:, :], self.colp[:, C_MGQ:C_MGQ + 1], rs[:, :],
                         ALU.mult, ALU.mult, [self.pst[2], trs], [tq])

        def m_s(blk, h):
            q = qn[blk % 2]
            tq = T_("m_qn", blk % 2)
            ct, r0 = h // 2, (h % 2) * 64
            R = slice(r0, r0 + 64)
            for mt in range(2):
                pb = (4 + mt) if h % 2 == 0 else mt
                self.mm(self.ps[pb][:, :], self.kmem[R, ct, mt * 128:(mt + 1) * 128], q[R, ct, :],
                        True, True, [tq], [self.pst[pb]])

        def m_rest(blk, h):
            ct, r0 = h // 2, (h % 2) * 64
            R = slice(r0, r0 + 64)
            for mt in range(2):
                pb = (4 + mt) if h % 2 == 0 else mt
                p = pT[(h % 2) * 2 + mt]
                tp = T_("m_pT", (h % 2) * 2 + mt)
                self.act(p[:, :], self.ps[pb][:, :], AF.Exp, [self.pst[pb]], [tp], scale=scale)
            tps = [T_("m_pT", (h % 2) * 2 + mt) for mt in range(2)]
            ps_o, ps_d = self.ps[6], self.ps[7]
            self.mmg(ps_o[R, :], [(self.vmem[:, mt, h * 64:(h + 1) * 64], pT[(h % 2) * 2 + mt][:, :])
                                  for mt in range(2)], tps, self.pst[6])
            self.mmg(ps_d[R, :], [(self.ones[:, 0:64], pT[(h % 2) * 2 + mt][:, :]) for mt in range(2)],
                     tps, self.pst[7])
            self.P.op("dve", lambda e, o=rc[R, :], i=ps_d[R, :]: e.reciprocal(out=o, in_=i),
                      reads=[self.pst[7]], writes=[trc])
            self.tt("dve", ot[R, :], ps_o[R, :], rc[R, :], ALU.mult, [self.pst[6], trc], [tot])
            sl = ya[R, 8 + ct, blk * 512:(blk + 1) * 512]
            self.tt("pool", sl, sl, ot[R, :], ALU.mult, [tot], [T_("ym", ct, blk)])

        m_prep(0)
        for blk in range(NB):
            m_s(blk, 0)
            if blk + 1 < NB:
                m_prep(blk + 1)
            for h in range(4):
                if h + 1 < 4:
                    m_s(blk, h + 1)
                m_rest(blk, h)
        self.dump("ym%d" % l, ya[:, 8:10, :], T_("ym", 0, 0))

    def stage_c(self, l):
        A, T_ = self.A, self.T
        ya = self.yall
        uT = A.alloc([2, T], BF16)
        ygT = A.alloc([2, T], BF16)
        TAc = A.alloc([1024], F32)
        TAs = A.alloc([1024], F32)
        Dr = A.alloc([8, 128], F32)
        Di = A.alloc([8, 128], F32)
        a128r = A.alloc([8], F32)
        a128i = A.alloc([8], F32)
        Bm = [A.alloc([2, 8, 64], BF16) for _ in range(2)]
        CreT = A.alloc([8, 128], BF16)
        CreTn = A.alloc([8, 128], BF16)
        CimTn = A.alloc([8, 128], BF16)
        wglu = A.alloc([2, 256], BF16)
        m2 = A.mark()
        wu = A.alloc([8, 256], BF16)
        wg = A.alloc([8, 256], BF16)
        twu, twg = T_("c_wu"), T_("c_wg")
        self.load_win(l, OFF["cin"], 256, wu, twu)
        self.load_win(l, OFF["cg"], 256, wg, twg)
        for ct in range(2):
            for blk in range(NB):
                pb = (ct * NB + blk) % 2
                self.mmg(self.ps[pb][:, :],
                         [(wu[:, kt, ct * 128:(ct + 1) * 128], self.hT[:, kt, blk * 512:(blk + 1) * 512])
                          for kt in range(8)], [twu], self.pst[pb])
                self.cp("act", uT[:, ct, blk * 512:(blk + 1) * 512], self.ps[pb][:, :], [self.pst[pb]],
                        [T_("c_uT", blk)])
        for ct in range(2):
            for blk in range(NB):
                pb = 2 + (ct * NB + blk) % 2
                self.mmg(self.ps[pb][:, :],
                         [(wg[:, kt, ct * 128:(ct + 1) * 128], self.hT[:, kt, blk * 512:(blk + 1) * 512])
                          for kt in range(8)], [twg], self.pst[pb])
                self.act(ya[:, 6 + ct, blk * 512:(blk + 1) * 512], self.ps[pb][:, :], AF.Silu,
                         [self.pst[pb]], [T_("yc", ct, blk)])
        tt_ = T_("c_tab")
        rowp = A.alloc([3, 1024], F32)
        self.dma(rowp, self.dr["rowpack"][l, :, R_SRE:R_SRE + 3072].rearrange("p (a b) -> p a b", a=3), [], [tt_])
        s1 = A.alloc([1024], F32)
        s2 = A.alloc([1024], F32)
        si = A.alloc([1024], I32)
        negs = A.alloc([1], F32)
        tcst = T_("cst")
        self.ts("dve", negs, self.cst[:, K_IOS:K_IOS + 1], -1.0, ALU.mult, [tcst], [tt_])
        self.act(rowp[:, 2, :], rowp[:, 2, :], AF.Exp, [tt_], [tt_])
        self.tt("dve", rowp[:, 0, :], rowp[:, 0, :], rowp[:, 2, :], ALU.mult, [tt_], [tt_])
        self.tt("dve", rowp[:, 1, :], rowp[:, 1, :], rowp[:, 2, :], ALU.mult, [tt_], [tt_])
        self.act(s1, rowp[:, 0, :], AF.Exp, [tt_], [tt_], scale=negs[:, 0:1])
        self.ts("dve", s2, rowp[:, 1, :], self.cst[:, K_IOS:K_IOS + 1], ALU.mult, [tt_, tcst], [tt_])
        self.sin_of(TAs, s2, 0.0, si, rowp[:, 2, :], tt_)
        self.sin_of(TAc, s2, math.pi / 2, si, rowp[:, 2, :], tt_)
        self.tt("dve", TAs, TAs, s1, ALU.mult, [tt_], [tt_])
        self.tt("dve", TAc, TAc, s1, ALU.mult, [tt_], [tt_])
        s1v = s1.rearrange("p (j t) -> p j t", j=8)
        s2v = s2.rearrange("p (j t) -> p j t", j=8)
        siv = si.rearrange("p (j t) -> p j t", j=8)
        kfv = rowp[:, 2, :].rearrange("p (j t) -> p j t", j=8)
        dtj = A.alloc([8], F32)
        thrj = A.alloc([8], F32)
        thij = A.alloc([8], F32)
        e128 = A.alloc([8], F32)
        p128 = A.alloc([8], F32)
        k128 = A.alloc([8], F32)
        i128 = A.alloc([8], I32)
        tcp = self.tcp
        self.act(dtj, self.colp[:, C_SDT:C_SDT + 8], AF.Exp, [tcp, tt_], [tt_])
        self.tt("dve", thrj, self.colp[:, C_SRE:C_SRE + 8], dtj, ALU.mult, [tcp, tt_], [tt_])
        self.tt("dve", thij, self.colp[:, C_SIM:C_SIM + 8], dtj, ALU.mult, [tcp, tt_], [tt_])
        iot = self.cst[:, K_IOT:K_IOT + 128]
        for j in range(8):
            self.act(s1v[:, j, :], iot, AF.Exp, [tt_, tcst], [tt_], scale=thrj[:, j:j + 1])
            self.ts("dve", s2v[:, j, :], iot, thij[:, j:j + 1], ALU.mult, [tt_, tcst], [tt_])
        self.sin_of(Di.rearrange("p j t -> p (j t)"), s2, 0.0, si, rowp[:, 2, :], tt_)
        self.sin_of(Dr.rearrange("p j t -> p (j t)"), s2, math.pi / 2, si, rowp[:, 2, :], tt_)
        self.tt("dve", Di, Di, s1v, ALU.mult, [tt_], [tt_])
        self.tt("dve", Dr, Dr, s1v, ALU.mult, [tt_], [tt_])
        self.act(e128, thrj, AF.Exp, [tt_], [tt_], scale=128.0)
        self.ts("dve", p128, thij, 128.0, ALU.mult, [tt_], [tt_])
        self.sin_of(a128i, p128, 0.0, i128, k128, tt_)
        self.sin_of(a128r, p128, math.pi / 2, i128, k128, tt_)
        self.tt("dve", a128i, a128i, e128, ALU.mult, [tt_], [tt_])
        self.tt("dve", a128r, a128r, e128, ALU.mult, [tt_], [tt_])
        w = [A.alloc([64], F32) for _ in range(8)]
        wi = A.alloc([64], I32)
        gmask = self.cst[:, K_GM:K_GM + 8]
        for ct in range(2):
            base = C_ROW + ct * 320
            are = self.colp[:, base:base + 64]
            aim = self.colp[:, base + 64:base + 128]
            ldt = self.colp[:, base + 128:base + 192]
            bre = self.colp[:, base + 192:base + 256]
            bim = self.colp[:, base + 256:base + 320]
            dt_, thr, thi, ea, abr, abi, t0, t1 = w
            rd = [tt_, tcp]
            self.act(dt_, ldt, AF.Exp, rd, [tt_])
            self.tt("dve", thr, are, dt_, ALU.mult, rd, [tt_])
            self.tt("dve", thi, aim, dt_, ALU.mult, rd, [tt_])
            self.act(ea, thr, AF.Exp, [tt_], [tt_])
            self.sin_of(abi, thi, 0.0, wi, t0, tt_)
            self.sin_of(abr, thi, math.pi / 2, wi, t0, tt_)
            self.tt("dve", abi, abi, ea, ALU.mult, [tt_], [tt_])
            self.tt("dve", abr, abr, ea, ALU.mult, [tt_], [tt_])
            self.ts("dve", abr, abr, -1.0, ALU.add, [tt_], [tt_])
            self.tt("dve", dt_, are, are, ALU.mult, rd, [tt_])
            self.tt("dve", t0, aim, aim, ALU.mult, rd, [tt_])
            self.tt("dve", dt_, dt_, t0, ALU.add, [tt_], [tt_])
            self.P.op("dve", lambda e, o=dt_, i=dt_: e.reciprocal(out=o, in_=i), reads=[tt_], writes=[tt_])
            self.tt("dve", t0, abr, are, ALU.mult, rd, [tt_])
            self.tt("dve", t1, abi, aim, ALU.mult, rd, [tt_])
            self.tt("dve", t0, t0, t1, ALU.add, [tt_], [tt_])
            self.tt("dve", thr, t0, dt_, ALU.mult, [tt_], [tt_])
            self.tt("dve", t0, abi, are, ALU.mult, rd, [tt_])
            self.tt("dve", t1, abr, aim, ALU.mult, rd, [tt_])
            self.tt("dve", t0, t0, t1, ALU.subtract, [tt_], [tt_])
            self.tt("dve", thi, t0, dt_, ALU.mult, [tt_], [tt_])
            self.tt("dve", t0, thr, bre, ALU.mult, rd, [tt_])
            self.tt("dve", t1, thi, bim, ALU.mult, rd, [tt_])
            self.tt("dve", ea, t0, t1, ALU.subtract, [tt_], [tt_])
            self.tt("dve", t0, thr, bim, ALU.mult, rd, [tt_])
            self.tt("dve", t1, thi, bre, ALU.mult, rd, [tt_])
            self.tt("dve", abi, t0, t1, ALU.add, [tt_], [tt_])
            for ri, src_ in enumerate((ea, abi)):
                self.tt("dve", Bm[ct][:, ri, :, :], src_.unsqueeze(1).to_broadcast([128, 8, 64]),
                        gmask.unsqueeze(2).to_broadcast([128, 8, 64]), ALU.mult, [tt_, tcst], [tt_])
        tct = T_("c_ct")
        for j0 in (0, 4):
            srcr = self.dr["c_reT"][l, j0 // 4]
            srci = self.dr["c_imT"][l, j0 // 4]
            self.wload(CreT[:, j0:j0 + 4, :], srcr, tct)
            self.wload(CreTn[:, j0:j0 + 4, :], srcr, tct, scale=-1.0)
            self.wload(CimTn[:, j0:j0 + 4, :], srci, tct, scale=-1.0)
        self.wload(wglu, self.dr["c_w_glu"][l], tct)
        self.dump("c_TAc%d" % l, TAc, tt_)
        self.dump("c_TAs%d" % l, TAs, tt_)
        self.dump("c_Dr%d" % l, Dr, tt_)
        self.dump("c_Di%d" % l, Di, tt_)
        self.dump("c_Bm%d" % l, Bm[0], tt_)
        self.dump("c_a128r%d" % l, a128r, tt_)
        self.P.barrier()
        A.reset(m2)
        if self.preM is not None:
            self.load_win(l, OFF["mq"], 256, self.preM[0], T_("m_wq"))
            self.load_win(l, OFF["mg"], 256, self.preM[1], T_("m_wg"))
        aS = A.alloc([16], F32)
        self.memset("dve", aS, 0.0, [T_("c_aS", 0), T_("c_aS", 1)])
        X = [A.alloc([4, 512], BF16) for _ in range(2)]
        Pfs = [A.alloc([8, 128], F32) for _ in range(2)]
        Ys = [A.alloc([4, 4, 128], BF16) for _ in range(2)]
        p127 = A.alloc([8], F32)
        c1 = A.alloc([8], F32)
        c2 = A.alloc([8], F32)
        yss = [A.alloc([128], F32) for _ in range(2)]
        tch = T_("c_ch")
        its = [(n, ct) for n in range(16) for ct in range(2)]

        def front(i):
            n, ct = its[i]
            par = i % 2
            blk = n // 4
            cs = slice(n * 128, (n + 1) * 128)
            pre_, pim_ = self.ps[0], self.ps[1]
            tpre, tpim = self.pst[0], self.pst[1]
            Bre = Bm[ct][:, 0, :, :].rearrange("p g q -> p (g q)")
            Bim = Bm[ct][:, 1, :, :].rearrange("p g q -> p (g q)")
            self.mm(pre_[:, :], uT[:, ct, cs], Bre, True, True, [T_("c_uT", blk), tt_], [tpre])
            self.mm(pim_[:, :], uT[:, ct, cs], Bim, True, True, [T_("c_uT", blk), tt_], [tpim])
            x = X[par]
            tx = T_("c_X", par)
            tc_ = TAc[:, ct * 512:(ct + 1) * 512]
            ts_ = TAs[:, ct * 512:(ct + 1) * 512]
            self.tt("dve", x[:, 0, :], pre_[:, :], tc_, ALU.mult, [tpre, tt_], [tx])
            self.tt("dve", x[:, 3, :], pre_[:, :], ts_, ALU.mult, [tpre, tt_], [tx])
            self.tt("dve", x[:, 1, :], pim_[:, :], ts_, ALU.mult, [tpim, tt_], [tx])
            self.tt("dve", x[:, 2, :], pim_[:, :], tc_, ALU.mult, [tpim, tt_], [tx])
            Pre, Pim = self.ps[2 + 2 * par], self.ps[3 + 2 * par]
            for jl in range(4):
                js = slice(jl * 128, (jl + 1) * 128)
                self.mmg(Pre[:, js], [(x[:, 0, js], self.tri[:, :]), (x[:, 1, js], self.tri[:, :])],
                         [tx], self.pst[2 + 2 * par])
                self.mmg(Pim[:, js], [(x[:, 2, js], self.tri[:, :]), (x[:, 3, js], self.ntri[:, :])],
                         [tx], self.pst[3 + 2 * par])

        def back_a(i):
            n, ct = its[i]
            par = i % 2
            Pre, Pim = self.ps[2 + 2 * par], self.ps[3 + 2 * par]
            tP0, tP1 = self.pst[2 + 2 * par], self.pst[3 + 2 * par]
            Pf, Y = Pfs[par], Ys[par]
            tPf, tY = T_("c_Pf", par), T_("c_Y", par)
            ta = T_("c_aS", ct)
            jr = slice(4 * ct, 4 * ct + 4)
            ji = slice(8 + 4 * ct, 8 + 4 * ct + 4)
            for jl in range(4):
                js = slice(jl * 128, (jl + 1) * 128)
                self.act(Pf[:, jl, :], Pre[:, js], AF.Identity, [tP0, ta], [tPf],
                         bias=aS[:, 4 * ct + jl:4 * ct + jl + 1])
                self.act(Pf[:, 4 + jl, :], Pim[:, js], AF.Identity, [tP1, ta], [tPf],
                         bias=aS[:, 8 + 4 * ct + jl:8 + 4 * ct + jl + 1])
            self.cp("dve", p127, Pf[:, :, 127], [tPf], [tch])
            self.tt("dve", c1[:, 0:4], a128r[:, jr], p127[:, 0:4], ALU.mult, [tch, tt_], [tch])
            self.tt("dve", c1[:, 4:8], a128i[:, jr], p127[:, 4:8], ALU.mult, [tch, tt_], [tch])
            self.tt("dve", c2[:, 0:4], a128r[:, jr], p127[:, 4:8], ALU.mult, [tch, tt_], [tch])
            self.tt("dve", c2[:, 4:8], a128i[:, jr], p127[:, 0:4], ALU.mult, [tch, tt_], [tch])
            self.tt("dve", aS[:, jr], c1[:, 0:4], c1[:, 4:8], ALU.subtract, [tch], [ta])
            self.tt("dve", aS[:, ji], c2[:, 0:4], c2[:, 4:8], ALU.add, [tch], [ta])
            self.tt("pool", Y[:, 0, :, :], Pf[:, 0:4, :], Dr[:, jr, :], ALU.mult, [tPf, tt_], [tY])
            self.tt("pool", Y[:, 1, :, :], Pf[:, 4:8, :], Di[:, jr, :], ALU.mult, [tPf, tt_], [tY])
            tY2 = T_("c_Y2", par)
            self.tt("dve", Y[:, 2, :, :], Pf[:, 4:8, :], Dr[:, jr, :], ALU.mult, [tPf, tt_], [tY2])
            self.tt("dve", Y[:, 3, :, :], Pf[:, 0:4, :], Di[:, jr, :], ALU.mult, [tPf, tt_], [tY2])

        def back_y(i):
            n, ct = its[i]
            par = i % 2
            Y = Ys[par]
            tY, tY2 = T_("c_Y", par), T_("c_Y2", par)
            py = self.ps[6 + par]
            pairs = []
            for jl in range(4):
                j = 4 * ct + jl
                pairs += [(CreT[:, j, :], Y[:, 0, jl, :]), (CreTn[:, j, :], Y[:, 1, jl, :]),
                          (CimTn[:, j, :], Y[:, 2, jl, :]), (CimTn[:, j, :], Y[:, 3, jl, :])]
            self.mmg(py[:, 0:128], pairs, [tY, tY2, tct], self.pst[6 + par])

        def back_b(i):
            n, ct = its[i]
            par = i % 2
            blk = n // 4
            cs = slice(n * 128, (n + 1) * 128)
            py = self.ps[6 + par]
            ys, tys = yss[par], T_("c_ys", par)
            self.stt("dve", ys, uT[:, ct, cs], self.colp[:, C_CD + ct:C_CD + ct + 1], py[:, 0:128],
                     ALU.mult, ALU.add, [self.pst[6 + par], T_("c_uT", blk), tcp], [tys])
            self.act(ygT[:, ct, cs], ys, AF.Gelu_apprx_tanh, [tys], [T_("c_yg", blk)])

        front(0)
        for i in range(len(its)):
            if i + 1 < len(its):
                front(i + 1)
            back_a(i)
            if i >= 1:
                back_y(i - 1)
            if i >= 2:
                back_b(i - 2)
        back_y(len(its) - 1)
        back_b(len(its) - 2)
        back_b(len(its) - 1)
        self.dump("c_yg%d" % l, ygT, T_("c_yg", 0))
        sg = A.alloc([512], BF16)
        tsg = T_("c_sg")
        for blk in range(NB):
            bs = slice(blk * 512, (blk + 1) * 512)
            for cc in range(2):
                pb = 7
                self.mmg(self.ps[pb][:, :], [(wglu[:, ct, cc * 128:(cc + 1) * 128], ygT[:, ct, bs]) for ct in range(2)],
                         [tct, T_("c_yg", blk)], self.pst[pb])
                self.act(sg, self.ps[pb][:, :], AF.Sigmoid, [self.pst[pb], tcp], [tsg],
                         bias=self.colp[:, C_BGLU + cc:C_BGLU + cc + 1])
                sl = ya[:, 6 + cc, bs]
                self.tt("pool", sg, sg, ygT[:, cc, bs], ALU.mult, [tsg, T_("c_yg", blk)], [tsg])
                self.tt("pool", sl, sl, sg, ALU.mult, [tsg], [T_("yc", cc, blk)])
        self.dump("yc%d" % l, ya[:, 6:8, :], T_("yc", 0, 0))

    def stage_b(self, l):
        A, T_, P = self.A, self.T, self.P
        ya = self.yall
        tcp = self.tcp
        tcst = T_("cst")
        cqn = A.alloc([6, T], BF16)
        ckvn = A.alloc([2, T], BF16)
        krr = A.alloc([T], F32)
        sqkr = A.alloc([T], BF16)
        m2 = A.mark()
        wq_in = A.alloc([8, 768], BF16)
        wkv_in = A.alloc([8, 256], BF16)
        wkr = [A.alloc([8, 96], BF16) for _ in range(2)]
        m3 = A.mark()
        wt = A.alloc([8, 512], BF16)
        tw = T_("b_w")
        tw1 = T_("b_w1")
        for i in range(2):
            self.load_win(l, OFF["bg"] + i * 256, 256, wt[:, :, i * 256:(i + 1) * 256], tw)
        for i in range(3):
            self.load_win(l, OFF["cq"] + i * 256, 256, wq_in[:, :, i * 256:(i + 1) * 256], tw1)
        self.load_win(l, OFF["ckv"], 256, wkv_in, tw1)
        self.wload(wkr[0], self.dr["w_krm"][l], tw1)
        self.wload(wkr[1], self.dr["w_krp"][l], tw1)
        for ct in range(4):
            for blk in range(NB):
                pb = (ct * NB + blk) % 2
                self.zproj_blk(wt[:, :, ct * 128:(ct + 1) * 128], 128, blk, pb, tw)
                self.act(ya[:, 2 + ct, blk * 512:(blk + 1) * 512], self.ps[pb][:, :], AF.Silu,
                         [self.pst[pb]], [T_("yb", ct, blk)])
        P.barrier()
        A.reset(m3)
        tw = tw1
        sq = A.alloc([512], BF16)
        rs = A.alloc([512], F32)
        t1 = A.alloc([512], F32)
        t2 = A.alloc([512], F32)
        cqf = A.alloc([6, 512], F32)
        ckf = A.alloc([2, 512], F32)
        rs2 = A.alloc([512], F32)
        tsq, trs = T_("b_sq"), T_("b_rs")
        R = slice(64, 96)
        for blk in range(NB):
            bs = slice(blk * 512, (blk + 1) * 512)
            tcf = T_("b_cqf")
            for i in range(6):
                pb = i % 2
                self.zproj_blk(wq_in[:, :, i * 128:(i + 1) * 128], 128, blk, pb, tw)
                self.act(sq, self.ps[pb][:, :], AF.Square, [self.pst[pb]], [tsq])
                self.cp("act", cqf[:, i, :], self.ps[pb][:, :], [self.pst[pb]], [tcf])
                self.mm(self.ps[6][:, :], self.ones[:, :], sq, i == 0, i == 5, [tsq], [self.pst[6]])
            tkf = T_("b_ckf")
            for i in range(2):
                pb = 4 + i
                self.zproj_blk(wkv_in[:, :, i * 128:(i + 1) * 128], 128, blk, pb, tw)
                self.act(sq, self.ps[pb][:, :], AF.Square, [self.pst[pb]], [tsq])
                self.cp("act", ckf[:, i, :], self.ps[pb][:, :], [self.pst[pb]], [tkf])
                self.mm(self.ps[7][:, :], self.ones[:, :], sq, i == 0, i == 1, [tsq], [self.pst[7]])
            self.rstd(rs, self.ps[6][:, :], 768.0, [self.pst[6]], [trs])
            for i in range(6):
                self.stt("dve", cqn[:, i, bs], cqf[:, i, :], self.colp[:, C_QNG + i:C_QNG + i + 1], rs,
                         ALU.mult, ALU.mult, [tcf, trs, tcp], [T_("b_cqn", blk)])
            trs2 = T_("b_rs2")
            self.rstd(rs2, self.ps[7][:, :], 256.0, [self.pst[7]], [trs2])
            for i in range(2):
                self.stt("dve", ckvn[:, i, bs], ckf[:, i, :], self.colp[:, C_KVNG + i:C_KVNG + i + 1], rs2,
                         ALU.mult, ALU.mult, [tkf, trs2, tcp], [T_("b_ckvn", blk)])
            for i in range(2):
                self.zproj_blk(wkr[i], 96, blk, 2 + i, tw)
            self.act(sqkr[R, bs], self.ps[2][R, :], AF.Square, [self.pst[2]], [T_("b_kr", blk)])
            tt1 = T_("b_t1")
            self.stt("dve", t1[R, :], self.ps[2][R, :], self.colp[R, C_GK:C_GK + 1], self.cosT[R, bs],
                     ALU.mult, ALU.mult, [self.pst[2], tcp], [tt1])
            self.stt("dve", t2[R, :], self.ps[3][R, :], self.colp[R, C_GKP:C_GKP + 1], self.sinT[R, bs],
                     ALU.mult, ALU.mult, [self.pst[3], tcp], [tt1])
            self.tt("pool", krr[R, bs], t1[R, :], t2[R, :], ALU.add, [tt1], [T_("b_kr", blk)])
        self.dump("b_cqn%d" % l, cqn, T_("b_cqn", 0))
        self.dump("b_ckvn%d" % l, ckvn, T_("b_ckvn", 0))
        self.dump("b_krr%d" % l, krr[R, :], T_("b_kr", 0))
        P.barrier()
        A.reset(m2)
        wqm = [A.alloc([6, 96], BF16) for _ in range(2)]
        wqp = [A.alloc([6, 96], BF16) for _ in range(2)]
        wkv = [A.alloc([2, 128], BF16) for _ in range(2)]
        wk = [w_[:, :, 0:64] for w_ in wkv]
        wv = [w_[:, :, 64:128] for w_ in wkv]
        qn = [A.alloc([T], BF16) for _ in range(2)]
        kn = [A.alloc([T], BF16) for _ in range(2)]
        vh = [A.alloc([16, 64], BF16) for _ in range(2)]
        sq = [A.alloc([512], BF16) for _ in range(2)]
        rs = [A.alloc([512], F32) for _ in range(2)]
        t1 = A.alloc([512], F32)
        t2 = A.alloc([512], F32)
        pT = [A.alloc([512], BF16) for _ in range(4)]
        rcs = [A.alloc([512], F32) for _ in range(2)]
        ots = [A.alloc([512], BF16) for _ in range(2)]
        scale = 96.0 ** -0.5
        def b_load(h_):
            s_ = h_ % 2
            twh_ = T_("b_wh", s_)
            self.wload(wqm[s_], self.dr["wq_m"][l, h_], twh_)
            self.wload(wqp[s_], self.dr["wq_p"][l, h_], twh_)
            self.wload(wkv[s_], self.dr["w_ukv"][l, h_], twh_)

        b_load(0)
        for h in range(8):
            s = h % 2
            twh = T_("b_wh", s)
            tq, tk, tv = T_("b_qn", s), T_("b_kn", s), T_("b_vh", s)
            Q = slice(0, 96)
            N_ = slice(0, 64)
            for half in range(2):
                pbv = 6 + half
                for t8 in range(8):
                    tt_ = half * 8 + t8
                    self.mmg(self.ps[pbv][:, t8 * 64:(t8 + 1) * 64],
                             [(ckvn[:, kt, tt_ * 128:(tt_ + 1) * 128], wv[s][:, kt, :]) for kt in range(2)],
                             [twh], self.pst[pbv])
                self.cp("dve", vh[s][:, half * 8:(half + 1) * 8, :],
                        self.ps[pbv][:, :].rearrange("p (a b) -> p a b", a=8), [self.pst[pbv]], [tv])

            def banks(blk):
                o = 0 if blk % 2 == 0 else 4
                return o, o + 1, o + 2, o + 3

            def prep_f1(blk):
                bs = slice(blk * 512, (blk + 1) * 512)
                bq, bp, _, bk = banks(blk)
                self.mmg(self.ps[bq][Q, :], [(wqm[s][:, kt, :], cqn[:, kt, bs]) for kt in range(6)], [twh], self.pst[bq])

            def prep_f2(blk):
                bs = slice(blk * 512, (blk + 1) * 512)
                bq, bp, _, bk = banks(blk)
                self.mmg(self.ps[bp][Q, :], [(wqp[s][:, kt, :], cqn[:, kt, bs]) for kt in range(6)], [twh], self.pst[bp])

            def prep_f3(blk):
                bs = slice(blk * 512, (blk + 1) * 512)
                bq, bp, _, bk = banks(blk)
                self.mmg(self.ps[bk][N_, :], [(wk[s][:, kt, :], ckvn[:, kt, bs]) for kt in range(2)], [twh], self.pst[bk])

            def prep_back(blk, nxt):
                bs = slice(blk * 512, (blk + 1) * 512)
                bq, bp, bsq, bk = banks(blk)
                tsq0, trs0 = T_("b_sq", 0), T_("b_rs", 0)
                tsq1, trs1 = T_("b_sq", 1), T_("b_rs", 1)
                self.act(sq[0][Q, :], self.ps[bq][Q, :], AF.Square, [self.pst[bq]], [tsq0])
                tsq1 = T_("b_sqk", blk)
                self.act(sqkr[N_, bs], self.ps[bk][N_, :], AF.Square, [self.pst[bk]], [tsq1])
                if nxt is not None:
                    prep_f1(nxt)
                self.mmg(self.ps[bsq][Q, :], [(self.ones[Q, 0:96], sq[0][Q, :])], [tsq0], self.pst[bsq])
                self.rstd(rs[0][Q, :], self.ps[bsq][Q, :], 96.0, [self.pst[bsq]], [trs0])
                if nxt is not None:
                    prep_f2(nxt)
                self.mmg(self.ps[bsq][Q, :], [(self.ones[Q, 0:96], sqkr[Q, bs])], [tsq1], self.pst[bsq])
                self.rstd(rs[1][Q, :], self.ps[bsq][Q, :], 96.0, [self.pst[bsq]], [trs1])
                if nxt is not None:
                    prep_f3(nxt)
                tt1 = T_("b_t1")
                self.stt("dve", t1[R, :], self.ps[bq][R, :], self.colp[R, C_GQ:C_GQ + 1], self.cosT[R, bs],
                         ALU.mult, ALU.mult, [self.pst[bq], tcp], [tt1])
                self.stt("dve", t2[R, :], self.ps[bp][R, :], self.colp[R, C_GQP:C_GQP + 1], self.sinT[R, bs],
                         ALU.mult, ALU.mult, [self.pst[bp], tcp], [tt1])
                self.stt("dve", qn[s][N_, bs], self.ps[bq][N_, :], self.colp[N_, C_GQ:C_GQ + 1], rs[0][N_, :],
                         ALU.mult, ALU.mult, [self.pst[bq], trs0, tcp], [tq])
                self.stt("dve", kn[s][N_, bs], self.ps[bk][N_, :], self.colp[N_, C_GK:C_GK + 1], rs[1][N_, :],
                         ALU.mult, ALU.mult, [self.pst[bk], trs1, tcp], [tk])
                self.tt("pool", t1[R, :], t1[R, :], t2[R, :], ALU.add, [tt1], [tt1])
                self.tt("pool", qn[s][R, bs], t1[R, :], rs[0][R, :], ALU.mult, [tt1, trs0], [tq])
                self.tt("pool", kn[s][R, bs], krr[R, bs], rs[1][R, :], ALU.mult, [trs1], [tk])

            prep_f1(0)
            prep_f2(0)
            prep_f3(0)
            for blk in range(NB):
                prep_back(blk, blk + 1 if blk + 1 < NB else None)
            if h == 0:
                self.dump("b_qn%d" % l, qn[0][0:96, :], tq)
                self.dump("b_kn%d" % l, kn[0][0:96, :], tk)
                self.dump("b_vh%d" % l, vh[0], tv)
            if h + 1 < 8:
                b_load(h + 1)
            ct, r0 = h // 2, (h % 2) * 64
            RR = slice(r0, r0 + 64)
            seq = [(b, j) for b in range(NB) for j in range(4 * b + 4)]

            def att_s(i):
                b, j = seq[i]
                jj = j - 4 * b
                c0 = 128 * jj if jj > 0 else 0
                pb = 4 + i % 2
                self.mm(self.ps[pb][:, c0:512], kn[s][0:96, j * 128:(j + 1) * 128],
                        qn[s][0:96, b * 512 + c0:(b + 1) * 512], True, jj < 0, [tq, tk], [self.pst[pb]])
                if jj >= 0:
                    self.mm(self.ps[pb][:, c0:c0 + 128], self.negI[:, :], self.slow[:, :], False, True,
                            [tcst], [self.pst[pb]])

            def att_rest(i):
                b, j = seq[i]
                nj = 4 * b + 4
                jj = j - 4 * b
                c0 = 128 * jj if jj > 0 else 0
                pb = 4 + i % 2
                p = pT[i % 4]
                tp = T_("b_pT", i % 4)
                po, pd = (6, 7) if b % 2 == 0 else (2, 3)
                self.act(p[:, c0:512], self.ps[pb][:, c0:512], AF.Exp, [self.pst[pb]], [tp], scale=scale)
                self.mm(self.ps[po][RR, c0:512], vh[s][:, j, :], p[:, c0:512], j == 0, j == nj - 1,
                        [tv, tp], [self.pst[po]])
                self.mm(self.ps[pd][RR, c0:512], self.ones[:, 0:64], p[:, c0:512], j == 0, j == nj - 1,
                        [tp], [self.pst[pd]])
                if j == nj - 1:
                    trc, tot = T_("b_rc", b % 2), T_("b_ot", b % 2)
                    rc_, ot_ = rcs[b % 2], ots[b % 2]
                    self.P.op("dve", lambda e, o=rc_[RR, :], i_=self.ps[pd][RR, :]: e.reciprocal(out=o, in_=i_),
                              reads=[self.pst[pd]], writes=[trc])
                    self.tt("dve", ot_[RR, :], self.ps[po][RR, :], rc_[RR, :], ALU.mult, [self.pst[po], trc], [tot])
                    sl = ya[RR, 2 + ct, b * 512:(b + 1) * 512]
                    self.tt("pool", sl, sl, ot_[RR, :], ALU.mult, [tot], [T_("yb", ct, b)])

            att_s(0)
            for i in range(len(seq)):
                if i + 1 < len(seq):
                    att_s(i + 1)
                att_rest(i)
        self.dump("yb%d" % l, ya[:, 2:6, :], T_("yb", 0, 0))

    def stage_p2(self, l, src, dst):
        A, T_, P = self.A, self.T, self.P
        ya = self.yall
        merged = A.alloc([8, T], BF16)
        m2 = A.mark()
        wls = [A.alloc([8, 4, 256], BF16) for _ in range(2)]
        wbs = [A.alloc([10, 256], BF16) for _ in range(2)]
        g = [A.alloc([512], F32) for _ in range(2)]
        macc = A.alloc([512], F32)
        t2 = [A.alloc([512], F32) for _ in range(2)]
        brk = {0: [0, 1], 1: [2, 3, 4, 5], 2: [6, 7], 3: [8, 9]}
        brw = ["w_br_a", "w_br_b", "w_br_c", "w_br_m"]
        tcp = self.tcp
        def p2_load(j2):
            wl_, wb_ = wls[j2 % 2], wbs[j2 % 2]
            twl_, twb_ = T_("p_wl", j2 % 2), T_("p_wb", j2 % 2)
            for br in range(4):
                c0 = OFF["mrg"] + br * 1024 + j2 * 256
                self.wload(wl_[:, :, br, :], self.dr["w_in"][l, WIN_G[c0]], twl_)
            self.wload(wb_[:, 0:6, :], self.dr["w_br"][l, j2, :, 0:6, :], twb_)
            self.wload(wb_[:, 6:10, :], self.dr["w_br"][l, j2, :, 6:10, :], twb_)

        p2_load(0)
        for j2 in range(4):
            if j2 + 1 < 4:
                p2_load(j2 + 1)
            wl, wb = wls[j2 % 2], wbs[j2 % 2]
            twl, twb = T_("p_wl", j2 % 2), T_("p_wb", j2 % 2)
            for jj in range(2):
                j = 2 * j2 + jj
                js = slice(jj * 128, (jj + 1) * 128)
                for blk in range(NB):
                    bs = slice(blk * 512, (blk + 1) * 512)
                    for br in range(4):
                        pl, pp = self.ps[2 * (br % 2)], self.ps[2 * (br % 2) + 1]
                        tpl, tpp = self.pst[2 * (br % 2)], self.pst[2 * (br % 2) + 1]
                        self.mmg(pl[:, :], [(wl[:, kt, br, js], self.hT[:, kt, bs]) for kt in range(8)], [twl], tpl)
                        self.mmg(pp[:, :], [(wb[:, kt, js], ya[:, kt, bs]) for kt in brk[br]], [twb], tpp)
                        gg, tg = g[br % 2], T_("p_g", br % 2)
                        self.act(gg, pl[:, :], AF.Sigmoid, [tpl, tcp], [tg],
                                 bias=self.colp[:, C_BM + br * 8 + j:C_BM + br * 8 + j + 1])
                        tm = T_("p_m")
                        if br == 0:
                            self.tt("dve", macc, pp[:, :], gg, ALU.mult, [tpp, tg], [tm])
                        else:
                            tt2 = T_("p_t2", br % 2)
                            self.tt("dve", t2[br % 2], pp[:, :], gg, ALU.mult, [tpp, tg], [tt2])
                            if br < 3:
                                self.tt("dve", macc, macc, t2[br % 2], ALU.add, [tt2, tm], [tm])
                            else:
                                self.tt("dve", merged[:, j, bs], macc, t2[br % 2], ALU.add, [tt2, tm],
                                        [T_("p_mg", blk)])
        self.dump("merged%d" % l, merged, T_("p_mg", 0))
        P.barrier()
        A.reset(m2)
        wo = A.alloc([8, D], BF16)
        two = T_("p_wo")
        for i in range(4):
            self.wload(wo[:, :, i * 256:(i + 1) * 256],
                       self.dr["w_out"][l, i], two)
        xt = A.alloc([8, 512], F32)
        xo = A.alloc([8, 512], F32)
        txt, txo = T_("p_xt"), T_("p_xo")
        for blk in range(NB):
            bs = slice(blk * 512, (blk + 1) * 512)
            for hh in range(2):
                self.dma(xt[:, hh * 4:(hh + 1) * 4, :], src[blk, :, hh * 4:(hh + 1) * 4, :], [], [txt])
            for d2 in range(8):
                pb = 4 + d2 % 2
                self.mmg(self.ps[pb][:, :], [(wo[:, kt, d2 * 128:(d2 + 1) * 128], merged[:, kt, bs]) for kt in range(8)],
                         [two, T_("p_mg", blk)], self.pst[pb])
                self.tt("dve", xo[:, d2, :], self.ps[pb][:, :], xt[:, d2, :], ALU.add, [self.pst[pb], txt], [txo])
            for hh in range(2):
                op = self.dma(dst[blk, :, hh * 4:(hh + 1) * 4, :], xo[:, hh * 4:(hh + 1) * 4, :], [txo], [], q="act")
                if dst is self.outT:
                    self.finals.append(op)


def make_consts():
    c = np.zeros((128, NCONST), np.float32)
    s = np.arange(128)
    c[:, K_TRI:K_TRI + 128] = (s[:, None] <= s[None, :]).astype(np.float32)
    c[:, K_IOT:K_IOT + 128] = s[None, :].astype(np.float32)
    c[:, K_IOS] = s.astype(np.float32)
    half = 16
    inv = (10000.0 ** (-np.arange(half, dtype=np.float32) / half)).astype(np.float32)
    for r in range(64, 96):
        c[r, K_INVF] = inv[(r - 64) % 16]
        c[r, K_SGN] = -1.0 if (r - 64) < 16 else 1.0
    for r in range(128):
        c[r, K_GM + r // 16] = 1.0
    return c


def host_prep(inp):
    f = lambda k: np.asarray(inp[k], dtype=np.float32)
    perm = np.array(PERM)
    sh = {}

    def rows_t(w):
        r, c = w.shape
        return np.ascontiguousarray(w.reshape(r // 128, 128, c).transpose(1, 0, 2))

    w_in = f("w_in")
    sh["w_in"] = np.stack([np.stack([rows_t(w_in[l][:, c0:c0 + 256]) for c0 in WIN_C0]) for l in range(L)])
    krm = np.zeros((L, D, 96), np.float32)
    krp = np.zeros((L, D, 96), np.float32)
    krm[:, :, 64:96] = w_in[:, :, OFF["kr"]:OFF["kr"] + 32]
    krp[:, :, 64:96] = w_in[:, :, OFF["kr"] + perm]
    sh["w_krm"] = np.stack([rows_t(krm[l]) for l in range(L)])
    sh["w_krp"] = np.stack([rows_t(krp[l]) for l in range(L)])
    wuq = f("b_w_uq").reshape(L, 768, 8, 96)
    wqp = np.zeros((L, 768, 8, 96), np.float32)
    wqp[:, :, :, 64:96] = wuq[:, :, :, 64 + perm]
    sh["wq_m"] = np.stack([np.stack([rows_t(wuq[l][:, h, :]) for h in range(8)]) for l in range(L)])
    sh["wq_p"] = np.stack([np.stack([rows_t(wqp[l][:, h, :]) for h in range(8)]) for l in range(L)])
    ukv = f("b_w_ukv")
    sh["w_ukv"] = np.stack([np.stack([rows_t(ukv[l][:, h * 128:(h + 1) * 128]) for h in range(8)]) for l in range(L)])
    sh["a_w_sT"] = np.ascontiguousarray(f("a_w_s").transpose(0, 3, 1, 2))
    c_re, c_im = f("c_c_re"), f("c_c_im")
    cre = np.zeros((L, 8, 128, 128), np.float32)
    cim = np.zeros((L, 8, 128, 128), np.float32)
    for j in range(8):
        for gl in range(2):
            g = 2 * j + gl
            col0 = 16 * (g % 8)
            cre[:, j, gl * 64:(gl + 1) * 64, col0:col0 + 16] = c_re[:, g].transpose(0, 2, 1)
            cim[:, j, gl * 64:(gl + 1) * 64, col0:col0 + 16] = c_im[:, g].transpose(0, 2, 1)
    sh["c_reT"] = np.ascontiguousarray(cre.reshape(L, 2, 4, 128, 128).transpose(0, 1, 3, 2, 4))
    sh["c_imT"] = np.ascontiguousarray(cim.reshape(L, 2, 4, 128, 128).transpose(0, 1, 3, 2, 4))
    sh["c_w_glu"] = np.stack([rows_t(f("c_w_glu")[l]) for l in range(L)])
    mkv = f("m_w_kv")
    sh["m_w_kv"] = np.stack([np.stack([rows_t(mkv[l][:, i * 256:(i + 1) * 256]) for i in range(2)]) for l in range(L)])
    wbr = np.concatenate([f("w_br_a"), f("w_br_b"), f("w_br_c"), f("w_br_m")], axis=1)
    sh["w_br"] = np.stack([np.stack([rows_t(wbr[l][:, i * 256:(i + 1) * 256]) for i in range(4)]) for l in range(L)])
    wo = f("w_out")
    sh["w_out"] = np.stack([np.stack([rows_t(wo[l][:, i * 256:(i + 1) * 256]) for i in range(4)]) for l in range(L)])
    cp = np.zeros((L, 128, NCOL), np.float32)
    cp[:, :, C_NG:C_NG + 8] = f("norm_g").reshape(L, 8, 128).transpose(0, 2, 1)
    cp[:, :, C_MNG:C_MNG + 8] = f("m_norm_g").reshape(L, 8, 128).transpose(0, 2, 1)
    cp[:, :, C_QNG:C_QNG + 6] = f("b_q_norm_g").reshape(L, 6, 128).transpose(0, 2, 1)
    cp[:, :, C_KVNG:C_KVNG + 2] = f("b_kv_norm_g").reshape(L, 2, 128).transpose(0, 2, 1)
    cp[:, :, C_BM:C_BM + 32] = f("b_merge").reshape(L, 32, 128).transpose(0, 2, 1)
    gq, gk = f("b_qk_g_q"), f("b_qk_g_k")
    cp[:, 0:96, C_GQ] = gq
    cp[:, 64:96, C_GQP] = gq[:, 64 + perm]
    cp[:, 0:96, C_GK] = gk
    cp[:, 64:96, C_GKP] = gk[:, 64 + perm]
    cp[:, :, C_MGQ] = np.tile(f("m_qk_g_q"), (1, 2))
    cp[:, :, C_MGK] = np.tile(f("m_qk_g_k"), (1, 2))
    cp[:, :, C_CD:C_CD + 2] = f("c_d").reshape(L, 2, 128).transpose(0, 2, 1)
    cp[:, :, C_BGLU:C_BGLU + 2] = f("c_b_glu").reshape(L, 2, 128).transpose(0, 2, 1)
    a_re, a_im, ldt = f("c_a_re"), f("c_a_im"), f("c_log_dt")
    cp[:, :, C_SRE:C_SRE + 8] = a_re.reshape(L, 8, 128).transpose(0, 2, 1)
    cp[:, :, C_SIM:C_SIM + 8] = a_im.reshape(L, 8, 128).transpose(0, 2, 1)
    ldt_rep = np.repeat(ldt[:, :, None], 64, axis=2)
    cp[:, :, C_SDT:C_SDT + 8] = ldt_rep.reshape(L, 8, 128).transpose(0, 2, 1)
    abs_ = f("a_b_s")
    for ct in range(2):
        for gl in range(2):
            cp[:, gl * 64:(gl + 1) * 64, C_ABS + ct * 128:C_ABS + (ct + 1) * 128] = abs_[:, 2 * ct + gl][:, None, :]
    b_re, b_im = f("c_b_re"), f("c_b_im")
    for ct in range(2):
        base = C_ROW + ct * 320
        for g8 in range(8):
            g = 8 * ct + g8
            rows = slice(16 * g8, 16 * g8 + 16)
            cp[:, rows, base + 0:base + 64] = a_re[:, g][:, None, :]
            cp[:, rows, base + 64:base + 128] = a_im[:, g][:, None, :]
            cp[:, rows, base + 128:base + 192] = ldt[:, g][:, None, None]
            cp[:, rows, base + 192:base + 256] = b_re[:, g].transpose(0, 2, 1)
            cp[:, rows, base + 256:base + 320] = b_im[:, g].transpose(0, 2, 1)
    sh["colpack"] = cp
    rp = np.zeros((L, 128, NROW), np.float32)
    rp[:, :, R_ANG:R_ANG + 256] = f("a_norm_g")[:, None, :]
    rp[:, :, R_SRE:R_SRE + 1024] = a_re.reshape(L, 1, 1024)
    rp[:, :, R_SIM:R_SIM + 1024] = a_im.reshape(L, 1, 1024)
    rp[:, :, R_SDT:R_SDT + 1024] = ldt_rep.reshape(L, 1, 1024)
    sh["rowpack"] = rp
    sh["consts"] = make_consts()
    x = f("x")
    mem = f("mem")
    pos = np.asarray(inp["positions"]).astype(np.int32)
    per_core = []
    for b in range(8):
        d = dict(sh)
        d["xT"] = tile_x(x[b])
        d["memT"] = np.ascontiguousarray(mem[b].T.reshape(8, 128, 1, 256).transpose(2, 1, 0, 3))
        d["pos"] = np.ascontiguousarray(pos[b][None, :])
        per_core.append(d)
    return per_core


def tile_x(xb):
    return np.ascontiguousarray(xb.T.reshape(8, 128, NB, 512).transpose(2, 1, 0, 3))


def untile_x(t):
    return np.ascontiguousarray(t.transpose(2, 1, 0, 3).reshape(D, T).T)


_CACHE = {}
LAYER_KEYS = ("colpack", "rowpack", "w_in", "w_krm", "w_krp", "wq_m", "wq_p", "w_ukv", "a_w_sT", "c_reT", "c_imT",
              "c_w_glu", "m_w_kv", "w_br", "w_out")
FUSED = True


def kernel(**inputs):
    in_maps = host_prep(inputs)
    if FUSED:
        if "nc" not in _CACHE:
            _CACHE["nc"] = Builder(nlayers=L).build()
        res = run_bass_kernel_spmd(_CACHE["nc"], in_maps, core_ids=list(range(8)))
        return np.stack([untile_x(r["outT"]) for r in res.results], axis=0).astype(np.float32)
    if "nc1" not in _CACHE:
        _CACHE["nc1"] = Builder(nlayers=1).build()
    nc = _CACHE["nc1"]
    xs = [m["xT"] for m in in_maps]
    for l in range(L):
        maps = []
        for c in range(8):
            d = dict(in_maps[c])
            for k in LAYER_KEYS:
                a = in_maps[c][k]
                d[k] = np.ascontiguousarray(np.concatenate([a[l:l + 1], a[l:l + 1]], axis=0))
            d["xT"] = xs[c]
            maps.append(d)
        res = run_bass_kernel_spmd(nc, maps, core_ids=list(range(8)))
        xs = [np.ascontiguousarray(r["outT"]) for r in res.results]
    return np.stack([untile_x(t) for t in xs], axis=0).astype(np.float32)
```
